# Optimizing a Trainium2 kernel written in Bass

```python
import math
import jax, jax.numpy as jnp
from jax import lax
import numpy as np

D_MODEL = 2048
BATCH = 1
SEQ = 8192
DEPTH = 4

CHUNK = 64
QBLOCK = 128
DA_HEADS = 8
DA_QK = 64
DA_V = 2 * DA_QK
MLA_HEADS = 8
MLA_Q_LORA = 512
MLA_KV_LORA = 256
MLA_NOPE = 128
MLA_ROPE = 64
MLA_V = 128
ROPE_BASE = 10000.0
SB_HEADS = 8
SB_DIM = 128
BRANCH_WIDTH = 1024
N_BRANCHES = 3
T5_BUCKETS = 32
T5_MAX_DIST = 128
D_FF = 4 * D_MODEL
N_MOD = 6
EPS = 1e-6
NEG_INF = -1e30

IN_SIZES = (
    DA_HEADS * 2 * DA_QK,
    DA_HEADS * 2 * DA_QK,
    DA_HEADS * DA_V,
    MLA_Q_LORA,
    MLA_KV_LORA,
    MLA_ROPE,
    SB_HEADS * SB_DIM,
    SB_HEADS * SB_DIM,
    SB_HEADS * SB_DIM,
    N_BRANCHES * D_MODEL,
)
IN_COLS = sum(IN_SIZES)

kernel_name = "hybrid_gated_diffattn_mla_stickbreak_block"


def _rms(x, g):
    xf = x.astype(jnp.float32)
    y = xf * lax.rsqrt(jnp.mean(xf * xf, axis=-1, keepdims=True) + EPS)
    return (y * g.astype(jnp.float32)).astype(x.dtype)


def _split_cols(proj):
    parts, o = [], 0
    for n in IN_SIZES:
        parts.append(proj[..., o:o + n])
        o += n
    return parts


def _sweep(block_fn, seq):
    out = lax.map(block_fn, jnp.arange(seq // QBLOCK))
    nb, b, q, h, e = out.shape
    return jnp.moveaxis(out, 0, 1).reshape(b, nb * q, h, e)


def _chunk_mask(qpos, kpos):
    return (kpos[None, :] // CHUNK) <= (qpos[:, None] // CHUNK)


def _t5_bucket(rel):
    nb = T5_BUCKETS // 2
    max_exact = nb // 2
    n = jnp.abs(rel)
    large = max_exact + (jnp.log(jnp.maximum(n, 1).astype(jnp.float32) / max_exact)
                         / math.log(T5_MAX_DIST / max_exact) * (nb - max_exact)).astype(jnp.int32)
    large = jnp.minimum(large, nb - 1)
    return jnp.where(rel > 0, nb, 0) + jnp.where(n < max_exact, n, large)


def _rope(x, pos):
    half = x.shape[-1] // 2
    inv = ROPE_BASE ** (-jnp.arange(half, dtype=jnp.float32) / half)
    ang = pos.astype(jnp.float32)[:, None] * inv[None, :]
    cos = jnp.cos(ang)[None, :, None, :]
    sin = jnp.sin(ang)[None, :, None, :]
    xf = x.astype(jnp.float32)
    x1, x2 = xf[..., :half], xf[..., half:]
    return jnp.concatenate([x1 * cos - x2 * sin, x1 * sin + x2 * cos], axis=-1).astype(x.dtype)


def _diff_attention(q, k, v, t5_bias, lam, lam_init, sub_g):
    seq = q.shape[1]
    kpos = jnp.arange(seq)
    scale = DA_QK ** -0.5
    table = t5_bias.astype(jnp.float32)

    def block(i):
        start = i * QBLOCK
        qb = lax.dynamic_slice_in_dim(q, start, QBLOCK, axis=1)
        qpos = start + jnp.arange(QBLOCK)
        logits = jnp.einsum('bqhcd,bkhcd->bhcqk', qb, k).astype(jnp.float32) * scale
        bias = table[_t5_bucket(kpos[None, :] - qpos[:, None])]
        logits = logits + jnp.transpose(bias, (2, 0, 1))[None, :, None]
        p = jax.nn.softmax(jnp.where(_chunk_mask(qpos, kpos), logits, NEG_INF), axis=-1)
        a = p[:, :, 0] - lam * p[:, :, 1]
        return jnp.einsum('bhqk,bkhe->bqhe', a.astype(v.dtype), v)

    o = _sweep(block, seq)
    return _rms(o, sub_g) * (1.0 - lam_init)


def _chunk_softmax_attention(q, k, v, scale):
    seq = q.shape[1]
    kpos = jnp.arange(seq)

    def block(i):
        start = i * QBLOCK
        qb = lax.dynamic_slice_in_dim(q, start, QBLOCK, axis=1)
        qpos = start + jnp.arange(QBLOCK)
        logits = jnp.einsum('bqhd,bkhd->bhqk', qb, k).astype(jnp.float32) * scale
        p = jax.nn.softmax(jnp.where(_chunk_mask(qpos, kpos), logits, NEG_INF), axis=-1)
        return jnp.einsum('bhqk,bkhe->bqhe', p.astype(v.dtype), v)

    return _sweep(block, seq)


def _mla(c_q, c_kv, k_pe, q_norm_g, kv_norm_g, w_q_up, w_kv_up, qk_g):
    b, s, _ = c_q.shape
    pos = jnp.arange(s)
    q = (_rms(c_q, q_norm_g) @ w_q_up).reshape(b, s, MLA_HEADS, MLA_NOPE + MLA_ROPE)
    kv = (_rms(c_kv, kv_norm_g) @ w_kv_up).reshape(b, s, MLA_HEADS, MLA_NOPE + MLA_V)
    v = kv[..., MLA_NOPE:]
    q_nope = _rms(q[..., :MLA_NOPE], qk_g[0, :MLA_NOPE])
    q_pe = _rope(_rms(q[..., MLA_NOPE:], qk_g[0, MLA_NOPE:]), pos)
    k_nope = _rms(kv[..., :MLA_NOPE], qk_g[1, :MLA_NOPE])
    k_pe = _rope(_rms(k_pe.reshape(b, s, 1, MLA_ROPE), qk_g[1, MLA_NOPE:]), pos)
    qh = jnp.concatenate([q_nope, q_pe], axis=-1)
    kh = jnp.concatenate([k_nope, jnp.broadcast_to(k_pe, (b, s, MLA_HEADS, MLA_ROPE))], axis=-1)
    return _chunk_softmax_attention(qh, kh, v, (MLA_NOPE + MLA_ROPE) ** -0.5)


def _stick_breaking(q, k, v):
    seq = q.shape[1]
    kpos = jnp.arange(seq)
    scale = SB_DIM ** -0.5

    def block(i):
        start = i * QBLOCK
        qb = lax.dynamic_slice_in_dim(q, start, QBLOCK, axis=1)
        qpos = start + jnp.arange(QBLOCK)
        z = jnp.einsum('bqhd,bkhd->bhqk', qb, k).astype(jnp.float32) * scale
        mask = kpos[None, :] < qpos[:, None]
        log_1m = jnp.where(mask, jax.nn.log_sigmoid(-z), 0.0)
        between = lax.cumsum(log_1m, axis=3, reverse=True) - log_1m
        w = jnp.where(mask, jnp.exp(jax.nn.log_sigmoid(z) + between), 0.0)
        return jnp.einsum('bhqk,bkhd->bqhd', w.astype(v.dtype), v)

    return _sweep(block, seq)


def setup_inputs(seed: int = 0) -> dict:
    key = jax.random.key(seed)
    ks = jax.random.split(key, 24)
    f32 = jnp.float32

    def nrm(k, shape, std):
        return jax.random.normal(k, shape, f32) * std

    def gain(k, shape):
        return 1.0 + 0.02 * jax.random.normal(k, shape, f32)

    return {
        "x": nrm(ks[0], (BATCH, SEQ, D_MODEL), 1.0),
        "c": nrm(ks[1], (BATCH, D_MODEL), 1.0),
        "w_ada": nrm(ks[2], (DEPTH, D_MODEL, N_MOD * D_MODEL), 0.3 * D_MODEL ** -0.5),
        "b_ada": nrm(ks[3], (DEPTH, N_MOD * D_MODEL), 0.02),
        "norm_mix_g": gain(ks[4], (DEPTH, D_MODEL)),
        "norm_mlp_g": gain(ks[5], (DEPTH, D_MODEL)),
        "w_in": nrm(ks[6], (DEPTH, D_MODEL, IN_COLS), D_MODEL ** -0.5),
        "diff_qk_g": gain(ks[7], (DEPTH, 2, DA_QK)),
        "diff_lambda": nrm(ks[8], (DEPTH, 4, DA_QK), 0.1),
        "diff_subln_g": gain(ks[9], (DEPTH, DA_V)),
        "t5_bias": nrm(ks[10], (T5_BUCKETS, DA_HEADS), 0.5),
        "mla_q_norm_g": gain(ks[11], (DEPTH, MLA_Q_LORA)),
        "mla_kv_norm_g": gain(ks[12], (DEPTH, MLA_KV_LORA)),
        "w_q_up": nrm(ks[13], (DEPTH, MLA_Q_LORA, MLA_HEADS * (MLA_NOPE + MLA_ROPE)), MLA_Q_LORA ** -0.5),
        "w_kv_up": nrm(ks[14], (DEPTH, MLA_KV_LORA, MLA_HEADS * (MLA_NOPE + MLA_V)), MLA_KV_LORA ** -0.5),
        "mla_qk_g": gain(ks[15], (DEPTH, 2, MLA_NOPE + MLA_ROPE)),
        "w_branch": nrm(ks[16], (DEPTH, N_BRANCHES, BRANCH_WIDTH, D_MODEL), BRANCH_WIDTH ** -0.5),
        "w_out": nrm(ks[17], (DEPTH, D_MODEL, D_MODEL), D_MODEL ** -0.5),
        "w_mlp_in": nrm(ks[18], (DEPTH, D_MODEL, D_FF), D_MODEL ** -0.5),
        "w_mlp_out": nrm(ks[19], (DEPTH, D_FF, D_MODEL), D_FF ** -0.5),
    }


def reference(x, c, w_ada, b_ada, norm_mix_g, norm_mlp_g, w_in, diff_qk_g, diff_lambda,
              diff_subln_g, t5_bias, mla_q_norm_g, mla_kv_norm_g, w_q_up, w_kv_up, mla_qk_g,
              w_branch, w_out, w_mlp_in, w_mlp_out):
    b, s, d = x.shape
    for l in range(DEPTH):
        mod = c @ w_ada[l] + b_ada[l]
        sh1, sc1, g1, sh2, sc2, g2 = jnp.split(mod, N_MOD, axis=-1)

        h = _rms(x, norm_mix_g[l]) * (1.0 + sc1[:, None]) + sh1[:, None]
        (da_q, da_k, da_v, mla_cq, mla_ckv, mla_kpe,
         sb_q, sb_k, sb_v, gates) = _split_cols(h @ w_in[l])

        lam_init = 0.8 - 0.6 * math.exp(-0.3 * l)
        lp = diff_lambda[l].astype(jnp.float32)
        lam = jnp.exp(jnp.sum(lp[0] * lp[1])) - jnp.exp(jnp.sum(lp[2] * lp[3])) + lam_init
        qa = _rms(da_q.reshape(b, s, DA_HEADS, 2, DA_QK), diff_qk_g[l, 0])
        ka = _rms(da_k.reshape(b, s, DA_HEADS, 2, DA_QK), diff_qk_g[l, 1])
        va = da_v.reshape(b, s, DA_HEADS, DA_V)
        ya = _diff_attention(qa, ka, va, t5_bias, lam, lam_init, diff_subln_g[l])

        yb = _mla(mla_cq, mla_ckv, mla_kpe, mla_q_norm_g[l], mla_kv_norm_g[l],
                  w_q_up[l], w_kv_up[l], mla_qk_g[l])

        yc = _stick_breaking(sb_q.reshape(b, s, SB_HEADS, SB_DIM),
                             sb_k.reshape(b, s, SB_HEADS, SB_DIM),
                             sb_v.reshape(b, s, SB_HEADS, SB_DIM))

        branches = jnp.stack([ya.reshape(b, s, BRANCH_WIDTH),
                              yb.reshape(b, s, BRANCH_WIDTH),
                              yc.reshape(b, s, BRANCH_WIDTH)], axis=2)
        up = jnp.einsum('bsne,ned->bsnd', branches, w_branch[l])
        gate = jax.nn.sigmoid(gates.reshape(b, s, N_BRANCHES, d).astype(jnp.float32)).astype(x.dtype)
        merged = jnp.sum(gate * up, axis=2)
        x = x + g1[:, None] * (merged @ w_out[l])

        h2 = _rms(x, norm_mlp_g[l]) * (1.0 + sc2[:, None]) + sh2[:, None]
        x = x + g2[:, None] * (jnp.square(jax.nn.relu(h2 @ w_mlp_in[l])) @ w_mlp_out[l])
    return x
```

```python
import os, time
import numpy as np
import ml_dtypes
import concourse.bass as bass
import concourse.mybir as mybir
from concourse.bass_utils import run_bass_kernel_spmd

F32 = mybir.dt.float32
BF16 = mybir.dt.bfloat16
AF = mybir.ActivationFunctionType
ALU = mybir.AluOpType


class FW:
    def __init__(self, nc):
        self.nc = nc
        self.eng = {'pe': nc.tensor, 'act': nc.scalar, 'dve': nc.vector, 'pool': nc.gpsimd, 'sp': nc.sync}
        self.sem = {}
        self.cnt = {}
        for k in ['pe', 'act', 'dve', 'pool']:
            self.sem[k] = nc.alloc_semaphore(name='s_' + k)
            self.cnt[k] = 0
        self.seen = {k: {} for k in self.eng}
        self.lastw = {}
        self.reads = {}
        self.dsem = {}

    def _wait(self, e, ev):
        if ev is None:
            return
        k, v = ev
        if e == 'pe' and k == 'pe':
            return
        if self.seen[e].get(k, 0) >= v:
            return
        self.eng[e].wait_ge(self.sem[k], v)
        self.seen[e][k] = v

    def deps(self, e, R, W):
        for t in R:
            self._wait(e, self.lastw.get(t))
        for t in W:
            self._wait(e, self.lastw.get(t))
            for ev in self.reads.get(t, []):
                self._wait(e, ev)

    def commit(self, ev, R, W):
        for t in R:
            self.reads.setdefault(t, []).append(ev)
        for t in W:
            self.lastw[t] = ev
            self.reads[t] = []

    def op(self, e, R, W, fn, inc=True):
        self.deps(e, R, W)
        ins = fn()
        if inc:
            self.cnt[e] += 1
            ins.then_inc(self.sem[e], 1)
            ev = (e, self.cnt[e])
        else:
            ev = (e, self.cnt[e] + 1)
        self.commit(ev, R, W)
        return ins

    def dma(self, q, R, W, out, in_, semtok=None, **kw):
        st = semtok if semtok is not None else (W[0] if W else R[0])
        if st not in self.dsem:
            key = 'd_' + str(st)
            self.sem[key] = self.nc.alloc_semaphore(name=key)
            self.cnt[key] = 0
            self.dsem[st] = key
        key = self.dsem[st]
        self.deps(q, R, W)
        ins = self.eng[q].dma_start(out=out, in_=in_, **kw)
        self.cnt[key] += 16
        ins.then_inc(self.sem[key], 16)
        ev = (key, self.cnt[key])
        self.commit(ev, R, W)
        return ins

    def finish(self, toks, e='sp'):
        for t in toks:
            self._wait(e, self.lastw.get(t))

    def barrier(self):
        for e in self.eng:
            for k in self.sem:
                if self.cnt[k] > 0 and not (e == 'pe' and k == 'pe') and self.seen[e].get(k, 0) < self.cnt[k]:
                    self.eng[e].wait_ge(self.sem[k], self.cnt[k])
                    self.seen[e][k] = self.cnt[k]

    def finish_all(self, e='sp'):
        for k in self.sem:
            if k.startswith('d_') and self.cnt[k] > 0 and self.seen[e].get(k, 0) < self.cnt[k]:
                self.eng[e].wait_ge(self.sem[k], self.cnt[k])
                self.seen[e][k] = self.cnt[k]

    def finish_dma(self, semtok, e='sp'):
        if semtok not in self.dsem:
            return
        key = self.dsem[semtok]
        self.eng[e].wait_ge(self.sem[key], self.cnt[key])


AX = mybir.AxisListType
NEG = -1.0e30
EPS = 1e-6


class Pool_:
    def __init__(self, nc, name, n, shape, dt):
        self.tiles = [nc.alloc_sbuf_tensor(f"{name}{i}", shape, dt) for i in range(n)]
        self.names = [f"{name}{i}" for i in range(n)]
        self.i = 0

    def get(self):
        t, n = self.tiles[self.i], self.names[self.i]
        self.i = (self.i + 1) % len(self.tiles)
        return t, n


def emit_attn(nc, fw, S, d, ps, cst_pool=None):
    NQB = S // 512
    NKT = S // 128
    sb = nc.alloc_sbuf_tensor
    Q1 = sb("aQ1", [128, S], BF16)
    Q2 = sb("aQ2", [64, S], BF16)
    K1 = sb("aK1", [128, S], BF16)
    K2 = sb("aK2", [64, S], BF16)
    V = sb("aV", [128, NKT, 128], BF16)
    OUT = sb("aOUT", [128, S], BF16)
    ones = sb("aones", [128, 128], BF16)
    mtri = sb("amtri", [128, 128], BF16)
    mones = sb("amones", [128, 128], BF16)
    biasA = sb("abiasA", [128, 5, 512], F32)
    maskB = sb("amaskB", [128, 4, 512], F32)
    maskC = sb("amaskC", [128, 4, 512], BF16)
    lamp = sb("alamp", [128, 256], F32)
    cst = sb("acst", [128, 4], F32)
    b15 = sb("ab15", [128, 1], F32)
    gsub = sb("agsub", [128, 1], F32)
    sm = sb("asm", [128, 16], F32)
    carry = sb("acarry", [128, 512], F32)
    Pp = Pool_(nc, "aP", 6, [128, 512], BF16)
    Tp = Pool_(nc, "aT", 4, [128, 512], F32)
    Lp_ = Pool_(nc, "aL", 3, [128, 512], BF16)

    fw.dma('sp', [], ['ones'], ones[:], d['ones'])
    fw.dma('sp', [], ['mtri'], mtri[:], d['mtri'])
    fw.dma('sp', [], ['biasA'], biasA[:], d['biasA'].rearrange("o p q -> p o q"))
    fw.dma('sp', [], ['maskB'], maskB[:], d['maskB'].rearrange("o p q -> p o q"))
    fw.dma('sp', [], ['maskC'], maskC[:], d['maskC'].rearrange("o p q -> p o q"))
    fw.dma('sp', [], ['lamp'], lamp[:], d['lamp'])
    fw.dma('sp', [], ['cst'], cst[:], d['cst'])
    fw.dma('sp', [], ['b15'], b15[:], d['b15'])
    fw.dma('sp', [], ['gsub'], gsub[:], d['gsub'])
    for o in range(4):
        fw.op('dve', ['biasA', 'maskB'], ['biasA'], lambda: nc.vector.tensor_tensor(biasA[:, o + 1, :], biasA[:, o + 1, :], maskB[:, o, :], ALU.add))
    T0, t0n = Tp.get()
    fw.op('dve', ['lamp'], [t0n], lambda: nc.vector.tensor_tensor(T0[:, 0:64], lamp[:, 0:64], lamp[:, 64:128], ALU.mult))
    fw.op('dve', [t0n], ['sm'], lambda: nc.vector.reduce_sum(sm[:, 0:1], T0[:, 0:64], AX.X))
    fw.op('dve', ['lamp'], [t0n], lambda: nc.vector.tensor_tensor(T0[:, 0:64], lamp[:, 128:192], lamp[:, 192:256], ALU.mult))
    fw.op('dve', [t0n], ['sm'], lambda: nc.vector.reduce_sum(sm[:, 1:2], T0[:, 0:64], AX.X))
    fw.op('act', ['sm'], ['sm'], lambda: nc.scalar.activation(sm[:, 2:4], sm[:, 0:2], AF.Exp))
    fw.op('dve', ['sm'], ['sm'], lambda: nc.vector.tensor_tensor(sm[:, 4:5], sm[:, 3:4], sm[:, 2:3], ALU.subtract))
    fw.op('dve', ['sm', 'cst'], ['sm'], lambda: nc.vector.tensor_tensor(sm[:, 4:5], sm[:, 4:5], cst[:, 0:1], ALU.subtract))
    fw.op('dve', ['gsub', 'cst'], ['sm'], lambda: nc.vector.tensor_tensor(sm[:, 5:6], gsub[:], cst[:, 1:2], ALU.mult))
    neglam = sm[:, 4:5]
    gcolA = sm[:, 5:6]

    def load_qkv(q1, q2, k1, k2, v):
        fw.dma('sp', [], ['Q1'], Q1[:], q1)
        fw.dma('act', [], ['K1'], K1[:], k1)
        if q2 is not None:
            fw.dma('sp', [], ['Q2'], Q2[:], q2)
            fw.dma('act', [], ['K2'], K2[:], k2)
        for c in range(0, NKT, 16):
            n = min(16, NKT - c)
            fw.dma('sp', [], ['V'], V[:, c:c + n, :], v[c * 128:(c + n) * 128, :].rearrange("(t p) e -> p t e", p=128))

    def store_out(dst, qb):
        fw.dma('sp', [f'out{qb}'], [], dst[:, qb * 512:(qb + 1) * 512], OUT[:, qb * 512:(qb + 1) * 512])

    def softmax_mixer(kind):
        nm = 2 if kind == 'A' else 1
        scale = 64 ** -0.5 if kind == 'A' else 192 ** -0.5
        Sb = [[ps[0], ps[1]], [ps[2], ps[3]]]
        Sn = [['ps0', 'ps1'], ['ps2', 'ps3']]
        Ob = [ps[4], ps[5]]; On = ['ps4', 'ps5']
        Db = [ps[6], ps[7]]; Dn = ['ps6', 'ps7']

        for qb in range(NQB):
            q0 = qb * 512
            nk = 4 * (qb + 1)
            qs = slice(q0, q0 + 512)

            def emit_S(kt):
                ks = slice(kt * 128, kt * 128 + 128)
                b = kt % 2
                for m in range(nm):
                    if kind == 'A':
                        rs = slice(64 * m, 64 * m + 64)
                        fw.op('pe', ['K1', 'Q1'], [Sn[m][b]], lambda: nc.tensor.matmul(Sb[m][b][:], K1[rs, ks], Q1[rs, qs], start=True, stop=True))
                    else:
                        fw.op('pe', ['K1', 'Q1'], [Sn[m][b]], lambda: nc.tensor.matmul(Sb[m][b][:], K1[:, ks], Q1[:, qs], start=True, stop=False), inc=False)
                        fw.op('pe', ['K2', 'Q2'], [Sn[m][b]], lambda: nc.tensor.matmul(Sb[m][b][:], K2[:, ks], Q2[:, qs], start=False, stop=True))

            emit_S(0)
            for kt in range(nk):
                b = kt % 2
                o = kt * 128 - q0
                Pt = []
                for m in range(nm):
                    P, pn = Pp.get()
                    near = (o >= -128) if kind == 'A' else (o >= 0)
                    if near:
                        oi = o // 128 + 1
                        T, tn = Tp.get()
                        btile = biasA[:, oi, :] if kind == 'A' else maskB[:, oi - 1, :]
                        btok = 'biasA' if kind == 'A' else 'maskB'
                        fw.op('dve', [Sn[m][b], btok], [tn], lambda: nc.vector.scalar_tensor_tensor(T[:], Sb[m][b][:], scale, btile, ALU.mult, ALU.add))
                        fw.op('act', [tn], [pn], lambda: nc.scalar.activation(P[:], T[:], AF.Exp))
                    else:
                        if kind == 'A':
                            fw.op('act', [Sn[m][b], 'b15'], [pn], lambda: nc.scalar.activation(P[:], Sb[m][b][:], AF.Exp, bias=b15[:], scale=scale))
                        else:
                            fw.op('act', [Sn[m][b]], [pn], lambda: nc.scalar.activation(P[:], Sb[m][b][:], AF.Exp, scale=scale))
                    Pt.append((P, pn))
                if kt + 1 < nk:
                    emit_S(kt + 1)
                for m in range(nm):
                    P, pn = Pt[m]
                    fw.op('pe', ['V', pn], [On[m]], lambda: nc.tensor.matmul(Ob[m][:], V[:, kt, :], P[:], start=(kt == 0), stop=(kt == nk - 1)), inc=False)
                    fw.op('pe', ['ones', pn], [Dn[m]], lambda: nc.tensor.matmul(Db[m][:], ones[:], P[:], start=(kt == 0), stop=(kt == nk - 1)))
            res = []
            for m in range(nm):
                R, rn = Tp.get()
                fw.op('dve', [Dn[m]], [rn], lambda: nc.vector.reciprocal(R[:], Db[m][:]))
                if kind == 'B':
                    fw.op('dve', [On[m], rn], [f'out{qb}'], lambda: nc.vector.tensor_tensor(OUT[:, qs], Ob[m][:], R[:], ALU.mult))
                else:
                    fw.op('dve', [On[m], rn], [rn], lambda: nc.vector.tensor_tensor(R[:], Ob[m][:], R[:], ALU.mult))
                    res.append((R, rn))
            if kind == 'A':
                (R1, r1n), (R2, r2n) = res
                fw.op('dve', [r1n, r2n, 'sm'], [r1n], lambda: nc.vector.scalar_tensor_tensor(R1[:], R2[:], neglam, R1[:], ALU.mult, ALU.add))
                SQ, sqn = Pp.get()
                fw.op('act', [r1n], [sqn], lambda: nc.scalar.activation(SQ[:], R1[:], AF.Square))
                fw.op('pe', ['ones', sqn], ['ps0'], lambda: nc.tensor.matmul(ps[0][:], ones[:], SQ[:], start=True, stop=True))
                fw.op('act', ['ps0'], [r2n], lambda: nc.scalar.activation(R2[:], ps[0][:], AF.Ln, bias=EPS, scale=1.0 / 128))
                fw.op('act', [r2n], [r2n], lambda: nc.scalar.activation(R2[:], R2[:], AF.Exp, scale=-0.5))
                fw.op('dve', [r1n, r2n, 'sm'], [f'out{qb}'], lambda: nc.vector.scalar_tensor_tensor(OUT[:, qs], R1[:], gcolA, R2[:], ALU.mult, ALU.mult))
            store_out(d['oa'] if kind == 'A' else d['ob'], qb)

    def sb_mixer():
        Zb = [ps[0], ps[1]]; Zn = ['ps0', 'ps1']
        Ab = [ps[2], ps[3]]; An = ['ps2', 'ps3']
        Tb = [ps[4], ps[5]]; Tn = ['ps4', 'ps5']
        Op, On_ = ps[6], 'ps6'
        for qb in range(NQB):
            q0 = qb * 512
            qs = slice(q0, q0 + 512)
            kts = list(range(4 * qb + 3, -1, -1))
            nk = len(kts)

            def emit_Z(i):
                kt = kts[i]
                ks = slice(kt * 128, kt * 128 + 128)
                fw.op('pe', ['K1', 'Q1'], [Zn[i % 2]], lambda: nc.tensor.matmul(Zb[i % 2][:], K1[:, ks], Q1[:, qs], start=True, stop=True))

            fw.op('dve', [], ['carry'], lambda: nc.vector.memset(carry[:], 0.0))
            emit_Z(0)
            for i, kt in enumerate(kts):
                b = i % 2
                ks = slice(kt * 128, kt * 128 + 128)
                o = kt * 128 - q0
                diag = o >= 0
                E, en = Tp.get()
                fw.op('act', [Zn[b]], [en], lambda: nc.scalar.activation(E[:], Zb[b][:], AF.Exp))
                L, ln_ = Lp_.get()
                fw.op('act', [en], [ln_], lambda: nc.scalar.activation(L[:], E[:], AF.Ln, bias=1.0))
                if diag:
                    fw.op('pool', [ln_, 'maskC'], [ln_], lambda: nc.gpsimd.tensor_tensor(L[:], L[:], maskC[:, o // 128, :], ALU.mult))
                fw.op('pe', ['K1', 'Q1'], [An[b]], lambda: nc.tensor.matmul(Ab[b][:], K1[:, ks], Q1[:, qs], start=True, stop=False), inc=False)
                fw.op('pe', ['mtri', ln_], [An[b]], lambda: nc.tensor.matmul(Ab[b][:], mtri[:], L[:], start=False, stop=True))
                fw.op('pe', ['ones', ln_], [Tn[b]], lambda: nc.tensor.matmul(Tb[b][:], ones[:], L[:], start=True, stop=True))
                if i + 1 < nk:
                    emit_Z(i + 1)
                G, gn = Tp.get()
                fw.op('dve', [An[b], 'carry'], [gn], lambda: nc.vector.tensor_tensor(G[:], Ab[b][:], carry[:], ALU.subtract))
                W, wn = Pp.get()
                fw.op('act', [gn], [wn], lambda: nc.scalar.activation(W[:], G[:], AF.Exp))
                if diag:
                    fw.op('pool', [wn, 'maskC'], [wn], lambda: nc.gpsimd.tensor_tensor(W[:], W[:], maskC[:, o // 128, :], ALU.mult))
                fw.op('dve', [Tn[b], 'carry'], ['carry'], lambda: nc.vector.tensor_tensor(carry[:], carry[:], Tb[b][:], ALU.add))
                fw.op('pe', ['V', wn], [On_], lambda: nc.tensor.matmul(Op[:], V[:, kt, :], W[:], start=(i == 0), stop=(i == nk - 1)))
            fw.op('act', [On_], [f'out{qb}'], lambda: nc.scalar.copy(OUT[:, qs], Op[:]))
            store_out(d['oc'], qb)

    load_qkv(d['qa'], None, d['ka'], None, d['va'])
    softmax_mixer('A')
    load_qkv(d['qbn'], d['qbr'], d['kbn'], d['kbr'], d['vb'])
    softmax_mixer('B')
    load_qkv(d['qc'], None, d['kc'], None, d['vc'])
    sb_mixer()


def t5_bucket_np(rel):
    nb = 16
    max_exact = 8
    n = np.abs(rel)
    large = max_exact + (np.log(np.maximum(n, 1).astype(np.float32) / np.float32(max_exact)) / np.float32(np.log(128 / 8)) * np.float32(nb - max_exact)).astype(np.int32)
    large = np.minimum(large, nb - 1)
    return np.where(rel > 0, nb, 0) + np.where(n < max_exact, n, large)


def attn_consts():
    kk = np.arange(128)[:, None]
    qq = np.arange(512)[None, :]
    offs = [-128, 0, 128, 256, 384]
    bucket = np.stack([t5_bucket_np((o + kk) - qq) for o in offs])
    maskB = np.stack([np.where(((o + kk) // 64) <= (qq // 64), 0.0, NEG) for o in offs[1:]]).astype(np.float32)
    maskC = np.stack([np.where((o + kk) < qq, 1.0, 0.0) for o in offs[1:]]).astype(ml_dtypes.bfloat16)
    jj = np.arange(128)[:, None]
    kc = np.arange(128)[None, :]
    mtri = np.where(jj >= kc, -1.0, 0.0).astype(ml_dtypes.bfloat16)
    ones = np.ones((128, 128), ml_dtypes.bfloat16)
    return dict(bucket=bucket, maskB=maskB, maskC=maskC, mtri=mtri, ones=ones)


TT = 1024
O_AQ, O_AK, O_AV, O_CQ, O_CKV, O_KPE, O_SQ, O_SK, O_SV, O_G = 0, 1024, 2048, 3072, 3584, 3840, 3904, 4928, 5952, 6976


class Slabs:
    def __init__(self, nc, fw, n=3):
        self.nc, self.fw = nc, fw
        self.pool = Pool_(nc, "slab", n, [128, 16, 512], BF16)

    def load(self, w, r0, nrows, c0, ncols):
        t, tok = self.pool.get()
        nch = nrows // 128
        step = 4
        for c in range(0, nch, step):
            n = min(step, nch - c)
            self.fw.dma('pool', [], [tok], t[:, c:c + n, 0:ncols],
                        w[r0 + c * 128:r0 + (c + n) * 128, c0:c0 + ncols].rearrange("(c p) n -> p c n", p=128))
        return t, tok


def emit_norm_h(nc, fw, X, H, acol, bcol, ones, ps, psn, Tp, Pp, RS):
    for tb in range(TT // 512):
        ts = slice(tb * 512, tb * 512 + 512)
        for c in range(16):
            SQ, sqn = Pp.get()
            fw.op('act', ['X'], [sqn], lambda: nc.scalar.activation(SQ[:], X[:, c, ts], AF.Square))
            fw.op('pe', ['ones', sqn], [psn], lambda: nc.tensor.matmul(ps[:], ones[:], SQ[:], start=(c == 0), stop=(c == 15)))
        fw.op('act', [psn], ['RS'], lambda: nc.scalar.activation(RS[:, ts], ps[:], AF.Ln, bias=EPS, scale=1.0 / 2048))
        fw.op('act', ['RS'], ['RS'], lambda: nc.scalar.activation(RS[:, ts], RS[:, ts], AF.Exp, scale=-0.5))
        for c in range(16):
            T, tn = Tp.get()
            fw.op('dve', ['X', 'RS', 'cols'], [tn], lambda: nc.vector.scalar_tensor_tensor(T[:], X[:, c, ts], acol[:, c:c + 1], RS[:, ts], ALU.mult, ALU.mult))
            fw.op('dve', [tn, 'cols'], ['H'], lambda: nc.vector.tensor_scalar(H[:, c, ts], T[:], bcol[:, c:c + 1], None, ALU.add))


def emit_proj(nc, fw, d, ps):
    sb = nc.alloc_sbuf_tensor
    H = sb("H", [128, 16, TT], BF16)
    RS = sb("RS", [128, TT], F32)
    ones = sb("pones", [128, 128], BF16)
    bd64 = sb("pbd64", [128, 128], BF16)
    rotm = sb("protm", [64, 64], BF16)
    modc = sb("pmodc", [128, 6, 16], F32)
    gmix = sb("pgmix", [128, 16], F32)
    cols = sb("pcols", [128, 2, 16], F32)
    dqk = sb("pdqk", [128, 2], F32)
    gq = sb("pgq", [128, 4], F32)
    gkv = sb("pgkv", [128, 2], F32)
    mqk = sb("pmqk", [128, 4], F32)
    cosT = sb("pcos", [64, TT], F32)
    sinT = sb("psin", [64, TT], F32)
    Pp = Pool_(nc, "pP", 4, [128, 512], BF16)
    Tp = Pool_(nc, "pT", 4, [128, 512], F32)
    fw.dma('act', [], ['ones'], ones[:], d['ones'])
    fw.dma('act', [], ['modc'], modc[:], d['modc'])
    fw.dma('act', [], ['gmix'], gmix[:], d['gmix'])
    with nc.sbuf_tensor("X", [128, 16, TT], F32) as X:
        for c in range(0, 16, 4):
            fw.dma('sp', [], ['X'], X[:, c:c + 4, :], d['xT'][c * 128:(c + 4) * 128, :].rearrange("(c p) t -> p c t", p=128))
        fw.op('dve', ['modc', 'gmix'], ['cols'], lambda: nc.vector.scalar_tensor_tensor(cols[:, 0, :], modc[:, 1, :], 1.0, gmix[:], ALU.add, ALU.mult))
        fw.op('dve', ['modc'], ['cols'], lambda: nc.vector.tensor_copy(cols[:, 1, :], modc[:, 0, :]))
        emit_norm_h(nc, fw, X, H, cols[:, 0, :], cols[:, 1, :], ones, ps[4], 'ps4', Tp, Pp, RS)
    fw.barrier()
    WQ = sb("pwq", [128, 4, 1536], BF16)
    WKV = sb("pwkv", [128, 2, 2048], BF16)
    CQ = sb("pCQ", [128, 4, TT], F32)
    CKV = sb("pCKV", [128, 2, TT], F32)
    KPE = sb("pKPE", [64, TT], F32)
    CQN = sb("pCQN", [128, 4, TT], BF16)
    CKVN = sb("pCKVN", [128, 2, TT], BF16)
    Op = Pool_(nc, "pO", 4, [128, 512], BF16)
    slabs = Slabs(nc, fw)
    projps = [(ps[i], f'ps{i}') for i in range(4)]
    pi = [0]

    def nextps():
        p = projps[pi[0] % 4]
        pi[0] += 1
        return p

    nps = [(ps[4], 'ps4'), (ps[5], 'ps5')]
    ni = [0]

    def nextnps():
        p = nps[ni[0] % 2]
        ni[0] += 1
        return p

    fw.dma('act', [], ['bd64'], bd64[:], d['bd64'])
    fw.dma('act', [], ['rotm'], rotm[:], d['rotm'])
    fw.dma('act', [], ['dqk'], dqk[:], d['dqk'])
    fw.dma('act', [], ['gq'], gq[:], d['gq'])
    fw.dma('act', [], ['gkv'], gkv[:], d['gkv'])
    fw.dma('act', [], ['mqk'], mqk[:], d['mqk'])
    fw.dma('act', [], ['cos'], cosT[:], d['cosT'])
    fw.dma('act', [], ['sin'], sinT[:], d['sinT'])
    import os
    for c in range(4 if not os.environ.get('NOWQ') else 0):
        fw.dma('pool', [], ['WQ'], WQ[:, c, :], d['wqup'][c * 128:(c + 1) * 128, :], semtok='WQ')
    for c in range(2 if not os.environ.get('NOWQ') else 0):
        fw.dma('pool', [], ['WKV'], WKV[:, c, :], d['wkvup'][c * 128:(c + 1) * 128, :], semtok='WKV')
    w = d['w_in']

    def rstd_of(sq_list, lhsT, ltok, nfeat, P=128):
        pp, ppn = nextnps()
        for i, (sq, sqn) in enumerate(sq_list):
            fw.op('pe', [ltok, sqn], [ppn], lambda: nc.tensor.matmul(pp[0:P, :], lhsT, sq, start=(i == 0), stop=(i == len(sq_list) - 1)))
        R, rn = Tp.get()
        fw.op('act', [ppn], [rn], lambda: nc.scalar.activation(R[0:P, :], pp[0:P, :], AF.Ln, bias=EPS, scale=1.0 / nfeat))
        fw.op('act', [rn], [rn], lambda: nc.scalar.activation(R[0:P, :], R[0:P, :], AF.Exp, scale=-0.5))
        return R, rn

    def fm_chunk(slab, stok, nrc, col, ncols, tb, rhs_src, rtok):
        p, pn = nextps()
        ts = slice(tb * 512, tb * 512 + 512)
        for c in range(nrc):
            fw.op('pe', [stok, rtok], [pn], lambda: nc.tensor.matmul(p[0:ncols, :], slab[:, c, col:col + ncols], rhs_src[:, c, ts], start=(c == 0), stop=(c == nrc - 1)), inc=(c == nrc - 1))
        return p, pn

    def out_dma(dst, O, on):
        fw.dma('sp', [on], [], dst, O)

    def qknorm_out(p, pn, P, lhsT, ltok, nfeat, gcol, gtok, dst, scale_extra=1.0):
        SQ, sqn = Pp.get()
        fw.op('act', [pn], [sqn], lambda: nc.scalar.activation(SQ[0:P, :], p[0:P, :], AF.Square))
        R, rn = rstd_of([(SQ[0:P, :], sqn)], lhsT, ltok, nfeat, P)
        O, on = Op.get()
        fw.op('dve', [pn, rn, gtok], [on], lambda: nc.vector.scalar_tensor_tensor(O[0:P, :], p[0:P, :], gcol, R[0:P, :], ALU.mult, ALU.mult))
        out_dma(dst, O[0:P, :], on)

    def rope_out(src, stok, P, gcol, gtok, tb, dst):
        ts = slice(tb * 512, tb * 512 + 512)
        SQ, sqn = Pp.get()
        fw.op('act', [stok], [sqn], lambda: nc.scalar.activation(SQ[0:64, :], src, AF.Square))
        R, rn = rstd_of([(SQ[0:64, :], sqn)], ones[0:64, 0:64], 'ones', 64, 64)
        QN, qn = Tp.get()
        fw.op('dve', [stok, rn, gtok], [qn], lambda: nc.vector.scalar_tensor_tensor(QN[0:64, :], src, gcol, R[0:64, :], ALU.mult, ALU.mult))
        QB, qbn_ = Pp.get()
        fw.op('act', [qn], [qbn_], lambda: nc.scalar.copy(QB[0:64, :], QN[0:64, :]))
        rp, rpn = (ps[6], 'ps6') if tb == 0 else (ps[7], 'ps7')
        fw.op('pe', ['rotm', qbn_], [rpn], lambda: nc.tensor.matmul(rp[0:64, :], rotm[:], QB[0:64, :], start=True, stop=True))
        T2, t2n = Tp.get()
        fw.op('dve', [rpn, 'sin'], [t2n], lambda: nc.vector.tensor_tensor(T2[0:64, :], rp[0:64, :], sinT[:, ts], ALU.mult))
        fw.op('dve', [qn, 'cos'], [qn], lambda: nc.vector.tensor_tensor(QN[0:64, :], QN[0:64, :], cosT[:, ts], ALU.mult))
        O, on = Op.get()
        fw.op('dve', [qn, t2n], [on], lambda: nc.vector.tensor_tensor(O[0:64, :], QN[0:64, :], T2[0:64, :], ALU.add))
        out_dma(dst, O[0:64, :], on)

    def fm_group(c0, ncols_total, handler):
        for s0 in range(0, ncols_total, 512):
            ncs = min(512, ncols_total - s0)
            slab, stok = slabs.load(w, 0, 2048, c0 + s0, ncs)
            for tb in range(2):
                for cc in range(0, ncs, 128):
                    nco = min(128, ncs - cc)
                    p, pn = fm_chunk(slab, stok, 16, cc, nco, tb, H, 'H')
                    handler((s0 + cc) // 128, tb, p, pn, nco)

    def tsl(tb):
        return slice(tb * 512, tb * 512 + 512)

    import os
    STAGE = int(os.environ.get('PSTAGE', '99'))
    if STAGE < 1:
        return
    fm_group(O_AQ, 1024, lambda ch, tb, p, pn, n: qknorm_out(p, pn, 128, bd64[:], 'bd64', 64, dqk[:, 0:1], 'dqk', d['qa'][ch, :, tsl(tb)]))
    fm_group(O_AK, 1024, lambda ch, tb, p, pn, n: qknorm_out(p, pn, 128, bd64[:], 'bd64', 64, dqk[:, 1:2], 'dqk', d['ka'][ch, :, tsl(tb)]))

    def cq_h(ch, tb, p, pn, n):
        O, on = Op.get()
        fw.op('act', [pn], [on], lambda: nc.scalar.activation(O[:], p[:], AF.Copy, scale=128 ** -0.5))
        out_dma(d['qc'][ch, :, tsl(tb)], O[:], on)

    def ck_h(ch, tb, p, pn, n):
        O, on = Op.get()
        fw.op('act', [pn], [on], lambda: nc.scalar.copy(O[:], p[:]))
        out_dma(d['kc'][ch, :, tsl(tb)], O[:], on)

    if STAGE < 2:
        return
    fm_group(O_SQ, 1024, cq_h)
    fm_group(O_SK, 1024, ck_h)

    def lat_cq(ch, tb, p, pn, n):
        fw.op('act', [pn], ['CQ'], lambda: nc.scalar.copy(CQ[:, ch, tsl(tb)], p[:]))

    def lat_ckv(ch, tb, p, pn, n):
        if ch < 2:
            fw.op('act', [pn], ['CKV'], lambda: nc.scalar.copy(CKV[:, ch, tsl(tb)], p[:]))
        else:
            fw.op('act', [pn], ['KPE'], lambda: nc.scalar.copy(KPE[:, tsl(tb)], p[0:64, :]))

    if STAGE < 3:
        return
    fm_group(O_CQ, 512, lat_cq)
    fm_group(O_CKV, 320, lat_ckv)

    if STAGE < 4:
        return
    for tb in range(2):
        ts = tsl(tb)
        for (RAW, rtok, NRM, ntok, nch, gg, gtok) in [(CQ, 'CQ', CQN, 'CQN', 4, gq, 'gq'), (CKV, 'CKV', CKVN, 'CKVN', 2, gkv, 'gkv')]:
            sql = []
            for c in range(nch):
                SQ, sqn = Pp.get()
                fw.op('act', [rtok], [sqn], lambda: nc.scalar.activation(SQ[:], RAW[:, c, ts], AF.Square))
                sql.append((SQ[:], sqn))
            R, rn = rstd_of(sql, ones[:], 'ones', nch * 128)
            for c in range(nch):
                fw.op('dve', [rtok, rn, gtok], [ntok], lambda: nc.vector.scalar_tensor_tensor(NRM[:, c, ts], RAW[:, c, ts], gg[:, c:c + 1], R[:], ALU.mult, ALU.mult))
        rope_out(KPE[:, ts], 'KPE', 64, mqk[0:64, 3:4], 'mqk', tb, d['kbr'][:, ts])

    if STAGE < 5:
        return
    for tb in range(2):
        for hh in range(8):
            p, pn = fm_chunk(WQ, 'WQ', 4, hh * 128, 128, tb, CQN, 'CQN')
            qknorm_out(p, pn, 128, ones[:], 'ones', 128, mqk[:, 0:1], 'mqk', d['qbn'][hh, :, tsl(tb)])
        for hh in range(8):
            p, pn = fm_chunk(WQ, 'WQ', 4, 1024 + hh * 64, 64, tb, CQN, 'CQN')
            rope_out(p[0:64, :], pn, 64, mqk[0:64, 2:3], 'mqk', tb, d['qbr'][hh, :, tsl(tb)])
        for hh in range(8):
            p, pn = fm_chunk(WKV, 'WKV', 2, hh * 128, 128, tb, CKVN, 'CKVN')
            qknorm_out(p, pn, 128, ones[:], 'ones', 128, mqk[:, 1:2], 'mqk', d['kbn'][hh, :, tsl(tb)])
    if STAGE < 6:
        return
    for tt in range(TT // 128):
        for nb in range(2):
            p, pn = nextps()
            for c in range(2):
                fw.op('pe', ['CKVN', 'WKV'], [pn], lambda: nc.tensor.matmul(p[:], CKVN[:, c, tt * 128:(tt + 1) * 128], WKV[:, c, 1024 + nb * 512:1024 + (nb + 1) * 512], start=(c == 0), stop=(c == 1)), inc=(c == 1))
            O, on = Op.get()
            fw.op('act', [pn], [on], lambda: nc.scalar.copy(O[:], p[:]))
            out_dma(d['vb'][tt * 128:(tt + 1) * 128, nb * 512:(nb + 1) * 512], O[:], on)

    if STAGE < 7:
        return
    for (c0, dst) in [(O_AV, d['va']), (O_SV, d['vc'])]:
        for s0 in range(0, 1024, 512):
            slab, stok = slabs.load(w, 0, 2048, c0 + s0, 512)
            for tt in range(TT // 128):
                p, pn = nextps()
                for c in range(16):
                    fw.op('pe', ['H', stok], [pn], lambda: nc.tensor.matmul(p[:], H[:, c, tt * 128:(tt + 1) * 128], slab[:, c, :], start=(c == 0), stop=(c == 15)), inc=(c == 15))
                O, on = Op.get()
                fw.op('act', [pn], [on], lambda: nc.scalar.copy(O[:], p[:]))
                out_dma(dst[tt * 128:(tt + 1) * 128, s0:s0 + 512], O[:], on)
    return ['dramout']


def proj_consts(core):
    half = 32
    inv = (10000.0 ** (-np.arange(half, dtype=np.float32) / half)).astype(np.float32)
    pos = np.arange(core * TT, (core + 1) * TT).astype(np.float32)
    ang = pos[None, :] * np.concatenate([inv, inv])[:, None]
    cosT = np.cos(ang).astype(np.float32)
    sinT = np.sin(ang).astype(np.float32)
    rot = np.zeros((64, 64), np.float32)
    for p in range(32):
        rot[p + 32, p] = -1.0
        rot[p, p + 32] = 1.0
    bd = np.zeros((128, 128), np.float32)
    bd[:64, :64] = 1
    bd[64:, 64:] = 1
    return dict(cosT=cosT, sinT=sinT, rotm=rot.astype(ml_dtypes.bfloat16), bd64=bd.astype(ml_dtypes.bfloat16), ones=np.ones((128, 128), ml_dtypes.bfloat16))


def _common(nc, fw, d, ps, Tp, Pp, H, RS, ones, modc, gvec, gtok, cols, sc_idx, sh_idx, X):
    fw.op('dve', ['modc', gtok], ['cols'], lambda: nc.vector.scalar_tensor_tensor(cols[:, 0, :], modc[:, sc_idx, :], 1.0, gvec[:], ALU.add, ALU.mult))
    fw.op('dve', ['modc'], ['cols'], lambda: nc.vector.tensor_copy(cols[:, 1, :], modc[:, sh_idx, :]))
    emit_norm_h(nc, fw, X, H, cols[:, 0, :], cols[:, 1, :], ones, ps[7], 'ps7', Tp, Pp, RS)


def tsl(tb):
    return slice(tb * 512, tb * 512 + 512)


def emit_merge(nc, fw, d, ps):
    sb = nc.alloc_sbuf_tensor
    H = sb("H", [128, 16, TT], BF16)
    RS = sb("RS", [128, TT], F32)
    MG = sb("MG", [128, 16, TT], BF16)
    MGF = [[sb(f"MGF{i}_{tb}", [128, 512], F32) for tb in range(2)] for i in range(4)]
    ones = sb("pones", [128, 128], BF16)
    modc = sb("pmodc", [128, 6, 16], F32)
    gmix = sb("pgmix", [128, 16], F32)
    cols = sb("pcols", [128, 2, 16], F32)
    Pp = Pool_(nc, "pP", 4, [128, 512], BF16)
    Tp = Pool_(nc, "pT", 6, [128, 512], F32)
    pslist = [(ps[i], f'ps{i}') for i in range(7)]
    pi = [0]

    def nextps():
        p = pslist[pi[0] % 7]
        pi[0] += 1
        return p

    fw.dma('act', [], ['ones'], ones[:], d['ones'])
    fw.dma('act', [], ['modc'], modc[:], d['modc'])
    fw.dma('act', [], ['gmix'], gmix[:], d['gmix'])
    with nc.sbuf_tensor("X", [128, 16, TT], F32) as X:
        for c in range(0, 16, 4):
            fw.dma('sp', [], ['X'], X[:, c:c + 4, :], d['xT'][c * 128:(c + 4) * 128, :].rearrange("(c p) t -> p c t", p=128))
        _common(nc, fw, d, ps, Tp, Pp, H, RS, ones, modc, gmix, 'gmix', cols, 1, 0, X)
    fw.barrier()
    slabs = Slabs(nc, fw, n=3)
    BRp = Pool_(nc, "BR", 2, [128, 8, TT], BF16)
    w_in, wbr = d['w_in'], d['w_branch']
    for s in range(4):
        for i in range(3):
            gs, gtok = slabs.load(w_in, 0, 2048, i * 2048 + s * 512, 512)
            us, utok = slabs.load(wbr[i], 0, 1024, s * 512, 512)
            BR, brn = BRp.get()
            fw.dma('act', [], [brn], BR[:], d['br'][i].rearrange("c p t -> p c t"))
            for tb in range(2):
                ts = tsl(tb)
                for cc in range(4):
                    n = s * 4 + cc
                    mtok = 'MGf%d_%d' % (cc, tb)
                    gp, gpn = nextps()
                    for c in range(16):
                        fw.op('pe', [gtok, 'H'], [gpn], lambda: nc.tensor.matmul(gp[:], gs[:, c, cc * 128:(cc + 1) * 128], H[:, c, ts], start=(c == 0), stop=(c == 15)), inc=(c == 15))
                    up, upn = nextps()
                    for c in range(8):
                        fw.op('pe', [utok, brn], [upn], lambda: nc.tensor.matmul(up[:], us[:, c, cc * 128:(cc + 1) * 128], BR[:, c, ts], start=(c == 0), stop=(c == 7)), inc=(c == 7))
                    SG, sgn = Tp.get()
                    fw.op('act', [gpn], [sgn], lambda: nc.scalar.activation(SG[:], gp[:], AF.Sigmoid))
                    if i == 0:
                        fw.op('dve', [upn, sgn], [mtok], lambda: nc.vector.tensor_tensor(MGF[cc][tb][:], up[:], SG[:], ALU.mult))
                    elif i == 1:
                        fw.op('dve', [upn, sgn], [sgn], lambda: nc.vector.tensor_tensor(SG[:], up[:], SG[:], ALU.mult))
                        fw.op('dve', [sgn, mtok], [mtok], lambda: nc.vector.tensor_tensor(MGF[cc][tb][:], MGF[cc][tb][:], SG[:], ALU.add))
                    else:
                        fw.op('dve', [upn, sgn], [sgn], lambda: nc.vector.tensor_tensor(SG[:], up[:], SG[:], ALU.mult))
                        fw.op('dve', [sgn, mtok], ['MG'], lambda: nc.vector.tensor_tensor(MG[:, n, ts], MGF[cc][tb][:], SG[:], ALU.add))
    for c in range(0, 16, 4):
        fw.dma('sp', ['MG'], [], d['mg'][c * 128:(c + 4) * 128, :].rearrange("(c p) t -> p c t", p=128), MG[:, c:c + 4, :])


def emit_mlp(nc, fw, d, ps):
    sb = nc.alloc_sbuf_tensor
    X = sb("X", [128, 16, TT], F32)
    H = sb("H", [128, 16, TT], BF16)
    RS = sb("RS", [128, TT], F32)
    MG = sb("MG", [128, 16, TT], BF16)
    ones = sb("pones", [128, 128], BF16)
    modc = sb("pmodc", [128, 6, 16], F32)
    gmlp = sb("pgmlp", [128, 16], F32)
    cols = sb("pcols", [128, 2, 16], F32)
    Pp = Pool_(nc, "pP", 4, [128, 512], BF16)
    Tp = Pool_(nc, "pT", 6, [128, 512], F32)
    slabs = Slabs(nc, fw, n=3)
    pslist = [(ps[i], f'ps{i}') for i in range(7)]
    pi = [0]

    def nextps():
        p = pslist[pi[0] % 7]
        pi[0] += 1
        return p

    for c in range(0, 16, 4):
        fw.dma('sp', [], ['X'], X[:, c:c + 4, :], d['xT'][c * 128:(c + 4) * 128, :].rearrange("(c p) t -> p c t", p=128))
        fw.dma('act', [], ['MG'], MG[:, c:c + 4, :], d['mg'][c * 128:(c + 4) * 128, :].rearrange("(c p) t -> p c t", p=128))
    fw.dma('act', [], ['ones'], ones[:], d['ones'])
    fw.dma('act', [], ['modc'], modc[:], d['modc'])
    fw.dma('act', [], ['gmlp'], gmlp[:], d['gmlp'])
    wout, w1, w2 = d['w_out'], d['w_mlp_in'], d['w_mlp_out']

    def resid_matmul(W, r0, gcolidx):
        for s in range(4):
            sl, stk = slabs.load(W, r0, 2048, s * 512, 512)
            for tb in range(2):
                ts = tsl(tb)
                for cc in range(4):
                    n = s * 4 + cc
                    p, pn = nextps()
                    for c in range(16):
                        fw.op('pe', [stk, 'MG'], [pn], lambda: nc.tensor.matmul(p[:], sl[:, c, cc * 128:(cc + 1) * 128], MG[:, c, ts], start=(c == 0), stop=(c == 15)), inc=(c == 15))
                    fw.op('dve', [pn, 'X', 'modc'], ['X'], lambda: nc.vector.scalar_tensor_tensor(X[:, n, ts], p[:], modc[:, gcolidx, n:n + 1], X[:, n, ts], ALU.mult, ALU.add))

    resid_matmul(wout, 0, 2)
    _common(nc, fw, d, ps, Tp, Pp, H, RS, ones, modc, gmlp, 'gmlp', cols, 4, 3, X)
    for g in range(4):
        for s in range(4):
            sl, stk = slabs.load(w1, 0, 2048, g * 2048 + s * 512, 512)
            for tb in range(2):
                ts = tsl(tb)
                for cc in range(4):
                    f = s * 4 + cc
                    p, pn = nextps()
                    for c in range(16):
                        fw.op('pe', [stk, 'H'], [pn], lambda: nc.tensor.matmul(p[:], sl[:, c, cc * 128:(cc + 1) * 128], H[:, c, ts], start=(c == 0), stop=(c == 15)), inc=(c == 15))
                    R, rn = Tp.get()
                    fw.op('act', [pn], [rn], lambda: nc.scalar.activation(R[:], p[:], AF.Relu))
                    fw.op('dve', [rn], ['MG'], lambda: nc.vector.tensor_tensor(MG[:, f, ts], R[:], R[:], ALU.mult))
        resid_matmul(w2, g * 2048, 5)
    for c in range(0, 16, 4):
        fw.dma('sp', ['X'], [], d['xo'][c * 128:(c + 4) * 128, :].rearrange("(c p) t -> p c t", p=128), X[:, c:c + 4, :])


def emit_mod(nc, fw, d, ps):
    sb = nc.alloc_sbuf_tensor
    cT = sb("cT_sb", [128, 16], F32)
    fw.dma('sp', [], ['cT'], cT[:], d['cT'])
    bia = sb("bia_sb", [1, 4 * 1536], F32)
    fw.dma('sp', [], ['bia'], bia[:], d['b'])
    res = sb("res_sb", [1, 4 * 1536], F32)
    wp = Pool_(nc, "wm", 2, [128, 16, 512], F32)
    k = 0
    for l in range(4):
        for s in range(3):
            W, wn = wp.get()
            for c in range(0, 16, 4):
                fw.dma('sp' if (k % 2 == 0) else 'act', [], [wn], W[:, c:c + 4, :], d['w'][l, c * 128:(c + 4) * 128, s * 512:(s + 1) * 512].rearrange("(c p) n -> p c n", p=128))
            p, pn = ps[k % 2], f'ps{k % 2}'
            for c in range(16):
                fw.op('pe', [wn, 'cT'], [pn], lambda: nc.tensor.matmul(p[0:1, :], cT[:, c:c + 1], W[:, c, :], start=(c == 0), stop=(c == 15)), inc=(c == 15))
            o0 = l * 1536 + s * 512
            fw.op('dve', [pn, 'bia'], ['res'], lambda: nc.vector.tensor_tensor(res[:, o0:o0 + 512], p[0:1, :], bia[:, o0:o0 + 512], ALU.add))
            k += 1
    fw.dma('sp', ['res'], [], d['out'], res[:])


bf = ml_dtypes.bfloat16
NCORES = 8
S = 8192
DEBUG = bool(int(os.environ.get("MKDEBUG", "0")))
_progs = {}


def _mk(name, builder):
    if name not in _progs:
        _progs[name] = builder()
    return _progs[name]


def _din(nc, d, name, shape, dt):
    d[name] = nc.dram_tensor(name, list(shape), dt, kind="ExternalInput").ap()


def _dout(nc, d, name, shape, dt):
    d[name] = nc.dram_tensor(name, list(shape), dt, kind="ExternalOutput").ap()


def _psum(nc):
    return [nc.alloc_psum_tensor(f"ps{i}", [128, 512], F32) for i in range(8)]


def build_mod():
    nc = bass.Bass("TRN2", target_bir_lowering=False)
    fw = FW(nc)
    d = {}
    _din(nc, d, 'cT', (128, 16), F32)
    _din(nc, d, 'b', (1, 6144), F32)
    _din(nc, d, 'w', (4, 2048, 1536), F32)
    _dout(nc, d, 'out', (1, 6144), F32)
    emit_mod(nc, fw, d, _psum(nc))
    fw.finish_all()
    return nc


def build_proj():
    nc = bass.Bass("TRN2", target_bir_lowering=False)
    fw = FW(nc)
    d = {}
    _din(nc, d, 'xT', (2048, TT), F32)
    _din(nc, d, 'w_in', (2048, 6976), F32)
    for n, s, t in [('ones', (128, 128), BF16), ('bd64', (128, 128), BF16), ('rotm', (64, 64), BF16), ('modc', (128, 6, 16), F32),
                    ('gmix', (128, 16), F32), ('dqk', (128, 2), F32), ('gq', (128, 4), F32), ('gkv', (128, 2), F32), ('mqk', (128, 4), F32),
                    ('cosT', (64, TT), F32), ('sinT', (64, TT), F32), ('wqup', (512, 1536), F32), ('wkvup', (256, 2048), F32)]:
        _din(nc, d, n, s, t)
    for n in ['qa', 'ka', 'qbn', 'kbn', 'qc', 'kc']:
        _dout(nc, d, n, (8, 128, TT), BF16)
    _dout(nc, d, 'qbr', (8, 64, TT), BF16)
    _dout(nc, d, 'kbr', (64, TT), BF16)
    for n in ['va', 'vb', 'vc']:
        _dout(nc, d, n, (TT, 1024), BF16)
    emit_proj(nc, fw, d, _psum(nc))
    fw.finish_all()
    return nc


def build_attn():
    nc = bass.Bass("TRN2", target_bir_lowering=False)
    fw = FW(nc)
    d = {}
    for n in ['qa', 'ka', 'qbn', 'kbn', 'qc', 'kc']:
        _din(nc, d, n, (128, S), BF16)
    for n in ['qbr', 'kbr']:
        _din(nc, d, n, (64, S), BF16)
    for n in ['va', 'vb', 'vc']:
        _din(nc, d, n, (S, 128), BF16)
    for n, s, t in [('ones', (128, 128), BF16), ('mtri', (128, 128), BF16), ('biasA', (5, 128, 512), F32), ('maskB', (4, 128, 512), F32),
                    ('maskC', (4, 128, 512), BF16), ('lamp', (128, 256), F32), ('cst', (128, 4), F32), ('b15', (128, 1), F32), ('gsub', (128, 1), F32)]:
        _din(nc, d, n, s, t)
    for n in ['oa', 'ob', 'oc']:
        _dout(nc, d, n, (128, S), BF16)
    emit_attn(nc, fw, S, d, _psum(nc))
    fw.finish_all()
    return nc


def build_merge():
    nc = bass.Bass("TRN2", target_bir_lowering=False)
    fw = FW(nc)
    d = {}
    _din(nc, d, 'xT', (2048, TT), F32)
    _din(nc, d, 'w_in', (2048, 6144), F32)
    _din(nc, d, 'w_branch', (3, 1024, 2048), F32)
    _din(nc, d, 'br', (3, 8, 128, TT), BF16)
    for n, s, t in [('ones', (128, 128), BF16), ('modc', (128, 6, 16), F32), ('gmix', (128, 16), F32)]:
        _din(nc, d, n, s, t)
    _dout(nc, d, 'mg', (2048, TT), BF16)
    emit_merge(nc, fw, d, _psum(nc))
    fw.finish_all()
    return nc


def build_mlp():
    nc = bass.Bass("TRN2", target_bir_lowering=False)
    fw = FW(nc)
    d = {}
    _din(nc, d, 'xT', (2048, TT), F32)
    _din(nc, d, 'mg', (2048, TT), BF16)
    _din(nc, d, 'w_out', (2048, 2048), F32)
    _din(nc, d, 'w_mlp_in', (2048, 8192), F32)
    _din(nc, d, 'w_mlp_out', (8192, 2048), F32)
    for n, s, t in [('ones', (128, 128), BF16), ('modc', (128, 6, 16), F32), ('gmlp', (128, 16), F32)]:
        _din(nc, d, n, s, t)
    _dout(nc, d, 'xo', (2048, TT), F32)
    emit_mlp(nc, fw, d, _psum(nc))
    fw.finish_all()
    return nc


def colvec(v, n):
    return np.ascontiguousarray(np.asarray(v, np.float32).reshape(n, 128).T)


def run(nc, in_maps):
    t = time.time()
    res = run_bass_kernel_spmd(nc, in_maps, core_ids=list(range(NCORES)))
    if DEBUG:
        print("  launch %.1fs" % (time.time() - t), flush=True)
    return res.results


def kernel(x, c, w_ada, b_ada, norm_mix_g, norm_mlp_g, w_in, diff_qk_g, diff_lambda, diff_subln_g, t5_bias,
           mla_q_norm_g, mla_kv_norm_g, w_q_up, w_kv_up, mla_qk_g, w_branch, w_out, w_mlp_in, w_mlp_out):
    f32 = np.float32
    x = np.asarray(x, f32)
    AC = attn_consts()
    PC = [proj_consts(i) for i in range(NCORES)]
    ones = AC['ones']
    nc_mod = _mk('mod', build_mod)
    cT = colvec(np.asarray(c, f32)[0], 16)
    w_ada = np.asarray(w_ada, f32)
    b_ada = np.asarray(b_ada, f32)
    ins = []
    for i in range(NCORES):
        ins.append(dict(cT=cT, b=np.ascontiguousarray(b_ada[:, 1536 * i:1536 * (i + 1)]).reshape(1, 6144),
                        w=np.ascontiguousarray(w_ada[:, :, 1536 * i:1536 * (i + 1)])))
    r = run(nc_mod, ins)
    mod = np.concatenate([r[i]['out'].reshape(4, 1536) for i in range(NCORES)], axis=1)
    if DEBUG:
        ref = np.asarray(c, f32) @ w_ada[0] + b_ada[0]
        print("mod err", np.abs(mod[0] - ref[0]).max(), np.abs(ref).max())

    xT = [np.ascontiguousarray(x[0, i * TT:(i + 1) * TT].T) for i in range(NCORES)]
    nc_proj = _mk('proj', build_proj)
    nc_attn = _mk('attn', build_attn)
    nc_merge = _mk('merge', build_merge)
    nc_mlp = _mk('mlp', build_mlp)
    for l in range(4):
        modc = np.ascontiguousarray(mod[l].reshape(6, 16, 128).transpose(2, 0, 1))
        gmix = colvec(norm_mix_g[l], 16)
        gmlp = colvec(norm_mlp_g[l], 16)
        wl = np.asarray(w_in[l], f32)
        w_qkv = np.ascontiguousarray(wl[:, :6976])
        w_gate = np.ascontiguousarray(wl[:, 6976:])
        dq = np.asarray(diff_qk_g[l], f32)
        dqk = np.stack([np.tile(dq[0], 2), np.tile(dq[1], 2)], axis=1)
        gq = colvec(mla_q_norm_g[l], 4)
        gkv = colvec(mla_kv_norm_g[l], 2)
        mg_ = np.asarray(mla_qk_g[l], f32)
        mqk = np.zeros((128, 4), f32)
        mqk[:, 0] = mg_[0, :128]
        mqk[:, 1] = mg_[1, :128]
        mqk[:64, 2] = mg_[0, 128:]
        mqk[:64, 3] = mg_[1, 128:]
        wq = np.asarray(w_q_up[l], f32).reshape(512, 8, 192)
        wqup = np.ascontiguousarray(np.concatenate([wq[:, :, :128].reshape(512, 1024), wq[:, :, 128:].reshape(512, 512)], axis=1))
        wkv = np.asarray(w_kv_up[l], f32).reshape(256, 8, 256)
        wkvup = np.ascontiguousarray(np.concatenate([wkv[:, :, :128].reshape(256, 1024), wkv[:, :, 128:].reshape(256, 1024)], axis=1))
        ins = []
        for i in range(NCORES):
            ins.append(dict(xT=xT[i], w_in=w_qkv, ones=ones, bd64=PC[i]['bd64'], rotm=PC[i]['rotm'], modc=modc, gmix=gmix, dqk=dqk, gq=gq, gkv=gkv,
                            mqk=mqk, cosT=PC[i]['cosT'], sinT=PC[i]['sinT'], wqup=wqup, wkvup=wkvup))
        rp = run(nc_proj, ins)
        if DEBUG and l == 0:
            debug_proj(x, mod, norm_mix_g, wl, dq, rp, mla_q_norm_g, mla_kv_norm_g, w_q_up, w_kv_up, mla_qk_g)
        lam_init = 0.8 - 0.6 * np.exp(-0.3 * l)
        cst = np.broadcast_to(np.array([lam_init, 1 - lam_init, 0, 0], f32), (128, 4)).copy()
        lamp = np.broadcast_to(np.asarray(diff_lambda[l], f32).reshape(1, 256), (128, 256)).copy()
        gsub = np.asarray(diff_subln_g[l], f32).reshape(128, 1).copy()
        t5 = np.asarray(t5_bias, f32)
        kbr = np.concatenate([rp[i]['kbr'] for i in range(NCORES)], axis=1)
        ins = []
        for h in range(NCORES):
            dd = dict(ones=ones, mtri=AC['mtri'], biasA=np.ascontiguousarray(t5[AC['bucket'], h]), maskB=AC['maskB'], maskC=AC['maskC'],
                      lamp=lamp, cst=cst, b15=np.full((128, 1), t5[15, h], f32), gsub=gsub, kbr=kbr)
            for n in ['qa', 'ka', 'qbn', 'kbn', 'qc', 'kc', 'qbr']:
                dd[n] = np.concatenate([rp[i][n][h] for i in range(NCORES)], axis=1)
            for n in ['va', 'vb', 'vc']:
                dd[n] = np.concatenate([rp[i][n][:, 128 * h:128 * (h + 1)] for i in range(NCORES)], axis=0)
            ins.append(dd)
        ra = run(nc_attn, ins)
        ins = []
        for i in range(NCORES):
            br = np.stack([np.stack([ra[h][n][:, i * TT:(i + 1) * TT] for h in range(NCORES)]) for n in ['oa', 'ob', 'oc']])
            ins.append(dict(xT=xT[i], w_in=w_gate, w_branch=np.asarray(w_branch[l], f32), br=br, ones=ones, modc=modc, gmix=gmix))
        rm = run(nc_merge, ins)
        ins = []
        for i in range(NCORES):
            ins.append(dict(xT=xT[i], mg=rm[i]['mg'], w_out=np.asarray(w_out[l], f32), w_mlp_in=np.asarray(w_mlp_in[l], f32),
                            w_mlp_out=np.asarray(w_mlp_out[l], f32), ones=ones, modc=modc, gmlp=gmlp))
        rx = run(nc_mlp, ins)
        if DEBUG and l == 0:
            debug_post(x, mod, norm_mix_g, norm_mlp_g, wl, w_branch, w_out, w_mlp_in, w_mlp_out, ra, rm, rx)
        xT = [rx[i]['xo'] for i in range(NCORES)]
    out = np.concatenate([xT[i].T for i in range(NCORES)], axis=0)[None]
    return np.ascontiguousarray(out.astype(f32))


def _rms(x, g):
    return x / np.sqrt((x * x).mean(-1, keepdims=True) + 1e-6) * g


def _err(name, got, ref):
    got = np.asarray(got).astype(np.float64)
    ref = np.asarray(ref).astype(np.float64)
    print("  DBG %-6s rel %.2e  max %.3e / %.3e" % (name, np.sqrt(((got - ref) ** 2).mean() / ((ref ** 2).mean() + 1e-30)), np.abs(got - ref).max(), np.abs(ref).max()), flush=True)


def debug_proj(x, mod, norm_mix_g, wl, dq, rp, mla_q_norm_g, mla_kv_norm_g, w_q_up, w_kv_up, mla_qk_g):
    T0 = 128
    xs = x[0, :T0].astype(np.float64)
    sh1, sc1 = mod[0, :2048], mod[0, 2048:4096]
    h = _rms(xs, np.asarray(norm_mix_g[0], np.float64)) * (1 + sc1) + sh1
    pr = h @ wl.astype(np.float64)
    r = rp[0]
    f = lambda a: np.asarray(a).astype(np.float32)
    qa = _rms(pr[:, 0:1024].reshape(T0, 8, 2, 64), dq[0]).reshape(T0, 8, 128)
    _err('qa', f(r['qa'])[:, :, :T0].transpose(2, 0, 1), qa)
    ka = _rms(pr[:, 1024:2048].reshape(T0, 8, 2, 64), dq[1]).reshape(T0, 8, 128)
    _err('ka', f(r['ka'])[:, :, :T0].transpose(2, 0, 1), ka)
    _err('va', f(r['va'])[:T0], pr[:, 2048:3072])
    _err('qc', f(r['qc'])[:, :, :T0].transpose(2, 0, 1), pr[:, 3904:4928].reshape(T0, 8, 128) * 128 ** -0.5)
    _err('kc', f(r['kc'])[:, :, :T0].transpose(2, 0, 1), pr[:, 4928:5952].reshape(T0, 8, 128))
    _err('vc', f(r['vc'])[:T0], pr[:, 5952:6976])
    cq = _rms(pr[:, 3072:3584], np.asarray(mla_q_norm_g[0], np.float64))
    ckv = _rms(pr[:, 3584:3840], np.asarray(mla_kv_norm_g[0], np.float64))
    kpe = pr[:, 3840:3904]
    q = (cq @ np.asarray(w_q_up[0], np.float64)).reshape(T0, 8, 192)
    kv = (ckv @ np.asarray(w_kv_up[0], np.float64)).reshape(T0, 8, 256)
    g = np.asarray(mla_qk_g[0], np.float64)
    pos = np.arange(T0)
    inv = 10000.0 ** (-np.arange(32) / 32)
    ang = pos[:, None] * inv[None]
    cos, sin = np.cos(ang)[:, None], np.sin(ang)[:, None]

    def rope(v):
        x1, x2 = v[..., :32], v[..., 32:]
        return np.concatenate([x1 * cos - x2 * sin, x1 * sin + x2 * cos], -1)
    _err('qbn', f(r['qbn'])[:, :, :T0].transpose(2, 0, 1), _rms(q[..., :128], g[0, :128]))
    _err('qbr', f(r['qbr'])[:, :, :T0].transpose(2, 0, 1), rope(_rms(q[..., 128:], g[0, 128:])))
    _err('kbn', f(r['kbn'])[:, :, :T0].transpose(2, 0, 1), _rms(kv[..., :128], g[1, :128]))
    _err('kbr', f(r['kbr'])[:, :T0].T, rope(_rms(kpe.reshape(T0, 1, 64), g[1, 128:]))[:, 0])
    _err('vb', f(r['vb'])[:T0].reshape(T0, 8, 128), kv[..., 128:])


def debug_post(x, mod, norm_mix_g, norm_mlp_g, wl, w_branch, w_out, w_mlp_in, w_mlp_out, ra, rm, rx):
    T0 = 128
    f64 = np.float64
    xs = x[0, :T0].astype(f64)
    sh1, sc1, g1, sh2, sc2, g2 = [mod[0, i * 2048:(i + 1) * 2048].astype(f64) for i in range(6)]
    h = _rms(xs, np.asarray(norm_mix_g[0], f64)) * (1 + sc1) + sh1
    gates = h @ wl[:, 6976:].astype(f64)
    br = np.stack([np.concatenate([np.asarray(ra[hh][n]).astype(np.float32)[:, :T0].T for hh in range(8)], axis=1) for n in ['oa', 'ob', 'oc']]).astype(f64)
    merged = 0
    for i in range(3):
        up = br[i] @ np.asarray(w_branch[0][i], f64)
        merged = merged + 1 / (1 + np.exp(-gates[:, i * 2048:(i + 1) * 2048])) * up
    _err('mg', np.asarray(rm[0]['mg']).astype(np.float32)[:, :T0].T, merged)
    x1 = xs + g1 * (merged @ np.asarray(w_out[0], f64))
    h2 = _rms(x1, np.asarray(norm_mlp_g[0], f64)) * (1 + sc2) + sh2
    hid = np.maximum(h2 @ np.asarray(w_mlp_in[0], f64), 0) ** 2
    x2 = x1 + g2 * (hid @ np.asarray(w_mlp_out[0], f64))
    _err('xo', rx[0]['xo'][:, :T0].T, x2)
    _err('dx', rx[0]['xo'][:, :T0].T - xs, x2 - xs)
```

```python
import os, time
import numpy as np
import ml_dtypes
import concourse.bass as bass
import concourse.mybir as mybir
from concourse.bass_utils import run_bass_kernel_spmd

F32 = mybir.dt.float32
BF16 = mybir.dt.bfloat16
AF = mybir.ActivationFunctionType
ALU = mybir.AluOpType


class FW:
    def __init__(self, nc):
        self.nc = nc
        self.eng = {'pe': nc.tensor, 'act': nc.scalar, 'dve': nc.vector, 'pool': nc.gpsimd, 'sp': nc.sync}
        self.sem = {}
        self.cnt = {}
        for k in ['pe', 'act', 'dve', 'pool']:
            self.sem[k] = nc.alloc_semaphore(name='s_' + k)
            self.cnt[k] = 0
        self.seen = {k: {} for k in self.eng}
        self.lastw = {}
        self.reads = {}
        self.dsem = {}

    def _wait(self, e, ev):
        if ev is None:
            return
        k, v = ev
        if e == 'pe' and k == 'pe':
            return
        if self.seen[e].get(k, 0) >= v:
            return
        self.eng[e].wait_ge(self.sem[k], v)
        self.seen[e][k] = v

    def deps(self, e, R, W):
        for t in R:
            self._wait(e, self.lastw.get(t))
        for t in W:
            self._wait(e, self.lastw.get(t))
            for ev in self.reads.get(t, []):
                self._wait(e, ev)

    def commit(self, ev, R, W):
        for t in R:
            self.reads.setdefault(t, []).append(ev)
        for t in W:
            self.lastw[t] = ev
            self.reads[t] = []

    def op(self, e, R, W, fn, inc=True):
        self.deps(e, R, W)
        ins = fn()
        if inc:
            self.cnt[e] += 1
            ins.then_inc(self.sem[e], 1)
            ev = (e, self.cnt[e])
        else:
            ev = (e, self.cnt[e] + 1)
        self.commit(ev, R, W)
        return ins

    def dma(self, q, R, W, out, in_, semtok=None, **kw):
        st = semtok if semtok is not None else (W[0] if W else R[0])
        if st not in self.dsem:
            key = 'd_' + str(st)
            self.sem[key] = self.nc.alloc_semaphore(name=key)
            self.cnt[key] = 0
            self.dsem[st] = key
        key = self.dsem[st]
        self.deps(q, R, W)
        ins = self.eng[q].dma_start(out=out, in_=in_, **kw)
        self.cnt[key] += 16
        ins.then_inc(self.sem[key], 16)
        ev = (key, self.cnt[key])
        self.commit(ev, R, W)
        return ins

    def finish(self, toks, e='sp'):
        for t in toks:
            self._wait(e, self.lastw.get(t))

    def barrier(self):
        for e in self.eng:
            for k in self.sem:
                if self.cnt[k] > 0 and not (e == 'pe' and k == 'pe') and self.seen[e].get(k, 0) < self.cnt[k]:
                    self.eng[e].wait_ge(self.sem[k], self.cnt[k])
                    self.seen[e][k] = self.cnt[k]

    def finish_all(self, e='sp'):
        for k in self.sem:
            if k.startswith('d_') and self.cnt[k] > 0 and self.seen[e].get(k, 0) < self.cnt[k]:
                self.eng[e].wait_ge(self.sem[k], self.cnt[k])
                self.seen[e][k] = self.cnt[k]

    def finish_dma(self, semtok, e='sp'):
        if semtok not in self.dsem:
            return
        key = self.dsem[semtok]
        self.eng[e].wait_ge(self.sem[key], self.cnt[key])


AX = mybir.AxisListType
NEG = -1.0e30
EPS = 1e-6


class Pool_:
    def __init__(self, nc, name, n, shape, dt):
        self.tiles = [nc.alloc_sbuf_tensor(f"{name}{i}", shape, dt) for i in range(n)]
        self.names = [f"{name}{i}" for i in range(n)]
        self.i = 0

    def get(self):
        t, n = self.tiles[self.i], self.names[self.i]
        self.i = (self.i + 1) % len(self.tiles)
        return t, n


def emit_attn(nc, fw, S, d, ps, cst_pool=None):
    NQB = S // 512
    NKT = S // 128
    sb = nc.alloc_sbuf_tensor
    Q1 = sb("aQ1", [128, S], BF16)
    Q2 = sb("aQ2", [128, S], BF16)
    K1 = sb("aK1", [128, S], BF16)
    K2 = sb("aK2", [128, S], BF16)
    V = sb("aV", [128, NKT, 128], BF16)
    OUT = sb("aOUT", [128, S], BF16)
    ones = sb("aones", [128, 128], BF16)
    mtri = sb("amtri", [128, 128], BF16)
    mones = sb("amones", [128, 128], BF16)
    biasA = sb("abiasA", [128, 5, 512], F32)
    maskB = sb("amaskB", [128, 4, 512], F32)
    maskC = sb("amaskC", [128, 4, 512], BF16)
    lamp = sb("alamp", [128, 256], F32)
    cst = sb("acst", [128, 4], F32)
    b15 = sb("ab15", [128, 1], F32)
    gsub = sb("agsub", [128, 1], F32)
    sm = sb("asm", [128, 16], F32)
    carry = sb("acarry", [128, 512], F32)
    Pp = Pool_(nc, "aP", 6, [128, 512], BF16)
    Tp = Pool_(nc, "aT", 6, [128, 512], F32)
    Lp_ = Pool_(nc, "aL", 4, [128, 512], BF16)

    fw.op('pool', [], ['Q2z'], lambda: nc.gpsimd.memset(Q2[64:128, :], 0.0))
    fw.op('pool', [], ['K2z'], lambda: nc.gpsimd.memset(K2[64:128, :], 0.0))
    fw.dma('sp', [], ['ones'], ones[:], d['ones'])
    fw.dma('sp', [], ['mtri'], mtri[:], d['mtri'])
    fw.dma('sp', [], ['biasA'], biasA[:], d['biasA'].rearrange("o p q -> p o q"))
    fw.dma('sp', [], ['maskB'], maskB[:], d['maskB'].rearrange("o p q -> p o q"))
    fw.dma('sp', [], ['maskC'], maskC[:], d['maskC'].rearrange("o p q -> p o q"))
    fw.dma('sp', [], ['lamp'], lamp[:], d['lamp'])
    fw.dma('sp', [], ['cst'], cst[:], d['cst'])
    fw.dma('sp', [], ['b15'], b15[:], d['b15'])
    fw.dma('sp', [], ['gsub'], gsub[:], d['gsub'])
    for o in range(4):
        fw.op('dve', ['biasA', 'maskB'], ['biasA'], lambda: nc.vector.tensor_tensor(biasA[:, o + 1, :], biasA[:, o + 1, :], maskB[:, o, :], ALU.add))
    T0, t0n = Tp.get()
    fw.op('dve', ['lamp'], [t0n], lambda: nc.vector.tensor_tensor(T0[:, 0:64], lamp[:, 0:64], lamp[:, 64:128], ALU.mult))
    fw.op('dve', [t0n], ['sm'], lambda: nc.vector.reduce_sum(sm[:, 0:1], T0[:, 0:64], AX.X))
    fw.op('dve', ['lamp'], [t0n], lambda: nc.vector.tensor_tensor(T0[:, 0:64], lamp[:, 128:192], lamp[:, 192:256], ALU.mult))
    fw.op('dve', [t0n], ['sm'], lambda: nc.vector.reduce_sum(sm[:, 1:2], T0[:, 0:64], AX.X))
    fw.op('act', ['sm'], ['sm'], lambda: nc.scalar.activation(sm[:, 2:4], sm[:, 0:2], AF.Exp))
    fw.op('dve', ['sm'], ['sm'], lambda: nc.vector.tensor_tensor(sm[:, 4:5], sm[:, 3:4], sm[:, 2:3], ALU.subtract))
    fw.op('dve', ['sm', 'cst'], ['sm'], lambda: nc.vector.tensor_tensor(sm[:, 4:5], sm[:, 4:5], cst[:, 0:1], ALU.subtract))
    fw.op('dve', ['gsub', 'cst'], ['sm'], lambda: nc.vector.tensor_tensor(sm[:, 5:6], gsub[:], cst[:, 1:2], ALU.mult))
    neglam = sm[:, 4:5]
    gcolA = sm[:, 5:6]

    def load_qkv(q1, q2, k1, k2, v):
        fw.dma('sp', [], ['Q1'], Q1[:], q1)
        fw.dma('act', [], ['K1'], K1[:], k1)
        if q2 is not None:
            fw.dma('sp', [], ['Q2'], Q2[0:64, :], q2)
            fw.dma('act', [], ['K2'], K2[0:64, :], k2)
        for c in range(0, NKT, 16):
            n = min(16, NKT - c)
            fw.dma('sp', [], ['V'], V[:, c:c + n, :], v[c * 128:(c + n) * 128, :].rearrange("(t p) e -> p t e", p=128))

    def store_out(dst, qb):
        fw.dma('sp', [f'out{qb}'], [], dst[:, qb * 512:(qb + 1) * 512], OUT[:, qb * 512:(qb + 1) * 512])

    def softmax_mixer(kind):
        nm = 2 if kind == 'A' else 1
        scale = 64 ** -0.5 if kind == 'A' else 192 ** -0.5
        if kind == 'A':
            Sb = [[ps[0], ps[1]], [ps[2], ps[3]]]
            Sn = [['ps0', 'ps1'], ['ps2', 'ps3']]
            NB, LA = 2, 1
        else:
            Sb = [[ps[0], ps[1], ps[2], ps[3]]]
            Sn = [['ps0', 'ps1', 'ps2', 'ps3']]
            NB, LA = 4, 2
        Ob = [ps[4], ps[5]]; On = ['ps4', 'ps5']
        Db = [ps[6], ps[7]]; Dn = ['ps6', 'ps7']

        for qb in range(NQB):
            q0 = qb * 512
            nk = 4 * (qb + 1)
            qs = slice(q0, q0 + 512)

            def emit_S(kt):
                ks = slice(kt * 128, kt * 128 + 128)
                b = kt % NB
                for m in range(nm):
                    if kind == 'A':
                        rs = slice(64 * m, 64 * m + 64)
                        fw.op('pe', ['K1', 'Q1'], [Sn[m][b]], lambda: nc.tensor.matmul(Sb[m][b][:], K1[rs, ks], Q1[rs, qs], start=True, stop=True))
                    else:
                        fw.op('pe', ['K1', 'Q1'], [Sn[m][b]], lambda: nc.tensor.matmul(Sb[m][b][:], K1[:, ks], Q1[:, qs], start=True, stop=False), inc=False)
                        fw.op('pe', ['K2', 'Q2', 'K2z', 'Q2z'], [Sn[m][b]], lambda: nc.tensor.matmul(Sb[m][b][:], K2[:, ks], Q2[:, qs], start=False, stop=True))

            for kk in range(min(LA, nk)):
                emit_S(kk)
            for kt in range(nk):
                b = kt % NB
                o = kt * 128 - q0
                Pt = []
                for m in range(nm):
                    P, pn = Pp.get()
                    near = (o >= -128) if kind == 'A' else (o >= 0)
                    if near:
                        oi = o // 128 + 1
                        T, tn = Tp.get()
                        btile = biasA[:, oi, :] if kind == 'A' else maskB[:, oi - 1, :]
                        btok = 'biasA' if kind == 'A' else 'maskB'
                        fw.op('dve', [Sn[m][b], btok], [tn], lambda: nc.vector.scalar_tensor_tensor(T[:], Sb[m][b][:], scale, btile, ALU.mult, ALU.add))
                        fw.op('act', [tn], [pn], lambda: nc.scalar.activation(P[:], T[:], AF.Exp))
                    else:
                        if kind == 'A':
                            fw.op('act', [Sn[m][b], 'b15'], [pn], lambda: nc.scalar.activation(P[:], Sb[m][b][:], AF.Exp, bias=b15[:], scale=scale))
                        else:
                            fw.op('act', [Sn[m][b]], [pn], lambda: nc.scalar.activation(P[:], Sb[m][b][:], AF.Exp, scale=scale))
                    Pt.append((P, pn))
                if kt + LA < nk:
                    emit_S(kt + LA)
                for m in range(nm):
                    P, pn = Pt[m]
                    fw.op('pe', ['V', pn], [On[m]], lambda: nc.tensor.matmul(Ob[m][:], V[:, kt, :], P[:], start=(kt == 0), stop=(kt == nk - 1)), inc=False)
                    fw.op('pe', ['ones', pn], [Dn[m]], lambda: nc.tensor.matmul(Db[m][:], ones[:], P[:], start=(kt == 0), stop=(kt == nk - 1)))
            res = []
            for m in range(nm):
                R, rn = Tp.get()
                fw.op('dve', [Dn[m]], [rn], lambda: nc.vector.reciprocal(R[:], Db[m][:]))
                if kind == 'B':
                    fw.op('dve', [On[m], rn], [f'out{qb}'], lambda: nc.vector.tensor_tensor(OUT[:, qs], Ob[m][:], R[:], ALU.mult))
                else:
                    fw.op('dve', [On[m], rn], [rn], lambda: nc.vector.tensor_tensor(R[:], Ob[m][:], R[:], ALU.mult))
                    res.append((R, rn))
            if kind == 'A':
                (R1, r1n), (R2, r2n) = res
                fw.op('dve', [r1n, r2n, 'sm'], [r1n], lambda: nc.vector.scalar_tensor_tensor(R1[:], R2[:], neglam, R1[:], ALU.mult, ALU.add))
                SQ, sqn = Pp.get()
                fw.op('act', [r1n], [sqn], lambda: nc.scalar.activation(SQ[:], R1[:], AF.Square))
                fw.op('pe', ['ones', sqn], ['ps0'], lambda: nc.tensor.matmul(ps[0][:], ones[:], SQ[:], start=True, stop=True))
                fw.op('act', ['ps0'], [r2n], lambda: nc.scalar.activation(R2[:], ps[0][:], AF.Ln, bias=EPS, scale=1.0 / 128))
                fw.op('act', [r2n], [r2n], lambda: nc.scalar.activation(R2[:], R2[:], AF.Exp, scale=-0.5))
                fw.op('dve', [r1n, r2n, 'sm'], [f'out{qb}'], lambda: nc.vector.scalar_tensor_tensor(OUT[:, qs], R1[:], gcolA, R2[:], ALU.mult, ALU.mult))
            store_out(d['oa'] if kind == 'A' else d['ob'], qb)

    def sb_mixer():
        Zb = [ps[0], ps[1]]; Zn = ['ps0', 'ps1']
        Ab = [ps[2], ps[3]]; An = ['ps2', 'ps3']
        Tb = [ps[4], ps[5]]; Tn = ['ps4', 'ps5']
        Op, On_ = ps[6], 'ps6'
        for qb in range(NQB):
            q0 = qb * 512
            qs = slice(q0, q0 + 512)
            kts = list(range(4 * qb + 3, -1, -1))
            nk = len(kts)
            Ls = {}
            Ws = {}

            def emit_Z(i):
                kt = kts[i]
                ks = slice(kt * 128, kt * 128 + 128)
                fw.op('pe', ['K1', 'Q1'], [Zn[i % 2]], lambda: nc.tensor.matmul(Zb[i % 2][:], K1[:, ks], Q1[:, qs], start=True, stop=True))

            def stage1(i):
                kt = kts[i]
                b = i % 2
                o = kt * 128 - q0
                E, en = Tp.get()
                fw.op('act', [Zn[b]], [en], lambda: nc.scalar.activation(E[:], Zb[b][:], AF.Exp))
                L, ln_ = Lp_.get()
                fw.op('act', [en], [ln_], lambda: nc.scalar.activation(L[:], E[:], AF.Ln, bias=1.0))
                if o >= 0:
                    fw.op('pool', [ln_, 'maskC'], [ln_], lambda: nc.gpsimd.tensor_tensor(L[:], L[:], maskC[:, o // 128, :], ALU.mult))
                Ls[i] = (L, ln_)

            def emit_O(i):
                kt = kts[i]
                W, wn = Ws.pop(i)
                fw.op('pe', ['V', wn], [On_], lambda: nc.tensor.matmul(Op[:], V[:, kt, :], W[:], start=(i == 0), stop=(i == nk - 1)))

            fw.op('dve', [], ['carry'], lambda: nc.vector.memset(carry[:], 0.0))
            emit_Z(0)
            if nk > 1:
                emit_Z(1)
            stage1(0)
            for i, kt in enumerate(kts):
                b = i % 2
                ks = slice(kt * 128, kt * 128 + 128)
                o = kt * 128 - q0
                L, ln_ = Ls.pop(i)
                fw.op('pe', ['K1', 'Q1'], [An[b]], lambda: nc.tensor.matmul(Ab[b][:], K1[:, ks], Q1[:, qs], start=True, stop=False), inc=False)
                fw.op('pe', ['mtri', ln_], [An[b]], lambda: nc.tensor.matmul(Ab[b][:], mtri[:], L[:], start=False, stop=True))
                fw.op('pe', ['ones', ln_], [Tn[b]], lambda: nc.tensor.matmul(Tb[b][:], ones[:], L[:], start=True, stop=True))
                if i >= 1:
                    emit_O(i - 1)
                if i + 1 < nk:
                    stage1(i + 1)
                if i + 2 < nk:
                    emit_Z(i + 2)
                G, gn = Tp.get()
                fw.op('dve', [An[b], 'carry'], [gn], lambda: nc.vector.tensor_tensor(G[:], Ab[b][:], carry[:], ALU.subtract))
                W, wn = Pp.get()
                fw.op('act', [gn], [wn], lambda: nc.scalar.activation(W[:], G[:], AF.Exp))
                if o >= 0:
                    fw.op('pool', [wn, 'maskC'], [wn], lambda: nc.gpsimd.tensor_tensor(W[:], W[:], maskC[:, o // 128, :], ALU.mult))
                fw.op('dve', [Tn[b], 'carry'], ['carry'], lambda: nc.vector.tensor_tensor(carry[:], carry[:], Tb[b][:], ALU.add))
                Ws[i] = (W, wn)
            emit_O(nk - 1)
            fw.op('act', [On_], [f'out{qb}'], lambda: nc.scalar.copy(OUT[:, qs], Op[:]))
            store_out(d['oc'], qb)

    import os
    MIX = os.environ.get('MIX', 'ABC')
    if 'A' in MIX:
        load_qkv(d['qa'], None, d['ka'], None, d['va'])
        softmax_mixer('A')
    if 'B' in MIX:
        load_qkv(d['qbn'], d['qbr'], d['kbn'], d['kbr'], d['vb'])
        softmax_mixer('B')
    if 'C' in MIX:
        load_qkv(d['qc'], None, d['kc'], None, d['vc'])
        sb_mixer()


def t5_bucket_np(rel):
    nb = 16
    max_exact = 8
    n = np.abs(rel)
    large = max_exact + (np.log(np.maximum(n, 1).astype(np.float32) / np.float32(max_exact)) / np.float32(np.log(128 / 8)) * np.float32(nb - max_exact)).astype(np.int32)
    large = np.minimum(large, nb - 1)
    return np.where(rel > 0, nb, 0) + np.where(n < max_exact, n, large)


def attn_consts():
    kk = np.arange(128)[:, None]
    qq = np.arange(512)[None, :]
    offs = [-128, 0, 128, 256, 384]
    bucket = np.stack([t5_bucket_np((o + kk) - qq) for o in offs])
    maskB = np.stack([np.where(((o + kk) // 64) <= (qq // 64), 0.0, NEG) for o in offs[1:]]).astype(np.float32)
    maskC = np.stack([np.where((o + kk) < qq, 1.0, 0.0) for o in offs[1:]]).astype(ml_dtypes.bfloat16)
    jj = np.arange(128)[:, None]
    kc = np.arange(128)[None, :]
    mtri = np.where(jj >= kc, -1.0, 0.0).astype(ml_dtypes.bfloat16)
    ones = np.ones((128, 128), ml_dtypes.bfloat16)
    return dict(bucket=bucket, maskB=maskB, maskC=maskC, mtri=mtri, ones=ones)


TT = 1024
O_AQ, O_AK, O_AV, O_CQ, O_CKV, O_KPE, O_SQ, O_SK, O_SV, O_G = 0, 1024, 2048, 3072, 3584, 3840, 3904, 4928, 5952, 6976


class Slabs:
    def __init__(self, nc, fw, n=3):
        self.nc, self.fw = nc, fw
        self.pool = Pool_(nc, "slab", n, [128, 16, 512], BF16)

    def load(self, w, r0, nrows, c0, ncols):
        t, tok = self.pool.get()
        nch = nrows // 128
        step = 4
        for c in range(0, nch, step):
            n = min(step, nch - c)
            self.fw.dma('pool', [], [tok], t[:, c:c + n, 0:ncols],
                        w[r0 + c * 128:r0 + (c + n) * 128, c0:c0 + ncols].rearrange("(c p) n -> p c n", p=128))
        return t, tok


def emit_norm_h(nc, fw, X, H, acol, bcol, ones, ps, psn, Tp, Pp, RS):
    for tb in range(TT // 512):
        ts = slice(tb * 512, tb * 512 + 512)
        for c in range(16):
            SQ, sqn = Pp.get()
            fw.op('act', ['X'], [sqn], lambda: nc.scalar.activation(SQ[:], X[:, c, ts], AF.Square))
            fw.op('pe', ['ones', sqn], [psn], lambda: nc.tensor.matmul(ps[:], ones[:], SQ[:], start=(c == 0), stop=(c == 15)))
        fw.op('act', [psn], ['RS'], lambda: nc.scalar.activation(RS[:, ts], ps[:], AF.Ln, bias=EPS, scale=1.0 / 2048))
        fw.op('act', ['RS'], ['RS'], lambda: nc.scalar.activation(RS[:, ts], RS[:, ts], AF.Exp, scale=-0.5))
        for c in range(16):
            T, tn = Tp.get()
            fw.op('dve', ['X', 'RS', 'cols'], [tn], lambda: nc.vector.scalar_tensor_tensor(T[:], X[:, c, ts], acol[:, c:c + 1], RS[:, ts], ALU.mult, ALU.mult))
            fw.op('dve', [tn, 'cols'], ['H'], lambda: nc.vector.tensor_scalar(H[:, c, ts], T[:], bcol[:, c:c + 1], None, ALU.add))


def emit_proj(nc, fw, d, ps):
    sb = nc.alloc_sbuf_tensor
    H = sb("H", [128, 16, TT], BF16)
    RS = sb("RS", [128, TT], F32)
    ones = sb("pones", [128, 128], BF16)
    bd64 = sb("pbd64", [128, 128], BF16)
    rotm = sb("protm", [64, 64], BF16)
    modc = sb("pmodc", [128, 6, 16], F32)
    gmix = sb("pgmix", [128, 16], F32)
    cols = sb("pcols", [128, 2, 16], F32)
    dqk = sb("pdqk", [128, 2], F32)
    gq = sb("pgq", [128, 4], F32)
    gkv = sb("pgkv", [128, 2], F32)
    mqk = sb("pmqk", [128, 4], F32)
    cosT = sb("pcos", [64, TT], F32)
    sinT = sb("psin", [64, TT], F32)
    Pp = Pool_(nc, "pP", 4, [128, 512], BF16)
    Tp = Pool_(nc, "pT", 4, [128, 512], F32)
    fw.dma('act', [], ['ones'], ones[:], d['ones'])
    fw.dma('act', [], ['modc'], modc[:], d['modc'])
    fw.dma('act', [], ['gmix'], gmix[:], d['gmix'])
    with nc.sbuf_tensor("X", [128, 16, TT], F32) as X:
        for c in range(0, 16, 4):
            fw.dma('sp', [], ['X'], X[:, c:c + 4, :], d['xT'][c * 128:(c + 4) * 128, :].rearrange("(c p) t -> p c t", p=128))
        fw.op('dve', ['modc', 'gmix'], ['cols'], lambda: nc.vector.scalar_tensor_tensor(cols[:, 0, :], modc[:, 1, :], 1.0, gmix[:], ALU.add, ALU.mult))
        fw.op('dve', ['modc'], ['cols'], lambda: nc.vector.tensor_copy(cols[:, 1, :], modc[:, 0, :]))
        emit_norm_h(nc, fw, X, H, cols[:, 0, :], cols[:, 1, :], ones, ps[4], 'ps4', Tp, Pp, RS)
    fw.barrier()
    WQ = sb("pwq", [128, 4, 1536], BF16)
    WKV = sb("pwkv", [128, 2, 2048], BF16)
    CQ = sb("pCQ", [128, 4, TT], F32)
    CKV = sb("pCKV", [128, 2, TT], F32)
    KPE = sb("pKPE", [64, TT], F32)
    CQN = sb("pCQN", [128, 4, TT], BF16)
    CKVN = sb("pCKVN", [128, 2, TT], BF16)
    Op = Pool_(nc, "pO", 4, [128, 512], BF16)
    slabs = Slabs(nc, fw)
    projps = [(ps[i], f'ps{i}') for i in range(4)]
    pi = [0]

    def nextps():
        p = projps[pi[0] % 4]
        pi[0] += 1
        return p

    nps = [(ps[4], 'ps4'), (ps[5], 'ps5')]
    ni = [0]

    def nextnps():
        p = nps[ni[0] % 2]
        ni[0] += 1
        return p

    fw.dma('act', [], ['bd64'], bd64[:], d['bd64'])
    fw.dma('act', [], ['rotm'], rotm[:], d['rotm'])
    fw.dma('act', [], ['dqk'], dqk[:], d['dqk'])
    fw.dma('act', [], ['gq'], gq[:], d['gq'])
    fw.dma('act', [], ['gkv'], gkv[:], d['gkv'])
    fw.dma('act', [], ['mqk'], mqk[:], d['mqk'])
    fw.dma('act', [], ['cos'], cosT[:], d['cosT'])
    fw.dma('act', [], ['sin'], sinT[:], d['sinT'])
    import os
    for c in range(4 if not os.environ.get('NOWQ') else 0):
        fw.dma('pool', [], ['WQ'], WQ[:, c, :], d['wqup'][c * 128:(c + 1) * 128, :], semtok='WQ')
    for c in range(2 if not os.environ.get('NOWQ') else 0):
        fw.dma('pool', [], ['WKV'], WKV[:, c, :], d['wkvup'][c * 128:(c + 1) * 128, :], semtok='WKV')
    w = d['w_in']

    def rstd_of(sq_list, lhsT, ltok, nfeat, P=128):
        pp, ppn = nextnps()
        for i, (sq, sqn) in enumerate(sq_list):
            fw.op('pe', [ltok, sqn], [ppn], lambda: nc.tensor.matmul(pp[0:P, :], lhsT, sq, start=(i == 0), stop=(i == len(sq_list) - 1)))
        R, rn = Tp.get()
        fw.op('act', [ppn], [rn], lambda: nc.scalar.activation(R[0:P, :], pp[0:P, :], AF.Ln, bias=EPS, scale=1.0 / nfeat))
        fw.op('act', [rn], [rn], lambda: nc.scalar.activation(R[0:P, :], R[0:P, :], AF.Exp, scale=-0.5))
        return R, rn

    def fm_chunk(slab, stok, nrc, col, ncols, tb, rhs_src, rtok):
        p, pn = nextps()
        ts = slice(tb * 512, tb * 512 + 512)
        for c in range(nrc):
            fw.op('pe', [stok, rtok], [pn], lambda: nc.tensor.matmul(p[0:ncols, :], slab[:, c, col:col + ncols], rhs_src[:, c, ts], start=(c == 0), stop=(c == nrc - 1)), inc=(c == nrc - 1))
        return p, pn

    def out_dma(dst, O, on):
        fw.dma('sp', [on], [], dst, O)

    def qknorm_out(p, pn, P, lhsT, ltok, nfeat, gcol, gtok, dst, scale_extra=1.0):
        SQ, sqn = Pp.get()
        fw.op('act', [pn], [sqn], lambda: nc.scalar.activation(SQ[0:P, :], p[0:P, :], AF.Square))
        R, rn = rstd_of([(SQ[0:P, :], sqn)], lhsT, ltok, nfeat, P)
        O, on = Op.get()
        fw.op('dve', [pn, rn, gtok], [on], lambda: nc.vector.scalar_tensor_tensor(O[0:P, :], p[0:P, :], gcol, R[0:P, :], ALU.mult, ALU.mult))
        out_dma(dst, O[0:P, :], on)

    def rope_out(src, stok, P, gcol, gtok, tb, dst):
        ts = slice(tb * 512, tb * 512 + 512)
        SQ, sqn = Pp.get()
        fw.op('act', [stok], [sqn], lambda: nc.scalar.activation(SQ[0:64, :], src, AF.Square))
        R, rn = rstd_of([(SQ[0:64, :], sqn)], ones[0:64, 0:64], 'ones', 64, 64)
        QN, qn = Tp.get()
        fw.op('dve', [stok, rn, gtok], [qn], lambda: nc.vector.scalar_tensor_tensor(QN[0:64, :], src, gcol, R[0:64, :], ALU.mult, ALU.mult))
        QB, qbn_ = Pp.get()
        fw.op('act', [qn], [qbn_], lambda: nc.scalar.copy(QB[0:64, :], QN[0:64, :]))
        rp, rpn = (ps[6], 'ps6') if tb == 0 else (ps[7], 'ps7')
        fw.op('pe', ['rotm', qbn_], [rpn], lambda: nc.tensor.matmul(rp[0:64, :], rotm[:], QB[0:64, :], start=True, stop=True))
        T2, t2n = Tp.get()
        fw.op('dve', [rpn, 'sin'], [t2n], lambda: nc.vector.tensor_tensor(T2[0:64, :], rp[0:64, :], sinT[:, ts], ALU.mult))
        fw.op('dve', [qn, 'cos'], [qn], lambda: nc.vector.tensor_tensor(QN[0:64, :], QN[0:64, :], cosT[:, ts], ALU.mult))
        O, on = Op.get()
        fw.op('dve', [qn, t2n], [on], lambda: nc.vector.tensor_tensor(O[0:64, :], QN[0:64, :], T2[0:64, :], ALU.add))
        out_dma(dst, O[0:64, :], on)

    def fm_group(c0, ncols_total, handler):
        for s0 in range(0, ncols_total, 512):
            ncs = min(512, ncols_total - s0)
            slab, stok = slabs.load(w, 0, 2048, c0 + s0, ncs)
            for tb in range(2):
                for cc in range(0, ncs, 128):
                    nco = min(128, ncs - cc)
                    p, pn = fm_chunk(slab, stok, 16, cc, nco, tb, H, 'H')
                    handler((s0 + cc) // 128, tb, p, pn, nco)

    def tsl(tb):
        return slice(tb * 512, tb * 512 + 512)

    import os
    STAGE = int(os.environ.get('PSTAGE', '99'))
    if STAGE < 1:
        return
    fm_group(O_AQ, 1024, lambda ch, tb, p, pn, n: qknorm_out(p, pn, 128, bd64[:], 'bd64', 64, dqk[:, 0:1], 'dqk', d['qa'][ch, :, tsl(tb)]))
    fm_group(O_AK, 1024, lambda ch, tb, p, pn, n: qknorm_out(p, pn, 128, bd64[:], 'bd64', 64, dqk[:, 1:2], 'dqk', d['ka'][ch, :, tsl(tb)]))

    def cq_h(ch, tb, p, pn, n):
        O, on = Op.get()
        fw.op('act', [pn], [on], lambda: nc.scalar.activation(O[:], p[:], AF.Copy, scale=128 ** -0.5))
        out_dma(d['qc'][ch, :, tsl(tb)], O[:], on)

    def ck_h(ch, tb, p, pn, n):
        O, on = Op.get()
        fw.op('act', [pn], [on], lambda: nc.scalar.copy(O[:], p[:]))
        out_dma(d['kc'][ch, :, tsl(tb)], O[:], on)

    if STAGE < 2:
        return
    fm_group(O_SQ, 1024, cq_h)
    fm_group(O_SK, 1024, ck_h)

    def lat_cq(ch, tb, p, pn, n):
        fw.op('act', [pn], ['CQ'], lambda: nc.scalar.copy(CQ[:, ch, tsl(tb)], p[:]))

    def lat_ckv(ch, tb, p, pn, n):
        if ch < 2:
            fw.op('act', [pn], ['CKV'], lambda: nc.scalar.copy(CKV[:, ch, tsl(tb)], p[:]))
        else:
            fw.op('act', [pn], ['KPE'], lambda: nc.scalar.copy(KPE[:, tsl(tb)], p[0:64, :]))

    if STAGE < 3:
        return
    fm_group(O_CQ, 512, lat_cq)
    fm_group(O_CKV, 320, lat_ckv)

    if STAGE < 4:
        return
    for tb in range(2):
        ts = tsl(tb)
        for (RAW, rtok, NRM, ntok, nch, gg, gtok) in [(CQ, 'CQ', CQN, 'CQN', 4, gq, 'gq'), (CKV, 'CKV', CKVN, 'CKVN', 2, gkv, 'gkv')]:
            sql = []
            for c in range(nch):
                SQ, sqn = Pp.get()
                fw.op('act', [rtok], [sqn], lambda: nc.scalar.activation(SQ[:], RAW[:, c, ts], AF.Square))
                sql.append((SQ[:], sqn))
            R, rn = rstd_of(sql, ones[:], 'ones', nch * 128)
            for c in range(nch):
                fw.op('dve', [rtok, rn, gtok], [ntok], lambda: nc.vector.scalar_tensor_tensor(NRM[:, c, ts], RAW[:, c, ts], gg[:, c:c + 1], R[:], ALU.mult, ALU.mult))
        rope_out(KPE[:, ts], 'KPE', 64, mqk[0:64, 3:4], 'mqk', tb, d['kbr'][:, ts])

    if STAGE < 5:
        return
    for tb in range(2):
        for hh in range(8):
            p, pn = fm_chunk(WQ, 'WQ', 4, hh * 128, 128, tb, CQN, 'CQN')
            qknorm_out(p, pn, 128, ones[:], 'ones', 128, mqk[:, 0:1], 'mqk', d['qbn'][hh, :, tsl(tb)])
        for hh in range(8):
            p, pn = fm_chunk(WQ, 'WQ', 4, 1024 + hh * 64, 64, tb, CQN, 'CQN')
            rope_out(p[0:64, :], pn, 64, mqk[0:64, 2:3], 'mqk', tb, d['qbr'][hh, :, tsl(tb)])
        for hh in range(8):
            p, pn = fm_chunk(WKV, 'WKV', 2, hh * 128, 128, tb, CKVN, 'CKVN')
            qknorm_out(p, pn, 128, ones[:], 'ones', 128, mqk[:, 1:2], 'mqk', d['kbn'][hh, :, tsl(tb)])
    if STAGE < 6:
        return
    for tt in range(TT // 128):
        for nb in range(2):
            p, pn = nextps()
            for c in range(2):
                fw.op('pe', ['CKVN', 'WKV'], [pn], lambda: nc.tensor.matmul(p[:], CKVN[:, c, tt * 128:(tt + 1) * 128], WKV[:, c, 1024 + nb * 512:1024 + (nb + 1) * 512], start=(c == 0), stop=(c == 1)), inc=(c == 1))
            O, on = Op.get()
            fw.op('act', [pn], [on], lambda: nc.scalar.copy(O[:], p[:]))
            out_dma(d['vb'][tt * 128:(tt + 1) * 128, nb * 512:(nb + 1) * 512], O[:], on)

    if STAGE < 7:
        return
    for (c0, dst) in [(O_AV, d['va']), (O_SV, d['vc'])]:
        for s0 in range(0, 1024, 512):
            slab, stok = slabs.load(w, 0, 2048, c0 + s0, 512)
            for tt in range(TT // 128):
                p, pn = nextps()
                for c in range(16):
                    fw.op('pe', ['H', stok], [pn], lambda: nc.tensor.matmul(p[:], H[:, c, tt * 128:(tt + 1) * 128], slab[:, c, :], start=(c == 0), stop=(c == 15)), inc=(c == 15))
                O, on = Op.get()
                fw.op('act', [pn], [on], lambda: nc.scalar.copy(O[:], p[:]))
                out_dma(dst[tt * 128:(tt + 1) * 128, s0:s0 + 512], O[:], on)
    return ['dramout']


def proj_consts(core):
    half = 32
    inv = (10000.0 ** (-np.arange(half, dtype=np.float32) / half)).astype(np.float32)
    pos = np.arange(core * TT, (core + 1) * TT).astype(np.float32)
    ang = pos[None, :] * np.concatenate([inv, inv])[:, None]
    cosT = np.cos(ang).astype(np.float32)
    sinT = np.sin(ang).astype(np.float32)
    rot = np.zeros((64, 64), np.float32)
    for p in range(32):
        rot[p + 32, p] = -1.0
        rot[p, p + 32] = 1.0
    bd = np.zeros((128, 128), np.float32)
    bd[:64, :64] = 1
    bd[64:, 64:] = 1
    return dict(cosT=cosT, sinT=sinT, rotm=rot.astype(ml_dtypes.bfloat16), bd64=bd.astype(ml_dtypes.bfloat16), ones=np.ones((128, 128), ml_dtypes.bfloat16))


def _common(nc, fw, d, ps, Tp, Pp, H, RS, ones, modc, gvec, gtok, cols, sc_idx, sh_idx, X):
    fw.op('dve', ['modc', gtok], ['cols'], lambda: nc.vector.scalar_tensor_tensor(cols[:, 0, :], modc[:, sc_idx, :], 1.0, gvec[:], ALU.add, ALU.mult))
    fw.op('dve', ['modc'], ['cols'], lambda: nc.vector.tensor_copy(cols[:, 1, :], modc[:, sh_idx, :]))
    emit_norm_h(nc, fw, X, H, cols[:, 0, :], cols[:, 1, :], ones, ps[7], 'ps7', Tp, Pp, RS)


def tsl(tb):
    return slice(tb * 512, tb * 512 + 512)


def emit_merge(nc, fw, d, ps):
    sb = nc.alloc_sbuf_tensor
    H = sb("H", [128, 16, TT], BF16)
    RS = sb("RS", [128, TT], F32)
    MG = sb("MG", [128, 16, TT], BF16)
    MGF = [[sb(f"MGF{i}_{tb}", [128, 512], F32) for tb in range(2)] for i in range(4)]
    ones = sb("pones", [128, 128], BF16)
    modc = sb("pmodc", [128, 6, 16], F32)
    gmix = sb("pgmix", [128, 16], F32)
    cols = sb("pcols", [128, 2, 16], F32)
    Pp = Pool_(nc, "pP", 4, [128, 512], BF16)
    Tp = Pool_(nc, "pT", 6, [128, 512], F32)
    pslist = [(ps[i], f'ps{i}') for i in range(7)]
    pi = [0]

    def nextps():
        p = pslist[pi[0] % 7]
        pi[0] += 1
        return p

    fw.dma('act', [], ['ones'], ones[:], d['ones'])
    fw.dma('act', [], ['modc'], modc[:], d['modc'])
    fw.dma('act', [], ['gmix'], gmix[:], d['gmix'])
    with nc.sbuf_tensor("X", [128, 16, TT], F32) as X:
        for c in range(0, 16, 4):
            fw.dma('sp', [], ['X'], X[:, c:c + 4, :], d['xT'][c * 128:(c + 4) * 128, :].rearrange("(c p) t -> p c t", p=128))
        _common(nc, fw, d, ps, Tp, Pp, H, RS, ones, modc, gmix, 'gmix', cols, 1, 0, X)
    fw.barrier()
    slabs = Slabs(nc, fw, n=3)
    BRp = Pool_(nc, "BR", 2, [128, 8, TT], BF16)
    w_in, wbr = d['w_in'], d['w_branch']
    for s in range(4):
        for i in range(3):
            gs, gtok = slabs.load(w_in, 0, 2048, i * 2048 + s * 512, 512)
            us, utok = slabs.load(wbr[i], 0, 1024, s * 512, 512)
            BR, brn = BRp.get()
            fw.dma('act', [], [brn], BR[:], d['br'][i].rearrange("c p t -> p c t"))
            for tb in range(2):
                ts = tsl(tb)
                for cc in range(4):
                    n = s * 4 + cc
                    mtok = 'MGf%d_%d' % (cc, tb)
                    gp, gpn = nextps()
                    for c in range(16):
                        fw.op('pe', [gtok, 'H'], [gpn], lambda: nc.tensor.matmul(gp[:], gs[:, c, cc * 128:(cc + 1) * 128], H[:, c, ts], start=(c == 0), stop=(c == 15)), inc=(c == 15))
                    up, upn = nextps()
                    for c in range(8):
                        fw.op('pe', [utok, brn], [upn], lambda: nc.tensor.matmul(up[:], us[:, c, cc * 128:(cc + 1) * 128], BR[:, c, ts], start=(c == 0), stop=(c == 7)), inc=(c == 7))
                    SG, sgn = Tp.get()
                    fw.op('act', [gpn], [sgn], lambda: nc.scalar.activation(SG[:], gp[:], AF.Sigmoid))
                    if i == 0:
                        fw.op('dve', [upn, sgn], [mtok], lambda: nc.vector.tensor_tensor(MGF[cc][tb][:], up[:], SG[:], ALU.mult))
                    elif i == 1:
                        fw.op('dve', [upn, sgn], [sgn], lambda: nc.vector.tensor_tensor(SG[:], up[:], SG[:], ALU.mult))
                        fw.op('dve', [sgn, mtok], [mtok], lambda: nc.vector.tensor_tensor(MGF[cc][tb][:], MGF[cc][tb][:], SG[:], ALU.add))
                    else:
                        fw.op('dve', [upn, sgn], [sgn], lambda: nc.vector.tensor_tensor(SG[:], up[:], SG[:], ALU.mult))
                        fw.op('dve', [sgn, mtok], ['MG'], lambda: nc.vector.tensor_tensor(MG[:, n, ts], MGF[cc][tb][:], SG[:], ALU.add))
    for c in range(0, 16, 4):
        fw.dma('sp', ['MG'], [], d['mg'][c * 128:(c + 4) * 128, :].rearrange("(c p) t -> p c t", p=128), MG[:, c:c + 4, :])


def emit_mlp(nc, fw, d, ps):
    sb = nc.alloc_sbuf_tensor
    X = sb("X", [128, 16, TT], F32)
    H = sb("H", [128, 16, TT], BF16)
    RS = sb("RS", [128, TT], F32)
    MG = sb("MG", [128, 16, TT], BF16)
    ones = sb("pones", [128, 128], BF16)
    modc = sb("pmodc", [128, 6, 16], F32)
    gmlp = sb("pgmlp", [128, 16], F32)
    cols = sb("pcols", [128, 2, 16], F32)
    Pp = Pool_(nc, "pP", 4, [128, 512], BF16)
    Tp = Pool_(nc, "pT", 6, [128, 512], F32)
    slabs = Slabs(nc, fw, n=3)
    pslist = [(ps[i], f'ps{i}') for i in range(7)]
    pi = [0]

    def nextps():
        p = pslist[pi[0] % 7]
        pi[0] += 1
        return p

    for c in range(0, 16, 4):
        fw.dma('sp', [], ['X'], X[:, c:c + 4, :], d['xT'][c * 128:(c + 4) * 128, :].rearrange("(c p) t -> p c t", p=128))
        fw.dma('act', [], ['MG'], MG[:, c:c + 4, :], d['mg'][c * 128:(c + 4) * 128, :].rearrange("(c p) t -> p c t", p=128))
    fw.dma('act', [], ['ones'], ones[:], d['ones'])
    fw.dma('act', [], ['modc'], modc[:], d['modc'])
    fw.dma('act', [], ['gmlp'], gmlp[:], d['gmlp'])
    wout, w1, w2 = d['w_out'], d['w_mlp_in'], d['w_mlp_out']

    def resid_matmul(W, r0, gcolidx):
        for s in range(4):
            sl, stk = slabs.load(W, r0, 2048, s * 512, 512)
            for tb in range(2):
                ts = tsl(tb)
                for cc in range(4):
                    n = s * 4 + cc
                    p, pn = nextps()
                    for c in range(16):
                        fw.op('pe', [stk, 'MG'], [pn], lambda: nc.tensor.matmul(p[:], sl[:, c, cc * 128:(cc + 1) * 128], MG[:, c, ts], start=(c == 0), stop=(c == 15)), inc=(c == 15))
                    fw.op('dve', [pn, 'X', 'modc'], ['X'], lambda: nc.vector.scalar_tensor_tensor(X[:, n, ts], p[:], modc[:, gcolidx, n:n + 1], X[:, n, ts], ALU.mult, ALU.add))

    resid_matmul(wout, 0, 2)
    _common(nc, fw, d, ps, Tp, Pp, H, RS, ones, modc, gmlp, 'gmlp', cols, 4, 3, X)
    for g in range(4):
        for s in range(4):
            sl, stk = slabs.load(w1, 0, 2048, g * 2048 + s * 512, 512)
            for tb in range(2):
                ts = tsl(tb)
                for cc in range(4):
                    f = s * 4 + cc
                    p, pn = nextps()
                    for c in range(16):
                        fw.op('pe', [stk, 'H'], [pn], lambda: nc.tensor.matmul(p[:], sl[:, c, cc * 128:(cc + 1) * 128], H[:, c, ts], start=(c == 0), stop=(c == 15)), inc=(c == 15))
                    R, rn = Tp.get()
                    fw.op('act', [pn], [rn], lambda: nc.scalar.activation(R[:], p[:], AF.Relu))
                    fw.op('dve', [rn], ['MG'], lambda: nc.vector.tensor_tensor(MG[:, f, ts], R[:], R[:], ALU.mult))
        resid_matmul(w2, g * 2048, 5)
    for c in range(0, 16, 4):
        fw.dma('sp', ['X'], [], d['xo'][c * 128:(c + 4) * 128, :].rearrange("(c p) t -> p c t", p=128), X[:, c:c + 4, :])


def emit_mod(nc, fw, d, ps):
    sb = nc.alloc_sbuf_tensor
    cT = sb("cT_sb", [128, 16], F32)
    fw.dma('sp', [], ['cT'], cT[:], d['cT'])
    bia = sb("bia_sb", [1, 4 * 1536], F32)
    fw.dma('sp', [], ['bia'], bia[:], d['b'])
    res = sb("res_sb", [1, 4 * 1536], F32)
    wp = Pool_(nc, "wm", 2, [128, 16, 512], F32)
    k = 0
    for l in range(4):
        for s in range(3):
            W, wn = wp.get()
            for c in range(0, 16, 4):
                fw.dma('sp' if (k % 2 == 0) else 'act', [], [wn], W[:, c:c + 4, :], d['w'][l, c * 128:(c + 4) * 128, s * 512:(s + 1) * 512].rearrange("(c p) n -> p c n", p=128))
            p, pn = ps[k % 2], f'ps{k % 2}'
            for c in range(16):
                fw.op('pe', [wn, 'cT'], [pn], lambda: nc.tensor.matmul(p[0:1, :], cT[:, c:c + 1], W[:, c, :], start=(c == 0), stop=(c == 15)), inc=(c == 15))
            o0 = l * 1536 + s * 512
            fw.op('dve', [pn, 'bia'], ['res'], lambda: nc.vector.tensor_tensor(res[:, o0:o0 + 512], p[0:1, :], bia[:, o0:o0 + 512], ALU.add))
            k += 1
    fw.dma('sp', ['res'], [], d['out'], res[:])


bf = ml_dtypes.bfloat16
NCORES = 8
S = 8192
DEBUG = bool(int(os.environ.get("MKDEBUG", "0")))
_progs = {}


def _mk(name, builder):
    if name not in _progs:
        _progs[name] = builder()
    return _progs[name]


def _din(nc, d, name, shape, dt):
    d[name] = nc.dram_tensor(name, list(shape), dt, kind="ExternalInput").ap()


def _dout(nc, d, name, shape, dt):
    d[name] = nc.dram_tensor(name, list(shape), dt, kind="ExternalOutput").ap()


def _psum(nc):
    return [nc.alloc_psum_tensor(f"ps{i}", [128, 512], F32) for i in range(8)]


def build_mod():
    nc = bass.Bass("TRN2", target_bir_lowering=False)
    fw = FW(nc)
    d = {}
    _din(nc, d, 'cT', (128, 16), F32)
    _din(nc, d, 'b', (1, 6144), F32)
    _din(nc, d, 'w', (4, 2048, 1536), F32)
    _dout(nc, d, 'out', (1, 6144), F32)
    emit_mod(nc, fw, d, _psum(nc))
    fw.finish_all()
    return nc


def build_proj():
    nc = bass.Bass("TRN2", target_bir_lowering=False)
    fw = FW(nc)
    d = {}
    _din(nc, d, 'xT', (2048, TT), F32)
    _din(nc, d, 'w_in', (2048, 6976), F32)
    for n, s, t in [('ones', (128, 128), BF16), ('bd64', (128, 128), BF16), ('rotm', (64, 64), BF16), ('modc', (128, 6, 16), F32),
                    ('gmix', (128, 16), F32), ('dqk', (128, 2), F32), ('gq', (128, 4), F32), ('gkv', (128, 2), F32), ('mqk', (128, 4), F32),
                    ('cosT', (64, TT), F32), ('sinT', (64, TT), F32), ('wqup', (512, 1536), F32), ('wkvup', (256, 2048), F32)]:
        _din(nc, d, n, s, t)
    for n in ['qa', 'ka', 'qbn', 'kbn', 'qc', 'kc']:
        _dout(nc, d, n, (8, 128, TT), BF16)
    _dout(nc, d, 'qbr', (8, 64, TT), BF16)
    _dout(nc, d, 'kbr', (64, TT), BF16)
    for n in ['va', 'vb', 'vc']:
        _dout(nc, d, n, (TT, 1024), BF16)
    emit_proj(nc, fw, d, _psum(nc))
    fw.finish_all()
    return nc


def build_attn():
    nc = bass.Bass("TRN2", target_bir_lowering=False)
    fw = FW(nc)
    d = {}
    for n in ['qa', 'ka', 'qbn', 'kbn', 'qc', 'kc']:
        _din(nc, d, n, (128, S), BF16)
    for n in ['qbr', 'kbr']:
        _din(nc, d, n, (64, S), BF16)
    for n in ['va', 'vb', 'vc']:
        _din(nc, d, n, (S, 128), BF16)
    for n, s, t in [('ones', (128, 128), BF16), ('mtri', (128, 128), BF16), ('biasA', (5, 128, 512), F32), ('maskB', (4, 128, 512), F32),
                    ('maskC', (4, 128, 512), BF16), ('lamp', (128, 256), F32), ('cst', (128, 4), F32), ('b15', (128, 1), F32), ('gsub', (128, 1), F32)]:
        _din(nc, d, n, s, t)
    for n in ['oa', 'ob', 'oc']:
        _dout(nc, d, n, (128, S), BF16)
    emit_attn(nc, fw, S, d, _psum(nc))
    fw.finish_all()
    return nc


def build_merge():
    nc = bass.Bass("TRN2", target_bir_lowering=False)
    fw = FW(nc)
    d = {}
    _din(nc, d, 'xT', (2048, TT), F32)
    _din(nc, d, 'w_in', (2048, 6144), F32)
    _din(nc, d, 'w_branch', (3, 1024, 2048), F32)
    _din(nc, d, 'br', (3, 8, 128, TT), BF16)
    for n, s, t in [('ones', (128, 128), BF16), ('modc', (128, 6, 16), F32), ('gmix', (128, 16), F32)]:
        _din(nc, d, n, s, t)
    _dout(nc, d, 'mg', (2048, TT), BF16)
    emit_merge(nc, fw, d, _psum(nc))
    fw.finish_all()
    return nc


def build_mlp():
    nc = bass.Bass("TRN2", target_bir_lowering=False)
    fw = FW(nc)
    d = {}
    _din(nc, d, 'xT', (2048, TT), F32)
    _din(nc, d, 'mg', (2048, TT), BF16)
    _din(nc, d, 'w_out', (2048, 2048), F32)
    _din(nc, d, 'w_mlp_in', (2048, 8192), F32)
    _din(nc, d, 'w_mlp_out', (8192, 2048), F32)
    for n, s, t in [('ones', (128, 128), BF16), ('modc', (128, 6, 16), F32), ('gmlp', (128, 16), F32)]:
        _din(nc, d, n, s, t)
    _dout(nc, d, 'xo', (2048, TT), F32)
    emit_mlp(nc, fw, d, _psum(nc))
    fw.finish_all()
    return nc


def colvec(v, n):
    return np.ascontiguousarray(np.asarray(v, np.float32).reshape(n, 128).T)


def run(nc, in_maps):
    t = time.time()
    res = run_bass_kernel_spmd(nc, in_maps, core_ids=list(range(NCORES)))
    if DEBUG:
        print("  launch %.1fs" % (time.time() - t), flush=True)
    return res.results


def kernel(x, c, w_ada, b_ada, norm_mix_g, norm_mlp_g, w_in, diff_qk_g, diff_lambda, diff_subln_g, t5_bias,
           mla_q_norm_g, mla_kv_norm_g, w_q_up, w_kv_up, mla_qk_g, w_branch, w_out, w_mlp_in, w_mlp_out):
    f32 = np.float32
    x = np.asarray(x, f32)
    AC = attn_consts()
    PC = [proj_consts(i) for i in range(NCORES)]
    ones = AC['ones']
    nc_mod = _mk('mod', build_mod)
    cT = colvec(np.asarray(c, f32)[0], 16)
    w_ada = np.asarray(w_ada, f32)
    b_ada = np.asarray(b_ada, f32)
    ins = []
    for i in range(NCORES):
        ins.append(dict(cT=cT, b=np.ascontiguousarray(b_ada[:, 1536 * i:1536 * (i + 1)]).reshape(1, 6144),
                        w=np.ascontiguousarray(w_ada[:, :, 1536 * i:1536 * (i + 1)])))
    r = run(nc_mod, ins)
    mod = np.concatenate([r[i]['out'].reshape(4, 1536) for i in range(NCORES)], axis=1)
    if DEBUG:
        ref = np.asarray(c, f32) @ w_ada[0] + b_ada[0]
        print("mod err", np.abs(mod[0] - ref[0]).max(), np.abs(ref).max())

    xT = [np.ascontiguousarray(x[0, i * TT:(i + 1) * TT].T) for i in range(NCORES)]
    nc_proj = _mk('proj', build_proj)
    nc_attn = _mk('attn', build_attn)
    nc_merge = _mk('merge', build_merge)
    nc_mlp = _mk('mlp', build_mlp)
    for l in range(4):
        modc = np.ascontiguousarray(mod[l].reshape(6, 16, 128).transpose(2, 0, 1))
        gmix = colvec(norm_mix_g[l], 16)
        gmlp = colvec(norm_mlp_g[l], 16)
        wl = np.asarray(w_in[l], f32)
        w_qkv = np.ascontiguousarray(wl[:, :6976])
        w_gate = np.ascontiguousarray(wl[:, 6976:])
        dq = np.asarray(diff_qk_g[l], f32)
        dqk = np.stack([np.tile(dq[0], 2), np.tile(dq[1], 2)], axis=1)
        gq = colvec(mla_q_norm_g[l], 4)
        gkv = colvec(mla_kv_norm_g[l], 2)
        mg_ = np.asarray(mla_qk_g[l], f32)
        mqk = np.zeros((128, 4), f32)
        mqk[:, 0] = mg_[0, :128]
        mqk[:, 1] = mg_[1, :128]
        mqk[:64, 2] = mg_[0, 128:]
        mqk[:64, 3] = mg_[1, 128:]
        wq = np.asarray(w_q_up[l], f32).reshape(512, 8, 192)
        wqup = np.ascontiguousarray(np.concatenate([wq[:, :, :128].reshape(512, 1024), wq[:, :, 128:].reshape(512, 512)], axis=1))
        wkv = np.asarray(w_kv_up[l], f32).reshape(256, 8, 256)
        wkvup = np.ascontiguousarray(np.concatenate([wkv[:, :, :128].reshape(256, 1024), wkv[:, :, 128:].reshape(256, 1024)], axis=1))
        ins = []
        for i in range(NCORES):
            ins.append(dict(xT=xT[i], w_in=w_qkv, ones=ones, bd64=PC[i]['bd64'], rotm=PC[i]['rotm'], modc=modc, gmix=gmix, dqk=dqk, gq=gq, gkv=gkv,
                            mqk=mqk, cosT=PC[i]['cosT'], sinT=PC[i]['sinT'], wqup=wqup, wkvup=wkvup))
        rp = run(nc_proj, ins)
        if DEBUG and l == 0:
            debug_proj(x, mod, norm_mix_g, wl, dq, rp, mla_q_norm_g, mla_kv_norm_g, w_q_up, w_kv_up, mla_qk_g)
        lam_init = 0.8 - 0.6 * np.exp(-0.3 * l)
        cst = np.broadcast_to(np.array([lam_init, 1 - lam_init, 0, 0], f32), (128, 4)).copy()
        lamp = np.broadcast_to(np.asarray(diff_lambda[l], f32).reshape(1, 256), (128, 256)).copy()
        gsub = np.asarray(diff_subln_g[l], f32).reshape(128, 1).copy()
        t5 = np.asarray(t5_bias, f32)
        kbr = np.concatenate([rp[i]['kbr'] for i in range(NCORES)], axis=1)
        ins = []
        for h in range(NCORES):
            dd = dict(ones=ones, mtri=AC['mtri'], biasA=np.ascontiguousarray(t5[AC['bucket'], h]), maskB=AC['maskB'], maskC=AC['maskC'],
                      lamp=lamp, cst=cst, b15=np.full((128, 1), t5[15, h], f32), gsub=gsub, kbr=kbr)
            for n in ['qa', 'ka', 'qbn', 'kbn', 'qc', 'kc', 'qbr']:
                dd[n] = np.concatenate([rp[i][n][h] for i in range(NCORES)], axis=1)
            for n in ['va', 'vb', 'vc']:
                dd[n] = np.concatenate([rp[i][n][:, 128 * h:128 * (h + 1)] for i in range(NCORES)], axis=0)
            ins.append(dd)
        ra = run(nc_attn, ins)
        ins = []
        for i in range(NCORES):
            br = np.stack([np.stack([ra[h][n][:, i * TT:(i + 1) * TT] for h in range(NCORES)]) for n in ['oa', 'ob', 'oc']])
            ins.append(dict(xT=xT[i], w_in=w_gate, w_branch=np.asarray(w_branch[l], f32), br=br, ones=ones, modc=modc, gmix=gmix))
        rm = run(nc_merge, ins)
        ins = []
        for i in range(NCORES):
            ins.append(dict(xT=xT[i], mg=rm[i]['mg'], w_out=np.asarray(w_out[l], f32), w_mlp_in=np.asarray(w_mlp_in[l], f32),
                            w_mlp_out=np.asarray(w_mlp_out[l], f32), ones=ones, modc=modc, gmlp=gmlp))
        rx = run(nc_mlp, ins)
        if DEBUG and l == 0:
            debug_post(x, mod, norm_mix_g, norm_mlp_g, wl, w_branch, w_out, w_mlp_in, w_mlp_out, ra, rm, rx)
        xT = [rx[i]['xo'] for i in range(NCORES)]
    out = np.concatenate([xT[i].T for i in range(NCORES)], axis=0)[None]
    return np.ascontiguousarray(out.astype(f32))


def _rms(x, g):
    return x / np.sqrt((x * x).mean(-1, keepdims=True) + 1e-6) * g


def _err(name, got, ref):
    got = np.asarray(got).astype(np.float64)
    ref = np.asarray(ref).astype(np.float64)
    print("  DBG %-6s rel %.2e  max %.3e / %.3e" % (name, np.sqrt(((got - ref) ** 2).mean() / ((ref ** 2).mean() + 1e-30)), np.abs(got - ref).max(), np.abs(ref).max()), flush=True)


def debug_proj(x, mod, norm_mix_g, wl, dq, rp, mla_q_norm_g, mla_kv_norm_g, w_q_up, w_kv_up, mla_qk_g):
    T0 = 128
    xs = x[0, :T0].astype(np.float64)
    sh1, sc1 = mod[0, :2048], mod[0, 2048:4096]
    h = _rms(xs, np.asarray(norm_mix_g[0], np.float64)) * (1 + sc1) + sh1
    pr = h @ wl.astype(np.float64)
    r = rp[0]
    f = lambda a: np.asarray(a).astype(np.float32)
    qa = _rms(pr[:, 0:1024].reshape(T0, 8, 2, 64), dq[0]).reshape(T0, 8, 128)
    _err('qa', f(r['qa'])[:, :, :T0].transpose(2, 0, 1), qa)
    ka = _rms(pr[:, 1024:2048].reshape(T0, 8, 2, 64), dq[1]).reshape(T0, 8, 128)
    _err('ka', f(r['ka'])[:, :, :T0].transpose(2, 0, 1), ka)
    _err('va', f(r['va'])[:T0], pr[:, 2048:3072])
    _err('qc', f(r['qc'])[:, :, :T0].transpose(2, 0, 1), pr[:, 3904:4928].reshape(T0, 8, 128) * 128 ** -0.5)
    _err('kc', f(r['kc'])[:, :, :T0].transpose(2, 0, 1), pr[:, 4928:5952].reshape(T0, 8, 128))
    _err('vc', f(r['vc'])[:T0], pr[:, 5952:6976])
    cq = _rms(pr[:, 3072:3584], np.asarray(mla_q_norm_g[0], np.float64))
    ckv = _rms(pr[:, 3584:3840], np.asarray(mla_kv_norm_g[0], np.float64))
    kpe = pr[:, 3840:3904]
    q = (cq @ np.asarray(w_q_up[0], np.float64)).reshape(T0, 8, 192)
    kv = (ckv @ np.asarray(w_kv_up[0], np.float64)).reshape(T0, 8, 256)
    g = np.asarray(mla_qk_g[0], np.float64)
    pos = np.arange(T0)
    inv = 10000.0 ** (-np.arange(32) / 32)
    ang = pos[:, None] * inv[None]
    cos, sin = np.cos(ang)[:, None], np.sin(ang)[:, None]

    def rope(v):
        x1, x2 = v[..., :32], v[..., 32:]
        return np.concatenate([x1 * cos - x2 * sin, x1 * sin + x2 * cos], -1)
    _err('qbn', f(r['qbn'])[:, :, :T0].transpose(2, 0, 1), _rms(q[..., :128], g[0, :128]))
    _err('qbr', f(r['qbr'])[:, :, :T0].transpose(2, 0, 1), rope(_rms(q[..., 128:], g[0, 128:])))
    _err('kbn', f(r['kbn'])[:, :, :T0].transpose(2, 0, 1), _rms(kv[..., :128], g[1, :128]))
    _err('kbr', f(r['kbr'])[:, :T0].T, rope(_rms(kpe.reshape(T0, 1, 64), g[1, 128:]))[:, 0])
    _err('vb', f(r['vb'])[:T0].reshape(T0, 8, 128), kv[..., 128:])


def debug_post(x, mod, norm_mix_g, norm_mlp_g, wl, w_branch, w_out, w_mlp_in, w_mlp_out, ra, rm, rx):
    T0 = 128
    f64 = np.float64
    xs = x[0, :T0].astype(f64)
    sh1, sc1, g1, sh2, sc2, g2 = [mod[0, i * 2048:(i + 1) * 2048].astype(f64) for i in range(6)]
    h = _rms(xs, np.asarray(norm_mix_g[0], f64)) * (1 + sc1) + sh1
    gates = h @ wl[:, 6976:].astype(f64)
    br = np.stack([np.concatenate([np.asarray(ra[hh][n]).astype(np.float32)[:, :T0].T for hh in range(8)], axis=1) for n in ['oa', 'ob', 'oc']]).astype(f64)
    merged = 0
    for i in range(3):
        up = br[i] @ np.asarray(w_branch[0][i], f64)
        merged = merged + 1 / (1 + np.exp(-gates[:, i * 2048:(i + 1) * 2048])) * up
    _err('mg', np.asarray(rm[0]['mg']).astype(np.float32)[:, :T0].T, merged)
    x1 = xs + g1 * (merged @ np.asarray(w_out[0], f64))
    h2 = _rms(x1, np.asarray(norm_mlp_g[0], f64)) * (1 + sc2) + sh2
    hid = np.maximum(h2 @ np.asarray(w_mlp_in[0], f64), 0) ** 2
    x2 = x1 + g2 * (hid @ np.asarray(w_mlp_out[0], f64))
    _err('xo', rx[0]['xo'][:, :T0].T, x2)
    _err('dx', rx[0]['xo'][:, :T0].T - xs, x2 - xs)
```

```python
import os, time
import numpy as np
import ml_dtypes
import concourse.bass as bass
import concourse.mybir as mybir
from concourse.bass_utils import run_bass_kernel_spmd

F32 = mybir.dt.float32
BF16 = mybir.dt.bfloat16
AF = mybir.ActivationFunctionType
ALU = mybir.AluOpType


class FW:
    def __init__(self, nc):
        self.nc = nc
        self.eng = {'pe': nc.tensor, 'act': nc.scalar, 'dve': nc.vector, 'pool': nc.gpsimd, 'sp': nc.sync}
        self.sem = {}
        self.cnt = {}
        for k in ['pe', 'act', 'dve', 'pool']:
            self.sem[k] = nc.alloc_semaphore(name='s_' + k)
            self.cnt[k] = 0
        self.seen = {k: {} for k in self.eng}
        self.lastw = {}
        self.reads = {}
        self.dsem = {}

    def _wait(self, e, ev):
        if ev is None:
            return
        k, v = ev
        if e == 'pe' and k == 'pe':
            return
        if self.seen[e].get(k, 0) >= v:
            return
        self.eng[e].wait_ge(self.sem[k], v)
        self.seen[e][k] = v

    def deps(self, e, R, W):
        for t in R:
            self._wait(e, self.lastw.get(t))
        for t in W:
            self._wait(e, self.lastw.get(t))
            for ev in self.reads.get(t, []):
                self._wait(e, ev)

    def commit(self, ev, R, W):
        for t in R:
            self.reads.setdefault(t, []).append(ev)
        for t in W:
            self.lastw[t] = ev
            self.reads[t] = []

    def op(self, e, R, W, fn, inc=True):
        self.deps(e, R, W)
        ins = fn()
        if inc:
            self.cnt[e] += 1
            ins.then_inc(self.sem[e], 1)
            ev = (e, self.cnt[e])
        else:
            ev = (e, self.cnt[e] + 1)
        self.commit(ev, R, W)
        return ins

    def dma(self, q, R, W, out, in_, semtok=None, **kw):
        st = semtok if semtok is not None else (W[0] if W else R[0])
        if st not in self.dsem:
            key = 'd_' + str(st)
            self.sem[key] = self.nc.alloc_semaphore(name=key)
            self.cnt[key] = 0
            self.dsem[st] = key
        key = self.dsem[st]
        self.deps(q, R, W)
        ins = self.eng[q].dma_start(out=out, in_=in_, **kw)
        self.cnt[key] += 16
        ins.then_inc(self.sem[key], 16)
        ev = (key, self.cnt[key])
        self.commit(ev, R, W)
        return ins

    def finish(self, toks, e='sp'):
        for t in toks:
            self._wait(e, self.lastw.get(t))

    def barrier(self):
        for e in self.eng:
            for k in self.sem:
                if self.cnt[k] > 0 and not (e == 'pe' and k == 'pe') and self.seen[e].get(k, 0) < self.cnt[k]:
                    self.eng[e].wait_ge(self.sem[k], self.cnt[k])
                    self.seen[e][k] = self.cnt[k]

    def finish_all(self, e='sp'):
        for k in self.sem:
            if k.startswith('d_') and self.cnt[k] > 0 and self.seen[e].get(k, 0) < self.cnt[k]:
                self.eng[e].wait_ge(self.sem[k], self.cnt[k])
                self.seen[e][k] = self.cnt[k]

    def finish_dma(self, semtok, e='sp'):
        if semtok not in self.dsem:
            return
        key = self.dsem[semtok]
        self.eng[e].wait_ge(self.sem[key], self.cnt[key])


AX = mybir.AxisListType
NEG = -1.0e30
EPS = 1e-6


class Pool_:
    def __init__(self, nc, name, n, shape, dt):
        self.tiles = [nc.alloc_sbuf_tensor(f"{name}{i}", shape, dt) for i in range(n)]
        self.names = [f"{name}{i}" for i in range(n)]
        self.i = 0

    def get(self):
        t, n = self.tiles[self.i], self.names[self.i]
        self.i = (self.i + 1) % len(self.tiles)
        return t, n


def emit_attn(nc, fw, S, d, ps, psz_in=None):
    NQB = S // 512
    NKT = S // 128
    sb = nc.alloc_sbuf_tensor
    Q1 = sb("aQ1", [128, S], BF16)
    Q2 = sb("aQ2", [128, S], BF16)
    K1 = sb("aK1", [128, S], BF16)
    K2 = sb("aK2", [128, S], BF16)
    K3 = sb("aK3", [128, S], BF16)
    V = sb("aV", [128, NKT, 128], BF16)
    OUT = sb("aOUT", [128, S], BF16)
    ones = sb("aones", [128, 128], BF16)
    mtri = sb("amtri", [128, 128], BF16)
    mones = sb("amones", [128, 128], BF16)
    biasA = sb("abiasA", [128, 5, 512], F32)
    maskB = sb("amaskB", [128, 4, 512], F32)
    maskC = sb("amaskC", [128, 4, 512], BF16)
    ident = sb("aident", [128, 128], BF16)
    lamp = sb("alamp", [128, 256], F32)
    cst = sb("acst", [128, 4], F32)
    b15 = sb("ab15", [128, 1], F32)
    gsub = sb("agsub", [128, 1], F32)
    sm = sb("asm", [128, 16], F32)
    carry = sb("acarry", [128, 512], F32)
    carry2 = sb("acarry2", [128, 512], F32)
    Pp = Pool_(nc, "aP", 6, [128, 512], BF16)
    Tp = Pool_(nc, "aT", 6, [128, 512], F32)
    E2p = Pool_(nc, "aE2", 5, [128, 1024], F32)
    L2p = Pool_(nc, "aL2", 3, [128, 1024], BF16)
    W2p = Pool_(nc, "aW2", 3, [128, 1024], BF16)
    psz = psz_in

    fw.op('pool', [], ['Q2z'], lambda: nc.gpsimd.memset(Q2[64:128, :], 0.0))
    fw.op('pool', [], ['K2z'], lambda: nc.gpsimd.memset(K2[64:128, :], 0.0))
    fw.op('pool', [], ['K3z'], lambda: nc.gpsimd.memset(K3[0:64, :], 0.0))
    fw.dma('sp', [], ['ones'], ones[:], d['ones'])
    fw.dma('sp', [], ['mtri'], mtri[:], d['mtri'])
    fw.dma('sp', [], ['biasA'], biasA[:], d['biasA'].rearrange("o p q -> p o q"))
    fw.dma('sp', [], ['maskB'], maskB[:], d['maskB'].rearrange("o p q -> p o q"))
    fw.dma('sp', [], ['maskC'], maskC[:], d['maskC'].rearrange("o p q -> p o q"))
    fw.dma('sp', [], ['ident'], ident[:], d['ident'])
    fw.dma('sp', [], ['lamp'], lamp[:], d['lamp'])
    fw.dma('sp', [], ['cst'], cst[:], d['cst'])
    fw.dma('sp', [], ['b15'], b15[:], d['b15'])
    fw.dma('sp', [], ['gsub'], gsub[:], d['gsub'])
    for o in range(4):
        fw.op('dve', ['biasA', 'maskB'], ['biasA'], lambda: nc.vector.tensor_tensor(biasA[:, o + 1, :], biasA[:, o + 1, :], maskB[:, o, :], ALU.add))
    T0, t0n = Tp.get()
    fw.op('dve', ['lamp'], [t0n], lambda: nc.vector.tensor_tensor(T0[:, 0:64], lamp[:, 0:64], lamp[:, 64:128], ALU.mult))
    fw.op('dve', [t0n], ['sm'], lambda: nc.vector.reduce_sum(sm[:, 0:1], T0[:, 0:64], AX.X))
    fw.op('dve', ['lamp'], [t0n], lambda: nc.vector.tensor_tensor(T0[:, 0:64], lamp[:, 128:192], lamp[:, 192:256], ALU.mult))
    fw.op('dve', [t0n], ['sm'], lambda: nc.vector.reduce_sum(sm[:, 1:2], T0[:, 0:64], AX.X))
    fw.op('act', ['sm'], ['sm'], lambda: nc.scalar.activation(sm[:, 2:4], sm[:, 0:2], AF.Exp))
    fw.op('dve', ['sm'], ['sm'], lambda: nc.vector.tensor_tensor(sm[:, 4:5], sm[:, 3:4], sm[:, 2:3], ALU.subtract))
    fw.op('dve', ['sm', 'cst'], ['sm'], lambda: nc.vector.tensor_tensor(sm[:, 4:5], sm[:, 4:5], cst[:, 0:1], ALU.subtract))
    fw.op('dve', ['gsub', 'cst'], ['sm'], lambda: nc.vector.tensor_tensor(sm[:, 5:6], gsub[:], cst[:, 1:2], ALU.mult))
    neglam = sm[:, 4:5]
    gcolA = sm[:, 5:6]

    def load_qkv(q1, q2, k1, k2, v):
        fw.dma('sp', [], ['Q1'], Q1[:], q1)
        fw.dma('act', [], ['K1'], K1[:], k1)
        if q2 is not None:
            fw.dma('sp', [], ['Q2'], Q2[0:64, :], q2)
            fw.dma('act', [], ['K2'], K2[0:64, :], k2)
        for c in range(0, NKT, 16):
            n = min(16, NKT - c)
            fw.dma('sp', [], ['V'], V[:, c:c + n, :], v[c * 128:(c + n) * 128, :].rearrange("(t p) e -> p t e", p=128))

    def store_out(dst, qb):
        fw.dma('sp', [f'out{qb}'], [], dst[:, qb * 512:(qb + 1) * 512], OUT[:, qb * 512:(qb + 1) * 512])

    def softmax_mixer(kind):
        nm = 2 if kind == 'A' else 1
        scale = 64 ** -0.5 if kind == 'A' else 192 ** -0.5
        if kind == 'A':
            Sb = [[ps[0], ps[1]], [ps[2], ps[3]]]
            Sn = [['ps0', 'ps1'], ['ps2', 'ps3']]
            NB, LA = 2, 1
        else:
            Sb = [[ps[0], ps[1], ps[2], ps[3]]]
            Sn = [['ps0', 'ps1', 'ps2', 'ps3']]
            NB, LA = 4, 2
        Ob = [ps[4], ps[5]]; On = ['ps4', 'ps5']
        Db = [ps[6], ps[7]]; Dn = ['ps6', 'ps7']

        for qb in range(NQB):
            q0 = qb * 512
            nk = 4 * (qb + 1)
            qs = slice(q0, q0 + 512)

            def emit_S(kt):
                ks = slice(kt * 128, kt * 128 + 128)
                b = kt % NB
                for m in range(nm):
                    if kind == 'A':
                        KA, katoks = (K2, ['K2', 'K2z']) if m == 0 else (K3, ['K3', 'K3z'])
                        fw.op('pe', katoks + ['Q1'], [Sn[m][b]], lambda: nc.tensor.matmul(Sb[m][b][:], KA[:, ks], Q1[:, qs], start=True, stop=True))
                    else:
                        fw.op('pe', ['K1', 'Q1'], [Sn[m][b]], lambda: nc.tensor.matmul(Sb[m][b][:], K1[:, ks], Q1[:, qs], start=True, stop=False), inc=False)
                        fw.op('pe', ['K2', 'Q2', 'K2z', 'Q2z'], [Sn[m][b]], lambda: nc.tensor.matmul(Sb[m][b][:], K2[:, ks], Q2[:, qs], start=False, stop=True))

            for kk in range(min(LA, nk)):
                emit_S(kk)
            for kt in range(nk):
                b = kt % NB
                o = kt * 128 - q0
                Pt = []
                for m in range(nm):
                    P, pn = Pp.get()
                    near = (o >= -128) if kind == 'A' else (o >= 0)
                    if near:
                        oi = o // 128 + 1
                        T, tn = Tp.get()
                        btile = biasA[:, oi, :] if kind == 'A' else maskB[:, oi - 1, :]
                        btok = 'biasA' if kind == 'A' else 'maskB'
                        fw.op('dve', [Sn[m][b], btok], [tn], lambda: nc.vector.scalar_tensor_tensor(T[:], Sb[m][b][:], scale, btile, ALU.mult, ALU.add))
                        fw.op('act', [tn], [pn], lambda: nc.scalar.activation(P[:], T[:], AF.Exp))
                    else:
                        if kind == 'A':
                            fw.op('act', [Sn[m][b], 'b15'], [pn], lambda: nc.scalar.activation(P[:], Sb[m][b][:], AF.Exp, bias=b15[:], scale=scale))
                        else:
                            fw.op('act', [Sn[m][b]], [pn], lambda: nc.scalar.activation(P[:], Sb[m][b][:], AF.Exp, scale=scale))
                    Pt.append((P, pn))
                if kt + LA < nk:
                    emit_S(kt + LA)
                for m in range(nm):
                    P, pn = Pt[m]
                    fw.op('pe', ['V', pn], [On[m]], lambda: nc.tensor.matmul(Ob[m][:], V[:, kt, :], P[:], start=(kt == 0), stop=(kt == nk - 1)), inc=False)
                    fw.op('pe', ['ones', pn], [Dn[m]], lambda: nc.tensor.matmul(Db[m][:], ones[:], P[:], start=(kt == 0), stop=(kt == nk - 1)))
            res = []
            for m in range(nm):
                R, rn = Tp.get()
                fw.op('dve', [Dn[m]], [rn], lambda: nc.vector.reciprocal(R[:], Db[m][:]))
                if kind == 'B':
                    fw.op('dve', [On[m], rn], [f'out{qb}'], lambda: nc.vector.tensor_tensor(OUT[:, qs], Ob[m][:], R[:], ALU.mult))
                else:
                    fw.op('dve', [On[m], rn], [rn], lambda: nc.vector.tensor_tensor(R[:], Ob[m][:], R[:], ALU.mult))
                    res.append((R, rn))
            if kind == 'A':
                (R1, r1n), (R2, r2n) = res
                fw.op('dve', [r1n, r2n, 'sm'], [r1n], lambda: nc.vector.scalar_tensor_tensor(R1[:], R2[:], neglam, R1[:], ALU.mult, ALU.add))
                SQ, sqn = Pp.get()
                fw.op('act', [r1n], [sqn], lambda: nc.scalar.activation(SQ[:], R1[:], AF.Square))
                fw.op('pe', ['ones', sqn], ['ps0'], lambda: nc.tensor.matmul(ps[0][:], ones[:], SQ[:], start=True, stop=True))
                fw.op('act', ['ps0'], [r2n], lambda: nc.scalar.activation(R2[:], ps[0][:], AF.Ln, bias=EPS, scale=1.0 / 128))
                fw.op('act', [r2n], [r2n], lambda: nc.scalar.activation(R2[:], R2[:], AF.Exp, scale=-0.5))
                fw.op('dve', [r1n, r2n, 'sm'], [f'out{qb}'], lambda: nc.vector.scalar_tensor_tensor(OUT[:, qs], R1[:], gcolA, R2[:], ALU.mult, ALU.mult))
            store_out(d['oa'] if kind == 'A' else d['ob'], qb)

    def sb_mixer():
        Zb = [psz[0], psz[1]]; Zn = ['psz0', 'psz1']
        Ab = [ps[4], ps[5]]; An = ['ps4', 'ps5']
        Tb_, Tn_ = ps[6], 'ps6'
        Op, On_ = ps[7], 'ps7'
        for qb in range(NQB):
            q0 = qb * 512
            qs = slice(q0, q0 + 512)
            kts = list(range(4 * qb + 3, -1, -1))
            nk = len(kts)
            npair = nk // 2
            Ls = {}
            Ws = {}

            def emit_Z(p):
                for h in range(2):
                    kt = kts[2 * p + h]
                    ks = slice(kt * 128, kt * 128 + 128)
                    o = kt * 128 - q0
                    if o >= 0:
                        fw.op('pe', ['K1', 'Q1'], [Zn[p % 2]], lambda: nc.tensor.matmul(Zb[p % 2][:, h * 512:(h + 1) * 512], K1[:, ks], Q1[:, qs], start=True, stop=False), inc=False)
                        fw.op('pe', ['ident', 'maskC'], [Zn[p % 2]], lambda: nc.tensor.matmul(Zb[p % 2][:, h * 512:(h + 1) * 512], ident[:], maskC[:, o // 128, :], start=False, stop=True), inc=(h == 1))
                    else:
                        fw.op('pe', ['K1', 'Q1'], [Zn[p % 2]], lambda: nc.tensor.matmul(Zb[p % 2][:, h * 512:(h + 1) * 512], K1[:, ks], Q1[:, qs], start=True, stop=True), inc=(h == 1))

            def stage1(p):
                b = p % 2
                E, en = E2p.get()
                fw.op('act', [Zn[b]], [en], lambda: nc.scalar.activation(E[:], Zb[b][:], AF.Exp))
                L, ln_ = L2p.get()
                fw.op('act', [en], [ln_], lambda: nc.scalar.activation(L[:], E[:], AF.Ln, bias=1.0))
                Ls[p] = (L, ln_)

            def emit_O(p):
                W, wn = Ws.pop(p)
                for h in range(2):
                    i = 2 * p + h
                    kt = kts[i]
                    fw.op('pe', ['V', wn], [On_], lambda: nc.tensor.matmul(Op[:], V[:, kt, :], W[:, h * 512:(h + 1) * 512], start=(i == 0), stop=(i == nk - 1)), inc=(h == 1))

            Gs = {}

            def emit_W(p):
                G, gn = Gs.pop(p)
                W, wn = W2p.get()
                fw.op('act', [gn], [wn], lambda: nc.scalar.activation(W[:], G[:], AF.Exp))
                Ws[p] = (W, wn)

            fw.op('dve', [], ['carry'], lambda: nc.vector.memset(carry[:], 0.0))
            emit_Z(0)
            if npair > 1:
                emit_Z(1)
            stage1(0)
            for p in range(npair):
                L, ln_ = Ls.pop(p)
                G, gn = E2p.get()
                for h in range(2):
                    i = 2 * p + h
                    kt = kts[i]
                    b = i % 2
                    ks = slice(kt * 128, kt * 128 + 128)
                    hs = slice(h * 512, (h + 1) * 512)
                    fw.op('pe', ['K1', 'Q1'], [An[b]], lambda: nc.tensor.matmul(Ab[b][:], K1[:, ks], Q1[:, qs], start=True, stop=False), inc=False)
                    od = kt * 128 - q0
                    if od >= 0:
                        fw.op('pe', ['ident', 'maskC'], [An[b]], lambda: nc.tensor.matmul(Ab[b][:], ident[:], maskC[:, od // 128, :], start=False, stop=False), inc=False)
                    fw.op('pe', ['mtri', ln_], [An[b]], lambda: nc.tensor.matmul(Ab[b][:], mtri[:], L[:, hs], start=False, stop=True))
                    fw.op('pe', ['ones', ln_], [Tn_], lambda: nc.tensor.matmul(Tb_[:], ones[:], L[:, hs], start=True, stop=True))
                    cur, ctok = (carry, 'carry') if h == 0 else (carry2, 'carry2')
                    nxt, ntok = (carry2, 'carry2') if h == 0 else (carry, 'carry')
                    fw.op('dve', [Tn_, ctok], [ntok], lambda: nc.vector.tensor_tensor(nxt[:], cur[:], Tb_[:], ALU.add))
                    fw.op('dve', [An[b], ctok], [gn], lambda: nc.vector.tensor_tensor(G[:, hs], Ab[b][:], cur[:], ALU.subtract))
                Gs[p] = (G, gn)
                if p >= 2:
                    emit_O(p - 2)
                if p + 1 < npair:
                    stage1(p + 1)
                if p >= 1:
                    emit_W(p - 1)
                if p + 2 < npair:
                    emit_Z(p + 2)
            emit_W(npair - 1)
            if npair >= 2:
                emit_O(npair - 2)
            emit_O(npair - 1)
            fw.op('act', [On_], [f'out{qb}'], lambda: nc.scalar.copy(OUT[:, qs], Op[:]))
            store_out(d['oc'], qb)

    import os
    MIX = os.environ.get('MIX', 'ABC')
    if 'A' in MIX:
        load_qkv(d['qa'], None, d['ka'], None, d['va'])
        fw.dma('act', [], ['K2'], K2[0:64, :], d['ka'][0:64, :])
        fw.dma('act', [], ['K3'], K3[64:128, :], d['ka'][64:128, :])
        softmax_mixer('A')
    if 'B' in MIX:
        load_qkv(d['qbn'], d['qbr'], d['kbn'], d['kbr'], d['vb'])
        softmax_mixer('B')
    if 'C' in MIX:
        load_qkv(d['qc'], None, d['kc'], None, d['vc'])
        sb_mixer()


def t5_bucket_np(rel):
    nb = 16
    max_exact = 8
    n = np.abs(rel)
    large = max_exact + (np.log(np.maximum(n, 1).astype(np.float32) / np.float32(max_exact)) / np.float32(np.log(128 / 8)) * np.float32(nb - max_exact)).astype(np.int32)
    large = np.minimum(large, nb - 1)
    return np.where(rel > 0, nb, 0) + np.where(n < max_exact, n, large)


def attn_consts():
    kk = np.arange(128)[:, None]
    qq = np.arange(512)[None, :]
    offs = [-128, 0, 128, 256, 384]
    bucket = np.stack([t5_bucket_np((o + kk) - qq) for o in offs])
    maskB = np.stack([np.where(((o + kk) // 64) <= (qq // 64), 0.0, NEG) for o in offs[1:]]).astype(np.float32)
    maskC = np.stack([np.where((o + kk) < qq, 0.0, -30000.0) for o in offs[1:]]).astype(ml_dtypes.bfloat16)
    ident = np.eye(128, dtype=np.float32).astype(ml_dtypes.bfloat16)
    jj = np.arange(128)[:, None]
    kc = np.arange(128)[None, :]
    mtri = np.where(jj >= kc, -1.0, 0.0).astype(ml_dtypes.bfloat16)
    ones = np.ones((128, 128), ml_dtypes.bfloat16)
    return dict(bucket=bucket, maskB=maskB, maskC=maskC, mtri=mtri, ones=ones, ident=ident)


TT = 1024
O_AQ, O_AK, O_AV, O_CQ, O_CKV, O_KPE, O_SQ, O_SK, O_SV, O_G = 0, 1024, 2048, 3072, 3584, 3840, 3904, 4928, 5952, 6976


class Slabs:
    def __init__(self, nc, fw, n=3):
        self.nc, self.fw = nc, fw
        self.pool = Pool_(nc, "slab", n, [128, 16, 512], BF16)

    def load(self, w, r0, nrows, c0, ncols):
        t, tok = self.pool.get()
        nch = nrows // 128
        step = 4
        for c in range(0, nch, step):
            n = min(step, nch - c)
            self.fw.dma('pool', [], [tok], t[:, c:c + n, 0:ncols],
                        w[r0 + c * 128:r0 + (c + n) * 128, c0:c0 + ncols].rearrange("(c p) n -> p c n", p=128))
        return t, tok


def emit_norm_h(nc, fw, X, H, acol, bcol, ones, ps, psn, Tp, Pp, RS):
    for tb in range(TT // 512):
        ts = slice(tb * 512, tb * 512 + 512)
        for c in range(16):
            SQ, sqn = Pp.get()
            fw.op('act', ['X'], [sqn], lambda: nc.scalar.activation(SQ[:], X[:, c, ts], AF.Square))
            fw.op('pe', ['ones', sqn], [psn], lambda: nc.tensor.matmul(ps[:], ones[:], SQ[:], start=(c == 0), stop=(c == 15)))
        fw.op('act', [psn], ['RS'], lambda: nc.scalar.activation(RS[:, ts], ps[:], AF.Ln, bias=EPS, scale=1.0 / 2048))
        fw.op('act', ['RS'], ['RS'], lambda: nc.scalar.activation(RS[:, ts], RS[:, ts], AF.Exp, scale=-0.5))
        for c in range(16):
            T, tn = Tp.get()
            fw.op('dve', ['X', 'RS', 'cols'], [tn], lambda: nc.vector.scalar_tensor_tensor(T[:], X[:, c, ts], acol[:, c:c + 1], RS[:, ts], ALU.mult, ALU.mult))
            fw.op('dve', [tn, 'cols'], ['H'], lambda: nc.vector.tensor_scalar(H[:, c, ts], T[:], bcol[:, c:c + 1], None, ALU.add))


def emit_proj(nc, fw, d, ps):
    sb = nc.alloc_sbuf_tensor
    H = sb("H", [128, 16, TT], BF16)
    RS = sb("RS", [128, TT], F32)
    ones = sb("pones", [128, 128], BF16)
    bd64 = sb("pbd64", [128, 128], BF16)
    rotm = sb("protm", [64, 64], BF16)
    modc = sb("pmodc", [128, 6, 16], F32)
    gmix = sb("pgmix", [128, 16], F32)
    cols = sb("pcols", [128, 2, 16], F32)
    dqk = sb("pdqk", [128, 2], F32)
    gq = sb("pgq", [128, 4], F32)
    gkv = sb("pgkv", [128, 2], F32)
    mqk = sb("pmqk", [128, 4], F32)
    cosT = sb("pcos", [64, TT], F32)
    sinT = sb("psin", [64, TT], F32)
    Pp = Pool_(nc, "pP", 4, [128, 512], BF16)
    Tp = Pool_(nc, "pT", 4, [128, 512], F32)
    fw.dma('act', [], ['ones'], ones[:], d['ones'])
    fw.dma('act', [], ['modc'], modc[:], d['modc'])
    fw.dma('act', [], ['gmix'], gmix[:], d['gmix'])
    with nc.sbuf_tensor("X", [128, 16, TT], F32) as X:
        for c in range(0, 16, 4):
            fw.dma('sp', [], ['X'], X[:, c:c + 4, :], d['xT'][c * 128:(c + 4) * 128, :].rearrange("(c p) t -> p c t", p=128))
        fw.op('dve', ['modc', 'gmix'], ['cols'], lambda: nc.vector.scalar_tensor_tensor(cols[:, 0, :], modc[:, 1, :], 1.0, gmix[:], ALU.add, ALU.mult))
        fw.op('dve', ['modc'], ['cols'], lambda: nc.vector.tensor_copy(cols[:, 1, :], modc[:, 0, :]))
        emit_norm_h(nc, fw, X, H, cols[:, 0, :], cols[:, 1, :], ones, ps[4], 'ps4', Tp, Pp, RS)
    for c in range(0, 16, 4):
        fw.dma('sp', ['H'], [], d['hT'][c * 128:(c + 4) * 128, :].rearrange("(c p) t -> p c t", p=128), H[:, c:c + 4, :], semtok='hout')
    fw.barrier()
    WQ = sb("pwq", [128, 4, 1536], BF16)
    WKV = sb("pwkv", [128, 2, 2048], BF16)
    CQ = sb("pCQ", [128, 4, TT], F32)
    CKV = sb("pCKV", [128, 2, TT], F32)
    KPE = sb("pKPE", [64, TT], F32)
    CQN = sb("pCQN", [128, 4, TT], BF16)
    CKVN = sb("pCKVN", [128, 2, TT], BF16)
    Op = Pool_(nc, "pO", 4, [128, 512], BF16)
    slabs = Slabs(nc, fw)
    projps = [(ps[i], f'ps{i}') for i in range(4)]
    pi = [0]

    def nextps():
        p = projps[pi[0] % 4]
        pi[0] += 1
        return p

    nps = [(ps[4], 'ps4'), (ps[5], 'ps5')]
    ni = [0]

    def nextnps():
        p = nps[ni[0] % 2]
        ni[0] += 1
        return p

    fw.dma('act', [], ['bd64'], bd64[:], d['bd64'])
    fw.dma('act', [], ['rotm'], rotm[:], d['rotm'])
    fw.dma('act', [], ['dqk'], dqk[:], d['dqk'])
    fw.dma('act', [], ['gq'], gq[:], d['gq'])
    fw.dma('act', [], ['gkv'], gkv[:], d['gkv'])
    fw.dma('act', [], ['mqk'], mqk[:], d['mqk'])
    fw.dma('act', [], ['cos'], cosT[:], d['cosT'])
    fw.dma('act', [], ['sin'], sinT[:], d['sinT'])
    import os
    for c in range(4 if not os.environ.get('NOWQ') else 0):
        fw.dma('pool', [], ['WQ'], WQ[:, c, :], d['wqup'][c * 128:(c + 1) * 128, :], semtok='WQ')
    for c in range(2 if not os.environ.get('NOWQ') else 0):
        fw.dma('pool', [], ['WKV'], WKV[:, c, :], d['wkvup'][c * 128:(c + 1) * 128, :], semtok='WKV')
    w = d['w_in']

    def rstd_of(sq_list, lhsT, ltok, nfeat, P=128):
        pp, ppn = nextnps()
        for i, (sq, sqn) in enumerate(sq_list):
            fw.op('pe', [ltok, sqn], [ppn], lambda: nc.tensor.matmul(pp[0:P, :], lhsT, sq, start=(i == 0), stop=(i == len(sq_list) - 1)))
        R, rn = Tp.get()
        fw.op('act', [ppn], [rn], lambda: nc.scalar.activation(R[0:P, :], pp[0:P, :], AF.Ln, bias=EPS, scale=1.0 / nfeat))
        fw.op('act', [rn], [rn], lambda: nc.scalar.activation(R[0:P, :], R[0:P, :], AF.Exp, scale=-0.5))
        return R, rn

    def fm_chunk(slab, stok, nrc, col, ncols, tb, rhs_src, rtok):
        p, pn = nextps()
        ts = slice(tb * 512, tb * 512 + 512)
        for c in range(nrc):
            fw.op('pe', [stok, rtok], [pn], lambda: nc.tensor.matmul(p[0:ncols, :], slab[:, c, col:col + ncols], rhs_src[:, c, ts], start=(c == 0), stop=(c == nrc - 1)), inc=(c == nrc - 1))
        return p, pn

    def out_dma(dst, O, on):
        fw.dma('sp', [on], [], dst, O)

    def qknorm_out(p, pn, P, lhsT, ltok, nfeat, gcol, gtok, dst, scale_extra=1.0):
        SQ, sqn = Pp.get()
        fw.op('act', [pn], [sqn], lambda: nc.scalar.activation(SQ[0:P, :], p[0:P, :], AF.Square))
        R, rn = rstd_of([(SQ[0:P, :], sqn)], lhsT, ltok, nfeat, P)
        O, on = Op.get()
        fw.op('dve', [pn, rn, gtok], [on], lambda: nc.vector.scalar_tensor_tensor(O[0:P, :], p[0:P, :], gcol, R[0:P, :], ALU.mult, ALU.mult))
        out_dma(dst, O[0:P, :], on)

    def rope_out(src, stok, P, gcol, gtok, tb, dst):
        ts = slice(tb * 512, tb * 512 + 512)
        SQ, sqn = Pp.get()
        fw.op('act', [stok], [sqn], lambda: nc.scalar.activation(SQ[0:64, :], src, AF.Square))
        R, rn = rstd_of([(SQ[0:64, :], sqn)], ones[0:64, 0:64], 'ones', 64, 64)
        QN, qn = Tp.get()
        fw.op('dve', [stok, rn, gtok], [qn], lambda: nc.vector.scalar_tensor_tensor(QN[0:64, :], src, gcol, R[0:64, :], ALU.mult, ALU.mult))
        QB, qbn_ = Pp.get()
        fw.op('act', [qn], [qbn_], lambda: nc.scalar.copy(QB[0:64, :], QN[0:64, :]))
        rp, rpn = (ps[6], 'ps6') if tb == 0 else (ps[7], 'ps7')
        fw.op('pe', ['rotm', qbn_], [rpn], lambda: nc.tensor.matmul(rp[0:64, :], rotm[:], QB[0:64, :], start=True, stop=True))
        T2, t2n = Tp.get()
        fw.op('dve', [rpn, 'sin'], [t2n], lambda: nc.vector.tensor_tensor(T2[0:64, :], rp[0:64, :], sinT[:, ts], ALU.mult))
        fw.op('dve', [qn, 'cos'], [qn], lambda: nc.vector.tensor_tensor(QN[0:64, :], QN[0:64, :], cosT[:, ts], ALU.mult))
        O, on = Op.get()
        fw.op('dve', [qn, t2n], [on], lambda: nc.vector.tensor_tensor(O[0:64, :], QN[0:64, :], T2[0:64, :], ALU.add))
        out_dma(dst, O[0:64, :], on)

    def fm_group(c0, ncols_total, handler):
        for s0 in range(0, ncols_total, 512):
            ncs = min(512, ncols_total - s0)
            slab, stok = slabs.load(w, 0, 2048, c0 + s0, ncs)
            for tb in range(2):
                for cc in range(0, ncs, 128):
                    nco = min(128, ncs - cc)
                    p, pn = fm_chunk(slab, stok, 16, cc, nco, tb, H, 'H')
                    handler((s0 + cc) // 128, tb, p, pn, nco)

    def tsl(tb):
        return slice(tb * 512, tb * 512 + 512)

    import os
    STAGE = int(os.environ.get('PSTAGE', '99'))
    if STAGE < 1:
        return
    fm_group(O_AQ, 1024, lambda ch, tb, p, pn, n: qknorm_out(p, pn, 128, bd64[:], 'bd64', 64, dqk[:, 0:1], 'dqk', d['qa'][ch, :, tsl(tb)]))
    fm_group(O_AK, 1024, lambda ch, tb, p, pn, n: qknorm_out(p, pn, 128, bd64[:], 'bd64', 64, dqk[:, 1:2], 'dqk', d['ka'][ch, :, tsl(tb)]))

    def cq_h(ch, tb, p, pn, n):
        O, on = Op.get()
        fw.op('act', [pn], [on], lambda: nc.scalar.activation(O[:], p[:], AF.Copy, scale=128 ** -0.5))
        out_dma(d['qc'][ch, :, tsl(tb)], O[:], on)

    def ck_h(ch, tb, p, pn, n):
        O, on = Op.get()
        fw.op('act', [pn], [on], lambda: nc.scalar.copy(O[:], p[:]))
        out_dma(d['kc'][ch, :, tsl(tb)], O[:], on)

    if STAGE < 2:
        return
    fm_group(O_SQ, 1024, cq_h)
    fm_group(O_SK, 1024, ck_h)

    def lat_cq(ch, tb, p, pn, n):
        fw.op('act', [pn], ['CQ'], lambda: nc.scalar.copy(CQ[:, ch, tsl(tb)], p[:]))

    def lat_ckv(ch, tb, p, pn, n):
        if ch < 2:
            fw.op('act', [pn], ['CKV'], lambda: nc.scalar.copy(CKV[:, ch, tsl(tb)], p[:]))
        else:
            fw.op('act', [pn], ['KPE'], lambda: nc.scalar.copy(KPE[:, tsl(tb)], p[0:64, :]))

    if STAGE < 3:
        return
    fm_group(O_CQ, 512, lat_cq)
    fm_group(O_CKV, 320, lat_ckv)

    if STAGE < 4:
        return
    for tb in range(2):
        ts = tsl(tb)
        for (RAW, rtok, NRM, ntok, nch, gg, gtok) in [(CQ, 'CQ', CQN, 'CQN', 4, gq, 'gq'), (CKV, 'CKV', CKVN, 'CKVN', 2, gkv, 'gkv')]:
            sql = []
            for c in range(nch):
                SQ, sqn = Pp.get()
                fw.op('act', [rtok], [sqn], lambda: nc.scalar.activation(SQ[:], RAW[:, c, ts], AF.Square))
                sql.append((SQ[:], sqn))
            R, rn = rstd_of(sql, ones[:], 'ones', nch * 128)
            for c in range(nch):
                fw.op('dve', [rtok, rn, gtok], [ntok], lambda: nc.vector.scalar_tensor_tensor(NRM[:, c, ts], RAW[:, c, ts], gg[:, c:c + 1], R[:], ALU.mult, ALU.mult))
        rope_out(KPE[:, ts], 'KPE', 64, mqk[0:64, 3:4], 'mqk', tb, d['kbr'][:, ts])

    if STAGE < 5:
        return
    for tb in range(2):
        for hh in range(8):
            p, pn = fm_chunk(WQ, 'WQ', 4, hh * 128, 128, tb, CQN, 'CQN')
            qknorm_out(p, pn, 128, ones[:], 'ones', 128, mqk[:, 0:1], 'mqk', d['qbn'][hh, :, tsl(tb)])
        for hh in range(8):
            p, pn = fm_chunk(WQ, 'WQ', 4, 1024 + hh * 64, 64, tb, CQN, 'CQN')
            rope_out(p[0:64, :], pn, 64, mqk[0:64, 2:3], 'mqk', tb, d['qbr'][hh, :, tsl(tb)])
        for hh in range(8):
            p, pn = fm_chunk(WKV, 'WKV', 2, hh * 128, 128, tb, CKVN, 'CKVN')
            qknorm_out(p, pn, 128, ones[:], 'ones', 128, mqk[:, 1:2], 'mqk', d['kbn'][hh, :, tsl(tb)])
    if STAGE < 6:
        return
    for tt in range(TT // 128):
        for nb in range(2):
            p, pn = nextps()
            for c in range(2):
                fw.op('pe', ['CKVN', 'WKV'], [pn], lambda: nc.tensor.matmul(p[:], CKVN[:, c, tt * 128:(tt + 1) * 128], WKV[:, c, 1024 + nb * 512:1024 + (nb + 1) * 512], start=(c == 0), stop=(c == 1)), inc=(c == 1))
            O, on = Op.get()
            fw.op('act', [pn], [on], lambda: nc.scalar.copy(O[:], p[:]))
            out_dma(d['vb'][tt * 128:(tt + 1) * 128, nb * 512:(nb + 1) * 512], O[:], on)

    if STAGE < 7:
        return
    for (c0, dst) in [(O_AV, d['va']), (O_SV, d['vc'])]:
        for s0 in range(0, 1024, 512):
            slab, stok = slabs.load(w, 0, 2048, c0 + s0, 512)
            for tt in range(TT // 128):
                p, pn = nextps()
                for c in range(16):
                    fw.op('pe', ['H', stok], [pn], lambda: nc.tensor.matmul(p[:], H[:, c, tt * 128:(tt + 1) * 128], slab[:, c, :], start=(c == 0), stop=(c == 15)), inc=(c == 15))
                O, on = Op.get()
                fw.op('act', [pn], [on], lambda: nc.scalar.copy(O[:], p[:]))
                out_dma(dst[tt * 128:(tt + 1) * 128, s0:s0 + 512], O[:], on)
    return ['dramout']


def proj_consts(core):
    half = 32
    inv = (10000.0 ** (-np.arange(half, dtype=np.float32) / half)).astype(np.float32)
    pos = np.arange(core * TT, (core + 1) * TT).astype(np.float32)
    ang = pos[None, :] * np.concatenate([inv, inv])[:, None]
    cosT = np.cos(ang).astype(np.float32)
    sinT = np.sin(ang).astype(np.float32)
    rot = np.zeros((64, 64), np.float32)
    for p in range(32):
        rot[p + 32, p] = -1.0
        rot[p, p + 32] = 1.0
    bd = np.zeros((128, 128), np.float32)
    bd[:64, :64] = 1
    bd[64:, 64:] = 1
    return dict(cosT=cosT, sinT=sinT, rotm=rot.astype(ml_dtypes.bfloat16), bd64=bd.astype(ml_dtypes.bfloat16), ones=np.ones((128, 128), ml_dtypes.bfloat16))


def _common(nc, fw, d, ps, Tp, Pp, H, RS, ones, modc, gvec, gtok, cols, sc_idx, sh_idx, X):
    fw.op('dve', ['modc', gtok], ['cols'], lambda: nc.vector.scalar_tensor_tensor(cols[:, 0, :], modc[:, sc_idx, :], 1.0, gvec[:], ALU.add, ALU.mult))
    fw.op('dve', ['modc'], ['cols'], lambda: nc.vector.tensor_copy(cols[:, 1, :], modc[:, sh_idx, :]))
    emit_norm_h(nc, fw, X, H, cols[:, 0, :], cols[:, 1, :], ones, ps[7], 'ps7', Tp, Pp, RS)


def tsl(tb):
    return slice(tb * 512, tb * 512 + 512)


def emit_merge(nc, fw, d, ps):
    sb = nc.alloc_sbuf_tensor
    H = sb("H", [128, 16, TT], BF16)
    RS = sb("RS", [128, TT], F32)
    MG = sb("MG", [128, 16, TT], BF16)
    MGF = [[sb(f"MGF{i}_{tb}", [128, 512], F32) for tb in range(2)] for i in range(4)]
    ones = sb("pones", [128, 128], BF16)
    modc = sb("pmodc", [128, 6, 16], F32)
    gmix = sb("pgmix", [128, 16], F32)
    cols = sb("pcols", [128, 2, 16], F32)
    Pp = Pool_(nc, "pP", 4, [128, 512], BF16)
    Tp = Pool_(nc, "pT", 6, [128, 512], F32)
    pslist = [(ps[i], f'ps{i}') for i in range(7)]
    pi = [0]

    def nextps():
        p = pslist[pi[0] % 7]
        pi[0] += 1
        return p

    for c in range(0, 16, 4):
        fw.dma('sp', [], ['H'], H[:, c:c + 4, :], d['hT'][c * 128:(c + 4) * 128, :].rearrange("(c p) t -> p c t", p=128))
    slabs = Slabs(nc, fw, n=3)
    BRp = Pool_(nc, "BR", 2, [128, 8, TT], BF16)
    w_in, wbr = d['w_in'], d['w_branch']
    for s in range(4):
        for i in range(3):
            gs, gtok = slabs.load(w_in, 0, 2048, i * 2048 + s * 512, 512)
            us, utok = slabs.load(wbr[i], 0, 1024, s * 512, 512)
            BR, brn = BRp.get()
            fw.dma('act', [], [brn], BR[:], d['br'][i].rearrange("c p t -> p c t"))
            for tb in range(2):
                ts = tsl(tb)
                for cc in range(4):
                    n = s * 4 + cc
                    mtok = 'MGf%d_%d' % (cc, tb)
                    gp, gpn = nextps()
                    for c in range(16):
                        fw.op('pe', [gtok, 'H'], [gpn], lambda: nc.tensor.matmul(gp[:], gs[:, c, cc * 128:(cc + 1) * 128], H[:, c, ts], start=(c == 0), stop=(c == 15)), inc=(c == 15))
                    up, upn = nextps()
                    for c in range(8):
                        fw.op('pe', [utok, brn], [upn], lambda: nc.tensor.matmul(up[:], us[:, c, cc * 128:(cc + 1) * 128], BR[:, c, ts], start=(c == 0), stop=(c == 7)), inc=(c == 7))
                    SG, sgn = Tp.get()
                    fw.op('act', [gpn], [sgn], lambda: nc.scalar.activation(SG[:], gp[:], AF.Sigmoid))
                    if i == 0:
                        fw.op('dve', [upn, sgn], [mtok], lambda: nc.vector.tensor_tensor(MGF[cc][tb][:], up[:], SG[:], ALU.mult))
                    elif i == 1:
                        fw.op('dve', [upn, sgn], [sgn], lambda: nc.vector.tensor_tensor(SG[:], up[:], SG[:], ALU.mult))
                        fw.op('dve', [sgn, mtok], [mtok], lambda: nc.vector.tensor_tensor(MGF[cc][tb][:], MGF[cc][tb][:], SG[:], ALU.add))
                    else:
                        fw.op('dve', [upn, sgn], [sgn], lambda: nc.vector.tensor_tensor(SG[:], up[:], SG[:], ALU.mult))
                        fw.op('dve', [sgn, mtok], ['MG'], lambda: nc.vector.tensor_tensor(MG[:, n, ts], MGF[cc][tb][:], SG[:], ALU.add))
    for c in range(0, 16, 4):
        fw.dma('sp', ['MG'], [], d['mg'][c * 128:(c + 4) * 128, :].rearrange("(c p) t -> p c t", p=128), MG[:, c:c + 4, :])


def emit_mlp(nc, fw, d, ps):
    sb = nc.alloc_sbuf_tensor
    X = sb("X", [128, 16, TT], F32)
    H = sb("H", [128, 16, TT], BF16)
    RS = sb("RS", [128, TT], F32)
    MG = sb("MG", [128, 16, TT], BF16)
    ones = sb("pones", [128, 128], BF16)
    modc = sb("pmodc", [128, 6, 16], F32)
    gmlp = sb("pgmlp", [128, 16], F32)
    cols = sb("pcols", [128, 2, 16], F32)
    Pp = Pool_(nc, "pP", 4, [128, 512], BF16)
    Tp = Pool_(nc, "pT", 6, [128, 512], F32)
    slabs = Slabs(nc, fw, n=3)
    pslist = [(ps[i], f'ps{i}') for i in range(7)]
    pi = [0]

    def nextps():
        p = pslist[pi[0] % 7]
        pi[0] += 1
        return p

    for c in range(0, 16, 4):
        fw.dma('sp', [], ['X'], X[:, c:c + 4, :], d['xT'][c * 128:(c + 4) * 128, :].rearrange("(c p) t -> p c t", p=128))
        fw.dma('act', [], ['MG'], MG[:, c:c + 4, :], d['mg'][c * 128:(c + 4) * 128, :].rearrange("(c p) t -> p c t", p=128))
    fw.dma('act', [], ['ones'], ones[:], d['ones'])
    fw.dma('act', [], ['modc'], modc[:], d['modc'])
    fw.dma('act', [], ['gmlp'], gmlp[:], d['gmlp'])
    wout, w1, w2 = d['w_out'], d['w_mlp_in'], d['w_mlp_out']

    def resid_matmul(W, r0, gcolidx):
        for s in range(4):
            sl, stk = slabs.load(W, r0, 2048, s * 512, 512)
            for tb in range(2):
                ts = tsl(tb)
                for cc in range(4):
                    n = s * 4 + cc
                    p, pn = nextps()
                    for c in range(16):
                        fw.op('pe', [stk, 'MG'], [pn], lambda: nc.tensor.matmul(p[:], sl[:, c, cc * 128:(cc + 1) * 128], MG[:, c, ts], start=(c == 0), stop=(c == 15)), inc=(c == 15))
                    fw.op('dve', [pn, 'X', 'modc'], ['X'], lambda: nc.vector.scalar_tensor_tensor(X[:, n, ts], p[:], modc[:, gcolidx, n:n + 1], X[:, n, ts], ALU.mult, ALU.add))

    resid_matmul(wout, 0, 2)
    _common(nc, fw, d, ps, Tp, Pp, H, RS, ones, modc, gmlp, 'gmlp', cols, 4, 3, X)
    for g in range(4):
        for s in range(4):
            sl, stk = slabs.load(w1, 0, 2048, g * 2048 + s * 512, 512)
            for tb in range(2):
                ts = tsl(tb)
                for cc in range(4):
                    f = s * 4 + cc
                    p, pn = nextps()
                    for c in range(16):
                        fw.op('pe', [stk, 'H'], [pn], lambda: nc.tensor.matmul(p[:], sl[:, c, cc * 128:(cc + 1) * 128], H[:, c, ts], start=(c == 0), stop=(c == 15)), inc=(c == 15))
                    R, rn = Tp.get()
                    fw.op('act', [pn], [rn], lambda: nc.scalar.activation(R[:], p[:], AF.Relu))
                    fw.op('dve', [rn], ['MG'], lambda: nc.vector.tensor_tensor(MG[:, f, ts], R[:], R[:], ALU.mult))
        resid_matmul(w2, g * 2048, 5)
    for c in range(0, 16, 4):
        fw.dma('sp', ['X'], [], d['xo'][c * 128:(c + 4) * 128, :].rearrange("(c p) t -> p c t", p=128), X[:, c:c + 4, :])


def emit_mod(nc, fw, d, ps):
    sb = nc.alloc_sbuf_tensor
    cT = sb("cT_sb", [128, 16], F32)
    fw.dma('sp', [], ['cT'], cT[:], d['cT'])
    bia = sb("bia_sb", [1, 4 * 1536], F32)
    fw.dma('sp', [], ['bia'], bia[:], d['b'])
    res = sb("res_sb", [1, 4 * 1536], F32)
    wp = Pool_(nc, "wm", 2, [128, 16, 512], F32)
    k = 0
    for l in range(4):
        for s in range(3):
            W, wn = wp.get()
            for c in range(0, 16, 4):
                fw.dma('sp' if (k % 2 == 0) else 'act', [], [wn], W[:, c:c + 4, :], d['w'][l, c * 128:(c + 4) * 128, s * 512:(s + 1) * 512].rearrange("(c p) n -> p c n", p=128))
            p, pn = ps[k % 2], f'ps{k % 2}'
            for c in range(16):
                fw.op('pe', [wn, 'cT'], [pn], lambda: nc.tensor.matmul(p[0:1, :], cT[:, c:c + 1], W[:, c, :], start=(c == 0), stop=(c == 15)), inc=(c == 15))
            o0 = l * 1536 + s * 512
            fw.op('dve', [pn, 'bia'], ['res'], lambda: nc.vector.tensor_tensor(res[:, o0:o0 + 512], p[0:1, :], bia[:, o0:o0 + 512], ALU.add))
            k += 1
    fw.dma('sp', ['res'], [], d['out'], res[:])


bf = ml_dtypes.bfloat16
NCORES = 8
S = 8192
DEBUG = bool(int(os.environ.get("MKDEBUG", "0")))
_progs = {}


def _mk(name, builder):
    if name not in _progs:
        _progs[name] = builder()
    return _progs[name]


def _din(nc, d, name, shape, dt):
    d[name] = nc.dram_tensor(name, list(shape), dt, kind="ExternalInput").ap()


def _dout(nc, d, name, shape, dt):
    d[name] = nc.dram_tensor(name, list(shape), dt, kind="ExternalOutput").ap()


def _psum(nc):
    return [nc.alloc_psum_tensor(f"ps{i}", [128, 512], F32) for i in range(8)]


def build_mod():
    nc = bass.Bass("TRN2", target_bir_lowering=False)
    fw = FW(nc)
    d = {}
    _din(nc, d, 'cT', (128, 16), F32)
    _din(nc, d, 'b', (1, 6144), F32)
    _din(nc, d, 'w', (4, 2048, 1536), F32)
    _dout(nc, d, 'out', (1, 6144), F32)
    emit_mod(nc, fw, d, _psum(nc))
    fw.finish_all()
    return nc


def build_proj():
    nc = bass.Bass("TRN2", target_bir_lowering=False)
    fw = FW(nc)
    d = {}
    _din(nc, d, 'xT', (2048, TT), F32)
    _din(nc, d, 'w_in', (2048, 6976), F32)
    for n, s, t in [('ones', (128, 128), BF16), ('bd64', (128, 128), BF16), ('rotm', (64, 64), BF16), ('modc', (128, 6, 16), F32),
                    ('gmix', (128, 16), F32), ('dqk', (128, 2), F32), ('gq', (128, 4), F32), ('gkv', (128, 2), F32), ('mqk', (128, 4), F32),
                    ('cosT', (64, TT), F32), ('sinT', (64, TT), F32), ('wqup', (512, 1536), F32), ('wkvup', (256, 2048), F32)]:
        _din(nc, d, n, s, t)
    for n in ['qa', 'ka', 'qbn', 'kbn', 'qc', 'kc']:
        _dout(nc, d, n, (8, 128, TT), BF16)
    _dout(nc, d, 'qbr', (8, 64, TT), BF16)
    _dout(nc, d, 'kbr', (64, TT), BF16)
    for n in ['va', 'vb', 'vc']:
        _dout(nc, d, n, (TT, 1024), BF16)
    _dout(nc, d, 'hT', (2048, TT), BF16)
    emit_proj(nc, fw, d, _psum(nc))
    fw.finish_all()
    return nc


def build_attn():
    nc = bass.Bass("TRN2", target_bir_lowering=False)
    fw = FW(nc)
    d = {}
    for n in ['qa', 'ka', 'qbn', 'kbn', 'qc', 'kc']:
        _din(nc, d, n, (128, S), BF16)
    for n in ['qbr', 'kbr']:
        _din(nc, d, n, (64, S), BF16)
    for n in ['va', 'vb', 'vc']:
        _din(nc, d, n, (S, 128), BF16)
    for n, s, t in [('ones', (128, 128), BF16), ('mtri', (128, 128), BF16), ('biasA', (5, 128, 512), F32), ('maskB', (4, 128, 512), F32),
                    ('maskC', (4, 128, 512), BF16), ('ident', (128, 128), BF16), ('lamp', (128, 256), F32), ('cst', (128, 4), F32), ('b15', (128, 1), F32), ('gsub', (128, 1), F32)]:
        _din(nc, d, n, s, t)
    for n in ['oa', 'ob', 'oc']:
        _dout(nc, d, n, (128, S), BF16)
    psz = [nc.alloc_psum_tensor(f"psz{i}", [128, 1024], F32) for i in range(2)]
    ps = [psz[0][:, 0:512], psz[0][:, 512:1024], psz[1][:, 0:512], psz[1][:, 512:1024]] + [nc.alloc_psum_tensor(f"ps{i}", [128, 512], F32) for i in range(4, 8)]
    emit_attn(nc, fw, S, d, ps, psz)
    fw.finish_all()
    return nc


def build_merge():
    nc = bass.Bass("TRN2", target_bir_lowering=False)
    fw = FW(nc)
    d = {}
    _din(nc, d, 'hT', (2048, TT), BF16)
    _din(nc, d, 'w_in', (2048, 6144), F32)
    _din(nc, d, 'w_branch', (3, 1024, 2048), F32)
    _din(nc, d, 'br', (3, 8, 128, TT), BF16)
    _dout(nc, d, 'mg', (2048, TT), BF16)
    emit_merge(nc, fw, d, _psum(nc))
    fw.finish_all()
    return nc


def build_mlp():
    nc = bass.Bass("TRN2", target_bir_lowering=False)
    fw = FW(nc)
    d = {}
    _din(nc, d, 'xT', (2048, TT), F32)
    _din(nc, d, 'mg', (2048, TT), BF16)
    _din(nc, d, 'w_out', (2048, 2048), F32)
    _din(nc, d, 'w_mlp_in', (2048, 8192), F32)
    _din(nc, d, 'w_mlp_out', (8192, 2048), F32)
    for n, s, t in [('ones', (128, 128), BF16), ('modc', (128, 6, 16), F32), ('gmlp', (128, 16), F32)]:
        _din(nc, d, n, s, t)
    _dout(nc, d, 'xo', (2048, TT), F32)
    emit_mlp(nc, fw, d, _psum(nc))
    fw.finish_all()
    return nc


def colvec(v, n):
    return np.ascontiguousarray(np.asarray(v, np.float32).reshape(n, 128).T)


def run(nc, in_maps):
    t = time.time()
    res = run_bass_kernel_spmd(nc, in_maps, core_ids=list(range(NCORES)))
    if DEBUG:
        print("  launch %.1fs" % (time.time() - t), flush=True)
    return res.results


def kernel(x, c, w_ada, b_ada, norm_mix_g, norm_mlp_g, w_in, diff_qk_g, diff_lambda, diff_subln_g, t5_bias,
           mla_q_norm_g, mla_kv_norm_g, w_q_up, w_kv_up, mla_qk_g, w_branch, w_out, w_mlp_in, w_mlp_out):
    f32 = np.float32
    x = np.asarray(x, f32)
    AC = attn_consts()
    PC = [proj_consts(i) for i in range(NCORES)]
    ones = AC['ones']
    nc_mod = _mk('mod', build_mod)
    cT = colvec(np.asarray(c, f32)[0], 16)
    w_ada = np.asarray(w_ada, f32)
    b_ada = np.asarray(b_ada, f32)
    ins = []
    for i in range(NCORES):
        ins.append(dict(cT=cT, b=np.ascontiguousarray(b_ada[:, 1536 * i:1536 * (i + 1)]).reshape(1, 6144),
                        w=np.ascontiguousarray(w_ada[:, :, 1536 * i:1536 * (i + 1)])))
    r = run(nc_mod, ins)
    mod = np.concatenate([r[i]['out'].reshape(4, 1536) for i in range(NCORES)], axis=1)
    if DEBUG:
        ref = np.asarray(c, f32) @ w_ada[0] + b_ada[0]
        print("mod err", np.abs(mod[0] - ref[0]).max(), np.abs(ref).max())

    xT = [np.ascontiguousarray(x[0, i * TT:(i + 1) * TT].T) for i in range(NCORES)]
    nc_proj = _mk('proj', build_proj)
    nc_attn = _mk('attn', build_attn)
    nc_merge = _mk('merge', build_merge)
    nc_mlp = _mk('mlp', build_mlp)
    for l in range(4):
        modc = np.ascontiguousarray(mod[l].reshape(6, 16, 128).transpose(2, 0, 1))
        gmix = colvec(norm_mix_g[l], 16)
        gmlp = colvec(norm_mlp_g[l], 16)
        wl = np.asarray(w_in[l], f32)
        w_qkv = np.ascontiguousarray(wl[:, :6976])
        w_gate = np.ascontiguousarray(wl[:, 6976:])
        dq = np.asarray(diff_qk_g[l], f32)
        dqk = np.stack([np.tile(dq[0], 2), np.tile(dq[1], 2)], axis=1)
        gq = colvec(mla_q_norm_g[l], 4)
        gkv = colvec(mla_kv_norm_g[l], 2)
        mg_ = np.asarray(mla_qk_g[l], f32)
        mqk = np.zeros((128, 4), f32)
        mqk[:, 0] = mg_[0, :128]
        mqk[:, 1] = mg_[1, :128]
        mqk[:64, 2] = mg_[0, 128:]
        mqk[:64, 3] = mg_[1, 128:]
        wq = np.asarray(w_q_up[l], f32).reshape(512, 8, 192)
        wqup = np.ascontiguousarray(np.concatenate([wq[:, :, :128].reshape(512, 1024), wq[:, :, 128:].reshape(512, 512)], axis=1))
        wkv = np.asarray(w_kv_up[l], f32).reshape(256, 8, 256)
        wkvup = np.ascontiguousarray(np.concatenate([wkv[:, :, :128].reshape(256, 1024), wkv[:, :, 128:].reshape(256, 1024)], axis=1))
        ins = []
        for i in range(NCORES):
            ins.append(dict(xT=xT[i], w_in=w_qkv, ones=ones, bd64=PC[i]['bd64'], rotm=PC[i]['rotm'], modc=modc, gmix=gmix, dqk=dqk, gq=gq, gkv=gkv,
                            mqk=mqk, cosT=PC[i]['cosT'], sinT=PC[i]['sinT'], wqup=wqup, wkvup=wkvup))
        rp = run(nc_proj, ins)
        if DEBUG and l == 0:
            debug_proj(x, mod, norm_mix_g, wl, dq, rp, mla_q_norm_g, mla_kv_norm_g, w_q_up, w_kv_up, mla_qk_g)
        lam_init = 0.8 - 0.6 * np.exp(-0.3 * l)
        cst = np.broadcast_to(np.array([lam_init, 1 - lam_init, 0, 0], f32), (128, 4)).copy()
        lamp = np.broadcast_to(np.asarray(diff_lambda[l], f32).reshape(1, 256), (128, 256)).copy()
        gsub = np.asarray(diff_subln_g[l], f32).reshape(128, 1).copy()
        t5 = np.asarray(t5_bias, f32)
        kbr = np.concatenate([rp[i]['kbr'] for i in range(NCORES)], axis=1)
        ins = []
        for h in range(NCORES):
            dd = dict(ones=ones, mtri=AC['mtri'], biasA=np.ascontiguousarray(t5[AC['bucket'], h]), maskB=AC['maskB'], maskC=AC['maskC'], ident=AC['ident'],
                      lamp=lamp, cst=cst, b15=np.full((128, 1), t5[15, h], f32), gsub=gsub, kbr=kbr)
            for n in ['qa', 'ka', 'qbn', 'kbn', 'qc', 'kc', 'qbr']:
                dd[n] = np.concatenate([rp[i][n][h] for i in range(NCORES)], axis=1)
            for n in ['va', 'vb', 'vc']:
                dd[n] = np.concatenate([rp[i][n][:, 128 * h:128 * (h + 1)] for i in range(NCORES)], axis=0)
            ins.append(dd)
        ra = run(nc_attn, ins)
        ins = []
        for i in range(NCORES):
            br = np.stack([np.stack([ra[h][n][:, i * TT:(i + 1) * TT] for h in range(NCORES)]) for n in ['oa', 'ob', 'oc']])
            ins.append(dict(hT=rp[i]['hT'], w_in=w_gate, w_branch=np.asarray(w_branch[l], f32), br=br))
        rm = run(nc_merge, ins)
        ins = []
        for i in range(NCORES):
            ins.append(dict(xT=xT[i], mg=rm[i]['mg'], w_out=np.asarray(w_out[l], f32), w_mlp_in=np.asarray(w_mlp_in[l], f32),
                            w_mlp_out=np.asarray(w_mlp_out[l], f32), ones=ones, modc=modc, gmlp=gmlp))
        rx = run(nc_mlp, ins)
        if DEBUG and l == 0:
            debug_post(x, mod, norm_mix_g, norm_mlp_g, wl, w_branch, w_out, w_mlp_in, w_mlp_out, ra, rm, rx)
        xT = [rx[i]['xo'] for i in range(NCORES)]
    out = np.concatenate([xT[i].T for i in range(NCORES)], axis=0)[None]
    return np.ascontiguousarray(out.astype(f32))


def _rms(x, g):
    return x / np.sqrt((x * x).mean(-1, keepdims=True) + 1e-6) * g


def _err(name, got, ref):
    got = np.asarray(got).astype(np.float64)
    ref = np.asarray(ref).astype(np.float64)
    print("  DBG %-6s rel %.2e  max %.3e / %.3e" % (name, np.sqrt(((got - ref) ** 2).mean() / ((ref ** 2).mean() + 1e-30)), np.abs(got - ref).max(), np.abs(ref).max()), flush=True)


def debug_proj(x, mod, norm_mix_g, wl, dq, rp, mla_q_norm_g, mla_kv_norm_g, w_q_up, w_kv_up, mla_qk_g):
    T0 = 128
    xs = x[0, :T0].astype(np.float64)
    sh1, sc1 = mod[0, :2048], mod[0, 2048:4096]
    h = _rms(xs, np.asarray(norm_mix_g[0], np.float64)) * (1 + sc1) + sh1
    pr = h @ wl.astype(np.float64)
    r = rp[0]
    f = lambda a: np.asarray(a).astype(np.float32)
    qa = _rms(pr[:, 0:1024].reshape(T0, 8, 2, 64), dq[0]).reshape(T0, 8, 128)
    _err('qa', f(r['qa'])[:, :, :T0].transpose(2, 0, 1), qa)
    ka = _rms(pr[:, 1024:2048].reshape(T0, 8, 2, 64), dq[1]).reshape(T0, 8, 128)
    _err('ka', f(r['ka'])[:, :, :T0].transpose(2, 0, 1), ka)
    _err('va', f(r['va'])[:T0], pr[:, 2048:3072])
    _err('qc', f(r['qc'])[:, :, :T0].transpose(2, 0, 1), pr[:, 3904:4928].reshape(T0, 8, 128) * 128 ** -0.5)
    _err('kc', f(r['kc'])[:, :, :T0].transpose(2, 0, 1), pr[:, 4928:5952].reshape(T0, 8, 128))
    _err('vc', f(r['vc'])[:T0], pr[:, 5952:6976])
    cq = _rms(pr[:, 3072:3584], np.asarray(mla_q_norm_g[0], np.float64))
    ckv = _rms(pr[:, 3584:3840], np.asarray(mla_kv_norm_g[0], np.float64))
    kpe = pr[:, 3840:3904]
    q = (cq @ np.asarray(w_q_up[0], np.float64)).reshape(T0, 8, 192)
    kv = (ckv @ np.asarray(w_kv_up[0], np.float64)).reshape(T0, 8, 256)
    g = np.asarray(mla_qk_g[0], np.float64)
    pos = np.arange(T0)
    inv = 10000.0 ** (-np.arange(32) / 32)
    ang = pos[:, None] * inv[None]
    cos, sin = np.cos(ang)[:, None], np.sin(ang)[:, None]

    def rope(v):
        x1, x2 = v[..., :32], v[..., 32:]
        return np.concatenate([x1 * cos - x2 * sin, x1 * sin + x2 * cos], -1)
    _err('qbn', f(r['qbn'])[:, :, :T0].transpose(2, 0, 1), _rms(q[..., :128], g[0, :128]))
    _err('qbr', f(r['qbr'])[:, :, :T0].transpose(2, 0, 1), rope(_rms(q[..., 128:], g[0, 128:])))
    _err('kbn', f(r['kbn'])[:, :, :T0].transpose(2, 0, 1), _rms(kv[..., :128], g[1, :128]))
    _err('kbr', f(r['kbr'])[:, :T0].T, rope(_rms(kpe.reshape(T0, 1, 64), g[1, 128:]))[:, 0])
    _err('vb', f(r['vb'])[:T0].reshape(T0, 8, 128), kv[..., 128:])


def debug_post(x, mod, norm_mix_g, norm_mlp_g, wl, w_branch, w_out, w_mlp_in, w_mlp_out, ra, rm, rx):
    T0 = 128
    f64 = np.float64
    xs = x[0, :T0].astype(f64)
    sh1, sc1, g1, sh2, sc2, g2 = [mod[0, i * 2048:(i + 1) * 2048].astype(f64) for i in range(6)]
    h = _rms(xs, np.asarray(norm_mix_g[0], f64)) * (1 + sc1) + sh1
    gates = h @ wl[:, 6976:].astype(f64)
    br = np.stack([np.concatenate([np.asarray(ra[hh][n]).astype(np.float32)[:, :T0].T for hh in range(8)], axis=1) for n in ['oa', 'ob', 'oc']]).astype(f64)
    merged = 0
    for i in range(3):
        up = br[i] @ np.asarray(w_branch[0][i], f64)
        merged = merged + 1 / (1 + np.exp(-gates[:, i * 2048:(i + 1) * 2048])) * up
    _err('mg', np.asarray(rm[0]['mg']).astype(np.float32)[:, :T0].T, merged)
    x1 = xs + g1 * (merged @ np.asarray(w_out[0], f64))
    h2 = _rms(x1, np.asarray(norm_mlp_g[0], f64)) * (1 + sc2) + sh2
    hid = np.maximum(h2 @ np.asarray(w_mlp_in[0], f64), 0) ** 2
    x2 = x1 + g2 * (hid @ np.asarray(w_mlp_out[0], f64))
    _err('xo', rx[0]['xo'][:, :T0].T, x2)
    _err('dx', rx[0]['xo'][:, :T0].T - xs, x2 - xs)
```

```python
import os, time
import numpy as np
import ml_dtypes
import concourse.bass as bass
import concourse.mybir as mybir
from concourse.bass_utils import run_bass_kernel_spmd

F32 = mybir.dt.float32
BF16 = mybir.dt.bfloat16
AF = mybir.ActivationFunctionType
ALU = mybir.AluOpType


class FW:
    def __init__(self, nc):
        self.nc = nc
        self.eng = {'pe': nc.tensor, 'act': nc.scalar, 'dve': nc.vector, 'pool': nc.gpsimd, 'sp': nc.sync}
        self.sem = {}
        self.cnt = {}
        for k in ['pe', 'act', 'dve', 'pool']:
            self.sem[k] = nc.alloc_semaphore(name='s_' + k)
            self.cnt[k] = 0
        self.seen = {k: {} for k in self.eng}
        self.lastw = {}
        self.reads = {}
        self.dsem = {}

    def _wait(self, e, ev):
        if ev is None:
            return
        k, v = ev
        if e == 'pe' and k == 'pe':
            return
        if self.seen[e].get(k, 0) >= v:
            return
        self.eng[e].wait_ge(self.sem[k], v)
        self.seen[e][k] = v

    def deps(self, e, R, W):
        for t in R:
            self._wait(e, self.lastw.get(t))
        for t in W:
            self._wait(e, self.lastw.get(t))
            for ev in self.reads.get(t, []):
                self._wait(e, ev)

    def commit(self, ev, R, W):
        for t in R:
            self.reads.setdefault(t, []).append(ev)
        for t in W:
            self.lastw[t] = ev
            self.reads[t] = []

    def op(self, e, R, W, fn, inc=True):
        self.deps(e, R, W)
        ins = fn()
        if inc:
            self.cnt[e] += 1
            ins.then_inc(self.sem[e], 1)
            ev = (e, self.cnt[e])
        else:
            ev = (e, self.cnt[e] + 1)
        self.commit(ev, R, W)
        return ins

    def dma(self, q, R, W, out, in_, semtok=None, **kw):
        st = semtok if semtok is not None else (W[0] if W else R[0])
        if st not in self.dsem:
            key = 'd_' + str(st)
            self.sem[key] = self.nc.alloc_semaphore(name=key)
            self.cnt[key] = 0
            self.dsem[st] = key
        key = self.dsem[st]
        self.deps(q, R, W)
        ins = self.eng[q].dma_start(out=out, in_=in_, **kw)
        self.cnt[key] += 16
        ins.then_inc(self.sem[key], 16)
        ev = (key, self.cnt[key])
        self.commit(ev, R, W)
        return ins

    def finish(self, toks, e='sp'):
        for t in toks:
            self._wait(e, self.lastw.get(t))

    def barrier(self):
        for e in self.eng:
            for k in self.sem:
                if self.cnt[k] > 0 and not (e == 'pe' and k == 'pe') and self.seen[e].get(k, 0) < self.cnt[k]:
                    self.eng[e].wait_ge(self.sem[k], self.cnt[k])
                    self.seen[e][k] = self.cnt[k]

    def finish_all(self, e='sp'):
        for k in self.sem:
            if k.startswith('d_') and self.cnt[k] > 0 and self.seen[e].get(k, 0) < self.cnt[k]:
                self.eng[e].wait_ge(self.sem[k], self.cnt[k])
                self.seen[e][k] = self.cnt[k]

    def finish_dma(self, semtok, e='sp'):
        if semtok not in self.dsem:
            return
        key = self.dsem[semtok]
        self.eng[e].wait_ge(self.sem[key], self.cnt[key])


AX = mybir.AxisListType
NEG = -1.0e30
EPS = 1e-6


class Pool_:
    def __init__(self, nc, name, n, shape, dt):
        self.tiles = [nc.alloc_sbuf_tensor(f"{name}{i}", shape, dt) for i in range(n)]
        self.names = [f"{name}{i}" for i in range(n)]
        self.i = 0

    def get(self):
        t, n = self.tiles[self.i], self.names[self.i]
        self.i = (self.i + 1) % len(self.tiles)
        return t, n


def emit_attn(nc, fw, S, d, ps, psz_in=None):
    NQB = S // 512
    NKT = S // 128
    sb = nc.alloc_sbuf_tensor
    Q1 = sb("aQ1", [128, S], BF16)
    Q2 = sb("aQ2", [128, S], BF16)
    K1 = sb("aK1", [128, S], BF16)
    K2 = sb("aK2", [128, S], BF16)
    K3 = sb("aK3", [128, S], BF16)
    V = sb("aV", [128, NKT, 128], BF16)
    OUT = sb("aOUT", [128, S], BF16)
    ones = sb("aones", [128, 128], BF16)
    mtri = sb("amtri", [128, 128], BF16)
    mones = sb("amones", [128, 128], BF16)
    biasA = sb("abiasA", [128, 5, 512], F32)
    maskB = sb("amaskB", [128, 4, 512], F32)
    maskC = sb("amaskC", [128, 4, 512], BF16)
    ident = sb("aident", [128, 128], BF16)
    lamp = sb("alamp", [128, 256], F32)
    cst = sb("acst", [128, 4], F32)
    b15 = sb("ab15", [128, 1], F32)
    gsub = sb("agsub", [128, 1], F32)
    sm = sb("asm", [128, 16], F32)
    carry = sb("acarry", [128, 512], F32)
    carry2 = sb("acarry2", [128, 512], F32)
    Pp = Pool_(nc, "aP", 6, [128, 512], BF16)
    Tp = Pool_(nc, "aT", 6, [128, 512], F32)
    Fp = Pool_(nc, "aF", 8, [128, 512], F32)
    E2p = Pool_(nc, "aE2", 5, [128, 1024], F32)
    L2p = Pool_(nc, "aL2", 3, [128, 1024], BF16)
    W2p = Pool_(nc, "aW2", 3, [128, 1024], BF16)
    psz = psz_in

    fw.op('pool', [], ['Q2z'], lambda: nc.gpsimd.memset(Q2[64:128, :], 0.0))
    fw.op('pool', [], ['K2z'], lambda: nc.gpsimd.memset(K2[64:128, :], 0.0))
    fw.op('pool', [], ['K3z'], lambda: nc.gpsimd.memset(K3[0:64, :], 0.0))
    fw.dma('sp', [], ['ones'], ones[:], d['ones'])
    fw.dma('sp', [], ['mtri'], mtri[:], d['mtri'])
    fw.dma('sp', [], ['biasA'], biasA[:], d['biasA'].rearrange("o p q -> p o q"))
    fw.dma('sp', [], ['maskB'], maskB[:], d['maskB'].rearrange("o p q -> p o q"))
    fw.dma('sp', [], ['maskC'], maskC[:], d['maskC'].rearrange("o p q -> p o q"))
    fw.dma('sp', [], ['ident'], ident[:], d['ident'])
    fw.dma('sp', [], ['lamp'], lamp[:], d['lamp'])
    fw.dma('sp', [], ['cst'], cst[:], d['cst'])
    fw.dma('sp', [], ['b15'], b15[:], d['b15'])
    fw.dma('sp', [], ['gsub'], gsub[:], d['gsub'])
    for o in range(4):
        fw.op('dve', ['biasA', 'maskB'], ['biasA'], lambda: nc.vector.tensor_tensor(biasA[:, o + 1, :], biasA[:, o + 1, :], maskB[:, o, :], ALU.add))
    T0, t0n = Tp.get()
    fw.op('dve', ['lamp'], [t0n], lambda: nc.vector.tensor_tensor(T0[:, 0:64], lamp[:, 0:64], lamp[:, 64:128], ALU.mult))
    fw.op('dve', [t0n], ['sm'], lambda: nc.vector.reduce_sum(sm[:, 0:1], T0[:, 0:64], AX.X))
    fw.op('dve', ['lamp'], [t0n], lambda: nc.vector.tensor_tensor(T0[:, 0:64], lamp[:, 128:192], lamp[:, 192:256], ALU.mult))
    fw.op('dve', [t0n], ['sm'], lambda: nc.vector.reduce_sum(sm[:, 1:2], T0[:, 0:64], AX.X))
    fw.op('act', ['sm'], ['sm'], lambda: nc.scalar.activation(sm[:, 2:4], sm[:, 0:2], AF.Exp))
    fw.op('dve', ['sm'], ['sm'], lambda: nc.vector.tensor_tensor(sm[:, 4:5], sm[:, 3:4], sm[:, 2:3], ALU.subtract))
    fw.op('dve', ['sm', 'cst'], ['sm'], lambda: nc.vector.tensor_tensor(sm[:, 4:5], sm[:, 4:5], cst[:, 0:1], ALU.subtract))
    fw.op('dve', ['gsub', 'cst'], ['sm'], lambda: nc.vector.tensor_tensor(sm[:, 5:6], gsub[:], cst[:, 1:2], ALU.mult))
    neglam = sm[:, 4:5]
    gcolA = sm[:, 5:6]

    def load_qkv(q1, q2, k1, k2, v):
        fw.dma('sp', [], ['Q1'], Q1[:], q1)
        fw.dma('act', [], ['K1'], K1[:], k1)
        if q2 is not None:
            fw.dma('sp', [], ['Q2'], Q2[0:64, :], q2)
            fw.dma('act', [], ['K2'], K2[0:64, :], k2)
        for c in range(0, NKT, 16):
            n = min(16, NKT - c)
            fw.dma('sp', [], ['V'], V[:, c:c + n, :], v[c * 128:(c + n) * 128, :].rearrange("(t p) e -> p t e", p=128))

    def store_out(dst, qb):
        fw.dma('sp', [f'out{qb}'], [], dst[:, qb * 512:(qb + 1) * 512], OUT[:, qb * 512:(qb + 1) * 512])

    def softmax_mixer(kind):
        nm = 2 if kind == 'A' else 1
        scale = 64 ** -0.5 if kind == 'A' else 192 ** -0.5
        if kind == 'A':
            Sb = [[ps[0], ps[1]], [ps[2], ps[3]]]
            Sn = [['ps0', 'ps1'], ['ps2', 'ps3']]
            NB, LA = 2, 1
        else:
            Sb = [[ps[0], ps[1], ps[2], ps[3]]]
            Sn = [['ps0', 'ps1', 'ps2', 'ps3']]
            NB, LA = 4, 2
        Ob = [ps[4], ps[5]]; On = ['ps4', 'ps5']
        Db = [ps[6], ps[7]]; Dn = ['ps6', 'ps7']

        deferred = []
        for qb in range(NQB):
            q0 = qb * 512
            nk = 4 * (qb + 1)
            qs = slice(q0, q0 + 512)

            def emit_S(kt):
                ks = slice(kt * 128, kt * 128 + 128)
                b = kt % NB
                for m in range(nm):
                    if kind == 'A':
                        KA, katoks = (K2, ['K2', 'K2z']) if m == 0 else (K3, ['K3', 'K3z'])
                        fw.op('pe', katoks + ['Q1'], [Sn[m][b]], lambda: nc.tensor.matmul(Sb[m][b][:], KA[:, ks], Q1[:, qs], start=True, stop=True))
                    else:
                        fw.op('pe', ['K1', 'Q1'], [Sn[m][b]], lambda: nc.tensor.matmul(Sb[m][b][:], K1[:, ks], Q1[:, qs], start=True, stop=False), inc=False)
                        fw.op('pe', ['K2', 'Q2', 'K2z', 'Q2z'], [Sn[m][b]], lambda: nc.tensor.matmul(Sb[m][b][:], K2[:, ks], Q2[:, qs], start=False, stop=True))

            for kk in range(min(LA, nk)):
                emit_S(kk)
            for kt in range(nk):
                b = kt % NB
                o = kt * 128 - q0
                Pt = []
                for m in range(nm):
                    P, pn = Pp.get()
                    near = (o >= -128) if kind == 'A' else (o >= 0)
                    if near:
                        oi = o // 128 + 1
                        T, tn = Tp.get()
                        btile = biasA[:, oi, :] if kind == 'A' else maskB[:, oi - 1, :]
                        btok = 'biasA' if kind == 'A' else 'maskB'
                        fw.op('dve', [Sn[m][b], btok], [tn], lambda: nc.vector.scalar_tensor_tensor(T[:], Sb[m][b][:], scale, btile, ALU.mult, ALU.add))
                        fw.op('act', [tn], [pn], lambda: nc.scalar.activation(P[:], T[:], AF.Exp))
                    else:
                        if kind == 'A':
                            fw.op('act', [Sn[m][b], 'b15'], [pn], lambda: nc.scalar.activation(P[:], Sb[m][b][:], AF.Exp, bias=b15[:], scale=scale))
                        else:
                            fw.op('act', [Sn[m][b]], [pn], lambda: nc.scalar.activation(P[:], Sb[m][b][:], AF.Exp, scale=scale))
                    Pt.append((P, pn))
                if kt == 1:
                    while deferred:
                        deferred.pop(0)()
                if kt + LA < nk:
                    emit_S(kt + LA)
                for m in range(nm):
                    P, pn = Pt[m]
                    fw.op('pe', ['V', pn], [On[m]], lambda: nc.tensor.matmul(Ob[m][:], V[:, kt, :], P[:], start=(kt == 0), stop=(kt == nk - 1)), inc=False)
                    fw.op('pe', ['ones', pn], [Dn[m]], lambda: nc.tensor.matmul(Db[m][:], ones[:], P[:], start=(kt == 0), stop=(kt == nk - 1)))
            if kind == 'B':
                R, rn = Tp.get()
                fw.op('dve', [Dn[0]], [rn], lambda: nc.vector.reciprocal(R[:], Db[0][:]))
                fw.op('dve', [On[0], rn], [f'out{qb}'], lambda: nc.vector.tensor_tensor(OUT[:, qs], Ob[0][:], R[:], ALU.mult))
                store_out(d['ob'], qb)
            else:
                ev = []
                for m in range(2):
                    R, rn = Fp.get()
                    fw.op('dve', [Dn[m]], [rn], lambda: nc.vector.reciprocal(R[:], Db[m][:]))
                    C, cn = Fp.get()
                    fw.op('act', [On[m]], [cn], lambda: nc.scalar.copy(C[:], Ob[m][:]))
                    ev.append((R, rn, C, cn))

                def tail(ev=ev, qb=qb, qs=qs):
                    (R1, r1n, C1, c1n), (R2, r2n, C2, c2n) = ev
                    fw.op('dve', [c1n, r1n], [c1n], lambda: nc.vector.tensor_tensor(C1[:], C1[:], R1[:], ALU.mult))
                    fw.op('dve', [c2n, r2n], [c2n], lambda: nc.vector.tensor_tensor(C2[:], C2[:], R2[:], ALU.mult))
                    fw.op('dve', [c1n, c2n, 'sm'], [c1n], lambda: nc.vector.scalar_tensor_tensor(C1[:], C2[:], neglam, C1[:], ALU.mult, ALU.add))
                    SQ, sqn = Pp.get()
                    fw.op('act', [c1n], [sqn], lambda: nc.scalar.activation(SQ[:], C1[:], AF.Square))
                    fw.op('pe', ['ones', sqn], ['ps0'], lambda: nc.tensor.matmul(ps[0][:], ones[:], SQ[:], start=True, stop=True))
                    fw.op('act', ['ps0'], [r2n], lambda: nc.scalar.activation(R2[:], ps[0][:], AF.Ln, bias=EPS, scale=1.0 / 128))
                    fw.op('act', [r2n], [r2n], lambda: nc.scalar.activation(R2[:], R2[:], AF.Exp, scale=-0.5))
                    fw.op('dve', [c1n, r2n, 'sm'], [f'out{qb}'], lambda: nc.vector.scalar_tensor_tensor(OUT[:, qs], C1[:], gcolA, R2[:], ALU.mult, ALU.mult))
                    store_out(d['oa'], qb)
                deferred.append(tail)
        while deferred:
            deferred.pop(0)()

    def sb_mixer():
        Zb = [psz[0], psz[1]]; Zn = ['psz0', 'psz1']
        Ab = [ps[4], ps[5]]; An = ['ps4', 'ps5']
        Tb_, Tn_ = ps[6], 'ps6'
        Op, On_ = ps[7], 'ps7'
        for qb in range(NQB):
            q0 = qb * 512
            qs = slice(q0, q0 + 512)
            kts = list(range(4 * qb + 3, -1, -1))
            nk = len(kts)
            npair = nk // 2
            Ls = {}
            Ws = {}

            def emit_Z(p):
                for h in range(2):
                    kt = kts[2 * p + h]
                    ks = slice(kt * 128, kt * 128 + 128)
                    o = kt * 128 - q0
                    if o >= 0:
                        fw.op('pe', ['K1', 'Q1'], [Zn[p % 2]], lambda: nc.tensor.matmul(Zb[p % 2][:, h * 512:(h + 1) * 512], K1[:, ks], Q1[:, qs], start=True, stop=False), inc=False)
                        fw.op('pe', ['ident', 'maskC'], [Zn[p % 2]], lambda: nc.tensor.matmul(Zb[p % 2][:, h * 512:(h + 1) * 512], ident[:], maskC[:, o // 128, :], start=False, stop=True), inc=(h == 1))
                    else:
                        fw.op('pe', ['K1', 'Q1'], [Zn[p % 2]], lambda: nc.tensor.matmul(Zb[p % 2][:, h * 512:(h + 1) * 512], K1[:, ks], Q1[:, qs], start=True, stop=True), inc=(h == 1))

            def stage1(p):
                b = p % 2
                E, en = E2p.get()
                fw.op('act', [Zn[b]], [en], lambda: nc.scalar.activation(E[:], Zb[b][:], AF.Exp))
                L, ln_ = L2p.get()
                fw.op('act', [en], [ln_], lambda: nc.scalar.activation(L[:], E[:], AF.Ln, bias=1.0))
                Ls[p] = (L, ln_)

            def emit_O(p):
                W, wn = Ws.pop(p)
                for h in range(2):
                    i = 2 * p + h
                    kt = kts[i]
                    fw.op('pe', ['V', wn], [On_], lambda: nc.tensor.matmul(Op[:], V[:, kt, :], W[:, h * 512:(h + 1) * 512], start=(i == 0), stop=(i == nk - 1)), inc=(h == 1))

            Gs = {}

            def emit_W(p):
                G, gn = Gs.pop(p)
                W, wn = W2p.get()
                fw.op('act', [gn], [wn], lambda: nc.scalar.activation(W[:], G[:], AF.Exp))
                Ws[p] = (W, wn)

            fw.op('dve', [], ['carry'], lambda: nc.vector.memset(carry[:], 0.0))
            emit_Z(0)
            if npair > 1:
                emit_Z(1)
            stage1(0)
            for p in range(npair):
                L, ln_ = Ls.pop(p)
                G, gn = E2p.get()
                for h in range(2):
                    i = 2 * p + h
                    kt = kts[i]
                    b = i % 2
                    ks = slice(kt * 128, kt * 128 + 128)
                    hs = slice(h * 512, (h + 1) * 512)
                    fw.op('pe', ['K1', 'Q1'], [An[b]], lambda: nc.tensor.matmul(Ab[b][:], K1[:, ks], Q1[:, qs], start=True, stop=False), inc=False)
                    od = kt * 128 - q0
                    if od >= 0:
                        fw.op('pe', ['ident', 'maskC'], [An[b]], lambda: nc.tensor.matmul(Ab[b][:], ident[:], maskC[:, od // 128, :], start=False, stop=False), inc=False)
                    fw.op('pe', ['mtri', ln_], [An[b]], lambda: nc.tensor.matmul(Ab[b][:], mtri[:], L[:, hs], start=False, stop=True))
                    fw.op('pe', ['ones', ln_], [Tn_], lambda: nc.tensor.matmul(Tb_[:], ones[:], L[:, hs], start=True, stop=True))
                    cur, ctok = (carry, 'carry') if h == 0 else (carry2, 'carry2')
                    nxt, ntok = (carry2, 'carry2') if h == 0 else (carry, 'carry')
                    fw.op('dve', [Tn_, ctok], [ntok], lambda: nc.vector.tensor_tensor(nxt[:], cur[:], Tb_[:], ALU.add))
                    fw.op('dve', [An[b], ctok], [gn], lambda: nc.vector.tensor_tensor(G[:, hs], Ab[b][:], cur[:], ALU.subtract))
                Gs[p] = (G, gn)
                if p >= 2:
                    emit_O(p - 2)
                if p + 1 < npair:
                    stage1(p + 1)
                if p >= 1:
                    emit_W(p - 1)
                if p + 2 < npair:
                    emit_Z(p + 2)
            emit_W(npair - 1)
            if npair >= 2:
                emit_O(npair - 2)
            emit_O(npair - 1)
            fw.op('act', [On_], [f'out{qb}'], lambda: nc.scalar.copy(OUT[:, qs], Op[:]))
            store_out(d['oc'], qb)

    import os
    MIX = os.environ.get('MIX', 'ABC')
    if 'A' in MIX:
        load_qkv(d['qa'], None, d['ka'], None, d['va'])
        fw.dma('act', [], ['K2'], K2[0:64, :], d['ka'][0:64, :])
        fw.dma('act', [], ['K3'], K3[64:128, :], d['ka'][64:128, :])
        softmax_mixer('A')
    if 'B' in MIX:
        load_qkv(d['qbn'], d['qbr'], d['kbn'], d['kbr'], d['vb'])
        softmax_mixer('B')
    if 'C' in MIX:
        load_qkv(d['qc'], None, d['kc'], None, d['vc'])
        sb_mixer()


def t5_bucket_np(rel):
    nb = 16
    max_exact = 8
    n = np.abs(rel)
    large = max_exact + (np.log(np.maximum(n, 1).astype(np.float32) / np.float32(max_exact)) / np.float32(np.log(128 / 8)) * np.float32(nb - max_exact)).astype(np.int32)
    large = np.minimum(large, nb - 1)
    return np.where(rel > 0, nb, 0) + np.where(n < max_exact, n, large)


def attn_consts():
    kk = np.arange(128)[:, None]
    qq = np.arange(512)[None, :]
    offs = [-128, 0, 128, 256, 384]
    bucket = np.stack([t5_bucket_np((o + kk) - qq) for o in offs])
    maskB = np.stack([np.where(((o + kk) // 64) <= (qq // 64), 0.0, NEG) for o in offs[1:]]).astype(np.float32)
    maskC = np.stack([np.where((o + kk) < qq, 0.0, -30000.0) for o in offs[1:]]).astype(ml_dtypes.bfloat16)
    ident = np.eye(128, dtype=np.float32).astype(ml_dtypes.bfloat16)
    jj = np.arange(128)[:, None]
    kc = np.arange(128)[None, :]
    mtri = np.where(jj >= kc, -1.0, 0.0).astype(ml_dtypes.bfloat16)
    ones = np.ones((128, 128), ml_dtypes.bfloat16)
    return dict(bucket=bucket, maskB=maskB, maskC=maskC, mtri=mtri, ones=ones, ident=ident)


TT = 1024
O_AQ, O_AK, O_AV, O_CQ, O_CKV, O_KPE, O_SQ, O_SK, O_SV, O_G = 0, 1024, 2048, 3072, 3584, 3840, 3904, 4928, 5952, 6976


class Slabs:
    def __init__(self, nc, fw, n=3):
        self.nc, self.fw = nc, fw
        self.pool = Pool_(nc, "slab", n, [128, 16, 512], BF16)

    def load(self, w, r0, nrows, c0, ncols):
        t, tok = self.pool.get()
        nch = nrows // 128
        step = 4
        for c in range(0, nch, step):
            n = min(step, nch - c)
            self.fw.dma('pool', [], [tok], t[:, c:c + n, 0:ncols],
                        w[r0 + c * 128:r0 + (c + n) * 128, c0:c0 + ncols].rearrange("(c p) n -> p c n", p=128))
        return t, tok


def emit_norm_h(nc, fw, X, H, acol, bcol, ones, ps, psn, Tp, Pp, RS):
    for tb in range(TT // 512):
        ts = slice(tb * 512, tb * 512 + 512)
        for c in range(16):
            SQ, sqn = Pp.get()
            fw.op('act', ['X'], [sqn], lambda: nc.scalar.activation(SQ[:], X[:, c, ts], AF.Square))
            fw.op('pe', ['ones', sqn], [psn], lambda: nc.tensor.matmul(ps[:], ones[:], SQ[:], start=(c == 0), stop=(c == 15)))
        fw.op('act', [psn], ['RS'], lambda: nc.scalar.activation(RS[:, ts], ps[:], AF.Ln, bias=EPS, scale=1.0 / 2048))
        fw.op('act', ['RS'], ['RS'], lambda: nc.scalar.activation(RS[:, ts], RS[:, ts], AF.Exp, scale=-0.5))
        for c in range(16):
            T, tn = Tp.get()
            fw.op('dve', ['X', 'RS', 'cols'], [tn], lambda: nc.vector.scalar_tensor_tensor(T[:], X[:, c, ts], acol[:, c:c + 1], RS[:, ts], ALU.mult, ALU.mult))
            fw.op('dve', [tn, 'cols'], ['H'], lambda: nc.vector.tensor_scalar(H[:, c, ts], T[:], bcol[:, c:c + 1], None, ALU.add))


def emit_proj(nc, fw, d, ps):
    sb = nc.alloc_sbuf_tensor
    H = sb("H", [128, 16, TT], BF16)
    RS = sb("RS", [128, TT], F32)
    ones = sb("pones", [128, 128], BF16)
    bd64 = sb("pbd64", [128, 128], BF16)
    rotm = sb("protm", [64, 64], BF16)
    modc = sb("pmodc", [128, 6, 16], F32)
    gmix = sb("pgmix", [128, 16], F32)
    cols = sb("pcols", [128, 2, 16], F32)
    dqk = sb("pdqk", [128, 2], F32)
    gq = sb("pgq", [128, 4], F32)
    gkv = sb("pgkv", [128, 2], F32)
    mqk = sb("pmqk", [128, 4], F32)
    cosT = sb("pcos", [64, TT], F32)
    sinT = sb("psin", [64, TT], F32)
    Pp = Pool_(nc, "pP", 4, [128, 512], BF16)
    Tp = Pool_(nc, "pT", 4, [128, 512], F32)
    fw.dma('act', [], ['ones'], ones[:], d['ones'])
    fw.dma('act', [], ['modc'], modc[:], d['modc'])
    fw.dma('act', [], ['gmix'], gmix[:], d['gmix'])
    with nc.sbuf_tensor("X", [128, 16, TT], F32) as X:
        for c in range(0, 16, 4):
            fw.dma('sp', [], ['X'], X[:, c:c + 4, :], d['xT'][c * 128:(c + 4) * 128, :].rearrange("(c p) t -> p c t", p=128))
        fw.op('dve', ['modc', 'gmix'], ['cols'], lambda: nc.vector.scalar_tensor_tensor(cols[:, 0, :], modc[:, 1, :], 1.0, gmix[:], ALU.add, ALU.mult))
        fw.op('dve', ['modc'], ['cols'], lambda: nc.vector.tensor_copy(cols[:, 1, :], modc[:, 0, :]))
        emit_norm_h(nc, fw, X, H, cols[:, 0, :], cols[:, 1, :], ones, ps[4], 'ps4', Tp, Pp, RS)
    for c in range(0, 16, 4):
        fw.dma('sp', ['H'], [], d['hT'][c * 128:(c + 4) * 128, :].rearrange("(c p) t -> p c t", p=128), H[:, c:c + 4, :], semtok='hout')
    fw.barrier()
    WQ = sb("pwq", [128, 4, 1536], BF16)
    WKV = sb("pwkv", [128, 2, 2048], BF16)
    CQ = sb("pCQ", [128, 4, TT], F32)
    CKV = sb("pCKV", [128, 2, TT], F32)
    KPE = sb("pKPE", [64, TT], F32)
    CQN = sb("pCQN", [128, 4, TT], BF16)
    CKVN = sb("pCKVN", [128, 2, TT], BF16)
    Op = Pool_(nc, "pO", 4, [128, 512], BF16)
    slabs = Slabs(nc, fw)
    projps = [(ps[i], f'ps{i}') for i in range(4)]
    pi = [0]

    def nextps():
        p = projps[pi[0] % 4]
        pi[0] += 1
        return p

    nps = [(ps[4], 'ps4'), (ps[5], 'ps5')]
    ni = [0]

    def nextnps():
        p = nps[ni[0] % 2]
        ni[0] += 1
        return p

    fw.dma('act', [], ['bd64'], bd64[:], d['bd64'])
    fw.dma('act', [], ['rotm'], rotm[:], d['rotm'])
    fw.dma('act', [], ['dqk'], dqk[:], d['dqk'])
    fw.dma('act', [], ['gq'], gq[:], d['gq'])
    fw.dma('act', [], ['gkv'], gkv[:], d['gkv'])
    fw.dma('act', [], ['mqk'], mqk[:], d['mqk'])
    fw.dma('act', [], ['cos'], cosT[:], d['cosT'])
    fw.dma('act', [], ['sin'], sinT[:], d['sinT'])
    import os
    for c in range(4 if not os.environ.get('NOWQ') else 0):
        fw.dma('pool', [], ['WQ'], WQ[:, c, :], d['wqup'][c * 128:(c + 1) * 128, :], semtok='WQ')
    for c in range(2 if not os.environ.get('NOWQ') else 0):
        fw.dma('pool', [], ['WKV'], WKV[:, c, :], d['wkvup'][c * 128:(c + 1) * 128, :], semtok='WKV')
    w = d['w_in']

    def rstd_of(sq_list, lhsT, ltok, nfeat, P=128):
        pp, ppn = nextnps()
        for i, (sq, sqn) in enumerate(sq_list):
            fw.op('pe', [ltok, sqn], [ppn], lambda: nc.tensor.matmul(pp[0:P, :], lhsT, sq, start=(i == 0), stop=(i == len(sq_list) - 1)))
        R, rn = Tp.get()
        fw.op('act', [ppn], [rn], lambda: nc.scalar.activation(R[0:P, :], pp[0:P, :], AF.Ln, bias=EPS, scale=1.0 / nfeat))
        fw.op('act', [rn], [rn], lambda: nc.scalar.activation(R[0:P, :], R[0:P, :], AF.Exp, scale=-0.5))
        return R, rn

    def fm_chunk(slab, stok, nrc, col, ncols, tb, rhs_src, rtok):
        p, pn = nextps()
        ts = slice(tb * 512, tb * 512 + 512)
        for c in range(nrc):
            fw.op('pe', [stok, rtok], [pn], lambda: nc.tensor.matmul(p[0:ncols, :], slab[:, c, col:col + ncols], rhs_src[:, c, ts], start=(c == 0), stop=(c == nrc - 1)), inc=(c == nrc - 1))
        return p, pn

    def out_dma(dst, O, on):
        fw.dma('sp', [on], [], dst, O)

    def qknorm_out(p, pn, P, lhsT, ltok, nfeat, gcol, gtok, dst, scale_extra=1.0):
        SQ, sqn = Pp.get()
        fw.op('act', [pn], [sqn], lambda: nc.scalar.activation(SQ[0:P, :], p[0:P, :], AF.Square))
        R, rn = rstd_of([(SQ[0:P, :], sqn)], lhsT, ltok, nfeat, P)
        O, on = Op.get()
        fw.op('dve', [pn, rn, gtok], [on], lambda: nc.vector.scalar_tensor_tensor(O[0:P, :], p[0:P, :], gcol, R[0:P, :], ALU.mult, ALU.mult))
        out_dma(dst, O[0:P, :], on)

    def rope_out(src, stok, P, gcol, gtok, tb, dst):
        ts = slice(tb * 512, tb * 512 + 512)
        SQ, sqn = Pp.get()
        fw.op('act', [stok], [sqn], lambda: nc.scalar.activation(SQ[0:64, :], src, AF.Square))
        R, rn = rstd_of([(SQ[0:64, :], sqn)], ones[0:64, 0:64], 'ones', 64, 64)
        QN, qn = Tp.get()
        fw.op('dve', [stok, rn, gtok], [qn], lambda: nc.vector.scalar_tensor_tensor(QN[0:64, :], src, gcol, R[0:64, :], ALU.mult, ALU.mult))
        QB, qbn_ = Pp.get()
        fw.op('act', [qn], [qbn_], lambda: nc.scalar.copy(QB[0:64, :], QN[0:64, :]))
        rp, rpn = (ps[6], 'ps6') if tb == 0 else (ps[7], 'ps7')
        fw.op('pe', ['rotm', qbn_], [rpn], lambda: nc.tensor.matmul(rp[0:64, :], rotm[:], QB[0:64, :], start=True, stop=True))
        T2, t2n = Tp.get()
        fw.op('dve', [rpn, 'sin'], [t2n], lambda: nc.vector.tensor_tensor(T2[0:64, :], rp[0:64, :], sinT[:, ts], ALU.mult))
        fw.op('dve', [qn, 'cos'], [qn], lambda: nc.vector.tensor_tensor(QN[0:64, :], QN[0:64, :], cosT[:, ts], ALU.mult))
        O, on = Op.get()
        fw.op('dve', [qn, t2n], [on], lambda: nc.vector.tensor_tensor(O[0:64, :], QN[0:64, :], T2[0:64, :], ALU.add))
        out_dma(dst, O[0:64, :], on)

    pending = []

    def defer(fn):
        pending.append(fn)
        while len(pending) > 1:
            pending.pop(0)()

    def flush():
        while pending:
            pending.pop(0)()

    def fm_group(c0, ncols_total, handler):
        for s0 in range(0, ncols_total, 512):
            ncs = min(512, ncols_total - s0)
            slab, stok = slabs.load(w, 0, 2048, c0 + s0, ncs)
            for tb in range(2):
                for cc in range(0, ncs, 128):
                    nco = min(128, ncs - cc)
                    p, pn = fm_chunk(slab, stok, 16, cc, nco, tb, H, 'H')
                    defer(lambda ch=(s0 + cc) // 128, tb=tb, p=p, pn=pn, nco=nco: handler(ch, tb, p, pn, nco))
        flush()

    def tsl(tb):
        return slice(tb * 512, tb * 512 + 512)

    import os
    STAGE = int(os.environ.get('PSTAGE', '99'))
    if STAGE < 1:
        return
    fm_group(O_AQ, 1024, lambda ch, tb, p, pn, n: qknorm_out(p, pn, 128, bd64[:], 'bd64', 64, dqk[:, 0:1], 'dqk', d['qa'][ch, :, tsl(tb)]))
    fm_group(O_AK, 1024, lambda ch, tb, p, pn, n: qknorm_out(p, pn, 128, bd64[:], 'bd64', 64, dqk[:, 1:2], 'dqk', d['ka'][ch, :, tsl(tb)]))

    def cq_h(ch, tb, p, pn, n):
        O, on = Op.get()
        fw.op('act', [pn], [on], lambda: nc.scalar.activation(O[:], p[:], AF.Copy, scale=128 ** -0.5))
        out_dma(d['qc'][ch, :, tsl(tb)], O[:], on)

    def ck_h(ch, tb, p, pn, n):
        O, on = Op.get()
        fw.op('act', [pn], [on], lambda: nc.scalar.copy(O[:], p[:]))
        out_dma(d['kc'][ch, :, tsl(tb)], O[:], on)

    if STAGE < 2:
        return
    fm_group(O_SQ, 1024, cq_h)
    fm_group(O_SK, 1024, ck_h)

    def lat_cq(ch, tb, p, pn, n):
        fw.op('act', [pn], ['CQ'], lambda: nc.scalar.copy(CQ[:, ch, tsl(tb)], p[:]))

    def lat_ckv(ch, tb, p, pn, n):
        if ch < 2:
            fw.op('act', [pn], ['CKV'], lambda: nc.scalar.copy(CKV[:, ch, tsl(tb)], p[:]))
        else:
            fw.op('act', [pn], ['KPE'], lambda: nc.scalar.copy(KPE[:, tsl(tb)], p[0:64, :]))

    if STAGE < 3:
        return
    fm_group(O_CQ, 512, lat_cq)
    fm_group(O_CKV, 320, lat_ckv)

    if STAGE < 4:
        return
    for tb in range(2):
        ts = tsl(tb)
        for (RAW, rtok, NRM, ntok, nch, gg, gtok) in [(CQ, 'CQ', CQN, 'CQN', 4, gq, 'gq'), (CKV, 'CKV', CKVN, 'CKVN', 2, gkv, 'gkv')]:
            sql = []
            for c in range(nch):
                SQ, sqn = Pp.get()
                fw.op('act', [rtok], [sqn], lambda: nc.scalar.activation(SQ[:], RAW[:, c, ts], AF.Square))
                sql.append((SQ[:], sqn))
            R, rn = rstd_of(sql, ones[:], 'ones', nch * 128)
            for c in range(nch):
                fw.op('dve', [rtok, rn, gtok], [ntok], lambda: nc.vector.scalar_tensor_tensor(NRM[:, c, ts], RAW[:, c, ts], gg[:, c:c + 1], R[:], ALU.mult, ALU.mult))
        rope_out(KPE[:, ts], 'KPE', 64, mqk[0:64, 3:4], 'mqk', tb, d['kbr'][:, ts])

    if STAGE < 5:
        return
    for tb in range(2):
        for hh in range(8):
            p, pn = fm_chunk(WQ, 'WQ', 4, hh * 128, 128, tb, CQN, 'CQN')
            defer(lambda p=p, pn=pn, hh=hh, tb=tb: qknorm_out(p, pn, 128, ones[:], 'ones', 128, mqk[:, 0:1], 'mqk', d['qbn'][hh, :, tsl(tb)]))
        for hh in range(8):
            p, pn = fm_chunk(WQ, 'WQ', 4, 1024 + hh * 64, 64, tb, CQN, 'CQN')
            defer(lambda p=p, pn=pn, hh=hh, tb=tb: rope_out(p[0:64, :], pn, 64, mqk[0:64, 2:3], 'mqk', tb, d['qbr'][hh, :, tsl(tb)]))
        for hh in range(8):
            p, pn = fm_chunk(WKV, 'WKV', 2, hh * 128, 128, tb, CKVN, 'CKVN')
            defer(lambda p=p, pn=pn, hh=hh, tb=tb: qknorm_out(p, pn, 128, ones[:], 'ones', 128, mqk[:, 1:2], 'mqk', d['kbn'][hh, :, tsl(tb)]))
        flush()
    if STAGE < 6:
        return
    for tt in range(TT // 128):
        for nb in range(2):
            p, pn = nextps()
            for c in range(2):
                fw.op('pe', ['CKVN', 'WKV'], [pn], lambda: nc.tensor.matmul(p[:], CKVN[:, c, tt * 128:(tt + 1) * 128], WKV[:, c, 1024 + nb * 512:1024 + (nb + 1) * 512], start=(c == 0), stop=(c == 1)), inc=(c == 1))
            O, on = Op.get()
            fw.op('act', [pn], [on], lambda: nc.scalar.copy(O[:], p[:]))
            out_dma(d['vb'][tt * 128:(tt + 1) * 128, nb * 512:(nb + 1) * 512], O[:], on)

    if STAGE < 7:
        return
    for (c0, dst) in [(O_AV, d['va']), (O_SV, d['vc'])]:
        for s0 in range(0, 1024, 512):
            slab, stok = slabs.load(w, 0, 2048, c0 + s0, 512)
            for tt in range(TT // 128):
                p, pn = nextps()
                for c in range(16):
                    fw.op('pe', ['H', stok], [pn], lambda: nc.tensor.matmul(p[:], H[:, c, tt * 128:(tt + 1) * 128], slab[:, c, :], start=(c == 0), stop=(c == 15)), inc=(c == 15))
                O, on = Op.get()
                fw.op('act', [pn], [on], lambda: nc.scalar.copy(O[:], p[:]))
                out_dma(dst[tt * 128:(tt + 1) * 128, s0:s0 + 512], O[:], on)
    return ['dramout']


def proj_consts(core):
    half = 32
    inv = (10000.0 ** (-np.arange(half, dtype=np.float32) / half)).astype(np.float32)
    pos = np.arange(core * TT, (core + 1) * TT).astype(np.float32)
    ang = pos[None, :] * np.concatenate([inv, inv])[:, None]
    cosT = np.cos(ang).astype(np.float32)
    sinT = np.sin(ang).astype(np.float32)
    rot = np.zeros((64, 64), np.float32)
    for p in range(32):
        rot[p + 32, p] = -1.0
        rot[p, p + 32] = 1.0
    bd = np.zeros((128, 128), np.float32)
    bd[:64, :64] = 1
    bd[64:, 64:] = 1
    return dict(cosT=cosT, sinT=sinT, rotm=rot.astype(ml_dtypes.bfloat16), bd64=bd.astype(ml_dtypes.bfloat16), ones=np.ones((128, 128), ml_dtypes.bfloat16))


def _common(nc, fw, d, ps, Tp, Pp, H, RS, ones, modc, gvec, gtok, cols, sc_idx, sh_idx, X):
    fw.op('dve', ['modc', gtok], ['cols'], lambda: nc.vector.scalar_tensor_tensor(cols[:, 0, :], modc[:, sc_idx, :], 1.0, gvec[:], ALU.add, ALU.mult))
    fw.op('dve', ['modc'], ['cols'], lambda: nc.vector.tensor_copy(cols[:, 1, :], modc[:, sh_idx, :]))
    emit_norm_h(nc, fw, X, H, cols[:, 0, :], cols[:, 1, :], ones, ps[7], 'ps7', Tp, Pp, RS)


def tsl(tb):
    return slice(tb * 512, tb * 512 + 512)


def emit_merge(nc, fw, d, ps):
    sb = nc.alloc_sbuf_tensor
    H = sb("H", [128, 16, TT], BF16)
    RS = sb("RS", [128, TT], F32)
    MG = sb("MG", [128, 16, TT], BF16)
    MGF = [[sb(f"MGF{i}_{tb}", [128, 512], F32) for tb in range(2)] for i in range(4)]
    ones = sb("pones", [128, 128], BF16)
    modc = sb("pmodc", [128, 6, 16], F32)
    gmix = sb("pgmix", [128, 16], F32)
    cols = sb("pcols", [128, 2, 16], F32)
    Pp = Pool_(nc, "pP", 4, [128, 512], BF16)
    Tp = Pool_(nc, "pT", 6, [128, 512], F32)
    pslist = [(ps[i], f'ps{i}') for i in range(7)]
    pi = [0]

    def nextps():
        p = pslist[pi[0] % 7]
        pi[0] += 1
        return p

    for c in range(0, 16, 4):
        fw.dma('sp', [], ['H'], H[:, c:c + 4, :], d['hT'][c * 128:(c + 4) * 128, :].rearrange("(c p) t -> p c t", p=128))
    slabs = Slabs(nc, fw, n=3)
    BRp = Pool_(nc, "BR", 2, [128, 8, TT], BF16)
    w_in, wbr = d['w_in'], d['w_branch']
    for s in range(4):
        for i in range(3):
            gs, gtok = slabs.load(w_in, 0, 2048, i * 2048 + s * 512, 512)
            us, utok = slabs.load(wbr[i], 0, 1024, s * 512, 512)
            BR, brn = BRp.get()
            fw.dma('act', [], [brn], BR[:], d['br'][i].rearrange("c p t -> p c t"))
            for tb in range(2):
                ts = tsl(tb)
                for cc in range(4):
                    n = s * 4 + cc
                    mtok = 'MGf%d_%d' % (cc, tb)
                    gp, gpn = nextps()
                    for c in range(16):
                        fw.op('pe', [gtok, 'H'], [gpn], lambda: nc.tensor.matmul(gp[:], gs[:, c, cc * 128:(cc + 1) * 128], H[:, c, ts], start=(c == 0), stop=(c == 15)), inc=(c == 15))
                    up, upn = nextps()
                    for c in range(8):
                        fw.op('pe', [utok, brn], [upn], lambda: nc.tensor.matmul(up[:], us[:, c, cc * 128:(cc + 1) * 128], BR[:, c, ts], start=(c == 0), stop=(c == 7)), inc=(c == 7))
                    SG, sgn = Tp.get()
                    fw.op('act', [gpn], [sgn], lambda: nc.scalar.activation(SG[:], gp[:], AF.Sigmoid))
                    if i == 0:
                        fw.op('dve', [upn, sgn], [mtok], lambda: nc.vector.tensor_tensor(MGF[cc][tb][:], up[:], SG[:], ALU.mult))
                    elif i == 1:
                        fw.op('dve', [upn, sgn], [sgn], lambda: nc.vector.tensor_tensor(SG[:], up[:], SG[:], ALU.mult))
                        fw.op('dve', [sgn, mtok], [mtok], lambda: nc.vector.tensor_tensor(MGF[cc][tb][:], MGF[cc][tb][:], SG[:], ALU.add))
                    else:
                        fw.op('dve', [upn, sgn], [sgn], lambda: nc.vector.tensor_tensor(SG[:], up[:], SG[:], ALU.mult))
                        fw.op('dve', [sgn, mtok], ['MG'], lambda: nc.vector.tensor_tensor(MG[:, n, ts], MGF[cc][tb][:], SG[:], ALU.add))
    for c in range(0, 16, 4):
        fw.dma('sp', ['MG'], [], d['mg'][c * 128:(c + 4) * 128, :].rearrange("(c p) t -> p c t", p=128), MG[:, c:c + 4, :])


def emit_mlp(nc, fw, d, ps):
    sb = nc.alloc_sbuf_tensor
    X = sb("X", [128, 16, TT], F32)
    H = sb("H", [128, 16, TT], BF16)
    RS = sb("RS", [128, TT], F32)
    MG = sb("MG", [128, 16, TT], BF16)
    ones = sb("pones", [128, 128], BF16)
    modc = sb("pmodc", [128, 6, 16], F32)
    gmlp = sb("pgmlp", [128, 16], F32)
    cols = sb("pcols", [128, 2, 16], F32)
    Pp = Pool_(nc, "pP", 4, [128, 512], BF16)
    Tp = Pool_(nc, "pT", 6, [128, 512], F32)
    slabs = Slabs(nc, fw, n=3)
    pslist = [(ps[i], f'ps{i}') for i in range(7)]
    pi = [0]

    def nextps():
        p = pslist[pi[0] % 7]
        pi[0] += 1
        return p

    for c in range(0, 16, 4):
        fw.dma('sp', [], ['X'], X[:, c:c + 4, :], d['xT'][c * 128:(c + 4) * 128, :].rearrange("(c p) t -> p c t", p=128))
        fw.dma('act', [], ['MG'], MG[:, c:c + 4, :], d['mg'][c * 128:(c + 4) * 128, :].rearrange("(c p) t -> p c t", p=128))
    fw.dma('act', [], ['ones'], ones[:], d['ones'])
    fw.dma('act', [], ['modc'], modc[:], d['modc'])
    fw.dma('act', [], ['gmlp'], gmlp[:], d['gmlp'])
    wout, w1, w2 = d['w_out'], d['w_mlp_in'], d['w_mlp_out']

    def resid_matmul(W, r0, gcolidx):
        for s in range(4):
            sl, stk = slabs.load(W, r0, 2048, s * 512, 512)
            for tb in range(2):
                ts = tsl(tb)
                for cc in range(4):
                    n = s * 4 + cc
                    p, pn = nextps()
                    for c in range(16):
                        fw.op('pe', [stk, 'MG'], [pn], lambda: nc.tensor.matmul(p[:], sl[:, c, cc * 128:(cc + 1) * 128], MG[:, c, ts], start=(c == 0), stop=(c == 15)), inc=(c == 15))
                    fw.op('dve', [pn, 'X', 'modc'], ['X'], lambda: nc.vector.scalar_tensor_tensor(X[:, n, ts], p[:], modc[:, gcolidx, n:n + 1], X[:, n, ts], ALU.mult, ALU.add))

    resid_matmul(wout, 0, 2)
    _common(nc, fw, d, ps, Tp, Pp, H, RS, ones, modc, gmlp, 'gmlp', cols, 4, 3, X)
    for g in range(4):
        for s in range(4):
            sl, stk = slabs.load(w1, 0, 2048, g * 2048 + s * 512, 512)
            for tb in range(2):
                ts = tsl(tb)
                for cc in range(4):
                    f = s * 4 + cc
                    p, pn = nextps()
                    for c in range(16):
                        fw.op('pe', [stk, 'H'], [pn], lambda: nc.tensor.matmul(p[:], sl[:, c, cc * 128:(cc + 1) * 128], H[:, c, ts], start=(c == 0), stop=(c == 15)), inc=(c == 15))
                    R, rn = Tp.get()
                    fw.op('act', [pn], [rn], lambda: nc.scalar.activation(R[:], p[:], AF.Relu))
                    fw.op('dve', [rn], ['MG'], lambda: nc.vector.tensor_tensor(MG[:, f, ts], R[:], R[:], ALU.mult))
        resid_matmul(w2, g * 2048, 5)
    for c in range(0, 16, 4):
        fw.dma('sp', ['X'], [], d['xo'][c * 128:(c + 4) * 128, :].rearrange("(c p) t -> p c t", p=128), X[:, c:c + 4, :])


def emit_mod(nc, fw, d, ps):
    sb = nc.alloc_sbuf_tensor
    cT = sb("cT_sb", [128, 16], F32)
    fw.dma('sp', [], ['cT'], cT[:], d['cT'])
    bia = sb("bia_sb", [1, 4 * 1536], F32)
    fw.dma('sp', [], ['bia'], bia[:], d['b'])
    res = sb("res_sb", [1, 4 * 1536], F32)
    wp = Pool_(nc, "wm", 2, [128, 16, 512], F32)
    k = 0
    for l in range(4):
        for s in range(3):
            W, wn = wp.get()
            for c in range(0, 16, 4):
                fw.dma('sp' if (k % 2 == 0) else 'act', [], [wn], W[:, c:c + 4, :], d['w'][l, c * 128:(c + 4) * 128, s * 512:(s + 1) * 512].rearrange("(c p) n -> p c n", p=128))
            p, pn = ps[k % 2], f'ps{k % 2}'
            for c in range(16):
                fw.op('pe', [wn, 'cT'], [pn], lambda: nc.tensor.matmul(p[0:1, :], cT[:, c:c + 1], W[:, c, :], start=(c == 0), stop=(c == 15)), inc=(c == 15))
            o0 = l * 1536 + s * 512
            fw.op('dve', [pn, 'bia'], ['res'], lambda: nc.vector.tensor_tensor(res[:, o0:o0 + 512], p[0:1, :], bia[:, o0:o0 + 512], ALU.add))
            k += 1
    fw.dma('sp', ['res'], [], d['out'], res[:])


bf = ml_dtypes.bfloat16
NCORES = 8
S = 8192
DEBUG = bool(int(os.environ.get("MKDEBUG", "0")))
_progs = {}


def _mk(name, builder):
    if name not in _progs:
        _progs[name] = builder()
    return _progs[name]


def _din(nc, d, name, shape, dt):
    d[name] = nc.dram_tensor(name, list(shape), dt, kind="ExternalInput").ap()


def _dout(nc, d, name, shape, dt):
    d[name] = nc.dram_tensor(name, list(shape), dt, kind="ExternalOutput").ap()


def _psum(nc):
    return [nc.alloc_psum_tensor(f"ps{i}", [128, 512], F32) for i in range(8)]


def build_mod():
    nc = bass.Bass("TRN2", target_bir_lowering=False)
    fw = FW(nc)
    d = {}
    _din(nc, d, 'cT', (128, 16), F32)
    _din(nc, d, 'b', (1, 6144), F32)
    _din(nc, d, 'w', (4, 2048, 1536), F32)
    _dout(nc, d, 'out', (1, 6144), F32)
    emit_mod(nc, fw, d, _psum(nc))
    fw.finish_all()
    return nc


def build_proj():
    nc = bass.Bass("TRN2", target_bir_lowering=False)
    fw = FW(nc)
    d = {}
    _din(nc, d, 'xT', (2048, TT), F32)
    _din(nc, d, 'w_in', (2048, 6976), F32)
    for n, s, t in [('ones', (128, 128), BF16), ('bd64', (128, 128), BF16), ('rotm', (64, 64), BF16), ('modc', (128, 6, 16), F32),
                    ('gmix', (128, 16), F32), ('dqk', (128, 2), F32), ('gq', (128, 4), F32), ('gkv', (128, 2), F32), ('mqk', (128, 4), F32),
                    ('cosT', (64, TT), F32), ('sinT', (64, TT), F32), ('wqup', (512, 1536), F32), ('wkvup', (256, 2048), F32)]:
        _din(nc, d, n, s, t)
    for n in ['qa', 'ka', 'qbn', 'kbn', 'qc', 'kc']:
        _dout(nc, d, n, (8, 128, TT), BF16)
    _dout(nc, d, 'qbr', (8, 64, TT), BF16)
    _dout(nc, d, 'kbr', (64, TT), BF16)
    for n in ['va', 'vb', 'vc']:
        _dout(nc, d, n, (TT, 1024), BF16)
    _dout(nc, d, 'hT', (2048, TT), BF16)
    emit_proj(nc, fw, d, _psum(nc))
    fw.finish_all()
    return nc


def build_attn():
    nc = bass.Bass("TRN2", target_bir_lowering=False)
    fw = FW(nc)
    d = {}
    for n in ['qa', 'ka', 'qbn', 'kbn', 'qc', 'kc']:
        _din(nc, d, n, (128, S), BF16)
    for n in ['qbr', 'kbr']:
        _din(nc, d, n, (64, S), BF16)
    for n in ['va', 'vb', 'vc']:
        _din(nc, d, n, (S, 128), BF16)
    for n, s, t in [('ones', (128, 128), BF16), ('mtri', (128, 128), BF16), ('biasA', (5, 128, 512), F32), ('maskB', (4, 128, 512), F32),
                    ('maskC', (4, 128, 512), BF16), ('ident', (128, 128), BF16), ('lamp', (128, 256), F32), ('cst', (128, 4), F32), ('b15', (128, 1), F32), ('gsub', (128, 1), F32)]:
        _din(nc, d, n, s, t)
    for n in ['oa', 'ob', 'oc']:
        _dout(nc, d, n, (128, S), BF16)
    psz = [nc.alloc_psum_tensor(f"psz{i}", [128, 1024], F32) for i in range(2)]
    ps = [psz[0][:, 0:512], psz[0][:, 512:1024], psz[1][:, 0:512], psz[1][:, 512:1024]] + [nc.alloc_psum_tensor(f"ps{i}", [128, 512], F32) for i in range(4, 8)]
    emit_attn(nc, fw, S, d, ps, psz)
    fw.finish_all()
    return nc


def build_merge():
    nc = bass.Bass("TRN2", target_bir_lowering=False)
    fw = FW(nc)
    d = {}
    _din(nc, d, 'hT', (2048, TT), BF16)
    _din(nc, d, 'w_in', (2048, 6144), F32)
    _din(nc, d, 'w_branch', (3, 1024, 2048), F32)
    _din(nc, d, 'br', (3, 8, 128, TT), BF16)
    _dout(nc, d, 'mg', (2048, TT), BF16)
    emit_merge(nc, fw, d, _psum(nc))
    fw.finish_all()
    return nc


def build_mlp():
    nc = bass.Bass("TRN2", target_bir_lowering=False)
    fw = FW(nc)
    d = {}
    _din(nc, d, 'xT', (2048, TT), F32)
    _din(nc, d, 'mg', (2048, TT), BF16)
    _din(nc, d, 'w_out', (2048, 2048), F32)
    _din(nc, d, 'w_mlp_in', (2048, 8192), F32)
    _din(nc, d, 'w_mlp_out', (8192, 2048), F32)
    for n, s, t in [('ones', (128, 128), BF16), ('modc', (128, 6, 16), F32), ('gmlp', (128, 16), F32)]:
        _din(nc, d, n, s, t)
    _dout(nc, d, 'xo', (2048, TT), F32)
    emit_mlp(nc, fw, d, _psum(nc))
    fw.finish_all()
    return nc


def colvec(v, n):
    return np.ascontiguousarray(np.asarray(v, np.float32).reshape(n, 128).T)


def run(nc, in_maps):
    t = time.time()
    res = run_bass_kernel_spmd(nc, in_maps, core_ids=list(range(NCORES)))
    if DEBUG:
        print("  launch %.1fs" % (time.time() - t), flush=True)
    return res.results


def kernel(x, c, w_ada, b_ada, norm_mix_g, norm_mlp_g, w_in, diff_qk_g, diff_lambda, diff_subln_g, t5_bias,
           mla_q_norm_g, mla_kv_norm_g, w_q_up, w_kv_up, mla_qk_g, w_branch, w_out, w_mlp_in, w_mlp_out):
    f32 = np.float32
    x = np.asarray(x, f32)
    AC = attn_consts()
    PC = [proj_consts(i) for i in range(NCORES)]
    ones = AC['ones']
    nc_mod = _mk('mod', build_mod)
    cT = colvec(np.asarray(c, f32)[0], 16)
    w_ada = np.asarray(w_ada, f32)
    b_ada = np.asarray(b_ada, f32)
    ins = []
    for i in range(NCORES):
        ins.append(dict(cT=cT, b=np.ascontiguousarray(b_ada[:, 1536 * i:1536 * (i + 1)]).reshape(1, 6144),
                        w=np.ascontiguousarray(w_ada[:, :, 1536 * i:1536 * (i + 1)])))
    r = run(nc_mod, ins)
    mod = np.concatenate([r[i]['out'].reshape(4, 1536) for i in range(NCORES)], axis=1)
    if DEBUG:
        ref = np.asarray(c, f32) @ w_ada[0] + b_ada[0]
        print("mod err", np.abs(mod[0] - ref[0]).max(), np.abs(ref).max())

    xT = [np.ascontiguousarray(x[0, i * TT:(i + 1) * TT].T) for i in range(NCORES)]
    nc_proj = _mk('proj', build_proj)
    nc_attn = _mk('attn', build_attn)
    nc_merge = _mk('merge', build_merge)
    nc_mlp = _mk('mlp', build_mlp)
    for l in range(4):
        modc = np.ascontiguousarray(mod[l].reshape(6, 16, 128).transpose(2, 0, 1))
        gmix = colvec(norm_mix_g[l], 16)
        gmlp = colvec(norm_mlp_g[l], 16)
        wl = np.asarray(w_in[l], f32)
        w_qkv = np.ascontiguousarray(wl[:, :6976])
        w_gate = np.ascontiguousarray(wl[:, 6976:])
        dq = np.asarray(diff_qk_g[l], f32)
        dqk = np.stack([np.tile(dq[0], 2), np.tile(dq[1], 2)], axis=1)
        gq = colvec(mla_q_norm_g[l], 4)
        gkv = colvec(mla_kv_norm_g[l], 2)
        mg_ = np.asarray(mla_qk_g[l], f32)
        mqk = np.zeros((128, 4), f32)
        mqk[:, 0] = mg_[0, :128]
        mqk[:, 1] = mg_[1, :128]
        mqk[:64, 2] = mg_[0, 128:]
        mqk[:64, 3] = mg_[1, 128:]
        wq = np.asarray(w_q_up[l], f32).reshape(512, 8, 192)
        wqup = np.ascontiguousarray(np.concatenate([wq[:, :, :128].reshape(512, 1024), wq[:, :, 128:].reshape(512, 512)], axis=1))
        wkv = np.asarray(w_kv_up[l], f32).reshape(256, 8, 256)
        wkvup = np.ascontiguousarray(np.concatenate([wkv[:, :, :128].reshape(256, 1024), wkv[:, :, 128:].reshape(256, 1024)], axis=1))
        ins = []
        for i in range(NCORES):
            ins.append(dict(xT=xT[i], w_in=w_qkv, ones=ones, bd64=PC[i]['bd64'], rotm=PC[i]['rotm'], modc=modc, gmix=gmix, dqk=dqk, gq=gq, gkv=gkv,
                            mqk=mqk, cosT=PC[i]['cosT'], sinT=PC[i]['sinT'], wqup=wqup, wkvup=wkvup))
        rp = run(nc_proj, ins)
        if DEBUG and l == 0:
            debug_proj(x, mod, norm_mix_g, wl, dq, rp, mla_q_norm_g, mla_kv_norm_g, w_q_up, w_kv_up, mla_qk_g)
        lam_init = 0.8 - 0.6 * np.exp(-0.3 * l)
        cst = np.broadcast_to(np.array([lam_init, 1 - lam_init, 0, 0], f32), (128, 4)).copy()
        lamp = np.broadcast_to(np.asarray(diff_lambda[l], f32).reshape(1, 256), (128, 256)).copy()
        gsub = np.asarray(diff_subln_g[l], f32).reshape(128, 1).copy()
        t5 = np.asarray(t5_bias, f32)
        kbr = np.concatenate([rp[i]['kbr'] for i in range(NCORES)], axis=1)
        ins = []
        for h in range(NCORES):
            dd = dict(ones=ones, mtri=AC['mtri'], biasA=np.ascontiguousarray(t5[AC['bucket'], h]), maskB=AC['maskB'], maskC=AC['maskC'], ident=AC['ident'],
                      lamp=lamp, cst=cst, b15=np.full((128, 1), t5[15, h], f32), gsub=gsub, kbr=kbr)
            for n in ['qa', 'ka', 'qbn', 'kbn', 'qc', 'kc', 'qbr']:
                dd[n] = np.concatenate([rp[i][n][h] for i in range(NCORES)], axis=1)
            for n in ['va', 'vb', 'vc']:
                dd[n] = np.concatenate([rp[i][n][:, 128 * h:128 * (h + 1)] for i in range(NCORES)], axis=0)
            ins.append(dd)
        ra = run(nc_attn, ins)
        ins = []
        for i in range(NCORES):
            br = np.stack([np.stack([ra[h][n][:, i * TT:(i + 1) * TT] for h in range(NCORES)]) for n in ['oa', 'ob', 'oc']])
            ins.append(dict(hT=rp[i]['hT'], w_in=w_gate, w_branch=np.asarray(w_branch[l], f32), br=br))
        rm = run(nc_merge, ins)
        ins = []
        for i in range(NCORES):
            ins.append(dict(xT=xT[i], mg=rm[i]['mg'], w_out=np.asarray(w_out[l], f32), w_mlp_in=np.asarray(w_mlp_in[l], f32),
                            w_mlp_out=np.asarray(w_mlp_out[l], f32), ones=ones, modc=modc, gmlp=gmlp))
        rx = run(nc_mlp, ins)
        if DEBUG and l == 0:
            debug_post(x, mod, norm_mix_g, norm_mlp_g, wl, w_branch, w_out, w_mlp_in, w_mlp_out, ra, rm, rx)
        xT = [rx[i]['xo'] for i in range(NCORES)]
    out = np.concatenate([xT[i].T for i in range(NCORES)], axis=0)[None]
    return np.ascontiguousarray(out.astype(f32))


def _rms(x, g):
    return x / np.sqrt((x * x).mean(-1, keepdims=True) + 1e-6) * g


def _err(name, got, ref):
    got = np.asarray(got).astype(np.float64)
    ref = np.asarray(ref).astype(np.float64)
    print("  DBG %-6s rel %.2e  max %.3e / %.3e" % (name, np.sqrt(((got - ref) ** 2).mean() / ((ref ** 2).mean() + 1e-30)), np.abs(got - ref).max(), np.abs(ref).max()), flush=True)


def debug_proj(x, mod, norm_mix_g, wl, dq, rp, mla_q_norm_g, mla_kv_norm_g, w_q_up, w_kv_up, mla_qk_g):
    T0 = 128
    xs = x[0, :T0].astype(np.float64)
    sh1, sc1 = mod[0, :2048], mod[0, 2048:4096]
    h = _rms(xs, np.asarray(norm_mix_g[0], np.float64)) * (1 + sc1) + sh1
    pr = h @ wl.astype(np.float64)
    r = rp[0]
    f = lambda a: np.asarray(a).astype(np.float32)
    qa = _rms(pr[:, 0:1024].reshape(T0, 8, 2, 64), dq[0]).reshape(T0, 8, 128)
    _err('qa', f(r['qa'])[:, :, :T0].transpose(2, 0, 1), qa)
    ka = _rms(pr[:, 1024:2048].reshape(T0, 8, 2, 64), dq[1]).reshape(T0, 8, 128)
    _err('ka', f(r['ka'])[:, :, :T0].transpose(2, 0, 1), ka)
    _err('va', f(r['va'])[:T0], pr[:, 2048:3072])
    _err('qc', f(r['qc'])[:, :, :T0].transpose(2, 0, 1), pr[:, 3904:4928].reshape(T0, 8, 128) * 128 ** -0.5)
    _err('kc', f(r['kc'])[:, :, :T0].transpose(2, 0, 1), pr[:, 4928:5952].reshape(T0, 8, 128))
    _err('vc', f(r['vc'])[:T0], pr[:, 5952:6976])
    cq = _rms(pr[:, 3072:3584], np.asarray(mla_q_norm_g[0], np.float64))
    ckv = _rms(pr[:, 3584:3840], np.asarray(mla_kv_norm_g[0], np.float64))
    kpe = pr[:, 3840:3904]
    q = (cq @ np.asarray(w_q_up[0], np.float64)).reshape(T0, 8, 192)
    kv = (ckv @ np.asarray(w_kv_up[0], np.float64)).reshape(T0, 8, 256)
    g = np.asarray(mla_qk_g[0], np.float64)
    pos = np.arange(T0)
    inv = 10000.0 ** (-np.arange(32) / 32)
    ang = pos[:, None] * inv[None]
    cos, sin = np.cos(ang)[:, None], np.sin(ang)[:, None]

    def rope(v):
        x1, x2 = v[..., :32], v[..., 32:]
        return np.concatenate([x1 * cos - x2 * sin, x1 * sin + x2 * cos], -1)
    _err('qbn', f(r['qbn'])[:, :, :T0].transpose(2, 0, 1), _rms(q[..., :128], g[0, :128]))
    _err('qbr', f(r['qbr'])[:, :, :T0].transpose(2, 0, 1), rope(_rms(q[..., 128:], g[0, 128:])))
    _err('kbn', f(r['kbn'])[:, :, :T0].transpose(2, 0, 1), _rms(kv[..., :128], g[1, :128]))
    _err('kbr', f(r['kbr'])[:, :T0].T, rope(_rms(kpe.reshape(T0, 1, 64), g[1, 128:]))[:, 0])
    _err('vb', f(r['vb'])[:T0].reshape(T0, 8, 128), kv[..., 128:])


def debug_post(x, mod, norm_mix_g, norm_mlp_g, wl, w_branch, w_out, w_mlp_in, w_mlp_out, ra, rm, rx):
    T0 = 128
    f64 = np.float64
    xs = x[0, :T0].astype(f64)
    sh1, sc1, g1, sh2, sc2, g2 = [mod[0, i * 2048:(i + 1) * 2048].astype(f64) for i in range(6)]
    h = _rms(xs, np.asarray(norm_mix_g[0], f64)) * (1 + sc1) + sh1
    gates = h @ wl[:, 6976:].astype(f64)
    br = np.stack([np.concatenate([np.asarray(ra[hh][n]).astype(np.float32)[:, :T0].T for hh in range(8)], axis=1) for n in ['oa', 'ob', 'oc']]).astype(f64)
    merged = 0
    for i in range(3):
        up = br[i] @ np.asarray(w_branch[0][i], f64)
        merged = merged + 1 / (1 + np.exp(-gates[:, i * 2048:(i + 1) * 2048])) * up
    _err('mg', np.asarray(rm[0]['mg']).astype(np.float32)[:, :T0].T, merged)
    x1 = xs + g1 * (merged @ np.asarray(w_out[0], f64))
    h2 = _rms(x1, np.asarray(norm_mlp_g[0], f64)) * (1 + sc2) + sh2
    hid = np.maximum(h2 @ np.asarray(w_mlp_in[0], f64), 0) ** 2
    x2 = x1 + g2 * (hid @ np.asarray(w_mlp_out[0], f64))
    _err('xo', rx[0]['xo'][:, :T0].T, x2)
    _err('dx', rx[0]['xo'][:, :T0].T - xs, x2 - xs)
```

```python
import os, time
import numpy as np
import ml_dtypes
import concourse.bass as bass
import concourse.mybir as mybir
from concourse.bass_utils import run_bass_kernel_spmd

F32 = mybir.dt.float32
BF16 = mybir.dt.bfloat16
AF = mybir.ActivationFunctionType
ALU = mybir.AluOpType


class FW:
    def __init__(self, nc):
        self.nc = nc
        self.eng = {'pe': nc.tensor, 'act': nc.scalar, 'dve': nc.vector, 'pool': nc.gpsimd, 'sp': nc.sync}
        self.sem = {}
        self.cnt = {}
        for k in ['pe', 'act', 'dve', 'pool']:
            self.sem[k] = nc.alloc_semaphore(name='s_' + k)
            self.cnt[k] = 0
        self.seen = {k: {} for k in self.eng}
        self.lastw = {}
        self.reads = {}
        self.dsem = {}

    def _wait(self, e, ev):
        if ev is None:
            return
        k, v = ev
        if e == 'pe' and k == 'pe':
            return
        if self.seen[e].get(k, 0) >= v:
            return
        self.eng[e].wait_ge(self.sem[k], v)
        self.seen[e][k] = v

    def deps(self, e, R, W):
        for t in R:
            self._wait(e, self.lastw.get(t))
        for t in W:
            self._wait(e, self.lastw.get(t))
            for ev in self.reads.get(t, []):
                self._wait(e, ev)

    def commit(self, ev, R, W):
        for t in R:
            self.reads.setdefault(t, []).append(ev)
        for t in W:
            self.lastw[t] = ev
            self.reads[t] = []

    def op(self, e, R, W, fn, inc=True):
        self.deps(e, R, W)
        ins = fn()
        if inc:
            self.cnt[e] += 1
            ins.then_inc(self.sem[e], 1)
            ev = (e, self.cnt[e])
        else:
            ev = (e, self.cnt[e] + 1)
        self.commit(ev, R, W)
        return ins

    def dma(self, q, R, W, out, in_, semtok=None, **kw):
        st = semtok if semtok is not None else (W[0] if W else R[0])
        if st not in self.dsem:
            key = 'd_' + str(st)
            self.sem[key] = self.nc.alloc_semaphore(name=key)
            self.cnt[key] = 0
            self.dsem[st] = key
        key = self.dsem[st]
        self.deps(q, R, W)
        ins = self.eng[q].dma_start(out=out, in_=in_, **kw)
        self.cnt[key] += 16
        ins.then_inc(self.sem[key], 16)
        ev = (key, self.cnt[key])
        self.commit(ev, R, W)
        return ins

    def finish(self, toks, e='sp'):
        for t in toks:
            self._wait(e, self.lastw.get(t))

    def barrier(self):
        for e in self.eng:
            for k in self.sem:
                if self.cnt[k] > 0 and not (e == 'pe' and k == 'pe') and self.seen[e].get(k, 0) < self.cnt[k]:
                    self.eng[e].wait_ge(self.sem[k], self.cnt[k])
                    self.seen[e][k] = self.cnt[k]

    def finish_all(self, e='sp'):
        for k in self.sem:
            if k.startswith('d_') and self.cnt[k] > 0 and self.seen[e].get(k, 0) < self.cnt[k]:
                self.eng[e].wait_ge(self.sem[k], self.cnt[k])
                self.seen[e][k] = self.cnt[k]

    def finish_dma(self, semtok, e='sp'):
        if semtok not in self.dsem:
            return
        key = self.dsem[semtok]
        self.eng[e].wait_ge(self.sem[key], self.cnt[key])


AX = mybir.AxisListType
NEG = -1.0e30
EPS = 1e-6


class Pool_:
    def __init__(self, nc, name, n, shape, dt):
        self.tiles = [nc.alloc_sbuf_tensor(f"{name}{i}", shape, dt) for i in range(n)]
        self.names = [f"{name}{i}" for i in range(n)]
        self.i = 0

    def get(self):
        t, n = self.tiles[self.i], self.names[self.i]
        self.i = (self.i + 1) % len(self.tiles)
        return t, n


def emit_attn(nc, fw, S, d, ps, psz_in=None):
    NQB = S // 512
    NKT = S // 128
    sb = nc.alloc_sbuf_tensor
    Q1 = sb("aQ1", [128, S], BF16)
    Q2 = sb("aQ2", [128, S], BF16)
    K1 = sb("aK1", [128, S], BF16)
    K2 = sb("aK2", [128, S], BF16)
    K3 = sb("aK3", [128, S], BF16)
    V = sb("aV", [128, NKT, 128], BF16)
    OUT = sb("aOUT", [128, S], BF16)
    ones = sb("aones", [128, 128], BF16)
    mtri = sb("amtri", [128, 128], BF16)
    mones = sb("amones", [128, 128], BF16)
    biasA = sb("abiasA", [128, 5, 512], F32)
    maskB = sb("amaskB", [128, 4, 512], F32)
    maskC = sb("amaskC", [128, 4, 512], BF16)
    ident = sb("aident", [128, 128], BF16)
    lamp = sb("alamp", [128, 256], F32)
    cst = sb("acst", [128, 4], F32)
    b15 = sb("ab15", [128, 1], F32)
    gsub = sb("agsub", [128, 1], F32)
    sm = sb("asm", [128, 16], F32)
    carry = sb("acarry", [128, 512], F32)
    carry2 = sb("acarry2", [128, 512], F32)
    Pp = Pool_(nc, "aP", 6, [128, 512], BF16)
    Tp = Pool_(nc, "aT", 6, [128, 512], F32)
    Fp = Pool_(nc, "aF", 8, [128, 512], F32)
    E2p = Pool_(nc, "aE2", 5, [128, 1024], F32)
    L2p = Pool_(nc, "aL2", 3, [128, 1024], BF16)
    W2p = Pool_(nc, "aW2", 3, [128, 1024], BF16)
    psz = psz_in

    fw.op('pool', [], ['Q2z'], lambda: nc.gpsimd.memset(Q2[64:128, :], 0.0))
    fw.op('pool', [], ['K2z'], lambda: nc.gpsimd.memset(K2[64:128, :], 0.0))
    fw.op('pool', [], ['K3z'], lambda: nc.gpsimd.memset(K3[0:64, :], 0.0))
    fw.dma('sp', [], ['ones'], ones[:], d['ones'])
    fw.dma('sp', [], ['mtri'], mtri[:], d['mtri'])
    fw.dma('sp', [], ['biasA'], biasA[:], d['biasA'].rearrange("o p q -> p o q"))
    fw.dma('sp', [], ['maskB'], maskB[:], d['maskB'].rearrange("o p q -> p o q"))
    fw.dma('sp', [], ['maskC'], maskC[:], d['maskC'].rearrange("o p q -> p o q"))
    fw.dma('sp', [], ['ident'], ident[:], d['ident'])
    fw.dma('sp', [], ['lamp'], lamp[:], d['lamp'])
    fw.dma('sp', [], ['cst'], cst[:], d['cst'])
    fw.dma('sp', [], ['b15'], b15[:], d['b15'])
    fw.dma('sp', [], ['gsub'], gsub[:], d['gsub'])
    for o in range(4):
        fw.op('dve', ['biasA', 'maskB'], ['biasA'], lambda: nc.vector.tensor_tensor(biasA[:, o + 1, :], biasA[:, o + 1, :], maskB[:, o, :], ALU.add))
    T0, t0n = Tp.get()
    fw.op('dve', ['lamp'], [t0n], lambda: nc.vector.tensor_tensor(T0[:, 0:64], lamp[:, 0:64], lamp[:, 64:128], ALU.mult))
    fw.op('dve', [t0n], ['sm'], lambda: nc.vector.reduce_sum(sm[:, 0:1], T0[:, 0:64], AX.X))
    fw.op('dve', ['lamp'], [t0n], lambda: nc.vector.tensor_tensor(T0[:, 0:64], lamp[:, 128:192], lamp[:, 192:256], ALU.mult))
    fw.op('dve', [t0n], ['sm'], lambda: nc.vector.reduce_sum(sm[:, 1:2], T0[:, 0:64], AX.X))
    fw.op('act', ['sm'], ['sm'], lambda: nc.scalar.activation(sm[:, 2:4], sm[:, 0:2], AF.Exp))
    fw.op('dve', ['sm'], ['sm'], lambda: nc.vector.tensor_tensor(sm[:, 4:5], sm[:, 3:4], sm[:, 2:3], ALU.subtract))
    fw.op('dve', ['sm', 'cst'], ['sm'], lambda: nc.vector.tensor_tensor(sm[:, 4:5], sm[:, 4:5], cst[:, 0:1], ALU.subtract))
    fw.op('dve', ['gsub', 'cst'], ['sm'], lambda: nc.vector.tensor_tensor(sm[:, 5:6], gsub[:], cst[:, 1:2], ALU.mult))
    neglam = sm[:, 4:5]
    gcolA = sm[:, 5:6]

    def load_qkv(q1, q2, k1, k2, v):
        fw.dma('sp', [], ['Q1'], Q1[:], q1)
        fw.dma('act', [], ['K1'], K1[:], k1)
        if q2 is not None:
            fw.dma('sp', [], ['Q2'], Q2[0:64, :], q2)
            fw.dma('act', [], ['K2'], K2[0:64, :], k2)
        for c in range(0, NKT, 16):
            n = min(16, NKT - c)
            fw.dma('sp', [], ['V'], V[:, c:c + n, :], v[c * 128:(c + n) * 128, :].rearrange("(t p) e -> p t e", p=128))

    def store_out(dst, qb):
        fw.dma('sp', [f'out{qb}'], [], dst[:, qb * 512:(qb + 1) * 512], OUT[:, qb * 512:(qb + 1) * 512])

    def softmax_mixer(kind):
        nm = 2 if kind == 'A' else 1
        scale = 64 ** -0.5 if kind == 'A' else 192 ** -0.5
        if kind == 'A':
            Sb = [[ps[0], ps[1]], [ps[2], ps[3]]]
            Sn = [['ps0', 'ps1'], ['ps2', 'ps3']]
            NB, LA = 2, 1
        else:
            Sb = [[ps[0], ps[1], ps[2], ps[3]]]
            Sn = [['ps0', 'ps1', 'ps2', 'ps3']]
            NB, LA = 4, 2
        Ob = [ps[4], ps[5]]; On = ['ps4', 'ps5']
        Db = [ps[6], ps[7]]; Dn = ['ps6', 'ps7']

        deferred = []
        for qb in range(NQB):
            q0 = qb * 512
            nk = 4 * (qb + 1)
            qs = slice(q0, q0 + 512)

            def emit_S(kt):
                ks = slice(kt * 128, kt * 128 + 128)
                b = kt % NB
                for m in range(nm):
                    if kind == 'A':
                        KA, katoks = (K2, ['K2', 'K2z']) if m == 0 else (K3, ['K3', 'K3z'])
                        fw.op('pe', katoks + ['Q1'], [Sn[m][b]], lambda: nc.tensor.matmul(Sb[m][b][:], KA[:, ks], Q1[:, qs], start=True, stop=True))
                    else:
                        fw.op('pe', ['K1', 'Q1'], [Sn[m][b]], lambda: nc.tensor.matmul(Sb[m][b][:], K1[:, ks], Q1[:, qs], start=True, stop=False), inc=False)
                        fw.op('pe', ['K2', 'Q2', 'K2z', 'Q2z'], [Sn[m][b]], lambda: nc.tensor.matmul(Sb[m][b][:], K2[:, ks], Q2[:, qs], start=False, stop=True))

            for kk in range(min(LA, nk)):
                emit_S(kk)
            for kt in range(nk):
                b = kt % NB
                o = kt * 128 - q0
                Pt = []
                for m in range(nm):
                    P, pn = Pp.get()
                    near = (o >= -128) if kind == 'A' else (o >= 0)
                    if near:
                        oi = o // 128 + 1
                        T, tn = Tp.get()
                        btile = biasA[:, oi, :] if kind == 'A' else maskB[:, oi - 1, :]
                        btok = 'biasA' if kind == 'A' else 'maskB'
                        fw.op('dve', [Sn[m][b], btok], [tn], lambda: nc.vector.scalar_tensor_tensor(T[:], Sb[m][b][:], scale, btile, ALU.mult, ALU.add))
                        fw.op('act', [tn], [pn], lambda: nc.scalar.activation(P[:], T[:], AF.Exp))
                    else:
                        if kind == 'A':
                            fw.op('act', [Sn[m][b], 'b15'], [pn], lambda: nc.scalar.activation(P[:], Sb[m][b][:], AF.Exp, bias=b15[:], scale=scale))
                        else:
                            fw.op('act', [Sn[m][b]], [pn], lambda: nc.scalar.activation(P[:], Sb[m][b][:], AF.Exp, scale=scale))
                    Pt.append((P, pn))
                if kt == 1:
                    while deferred:
                        deferred.pop(0)()
                if kt + LA < nk:
                    emit_S(kt + LA)
                for m in range(nm):
                    P, pn = Pt[m]
                    fw.op('pe', ['V', pn], [On[m]], lambda: nc.tensor.matmul(Ob[m][:], V[:, kt, :], P[:], start=(kt == 0), stop=(kt == nk - 1)), inc=False)
                    fw.op('pe', ['ones', pn], [Dn[m]], lambda: nc.tensor.matmul(Db[m][:], ones[:], P[:], start=(kt == 0), stop=(kt == nk - 1)))
            if kind == 'B':
                R, rn = Tp.get()
                fw.op('dve', [Dn[0]], [rn], lambda: nc.vector.reciprocal(R[:], Db[0][:]))
                fw.op('dve', [On[0], rn], [f'out{qb}'], lambda: nc.vector.tensor_tensor(OUT[:, qs], Ob[0][:], R[:], ALU.mult))
                store_out(d['ob'], qb)
            else:
                ev = []
                for m in range(2):
                    R, rn = Fp.get()
                    fw.op('dve', [Dn[m]], [rn], lambda: nc.vector.reciprocal(R[:], Db[m][:]))
                    C, cn = Fp.get()
                    fw.op('act', [On[m]], [cn], lambda: nc.scalar.copy(C[:], Ob[m][:]))
                    ev.append((R, rn, C, cn))

                def tail(ev=ev, qb=qb, qs=qs):
                    (R1, r1n, C1, c1n), (R2, r2n, C2, c2n) = ev
                    fw.op('dve', [c1n, r1n], [c1n], lambda: nc.vector.tensor_tensor(C1[:], C1[:], R1[:], ALU.mult))
                    fw.op('dve', [c2n, r2n], [c2n], lambda: nc.vector.tensor_tensor(C2[:], C2[:], R2[:], ALU.mult))
                    fw.op('dve', [c1n, c2n, 'sm'], [c1n], lambda: nc.vector.scalar_tensor_tensor(C1[:], C2[:], neglam, C1[:], ALU.mult, ALU.add))
                    SQ, sqn = Pp.get()
                    fw.op('act', [c1n], [sqn], lambda: nc.scalar.activation(SQ[:], C1[:], AF.Square))
                    fw.op('pe', ['ones', sqn], ['ps0'], lambda: nc.tensor.matmul(ps[0][:], ones[:], SQ[:], start=True, stop=True))
                    fw.op('act', ['ps0'], [r2n], lambda: nc.scalar.activation(R2[:], ps[0][:], AF.Ln, bias=EPS, scale=1.0 / 128))
                    fw.op('act', [r2n], [r2n], lambda: nc.scalar.activation(R2[:], R2[:], AF.Exp, scale=-0.5))
                    fw.op('dve', [c1n, r2n, 'sm'], [f'out{qb}'], lambda: nc.vector.scalar_tensor_tensor(OUT[:, qs], C1[:], gcolA, R2[:], ALU.mult, ALU.mult))
                    store_out(d['oa'], qb)
                deferred.append(tail)
        while deferred:
            deferred.pop(0)()

    def sb_mixer():
        Zb = [psz[0], psz[1]]; Zn = ['psz0', 'psz1']
        Ab = [ps[4], ps[5]]; An = ['ps4', 'ps5']
        Tb_, Tn_ = ps[6], 'ps6'
        Op, On_ = ps[7], 'ps7'
        for qb in range(NQB):
            q0 = qb * 512
            qs = slice(q0, q0 + 512)
            kts = list(range(4 * qb + 3, -1, -1))
            nk = len(kts)
            npair = nk // 2
            Ls = {}
            Ws = {}

            def emit_Z(p):
                for h in range(2):
                    kt = kts[2 * p + h]
                    ks = slice(kt * 128, kt * 128 + 128)
                    o = kt * 128 - q0
                    if o >= 0:
                        fw.op('pe', ['K1', 'Q1'], [Zn[p % 2]], lambda: nc.tensor.matmul(Zb[p % 2][:, h * 512:(h + 1) * 512], K1[:, ks], Q1[:, qs], start=True, stop=False), inc=False)
                        fw.op('pe', ['ident', 'maskC'], [Zn[p % 2]], lambda: nc.tensor.matmul(Zb[p % 2][:, h * 512:(h + 1) * 512], ident[:], maskC[:, o // 128, :], start=False, stop=True), inc=(h == 1))
                    else:
                        fw.op('pe', ['K1', 'Q1'], [Zn[p % 2]], lambda: nc.tensor.matmul(Zb[p % 2][:, h * 512:(h + 1) * 512], K1[:, ks], Q1[:, qs], start=True, stop=True), inc=(h == 1))

            def stage1(p):
                b = p % 2
                E, en = E2p.get()
                fw.op('act', [Zn[b]], [en], lambda: nc.scalar.activation(E[:], Zb[b][:], AF.Exp))
                L, ln_ = L2p.get()
                fw.op('act', [en], [ln_], lambda: nc.scalar.activation(L[:], E[:], AF.Ln, bias=1.0))
                Ls[p] = (L, ln_)

            def emit_O(p):
                W, wn = Ws.pop(p)
                for h in range(2):
                    i = 2 * p + h
                    kt = kts[i]
                    fw.op('pe', ['V', wn], [On_], lambda: nc.tensor.matmul(Op[:], V[:, kt, :], W[:, h * 512:(h + 1) * 512], start=(i == 0), stop=(i == nk - 1)), inc=(h == 1))

            Gs = {}

            def emit_W(p):
                G, gn = Gs.pop(p)
                W, wn = W2p.get()
                fw.op('act', [gn], [wn], lambda: nc.scalar.activation(W[:], G[:], AF.Exp))
                Ws[p] = (W, wn)

            fw.op('dve', [], ['carry'], lambda: nc.vector.memset(carry[:], 0.0))
            emit_Z(0)
            if npair > 1:
                emit_Z(1)
            stage1(0)
            for p in range(npair):
                L, ln_ = Ls.pop(p)
                G, gn = E2p.get()
                for h in range(2):
                    i = 2 * p + h
                    kt = kts[i]
                    b = i % 2
                    ks = slice(kt * 128, kt * 128 + 128)
                    hs = slice(h * 512, (h + 1) * 512)
                    fw.op('pe', ['K1', 'Q1'], [An[b]], lambda: nc.tensor.matmul(Ab[b][:], K1[:, ks], Q1[:, qs], start=True, stop=False), inc=False)
                    od = kt * 128 - q0
                    if od >= 0:
                        fw.op('pe', ['ident', 'maskC'], [An[b]], lambda: nc.tensor.matmul(Ab[b][:], ident[:], maskC[:, od // 128, :], start=False, stop=False), inc=False)
                    fw.op('pe', ['mtri', ln_], [An[b]], lambda: nc.tensor.matmul(Ab[b][:], mtri[:], L[:, hs], start=False, stop=True))
                    fw.op('pe', ['ones', ln_], [Tn_], lambda: nc.tensor.matmul(Tb_[:], ones[:], L[:, hs], start=True, stop=True))
                    cur, ctok = (carry, 'carry') if h == 0 else (carry2, 'carry2')
                    nxt, ntok = (carry2, 'carry2') if h == 0 else (carry, 'carry')
                    fw.op('dve', [Tn_, ctok], [ntok], lambda: nc.vector.tensor_tensor(nxt[:], cur[:], Tb_[:], ALU.add))
                    fw.op('dve', [An[b], ctok], [gn], lambda: nc.vector.tensor_tensor(G[:, hs], Ab[b][:], cur[:], ALU.subtract))
                Gs[p] = (G, gn)
                if p >= 2:
                    emit_O(p - 2)
                if p + 1 < npair:
                    stage1(p + 1)
                if p >= 1:
                    emit_W(p - 1)
                if p + 2 < npair:
                    emit_Z(p + 2)
            emit_W(npair - 1)
            if npair >= 2:
                emit_O(npair - 2)
            emit_O(npair - 1)
            fw.op('act', [On_], [f'out{qb}'], lambda: nc.scalar.copy(OUT[:, qs], Op[:]))
            store_out(d['oc'], qb)

    import os
    MIX = os.environ.get('MIX', 'ABC')
    if 'A' in MIX:
        load_qkv(d['qa'], None, d['ka'], None, d['va'])
        fw.dma('act', [], ['K2'], K2[0:64, :], d['ka'][0:64, :])
        fw.dma('act', [], ['K3'], K3[64:128, :], d['ka'][64:128, :])
        softmax_mixer('A')
    if 'B' in MIX:
        load_qkv(d['qbn'], d['qbr'], d['kbn'], d['kbr'], d['vb'])
        softmax_mixer('B')
    if 'C' in MIX:
        load_qkv(d['qc'], None, d['kc'], None, d['vc'])
        sb_mixer()


def t5_bucket_np(rel):
    nb = 16
    max_exact = 8
    n = np.abs(rel)
    large = max_exact + (np.log(np.maximum(n, 1).astype(np.float32) / np.float32(max_exact)) / np.float32(np.log(128 / 8)) * np.float32(nb - max_exact)).astype(np.int32)
    large = np.minimum(large, nb - 1)
    return np.where(rel > 0, nb, 0) + np.where(n < max_exact, n, large)


def attn_consts():
    kk = np.arange(128)[:, None]
    qq = np.arange(512)[None, :]
    offs = [-128, 0, 128, 256, 384]
    bucket = np.stack([t5_bucket_np((o + kk) - qq) for o in offs])
    maskB = np.stack([np.where(((o + kk) // 64) <= (qq // 64), 0.0, NEG) for o in offs[1:]]).astype(np.float32)
    maskC = np.stack([np.where((o + kk) < qq, 0.0, -30000.0) for o in offs[1:]]).astype(ml_dtypes.bfloat16)
    ident = np.eye(128, dtype=np.float32).astype(ml_dtypes.bfloat16)
    jj = np.arange(128)[:, None]
    kc = np.arange(128)[None, :]
    mtri = np.where(jj >= kc, -1.0, 0.0).astype(ml_dtypes.bfloat16)
    ones = np.ones((128, 128), ml_dtypes.bfloat16)
    return dict(bucket=bucket, maskB=maskB, maskC=maskC, mtri=mtri, ones=ones, ident=ident)


TT = 1024
O_AQ, O_AK, O_AV, O_CQ, O_CKV, O_KPE, O_SQ, O_SK, O_SV, O_G = 0, 1024, 2048, 3072, 3584, 3840, 3904, 4928, 5952, 6976


class Slabs:
    def __init__(self, nc, fw, n=3):
        self.nc, self.fw = nc, fw
        self.pool = Pool_(nc, "slab", n, [128, 16, 512], BF16)

    def load(self, w, r0, nrows, c0, ncols):
        t, tok = self.pool.get()
        nch = nrows // 128
        step = 4
        for c in range(0, nch, step):
            n = min(step, nch - c)
            self.fw.dma('pool', [], [tok], t[:, c:c + n, 0:ncols],
                        w[r0 + c * 128:r0 + (c + n) * 128, c0:c0 + ncols].rearrange("(c p) n -> p c n", p=128))
        return t, tok


def emit_norm_h(nc, fw, X, H, acol, bcol, ones, ps, psn, Tp, Pp, RS):
    for tb in range(TT // 512):
        ts = slice(tb * 512, tb * 512 + 512)
        for c in range(16):
            SQ, sqn = Pp.get()
            fw.op('act', ['X'], [sqn], lambda: nc.scalar.activation(SQ[:], X[:, c, ts], AF.Square))
            fw.op('pe', ['ones', sqn], [psn], lambda: nc.tensor.matmul(ps[:], ones[:], SQ[:], start=(c == 0), stop=(c == 15)))
        fw.op('act', [psn], ['RS'], lambda: nc.scalar.activation(RS[:, ts], ps[:], AF.Ln, bias=EPS, scale=1.0 / 2048))
        fw.op('act', ['RS'], ['RS'], lambda: nc.scalar.activation(RS[:, ts], RS[:, ts], AF.Exp, scale=-0.5))
        for c in range(16):
            T, tn = Tp.get()
            fw.op('dve', ['X', 'RS', 'cols'], [tn], lambda: nc.vector.scalar_tensor_tensor(T[:], X[:, c, ts], acol[:, c:c + 1], RS[:, ts], ALU.mult, ALU.mult))
            fw.op('dve', [tn, 'cols'], ['H'], lambda: nc.vector.tensor_scalar(H[:, c, ts], T[:], bcol[:, c:c + 1], None, ALU.add))


def emit_proj(nc, fw, d, ps):
    sb = nc.alloc_sbuf_tensor
    H = sb("H", [128, 16, TT], BF16)
    RS = sb("RS", [128, TT], F32)
    ones = sb("pones", [128, 128], BF16)
    modc = sb("pmodc", [128, 6, 16], F32)
    gmix = sb("pgmix", [128, 16], F32)
    cols = sb("pcols", [128, 2, 16], F32)
    Pp = Pool_(nc, "pP", 4, [128, 512], BF16)
    Tp = Pool_(nc, "pT", 4, [128, 512], F32)
    fw.dma('act', [], ['ones'], ones[:], d['ones'])
    fw.dma('act', [], ['modc'], modc[:], d['modc'])
    fw.dma('act', [], ['gmix'], gmix[:], d['gmix'])
    with nc.sbuf_tensor("X", [128, 16, TT], F32) as X:
        for c in range(0, 16, 4):
            fw.dma('sp', [], ['X'], X[:, c:c + 4, :], d['xT'][c * 128:(c + 4) * 128, :].rearrange("(c p) t -> p c t", p=128))
        fw.op('dve', ['modc', 'gmix'], ['cols'], lambda: nc.vector.scalar_tensor_tensor(cols[:, 0, :], modc[:, 1, :], 1.0, gmix[:], ALU.add, ALU.mult))
        fw.op('dve', ['modc'], ['cols'], lambda: nc.vector.tensor_copy(cols[:, 1, :], modc[:, 0, :]))
        emit_norm_h(nc, fw, X, H, cols[:, 0, :], cols[:, 1, :], ones, ps[4], 'ps4', Tp, Pp, RS)
    for c in range(0, 16, 4):
        fw.dma('sp', ['H'], [], d['hT'][c * 128:(c + 4) * 128, :].rearrange("(c p) t -> p c t", p=128), H[:, c:c + 4, :], semtok='hout')
    fw.barrier()
    emit_proj_body(nc, fw, d, ps, H, ones, Pp, Tp, Slabs(nc, fw))


def emit_proj_body(nc, fw, d, ps, H, ones, Pp, Tp, slabs):
    sb = nc.alloc_sbuf_tensor
    bd64 = sb("pbd64", [128, 128], BF16)
    rotm = sb("protm", [64, 64], BF16)
    dqk = sb("pdqk", [128, 2], F32)
    gq = sb("pgq", [128, 4], F32)
    gkv = sb("pgkv", [128, 2], F32)
    mqk = sb("pmqk", [128, 4], F32)
    cosT = sb("pcos", [64, TT], F32)
    sinT = sb("psin", [64, TT], F32)
    WQ = sb("pwq", [128, 4, 1536], BF16)
    WKV = sb("pwkv", [128, 2, 2048], BF16)
    CQ = sb("pCQ", [128, 4, TT], F32)
    CKV = sb("pCKV", [128, 2, TT], F32)
    KPE = sb("pKPE", [64, TT], F32)
    CQN = sb("pCQN", [128, 4, TT], BF16)
    CKVN = sb("pCKVN", [128, 2, TT], BF16)
    Op = Pool_(nc, "pO", 4, [128, 512], BF16)
    projps = [(ps[i], f'ps{i}') for i in range(4)]
    pi = [0]

    def nextps():
        p = projps[pi[0] % 4]
        pi[0] += 1
        return p

    nps = [(ps[4], 'ps4'), (ps[5], 'ps5')]
    ni = [0]

    def nextnps():
        p = nps[ni[0] % 2]
        ni[0] += 1
        return p

    fw.dma('act', [], ['bd64'], bd64[:], d['bd64'])
    fw.dma('act', [], ['rotm'], rotm[:], d['rotm'])
    fw.dma('act', [], ['dqk'], dqk[:], d['dqk'])
    fw.dma('act', [], ['gq'], gq[:], d['gq'])
    fw.dma('act', [], ['gkv'], gkv[:], d['gkv'])
    fw.dma('act', [], ['mqk'], mqk[:], d['mqk'])
    fw.dma('act', [], ['cos'], cosT[:], d['cosT'])
    fw.dma('act', [], ['sin'], sinT[:], d['sinT'])
    import os
    for c in range(4 if not os.environ.get('NOWQ') else 0):
        fw.dma('pool', [], ['WQ'], WQ[:, c, :], d['wqup'][c * 128:(c + 1) * 128, :], semtok='WQ')
    for c in range(2 if not os.environ.get('NOWQ') else 0):
        fw.dma('pool', [], ['WKV'], WKV[:, c, :], d['wkvup'][c * 128:(c + 1) * 128, :], semtok='WKV')
    w = d['w_in']

    def rstd_of(sq_list, lhsT, ltok, nfeat, P=128):
        pp, ppn = nextnps()
        for i, (sq, sqn) in enumerate(sq_list):
            fw.op('pe', [ltok, sqn], [ppn], lambda: nc.tensor.matmul(pp[0:P, :], lhsT, sq, start=(i == 0), stop=(i == len(sq_list) - 1)))
        R, rn = Tp.get()
        fw.op('act', [ppn], [rn], lambda: nc.scalar.activation(R[0:P, :], pp[0:P, :], AF.Ln, bias=EPS, scale=1.0 / nfeat))
        fw.op('act', [rn], [rn], lambda: nc.scalar.activation(R[0:P, :], R[0:P, :], AF.Exp, scale=-0.5))
        return R, rn

    def fm_chunk(slab, stok, nrc, col, ncols, tb, rhs_src, rtok):
        p, pn = nextps()
        ts = slice(tb * 512, tb * 512 + 512)
        for c in range(nrc):
            fw.op('pe', [stok, rtok], [pn], lambda: nc.tensor.matmul(p[0:ncols, :], slab[:, c, col:col + ncols], rhs_src[:, c, ts], start=(c == 0), stop=(c == nrc - 1)), inc=(c == nrc - 1))
        return p, pn

    def out_dma(dst, O, on):
        fw.dma('sp', [on], [], dst, O)

    def qknorm_out(p, pn, P, lhsT, ltok, nfeat, gcol, gtok, dst, scale_extra=1.0):
        SQ, sqn = Pp.get()
        fw.op('act', [pn], [sqn], lambda: nc.scalar.activation(SQ[0:P, :], p[0:P, :], AF.Square))
        R, rn = rstd_of([(SQ[0:P, :], sqn)], lhsT, ltok, nfeat, P)
        O, on = Op.get()
        fw.op('dve', [pn, rn, gtok], [on], lambda: nc.vector.scalar_tensor_tensor(O[0:P, :], p[0:P, :], gcol, R[0:P, :], ALU.mult, ALU.mult))
        out_dma(dst, O[0:P, :], on)

    def rope_out(src, stok, P, gcol, gtok, tb, dst):
        ts = slice(tb * 512, tb * 512 + 512)
        SQ, sqn = Pp.get()
        fw.op('act', [stok], [sqn], lambda: nc.scalar.activation(SQ[0:64, :], src, AF.Square))
        R, rn = rstd_of([(SQ[0:64, :], sqn)], ones[0:64, 0:64], 'ones', 64, 64)
        QN, qn = Tp.get()
        fw.op('dve', [stok, rn, gtok], [qn], lambda: nc.vector.scalar_tensor_tensor(QN[0:64, :], src, gcol, R[0:64, :], ALU.mult, ALU.mult))
        QB, qbn_ = Pp.get()
        fw.op('act', [qn], [qbn_], lambda: nc.scalar.copy(QB[0:64, :], QN[0:64, :]))
        rp, rpn = (ps[6], 'ps6') if tb == 0 else (ps[7], 'ps7')
        fw.op('pe', ['rotm', qbn_], [rpn], lambda: nc.tensor.matmul(rp[0:64, :], rotm[:], QB[0:64, :], start=True, stop=True))
        T2, t2n = Tp.get()
        fw.op('dve', [rpn, 'sin'], [t2n], lambda: nc.vector.tensor_tensor(T2[0:64, :], rp[0:64, :], sinT[:, ts], ALU.mult))
        fw.op('dve', [qn, 'cos'], [qn], lambda: nc.vector.tensor_tensor(QN[0:64, :], QN[0:64, :], cosT[:, ts], ALU.mult))
        O, on = Op.get()
        fw.op('dve', [qn, t2n], [on], lambda: nc.vector.tensor_tensor(O[0:64, :], QN[0:64, :], T2[0:64, :], ALU.add))
        out_dma(dst, O[0:64, :], on)

    pending = []

    def defer(fn):
        pending.append(fn)
        while len(pending) > 1:
            pending.pop(0)()

    def flush():
        while pending:
            pending.pop(0)()

    def fm_group(c0, ncols_total, handler):
        for s0 in range(0, ncols_total, 512):
            ncs = min(512, ncols_total - s0)
            slab, stok = slabs.load(w, 0, 2048, c0 + s0, ncs)
            for tb in range(2):
                for cc in range(0, ncs, 128):
                    nco = min(128, ncs - cc)
                    p, pn = fm_chunk(slab, stok, 16, cc, nco, tb, H, 'H')
                    defer(lambda ch=(s0 + cc) // 128, tb=tb, p=p, pn=pn, nco=nco: handler(ch, tb, p, pn, nco))
        flush()

    def tsl(tb):
        return slice(tb * 512, tb * 512 + 512)

    import os
    STAGE = int(os.environ.get('PSTAGE', '99'))
    if STAGE < 1:
        return
    fm_group(O_AQ, 1024, lambda ch, tb, p, pn, n: qknorm_out(p, pn, 128, bd64[:], 'bd64', 64, dqk[:, 0:1], 'dqk', d['qa'][ch, :, tsl(tb)]))
    fm_group(O_AK, 1024, lambda ch, tb, p, pn, n: qknorm_out(p, pn, 128, bd64[:], 'bd64', 64, dqk[:, 1:2], 'dqk', d['ka'][ch, :, tsl(tb)]))

    def cq_h(ch, tb, p, pn, n):
        O, on = Op.get()
        fw.op('act', [pn], [on], lambda: nc.scalar.activation(O[:], p[:], AF.Copy, scale=128 ** -0.5))
        out_dma(d['qc'][ch, :, tsl(tb)], O[:], on)

    def ck_h(ch, tb, p, pn, n):
        O, on = Op.get()
        fw.op('act', [pn], [on], lambda: nc.scalar.copy(O[:], p[:]))
        out_dma(d['kc'][ch, :, tsl(tb)], O[:], on)

    if STAGE < 2:
        return
    fm_group(O_SQ, 1024, cq_h)
    fm_group(O_SK, 1024, ck_h)

    def lat_cq(ch, tb, p, pn, n):
        fw.op('act', [pn], ['CQ'], lambda: nc.scalar.copy(CQ[:, ch, tsl(tb)], p[:]))

    def lat_ckv(ch, tb, p, pn, n):
        if ch < 2:
            fw.op('act', [pn], ['CKV'], lambda: nc.scalar.copy(CKV[:, ch, tsl(tb)], p[:]))
        else:
            fw.op('act', [pn], ['KPE'], lambda: nc.scalar.copy(KPE[:, tsl(tb)], p[0:64, :]))

    if STAGE < 3:
        return
    fm_group(O_CQ, 512, lat_cq)
    fm_group(O_CKV, 320, lat_ckv)

    if STAGE < 4:
        return
    for tb in range(2):
        ts = tsl(tb)
        for (RAW, rtok, NRM, ntok, nch, gg, gtok) in [(CQ, 'CQ', CQN, 'CQN', 4, gq, 'gq'), (CKV, 'CKV', CKVN, 'CKVN', 2, gkv, 'gkv')]:
            sql = []
            for c in range(nch):
                SQ, sqn = Pp.get()
                fw.op('act', [rtok], [sqn], lambda: nc.scalar.activation(SQ[:], RAW[:, c, ts], AF.Square))
                sql.append((SQ[:], sqn))
            R, rn = rstd_of(sql, ones[:], 'ones', nch * 128)
            for c in range(nch):
                fw.op('dve', [rtok, rn, gtok], [ntok], lambda: nc.vector.scalar_tensor_tensor(NRM[:, c, ts], RAW[:, c, ts], gg[:, c:c + 1], R[:], ALU.mult, ALU.mult))
        rope_out(KPE[:, ts], 'KPE', 64, mqk[0:64, 3:4], 'mqk', tb, d['kbr'][:, ts])

    if STAGE < 5:
        return
    for tb in range(2):
        for hh in range(8):
            p, pn = fm_chunk(WQ, 'WQ', 4, hh * 128, 128, tb, CQN, 'CQN')
            defer(lambda p=p, pn=pn, hh=hh, tb=tb: qknorm_out(p, pn, 128, ones[:], 'ones', 128, mqk[:, 0:1], 'mqk', d['qbn'][hh, :, tsl(tb)]))
        for hh in range(8):
            p, pn = fm_chunk(WQ, 'WQ', 4, 1024 + hh * 64, 64, tb, CQN, 'CQN')
            defer(lambda p=p, pn=pn, hh=hh, tb=tb: rope_out(p[0:64, :], pn, 64, mqk[0:64, 2:3], 'mqk', tb, d['qbr'][hh, :, tsl(tb)]))
        for hh in range(8):
            p, pn = fm_chunk(WKV, 'WKV', 2, hh * 128, 128, tb, CKVN, 'CKVN')
            defer(lambda p=p, pn=pn, hh=hh, tb=tb: qknorm_out(p, pn, 128, ones[:], 'ones', 128, mqk[:, 1:2], 'mqk', d['kbn'][hh, :, tsl(tb)]))
        flush()
    if STAGE < 6:
        return
    for tt in range(TT // 128):
        for nb in range(2):
            p, pn = nextps()
            for c in range(2):
                fw.op('pe', ['CKVN', 'WKV'], [pn], lambda: nc.tensor.matmul(p[:], CKVN[:, c, tt * 128:(tt + 1) * 128], WKV[:, c, 1024 + nb * 512:1024 + (nb + 1) * 512], start=(c == 0), stop=(c == 1)), inc=(c == 1))
            O, on = Op.get()
            fw.op('act', [pn], [on], lambda: nc.scalar.copy(O[:], p[:]))
            out_dma(d['vb'][tt * 128:(tt + 1) * 128, nb * 512:(nb + 1) * 512], O[:], on)

    if STAGE < 7:
        return
    for (c0, dst) in [(O_AV, d['va']), (O_SV, d['vc'])]:
        for s0 in range(0, 1024, 512):
            slab, stok = slabs.load(w, 0, 2048, c0 + s0, 512)
            for tt in range(TT // 128):
                p, pn = nextps()
                for c in range(16):
                    fw.op('pe', ['H', stok], [pn], lambda: nc.tensor.matmul(p[:], H[:, c, tt * 128:(tt + 1) * 128], slab[:, c, :], start=(c == 0), stop=(c == 15)), inc=(c == 15))
                O, on = Op.get()
                fw.op('act', [pn], [on], lambda: nc.scalar.copy(O[:], p[:]))
                out_dma(dst[tt * 128:(tt + 1) * 128, s0:s0 + 512], O[:], on)
    return ['dramout']


def proj_consts(core):
    half = 32
    inv = (10000.0 ** (-np.arange(half, dtype=np.float32) / half)).astype(np.float32)
    pos = np.arange(core * TT, (core + 1) * TT).astype(np.float32)
    ang = pos[None, :] * np.concatenate([inv, inv])[:, None]
    cosT = np.cos(ang).astype(np.float32)
    sinT = np.sin(ang).astype(np.float32)
    rot = np.zeros((64, 64), np.float32)
    for p in range(32):
        rot[p + 32, p] = -1.0
        rot[p, p + 32] = 1.0
    bd = np.zeros((128, 128), np.float32)
    bd[:64, :64] = 1
    bd[64:, 64:] = 1
    return dict(cosT=cosT, sinT=sinT, rotm=rot.astype(ml_dtypes.bfloat16), bd64=bd.astype(ml_dtypes.bfloat16), ones=np.ones((128, 128), ml_dtypes.bfloat16))


def _common(nc, fw, d, ps, Tp, Pp, H, RS, ones, modc, gvec, gtok, cols, sc_idx, sh_idx, X, mtok='modc'):
    fw.op('dve', [mtok, gtok], ['cols'], lambda: nc.vector.scalar_tensor_tensor(cols[:, 0, :], modc[:, sc_idx, :], 1.0, gvec[:], ALU.add, ALU.mult))
    fw.op('dve', [mtok], ['cols'], lambda: nc.vector.tensor_copy(cols[:, 1, :], modc[:, sh_idx, :]))
    emit_norm_h(nc, fw, X, H, cols[:, 0, :], cols[:, 1, :], ones, ps[7], 'ps7', Tp, Pp, RS)


def tsl(tb):
    return slice(tb * 512, tb * 512 + 512)


def emit_merge(nc, fw, d, ps):
    sb = nc.alloc_sbuf_tensor
    H = sb("H", [128, 16, TT], BF16)
    RS = sb("RS", [128, TT], F32)
    MG = sb("MG", [128, 16, TT], BF16)
    MGF = [[sb(f"MGF{i}_{tb}", [128, 512], F32) for tb in range(2)] for i in range(4)]
    ones = sb("pones", [128, 128], BF16)
    modc = sb("pmodc", [128, 6, 16], F32)
    gmix = sb("pgmix", [128, 16], F32)
    cols = sb("pcols", [128, 2, 16], F32)
    Pp = Pool_(nc, "pP", 4, [128, 512], BF16)
    Tp = Pool_(nc, "pT", 6, [128, 512], F32)
    pslist = [(ps[i], f'ps{i}') for i in range(7)]
    pi = [0]

    def nextps():
        p = pslist[pi[0] % 7]
        pi[0] += 1
        return p

    for c in range(0, 16, 4):
        fw.dma('sp', [], ['H'], H[:, c:c + 4, :], d['hT'][c * 128:(c + 4) * 128, :].rearrange("(c p) t -> p c t", p=128))
    slabs = Slabs(nc, fw, n=3)
    BRp = Pool_(nc, "BR", 2, [128, 8, TT], BF16)
    w_in, wbr = d['w_in'], d['w_branch']
    for s in range(4):
        for i in range(3):
            gs, gtok = slabs.load(w_in, 0, 2048, i * 2048 + s * 512, 512)
            us, utok = slabs.load(wbr[i], 0, 1024, s * 512, 512)
            BR, brn = BRp.get()
            fw.dma('act', [], [brn], BR[:], d['br'][i].rearrange("c p t -> p c t"))
            for tb in range(2):
                ts = tsl(tb)
                for cc in range(4):
                    n = s * 4 + cc
                    mtok = 'MGf%d_%d' % (cc, tb)
                    gp, gpn = nextps()
                    for c in range(16):
                        fw.op('pe', [gtok, 'H'], [gpn], lambda: nc.tensor.matmul(gp[:], gs[:, c, cc * 128:(cc + 1) * 128], H[:, c, ts], start=(c == 0), stop=(c == 15)), inc=(c == 15))
                    up, upn = nextps()
                    for c in range(8):
                        fw.op('pe', [utok, brn], [upn], lambda: nc.tensor.matmul(up[:], us[:, c, cc * 128:(cc + 1) * 128], BR[:, c, ts], start=(c == 0), stop=(c == 7)), inc=(c == 7))
                    SG, sgn = Tp.get()
                    fw.op('act', [gpn], [sgn], lambda: nc.scalar.activation(SG[:], gp[:], AF.Sigmoid))
                    if i == 0:
                        fw.op('dve', [upn, sgn], [mtok], lambda: nc.vector.tensor_tensor(MGF[cc][tb][:], up[:], SG[:], ALU.mult))
                    elif i == 1:
                        fw.op('dve', [upn, sgn], [sgn], lambda: nc.vector.tensor_tensor(SG[:], up[:], SG[:], ALU.mult))
                        fw.op('dve', [sgn, mtok], [mtok], lambda: nc.vector.tensor_tensor(MGF[cc][tb][:], MGF[cc][tb][:], SG[:], ALU.add))
                    else:
                        fw.op('dve', [upn, sgn], [sgn], lambda: nc.vector.tensor_tensor(SG[:], up[:], SG[:], ALU.mult))
                        fw.op('dve', [sgn, mtok], ['MG'], lambda: nc.vector.tensor_tensor(MG[:, n, ts], MGF[cc][tb][:], SG[:], ALU.add))
    for c in range(0, 16, 4):
        fw.dma('sp', ['MG'], [], d['mg'][c * 128:(c + 4) * 128, :].rearrange("(c p) t -> p c t", p=128), MG[:, c:c + 4, :])


def emit_mlp(nc, fw, d, ps):
    sb = nc.alloc_sbuf_tensor
    X = sb("X", [128, 16, TT], F32)
    H = sb("H", [128, 16, TT], BF16)
    RS = sb("RS", [128, TT], F32)
    MG = sb("MG", [128, 16, TT], BF16)
    ones = sb("pones", [128, 128], BF16)
    modc = sb("pmodc", [128, 6, 16], F32)
    gmlp = sb("pgmlp", [128, 16], F32)
    cols = sb("pcols", [128, 2, 16], F32)
    Pp = Pool_(nc, "pP", 4, [128, 512], BF16)
    Tp = Pool_(nc, "pT", 6, [128, 512], F32)
    slabs = Slabs(nc, fw, n=3)
    pslist = [(ps[i], f'ps{i}') for i in range(7)]
    pi = [0]

    def nextps():
        p = pslist[pi[0] % 7]
        pi[0] += 1
        return p

    for c in range(0, 16, 4):
        fw.dma('sp', [], ['X'], X[:, c:c + 4, :], d['xT'][c * 128:(c + 4) * 128, :].rearrange("(c p) t -> p c t", p=128))
        fw.dma('act', [], ['MG'], MG[:, c:c + 4, :], d['mg'][c * 128:(c + 4) * 128, :].rearrange("(c p) t -> p c t", p=128))
    fw.dma('act', [], ['ones'], ones[:], d['ones'])
    fw.dma('act', [], ['modc'], modc[:], d['modc'])
    fw.dma('act', [], ['gmlp'], gmlp[:], d['gmlp'])
    wout, w1, w2 = d['w_out'], d['w_mlp_in'], d['w_mlp_out']

    def resid_matmul(W, r0, gcolidx):
        for s in range(4):
            sl, stk = slabs.load(W, r0, 2048, s * 512, 512)
            for tb in range(2):
                ts = tsl(tb)
                for cc in range(4):
                    n = s * 4 + cc
                    p, pn = nextps()
                    for c in range(16):
                        fw.op('pe', [stk, 'MG'], [pn], lambda: nc.tensor.matmul(p[:], sl[:, c, cc * 128:(cc + 1) * 128], MG[:, c, ts], start=(c == 0), stop=(c == 15)), inc=(c == 15))
                    fw.op('dve', [pn, 'X', 'modc'], ['X'], lambda: nc.vector.scalar_tensor_tensor(X[:, n, ts], p[:], modc[:, gcolidx, n:n + 1], X[:, n, ts], ALU.mult, ALU.add))

    resid_matmul(wout, 0, 2)
    _common(nc, fw, d, ps, Tp, Pp, H, RS, ones, modc, gmlp, 'gmlp', cols, 4, 3, X)
    for g in range(4):
        for s in range(4):
            sl, stk = slabs.load(w1, 0, 2048, g * 2048 + s * 512, 512)
            for tb in range(2):
                ts = tsl(tb)
                for cc in range(4):
                    f = s * 4 + cc
                    p, pn = nextps()
                    for c in range(16):
                        fw.op('pe', [stk, 'H'], [pn], lambda: nc.tensor.matmul(p[:], sl[:, c, cc * 128:(cc + 1) * 128], H[:, c, ts], start=(c == 0), stop=(c == 15)), inc=(c == 15))
                    R, rn = Tp.get()
                    fw.op('act', [pn], [rn], lambda: nc.scalar.activation(R[:], p[:], AF.Relu))
                    fw.op('dve', [rn], ['MG'], lambda: nc.vector.tensor_tensor(MG[:, f, ts], R[:], R[:], ALU.mult))
        resid_matmul(w2, g * 2048, 5)
    for c in range(0, 16, 4):
        fw.dma('sp', ['X'], [], d['xo'][c * 128:(c + 4) * 128, :].rearrange("(c p) t -> p c t", p=128), X[:, c:c + 4, :])


def emit_mod(nc, fw, d, ps):
    sb = nc.alloc_sbuf_tensor
    cT = sb("cT_sb", [128, 16], F32)
    fw.dma('sp', [], ['cT'], cT[:], d['cT'])
    bia = sb("bia_sb", [1, 4 * 1536], F32)
    fw.dma('sp', [], ['bia'], bia[:], d['b'])
    res = sb("res_sb", [1, 4 * 1536], F32)
    wp = Pool_(nc, "wm", 2, [128, 16, 512], F32)
    k = 0
    for l in range(4):
        for s in range(3):
            W, wn = wp.get()
            for c in range(0, 16, 4):
                fw.dma('sp' if (k % 2 == 0) else 'act', [], [wn], W[:, c:c + 4, :], d['w'][l, c * 128:(c + 4) * 128, s * 512:(s + 1) * 512].rearrange("(c p) n -> p c n", p=128))
            p, pn = ps[k % 2], f'ps{k % 2}'
            for c in range(16):
                fw.op('pe', [wn, 'cT'], [pn], lambda: nc.tensor.matmul(p[0:1, :], cT[:, c:c + 1], W[:, c, :], start=(c == 0), stop=(c == 15)), inc=(c == 15))
            o0 = l * 1536 + s * 512
            fw.op('dve', [pn, 'bia'], ['res'], lambda: nc.vector.tensor_tensor(res[:, o0:o0 + 512], p[0:1, :], bia[:, o0:o0 + 512], ALU.add))
            k += 1
    fw.dma('sp', ['res'], [], d['out'], res[:])


def emit_dense(nc, fw, d, ps, with_proj):
    from contextlib import ExitStack
    sb = nc.alloc_sbuf_tensor
    H = sb("H", [128, 16, TT], BF16)
    RS = sb("RS", [128, TT], F32)
    ones = sb("pones", [128, 128], BF16)
    modc = sb("pmodc", [128, 6, 16], F32)
    gmlp = sb("pgmlp", [128, 16], F32)
    cols = sb("pcols", [128, 2, 16], F32)
    Pp = Pool_(nc, "pP", 4, [128, 512], BF16)
    Tp = Pool_(nc, "pT", 6, [128, 512], F32)
    slabs = Slabs(nc, fw, n=3)
    if with_proj:
        modc2 = sb("pmodc2", [128, 6, 16], F32)
        gmix2 = sb("pgmix2", [128, 16], F32)
        fw.dma('act', [], ['modc2'], modc2[:], d['modc2'])
        fw.dma('act', [], ['gmix2'], gmix2[:], d['gmix2'])
    pslist = [(ps[i], f'ps{i}') for i in range(7)]
    pi = [0]

    def nextps():
        p = pslist[pi[0] % 7]
        pi[0] += 1
        return p

    fw.dma('act', [], ['ones'], ones[:], d['ones'])
    fw.dma('act', [], ['modc'], modc[:], d['modc'])
    fw.dma('act', [], ['gmlp'], gmlp[:], d['gmlp'])
    for c in range(0, 16, 4):
        fw.dma('sp', [], ['H'], H[:, c:c + 4, :], d['hT'][c * 128:(c + 4) * 128, :].rearrange("(c p) t -> p c t", p=128))
    w_gate, wbr = d['w_gate'], d['w_branch']
    wout, w1, w2 = d['w_out'], d['w_mlp_in'], d['w_mlp_out']
    with nc.sbuf_tensor("MG", [128, 16, TT], BF16) as MG:
        with ExitStack() as st:
            MGF = [[st.enter_context(nc.sbuf_tensor(f"MGF{i}_{tb}", [128, 512], F32)) for tb in range(2)] for i in range(4)]
            BRt = [st.enter_context(nc.sbuf_tensor(f"BR{i}", [128, 8, TT], BF16)) for i in range(2)]
            bri = 0
            for s in range(4):
                for i in range(3):
                    gs, gtok = slabs.load(w_gate, 0, 2048, i * 2048 + s * 512, 512)
                    us, utok = slabs.load(wbr[i], 0, 1024, s * 512, 512)
                    BR, brn = BRt[bri % 2], f"BR{bri % 2}"
                    bri += 1
                    fw.dma('act', [], [brn], BR[:], d['br'][i].rearrange("c p t -> p c t"))
                    for tb in range(2):
                        ts = tsl(tb)
                        for cc in range(4):
                            n = s * 4 + cc
                            mtok = 'MGf%d_%d' % (cc, tb)
                            gp, gpn = nextps()
                            for c in range(16):
                                fw.op('pe', [gtok, 'H'], [gpn], lambda: nc.tensor.matmul(gp[:], gs[:, c, cc * 128:(cc + 1) * 128], H[:, c, ts], start=(c == 0), stop=(c == 15)), inc=(c == 15))
                            up, upn = nextps()
                            for c in range(8):
                                fw.op('pe', [utok, brn], [upn], lambda: nc.tensor.matmul(up[:], us[:, c, cc * 128:(cc + 1) * 128], BR[:, c, ts], start=(c == 0), stop=(c == 7)), inc=(c == 7))
                            SG, sgn = Tp.get()
                            fw.op('act', [gpn], [sgn], lambda: nc.scalar.activation(SG[:], gp[:], AF.Sigmoid))
                            if i == 0:
                                fw.op('dve', [upn, sgn], [mtok], lambda: nc.vector.tensor_tensor(MGF[cc][tb][:], up[:], SG[:], ALU.mult))
                            elif i == 1:
                                fw.op('dve', [upn, sgn], [sgn], lambda: nc.vector.tensor_tensor(SG[:], up[:], SG[:], ALU.mult))
                                fw.op('dve', [sgn, mtok], [mtok], lambda: nc.vector.tensor_tensor(MGF[cc][tb][:], MGF[cc][tb][:], SG[:], ALU.add))
                            else:
                                fw.op('dve', [upn, sgn], [sgn], lambda: nc.vector.tensor_tensor(SG[:], up[:], SG[:], ALU.mult))
                                fw.op('dve', [sgn, mtok], ['MG'], lambda: nc.vector.tensor_tensor(MG[:, n, ts], MGF[cc][tb][:], SG[:], ALU.add))
        fw.barrier()
        with nc.sbuf_tensor("X", [128, 16, TT], F32) as X:
            for c in range(0, 16, 4):
                fw.dma('sp', [], ['X'], X[:, c:c + 4, :], d['xT'][c * 128:(c + 4) * 128, :].rearrange("(c p) t -> p c t", p=128))

            def resid_matmul(W, r0, gcolidx):
                for s in range(4):
                    sl, stk = slabs.load(W, r0, 2048, s * 512, 512)
                    for tb in range(2):
                        ts = tsl(tb)
                        for cc in range(4):
                            n = s * 4 + cc
                            p, pn = nextps()
                            for c in range(16):
                                fw.op('pe', [stk, 'MG'], [pn], lambda: nc.tensor.matmul(p[:], sl[:, c, cc * 128:(cc + 1) * 128], MG[:, c, ts], start=(c == 0), stop=(c == 15)), inc=(c == 15))
                            fw.op('dve', [pn, 'X', 'modc'], ['X'], lambda: nc.vector.scalar_tensor_tensor(X[:, n, ts], p[:], modc[:, gcolidx, n:n + 1], X[:, n, ts], ALU.mult, ALU.add))

            resid_matmul(wout, 0, 2)
            _common(nc, fw, d, ps, Tp, Pp, H, RS, ones, modc, gmlp, 'gmlp', cols, 4, 3, X)
            for g in range(4):
                for s in range(4):
                    sl, stk = slabs.load(w1, 0, 2048, g * 2048 + s * 512, 512)
                    for tb in range(2):
                        ts = tsl(tb)
                        for cc in range(4):
                            f = s * 4 + cc
                            p, pn = nextps()
                            for c in range(16):
                                fw.op('pe', [stk, 'H'], [pn], lambda: nc.tensor.matmul(p[:], sl[:, c, cc * 128:(cc + 1) * 128], H[:, c, ts], start=(c == 0), stop=(c == 15)), inc=(c == 15))
                            R, rn = Tp.get()
                            fw.op('act', [pn], [rn], lambda: nc.scalar.activation(R[:], p[:], AF.Relu))
                            fw.op('dve', [rn], ['MG'], lambda: nc.vector.tensor_tensor(MG[:, f, ts], R[:], R[:], ALU.mult))
                resid_matmul(w2, g * 2048, 5)
            for c in range(0, 16, 4):
                fw.dma('sp', ['X'], [], d['xo'][c * 128:(c + 4) * 128, :].rearrange("(c p) t -> p c t", p=128), X[:, c:c + 4, :], semtok='xout')
            if with_proj:
                _common(nc, fw, d, ps, Tp, Pp, H, RS, ones, modc2, gmix2, 'gmix2', cols, 1, 0, X, mtok='modc2')
                for c in range(0, 16, 4):
                    fw.dma('sp', ['H'], [], d['hTo'][c * 128:(c + 4) * 128, :].rearrange("(c p) t -> p c t", p=128), H[:, c:c + 4, :], semtok='hout')
        fw.barrier()
    if with_proj:
        emit_proj_body(nc, fw, d, ps, H, ones, Pp, Tp, slabs)


bf = ml_dtypes.bfloat16
NCORES = 8
S = 8192
DEBUG = bool(int(os.environ.get("MKDEBUG", "0")))
_progs = {}


def _mk(name, builder):
    if name not in _progs:
        _progs[name] = builder()
    return _progs[name]


def _din(nc, d, name, shape, dt):
    d[name] = nc.dram_tensor(name, list(shape), dt, kind="ExternalInput").ap()


def _dout(nc, d, name, shape, dt):
    d[name] = nc.dram_tensor(name, list(shape), dt, kind="ExternalOutput").ap()


def _psum(nc):
    return [nc.alloc_psum_tensor(f"ps{i}", [128, 512], F32) for i in range(8)]


def build_mod():
    nc = bass.Bass("TRN2", target_bir_lowering=False)
    fw = FW(nc)
    d = {}
    _din(nc, d, 'cT', (128, 16), F32)
    _din(nc, d, 'b', (1, 6144), F32)
    _din(nc, d, 'w', (4, 2048, 1536), F32)
    _dout(nc, d, 'out', (1, 6144), F32)
    emit_mod(nc, fw, d, _psum(nc))
    fw.finish_all()
    return nc


def build_proj():
    nc = bass.Bass("TRN2", target_bir_lowering=False)
    fw = FW(nc)
    d = {}
    _din(nc, d, 'xT', (2048, TT), F32)
    _din(nc, d, 'w_in', (2048, 6976), F32)
    for n, s, t in [('ones', (128, 128), BF16), ('bd64', (128, 128), BF16), ('rotm', (64, 64), BF16), ('modc', (128, 6, 16), F32),
                    ('gmix', (128, 16), F32), ('dqk', (128, 2), F32), ('gq', (128, 4), F32), ('gkv', (128, 2), F32), ('mqk', (128, 4), F32),
                    ('cosT', (64, TT), F32), ('sinT', (64, TT), F32), ('wqup', (512, 1536), F32), ('wkvup', (256, 2048), F32)]:
        _din(nc, d, n, s, t)
    for n in ['qa', 'ka', 'qbn', 'kbn', 'qc', 'kc']:
        _dout(nc, d, n, (8, 128, TT), BF16)
    _dout(nc, d, 'qbr', (8, 64, TT), BF16)
    _dout(nc, d, 'kbr', (64, TT), BF16)
    for n in ['va', 'vb', 'vc']:
        _dout(nc, d, n, (TT, 1024), BF16)
    _dout(nc, d, 'hT', (2048, TT), BF16)
    emit_proj(nc, fw, d, _psum(nc))
    fw.finish_all()
    return nc


def build_attn():
    nc = bass.Bass("TRN2", target_bir_lowering=False)
    fw = FW(nc)
    d = {}
    for n in ['qa', 'ka', 'qbn', 'kbn', 'qc', 'kc']:
        _din(nc, d, n, (128, S), BF16)
    for n in ['qbr', 'kbr']:
        _din(nc, d, n, (64, S), BF16)
    for n in ['va', 'vb', 'vc']:
        _din(nc, d, n, (S, 128), BF16)
    for n, s, t in [('ones', (128, 128), BF16), ('mtri', (128, 128), BF16), ('biasA', (5, 128, 512), F32), ('maskB', (4, 128, 512), F32),
                    ('maskC', (4, 128, 512), BF16), ('ident', (128, 128), BF16), ('lamp', (128, 256), F32), ('cst', (128, 4), F32), ('b15', (128, 1), F32), ('gsub', (128, 1), F32)]:
        _din(nc, d, n, s, t)
    for n in ['oa', 'ob', 'oc']:
        _dout(nc, d, n, (128, S), BF16)
    psz = [nc.alloc_psum_tensor(f"psz{i}", [128, 1024], F32) for i in range(2)]
    ps = [psz[0][:, 0:512], psz[0][:, 512:1024], psz[1][:, 0:512], psz[1][:, 512:1024]] + [nc.alloc_psum_tensor(f"ps{i}", [128, 512], F32) for i in range(4, 8)]
    emit_attn(nc, fw, S, d, ps, psz)
    fw.finish_all()
    return nc


def build_merge():
    nc = bass.Bass("TRN2", target_bir_lowering=False)
    fw = FW(nc)
    d = {}
    _din(nc, d, 'hT', (2048, TT), BF16)
    _din(nc, d, 'w_in', (2048, 6144), F32)
    _din(nc, d, 'w_branch', (3, 1024, 2048), F32)
    _din(nc, d, 'br', (3, 8, 128, TT), BF16)
    _dout(nc, d, 'mg', (2048, TT), BF16)
    emit_merge(nc, fw, d, _psum(nc))
    fw.finish_all()
    return nc


def build_mlp():
    nc = bass.Bass("TRN2", target_bir_lowering=False)
    fw = FW(nc)
    d = {}
    _din(nc, d, 'xT', (2048, TT), F32)
    _din(nc, d, 'mg', (2048, TT), BF16)
    _din(nc, d, 'w_out', (2048, 2048), F32)
    _din(nc, d, 'w_mlp_in', (2048, 8192), F32)
    _din(nc, d, 'w_mlp_out', (8192, 2048), F32)
    for n, s, t in [('ones', (128, 128), BF16), ('modc', (128, 6, 16), F32), ('gmlp', (128, 16), F32)]:
        _din(nc, d, n, s, t)
    _dout(nc, d, 'xo', (2048, TT), F32)
    emit_mlp(nc, fw, d, _psum(nc))
    fw.finish_all()
    return nc


def build_dense(with_proj):
    nc = bass.Bass("TRN2", target_bir_lowering=False)
    fw = FW(nc)
    d = {}
    _din(nc, d, 'hT', (2048, TT), BF16)
    _din(nc, d, 'xT', (2048, TT), F32)
    _din(nc, d, 'w_gate', (2048, 6144), F32)
    _din(nc, d, 'w_branch', (3, 1024, 2048), F32)
    _din(nc, d, 'br', (3, 8, 128, TT), BF16)
    _din(nc, d, 'w_out', (2048, 2048), F32)
    _din(nc, d, 'w_mlp_in', (2048, 8192), F32)
    _din(nc, d, 'w_mlp_out', (8192, 2048), F32)
    for n, s, t in [('ones', (128, 128), BF16), ('modc', (128, 6, 16), F32), ('gmlp', (128, 16), F32)]:
        _din(nc, d, n, s, t)
    _dout(nc, d, 'xo', (2048, TT), F32)
    if with_proj:
        _din(nc, d, 'w_in', (2048, 6976), F32)
        for n, s, t in [('modc2', (128, 6, 16), F32), ('gmix2', (128, 16), F32), ('bd64', (128, 128), BF16), ('rotm', (64, 64), BF16),
                        ('dqk', (128, 2), F32), ('gq', (128, 4), F32), ('gkv', (128, 2), F32), ('mqk', (128, 4), F32),
                        ('cosT', (64, TT), F32), ('sinT', (64, TT), F32), ('wqup', (512, 1536), F32), ('wkvup', (256, 2048), F32)]:
            _din(nc, d, n, s, t)
        for n in ['qa', 'ka', 'qbn', 'kbn', 'qc', 'kc']:
            _dout(nc, d, n, (8, 128, TT), BF16)
        _dout(nc, d, 'qbr', (8, 64, TT), BF16)
        _dout(nc, d, 'kbr', (64, TT), BF16)
        for n in ['va', 'vb', 'vc']:
            _dout(nc, d, n, (TT, 1024), BF16)
        _dout(nc, d, 'hTo', (2048, TT), BF16)
    emit_dense(nc, fw, d, _psum(nc), with_proj)
    fw.finish_all()
    return nc


def colvec(v, n):
    return np.ascontiguousarray(np.asarray(v, np.float32).reshape(n, 128).T)


def run(nc, in_maps):
    t = time.time()
    res = run_bass_kernel_spmd(nc, in_maps, core_ids=list(range(NCORES)))
    if DEBUG:
        print("  launch %.1fs" % (time.time() - t), flush=True)
    return res.results


def kernel(x, c, w_ada, b_ada, norm_mix_g, norm_mlp_g, w_in, diff_qk_g, diff_lambda, diff_subln_g, t5_bias,
           mla_q_norm_g, mla_kv_norm_g, w_q_up, w_kv_up, mla_qk_g, w_branch, w_out, w_mlp_in, w_mlp_out):
    f32 = np.float32
    x = np.asarray(x, f32)
    AC = attn_consts()
    PC = [proj_consts(i) for i in range(NCORES)]
    ones = AC['ones']
    nc_mod = _mk('mod', build_mod)
    cT = colvec(np.asarray(c, f32)[0], 16)
    w_ada = np.asarray(w_ada, f32)
    b_ada = np.asarray(b_ada, f32)
    ins = []
    for i in range(NCORES):
        ins.append(dict(cT=cT, b=np.ascontiguousarray(b_ada[:, 1536 * i:1536 * (i + 1)]).reshape(1, 6144),
                        w=np.ascontiguousarray(w_ada[:, :, 1536 * i:1536 * (i + 1)])))
    r = run(nc_mod, ins)
    mod = np.concatenate([r[i]['out'].reshape(4, 1536) for i in range(NCORES)], axis=1)
    if DEBUG:
        ref = np.asarray(c, f32) @ w_ada[0] + b_ada[0]
        print("mod err", np.abs(mod[0] - ref[0]).max(), np.abs(ref).max())

    xT = [np.ascontiguousarray(x[0, i * TT:(i + 1) * TT].T) for i in range(NCORES)]
    nc_proj = _mk('proj', build_proj)
    nc_attn = _mk('attn', build_attn)
    nc_dense = _mk('dense', lambda: build_dense(True))
    nc_last = _mk('dense_last', lambda: build_dense(False))
    t5 = np.asarray(t5_bias, f32)

    def proj_params(l):
        wl = np.asarray(w_in[l], f32)
        dq = np.asarray(diff_qk_g[l], f32)
        mg_ = np.asarray(mla_qk_g[l], f32)
        mqk = np.zeros((128, 4), f32)
        mqk[:, 0] = mg_[0, :128]
        mqk[:, 1] = mg_[1, :128]
        mqk[:64, 2] = mg_[0, 128:]
        mqk[:64, 3] = mg_[1, 128:]
        wq = np.asarray(w_q_up[l], f32).reshape(512, 8, 192)
        wkv = np.asarray(w_kv_up[l], f32).reshape(256, 8, 256)
        return dict(w_in=np.ascontiguousarray(wl[:, :6976]),
                    dqk=np.stack([np.tile(dq[0], 2), np.tile(dq[1], 2)], axis=1),
                    gq=colvec(mla_q_norm_g[l], 4), gkv=colvec(mla_kv_norm_g[l], 2), mqk=mqk,
                    wqup=np.ascontiguousarray(np.concatenate([wq[:, :, :128].reshape(512, 1024), wq[:, :, 128:].reshape(512, 512)], axis=1)),
                    wkvup=np.ascontiguousarray(np.concatenate([wkv[:, :, :128].reshape(256, 1024), wkv[:, :, 128:].reshape(256, 1024)], axis=1)))

    def modcols(l):
        return np.ascontiguousarray(mod[l].reshape(6, 16, 128).transpose(2, 0, 1))

    pp = proj_params(0)
    ins = []
    for i in range(NCORES):
        ins.append(dict(xT=xT[i], ones=ones, bd64=PC[i]['bd64'], rotm=PC[i]['rotm'], modc=modcols(0), gmix=colvec(norm_mix_g[0], 16),
                        cosT=PC[i]['cosT'], sinT=PC[i]['sinT'], **pp))
    rp = run(nc_proj, ins)
    for l in range(4):
        lam_init = 0.8 - 0.6 * np.exp(-0.3 * l)
        cst = np.broadcast_to(np.array([lam_init, 1 - lam_init, 0, 0], f32), (128, 4)).copy()
        lamp = np.broadcast_to(np.asarray(diff_lambda[l], f32).reshape(1, 256), (128, 256)).copy()
        gsub = np.asarray(diff_subln_g[l], f32).reshape(128, 1).copy()
        kbr = np.concatenate([rp[i]['kbr'] for i in range(NCORES)], axis=1)
        ins = []
        for h in range(NCORES):
            dd = dict(ones=ones, mtri=AC['mtri'], biasA=np.ascontiguousarray(t5[AC['bucket'], h]), maskB=AC['maskB'], maskC=AC['maskC'], ident=AC['ident'],
                      lamp=lamp, cst=cst, b15=np.full((128, 1), t5[15, h], f32), gsub=gsub, kbr=kbr)
            for n in ['qa', 'ka', 'qbn', 'kbn', 'qc', 'kc', 'qbr']:
                dd[n] = np.concatenate([rp[i][n][h] for i in range(NCORES)], axis=1)
            for n in ['va', 'vb', 'vc']:
                dd[n] = np.concatenate([rp[i][n][:, 128 * h:128 * (h + 1)] for i in range(NCORES)], axis=0)
            ins.append(dd)
        ra = run(nc_attn, ins)
        wl = np.asarray(w_in[l], f32)
        w_gate = np.ascontiguousarray(wl[:, 6976:])
        last = (l == 3)
        pp = {} if last else proj_params(l + 1)
        ins = []
        for i in range(NCORES):
            br = np.stack([np.stack([ra[h][n][:, i * TT:(i + 1) * TT] for h in range(NCORES)]) for n in ['oa', 'ob', 'oc']])
            dd = dict(hT=rp[i]['hT'], xT=xT[i], w_gate=w_gate, w_branch=np.asarray(w_branch[l], f32), br=br,
                      w_out=np.asarray(w_out[l], f32), w_mlp_in=np.asarray(w_mlp_in[l], f32), w_mlp_out=np.asarray(w_mlp_out[l], f32),
                      ones=ones, modc=modcols(l), gmlp=colvec(norm_mlp_g[l], 16))
            if not last:
                dd.update(dict(modc2=modcols(l + 1), gmix2=colvec(norm_mix_g[l + 1], 16), bd64=PC[i]['bd64'], rotm=PC[i]['rotm'],
                               cosT=PC[i]['cosT'], sinT=PC[i]['sinT'], **pp))
            ins.append(dd)
        rd = run(nc_last if last else nc_dense, ins)
        xT = [rd[i]['xo'] for i in range(NCORES)]
        if not last:
            rp = [dict(rd[i], hT=rd[i]['hTo']) for i in range(NCORES)]
    out = np.concatenate([xT[i].T for i in range(NCORES)], axis=0)[None]
    return np.ascontiguousarray(out.astype(f32))


def _rms(x, g):
    return x / np.sqrt((x * x).mean(-1, keepdims=True) + 1e-6) * g


def _err(name, got, ref):
    got = np.asarray(got).astype(np.float64)
    ref = np.asarray(ref).astype(np.float64)
    print("  DBG %-6s rel %.2e  max %.3e / %.3e" % (name, np.sqrt(((got - ref) ** 2).mean() / ((ref ** 2).mean() + 1e-30)), np.abs(got - ref).max(), np.abs(ref).max()), flush=True)


def debug_proj(x, mod, norm_mix_g, wl, dq, rp, mla_q_norm_g, mla_kv_norm_g, w_q_up, w_kv_up, mla_qk_g):
    T0 = 128
    xs = x[0, :T0].astype(np.float64)
    sh1, sc1 = mod[0, :2048], mod[0, 2048:4096]
    h = _rms(xs, np.asarray(norm_mix_g[0], np.float64)) * (1 + sc1) + sh1
    pr = h @ wl.astype(np.float64)
    r = rp[0]
    f = lambda a: np.asarray(a).astype(np.float32)
    qa = _rms(pr[:, 0:1024].reshape(T0, 8, 2, 64), dq[0]).reshape(T0, 8, 128)
    _err('qa', f(r['qa'])[:, :, :T0].transpose(2, 0, 1), qa)
    ka = _rms(pr[:, 1024:2048].reshape(T0, 8, 2, 64), dq[1]).reshape(T0, 8, 128)
    _err('ka', f(r['ka'])[:, :, :T0].transpose(2, 0, 1), ka)
    _err('va', f(r['va'])[:T0], pr[:, 2048:3072])
    _err('qc', f(r['qc'])[:, :, :T0].transpose(2, 0, 1), pr[:, 3904:4928].reshape(T0, 8, 128) * 128 ** -0.5)
    _err('kc', f(r['kc'])[:, :, :T0].transpose(2, 0, 1), pr[:, 4928:5952].reshape(T0, 8, 128))
    _err('vc', f(r['vc'])[:T0], pr[:, 5952:6976])
    cq = _rms(pr[:, 3072:3584], np.asarray(mla_q_norm_g[0], np.float64))
    ckv = _rms(pr[:, 3584:3840], np.asarray(mla_kv_norm_g[0], np.float64))
    kpe = pr[:, 3840:3904]
    q = (cq @ np.asarray(w_q_up[0], np.float64)).reshape(T0, 8, 192)
    kv = (ckv @ np.asarray(w_kv_up[0], np.float64)).reshape(T0, 8, 256)
    g = np.asarray(mla_qk_g[0], np.float64)
    pos = np.arange(T0)
    inv = 10000.0 ** (-np.arange(32) / 32)
    ang = pos[:, None] * inv[None]
    cos, sin = np.cos(ang)[:, None], np.sin(ang)[:, None]

    def rope(v):
        x1, x2 = v[..., :32], v[..., 32:]
        return np.concatenate([x1 * cos - x2 * sin, x1 * sin + x2 * cos], -1)
    _err('qbn', f(r['qbn'])[:, :, :T0].transpose(2, 0, 1), _rms(q[..., :128], g[0, :128]))
    _err('qbr', f(r['qbr'])[:, :, :T0].transpose(2, 0, 1), rope(_rms(q[..., 128:], g[0, 128:])))
    _err('kbn', f(r['kbn'])[:, :, :T0].transpose(2, 0, 1), _rms(kv[..., :128], g[1, :128]))
    _err('kbr', f(r['kbr'])[:, :T0].T, rope(_rms(kpe.reshape(T0, 1, 64), g[1, 128:]))[:, 0])
    _err('vb', f(r['vb'])[:T0].reshape(T0, 8, 128), kv[..., 128:])


def debug_post(x, mod, norm_mix_g, norm_mlp_g, wl, w_branch, w_out, w_mlp_in, w_mlp_out, ra, rm, rx):
    T0 = 128
    f64 = np.float64
    xs = x[0, :T0].astype(f64)
    sh1, sc1, g1, sh2, sc2, g2 = [mod[0, i * 2048:(i + 1) * 2048].astype(f64) for i in range(6)]
    h = _rms(xs, np.asarray(norm_mix_g[0], f64)) * (1 + sc1) + sh1
    gates = h @ wl[:, 6976:].astype(f64)
    br = np.stack([np.concatenate([np.asarray(ra[hh][n]).astype(np.float32)[:, :T0].T for hh in range(8)], axis=1) for n in ['oa', 'ob', 'oc']]).astype(f64)
    merged = 0
    for i in range(3):
        up = br[i] @ np.asarray(w_branch[0][i], f64)
        merged = merged + 1 / (1 + np.exp(-gates[:, i * 2048:(i + 1) * 2048])) * up
    _err('mg', np.asarray(rm[0]['mg']).astype(np.float32)[:, :T0].T, merged)
    x1 = xs + g1 * (merged @ np.asarray(w_out[0], f64))
    h2 = _rms(x1, np.asarray(norm_mlp_g[0], f64)) * (1 + sc2) + sh2
    hid = np.maximum(h2 @ np.asarray(w_mlp_in[0], f64), 0) ** 2
    x2 = x1 + g2 * (hid @ np.asarray(w_mlp_out[0], f64))
    _err('xo', rx[0]['xo'][:, :T0].T, x2)
    _err('dx', rx[0]['xo'][:, :T0].T - xs, x2 - xs)
```

```python
import os, time
import numpy as np
import ml_dtypes
import concourse.bass as bass
import concourse.mybir as mybir
from concourse.bass_utils import run_bass_kernel_spmd

F32 = mybir.dt.float32
BF16 = mybir.dt.bfloat16
AF = mybir.ActivationFunctionType
ALU = mybir.AluOpType


class FW:
    def __init__(self, nc):
        self.nc = nc
        self.eng = {'pe': nc.tensor, 'act': nc.scalar, 'dve': nc.vector, 'pool': nc.gpsimd, 'sp': nc.sync}
        self.sem = {}
        self.cnt = {}
        for k in ['pe', 'act', 'dve', 'pool']:
            self.sem[k] = nc.alloc_semaphore(name='s_' + k)
            self.cnt[k] = 0
        self.seen = {k: {} for k in self.eng}
        self.lastw = {}
        self.reads = {}
        self.dsem = {}

    def _wait(self, e, ev):
        if ev is None:
            return
        k, v = ev
        if e == 'pe' and k == 'pe':
            return
        if self.seen[e].get(k, 0) >= v:
            return
        self.eng[e].wait_ge(self.sem[k], v)
        self.seen[e][k] = v

    def deps(self, e, R, W):
        for t in R:
            self._wait(e, self.lastw.get(t))
        for t in W:
            self._wait(e, self.lastw.get(t))
            for ev in self.reads.get(t, []):
                self._wait(e, ev)

    def commit(self, ev, R, W):
        for t in R:
            self.reads.setdefault(t, []).append(ev)
        for t in W:
            self.lastw[t] = ev
            self.reads[t] = []

    def op(self, e, R, W, fn, inc=True):
        self.deps(e, R, W)
        ins = fn()
        if inc:
            self.cnt[e] += 1
            ins.then_inc(self.sem[e], 1)
            ev = (e, self.cnt[e])
        else:
            ev = (e, self.cnt[e] + 1)
        self.commit(ev, R, W)
        return ins

    def dma(self, q, R, W, out, in_, semtok=None, **kw):
        st = semtok if semtok is not None else (W[0] if W else R[0])
        if st not in self.dsem:
            key = 'd_' + str(st)
            self.sem[key] = self.nc.alloc_semaphore(name=key)
            self.cnt[key] = 0
            self.dsem[st] = key
        key = self.dsem[st]
        self.deps(q, R, W)
        ins = self.eng[q].dma_start(out=out, in_=in_, **kw)
        self.cnt[key] += 16
        ins.then_inc(self.sem[key], 16)
        ev = (key, self.cnt[key])
        self.commit(ev, R, W)
        return ins

    def finish(self, toks, e='sp'):
        for t in toks:
            self._wait(e, self.lastw.get(t))

    def barrier(self):
        for e in self.eng:
            for k in self.sem:
                if self.cnt[k] > 0 and not (e == 'pe' and k == 'pe') and self.seen[e].get(k, 0) < self.cnt[k]:
                    self.eng[e].wait_ge(self.sem[k], self.cnt[k])
                    self.seen[e][k] = self.cnt[k]

    def finish_all(self, e='sp'):
        for k in self.sem:
            if k.startswith('d_') and self.cnt[k] > 0 and self.seen[e].get(k, 0) < self.cnt[k]:
                self.eng[e].wait_ge(self.sem[k], self.cnt[k])
                self.seen[e][k] = self.cnt[k]

    def finish_dma(self, semtok, e='sp'):
        if semtok not in self.dsem:
            return
        key = self.dsem[semtok]
        self.eng[e].wait_ge(self.sem[key], self.cnt[key])


AX = mybir.AxisListType
NEG = -1.0e30
EPS = 1e-6


class Pool_:
    def __init__(self, nc, name, n, shape, dt):
        self.tiles = [nc.alloc_sbuf_tensor(f"{name}{i}", shape, dt) for i in range(n)]
        self.names = [f"{name}{i}" for i in range(n)]
        self.i = 0

    def get(self):
        t, n = self.tiles[self.i], self.names[self.i]
        self.i = (self.i + 1) % len(self.tiles)
        return t, n


def emit_attn(nc, fw, S, d, ps, psz_in=None):
    NQB = S // 512
    NKT = S // 128
    sb = nc.alloc_sbuf_tensor
    Q1 = sb("aQ1", [128, S], BF16)
    Q2 = sb("aQ2", [128, S], BF16)
    K1 = sb("aK1", [128, S], BF16)
    K2 = sb("aK2", [128, S], BF16)
    K3 = sb("aK3", [128, S], BF16)
    V = sb("aV", [128, NKT, 128], BF16)
    OUT = sb("aOUT", [128, S], BF16)
    ones = sb("aones", [128, 128], BF16)
    mtri = sb("amtri", [128, 128], BF16)
    mones = sb("amones", [128, 128], BF16)
    biasA = sb("abiasA", [128, 5, 512], F32)
    maskB = sb("amaskB", [128, 4, 512], F32)
    maskC = sb("amaskC", [128, 4, 512], BF16)
    ident = sb("aident", [128, 128], BF16)
    lamp = sb("alamp", [128, 256], F32)
    cst = sb("acst", [128, 4], F32)
    b15 = sb("ab15", [128, 1], F32)
    gsub = sb("agsub", [128, 1], F32)
    sm = sb("asm", [128, 16], F32)
    carry = sb("acarry", [128, 512], F32)
    carry2 = sb("acarry2", [128, 512], F32)
    Pp = Pool_(nc, "aP", 6, [128, 512], BF16)
    Tp = Pool_(nc, "aT", 6, [128, 512], F32)
    Fp = Pool_(nc, "aF", 8, [128, 512], F32)
    E2p = Pool_(nc, "aE2", 5, [128, 1024], F32)
    L2p = Pool_(nc, "aL2", 3, [128, 1024], BF16)
    W2p = Pool_(nc, "aW2", 3, [128, 1024], BF16)
    psz = psz_in

    fw.op('pool', [], ['Q2z'], lambda: nc.gpsimd.memset(Q2[64:128, :], 0.0))
    fw.op('pool', [], ['K2z'], lambda: nc.gpsimd.memset(K2[64:128, :], 0.0))
    fw.op('pool', [], ['K3z'], lambda: nc.gpsimd.memset(K3[0:64, :], 0.0))
    fw.dma('sp', [], ['ones'], ones[:], d['ones'])
    fw.dma('sp', [], ['mtri'], mtri[:], d['mtri'])
    fw.dma('sp', [], ['biasA'], biasA[:], d['biasA'].rearrange("o p q -> p o q"))
    fw.dma('sp', [], ['maskB'], maskB[:], d['maskB'].rearrange("o p q -> p o q"))
    fw.dma('sp', [], ['maskC'], maskC[:], d['maskC'].rearrange("o p q -> p o q"))
    fw.dma('sp', [], ['ident'], ident[:], d['ident'])
    fw.dma('sp', [], ['lamp'], lamp[:], d['lamp'])
    fw.dma('sp', [], ['cst'], cst[:], d['cst'])
    fw.dma('sp', [], ['b15'], b15[:], d['b15'])
    fw.dma('sp', [], ['gsub'], gsub[:], d['gsub'])
    for o in range(4):
        fw.op('dve', ['biasA', 'maskB'], ['biasA'], lambda: nc.vector.tensor_tensor(biasA[:, o + 1, :], biasA[:, o + 1, :], maskB[:, o, :], ALU.add))
    T0, t0n = Tp.get()
    fw.op('dve', ['lamp'], [t0n], lambda: nc.vector.tensor_tensor(T0[:, 0:64], lamp[:, 0:64], lamp[:, 64:128], ALU.mult))
    fw.op('dve', [t0n], ['sm'], lambda: nc.vector.reduce_sum(sm[:, 0:1], T0[:, 0:64], AX.X))
    fw.op('dve', ['lamp'], [t0n], lambda: nc.vector.tensor_tensor(T0[:, 0:64], lamp[:, 128:192], lamp[:, 192:256], ALU.mult))
    fw.op('dve', [t0n], ['sm'], lambda: nc.vector.reduce_sum(sm[:, 1:2], T0[:, 0:64], AX.X))
    fw.op('act', ['sm'], ['sm'], lambda: nc.scalar.activation(sm[:, 2:4], sm[:, 0:2], AF.Exp))
    fw.op('dve', ['sm'], ['sm'], lambda: nc.vector.tensor_tensor(sm[:, 4:5], sm[:, 3:4], sm[:, 2:3], ALU.subtract))
    fw.op('dve', ['sm', 'cst'], ['sm'], lambda: nc.vector.tensor_tensor(sm[:, 4:5], sm[:, 4:5], cst[:, 0:1], ALU.subtract))
    fw.op('dve', ['gsub', 'cst'], ['sm'], lambda: nc.vector.tensor_tensor(sm[:, 5:6], gsub[:], cst[:, 1:2], ALU.mult))
    neglam = sm[:, 4:5]
    gcolA = sm[:, 5:6]

    def load_qkv(q1, q2, k1, k2, v):
        fw.dma('sp', [], ['Q1'], Q1[:], q1)
        fw.dma('act', [], ['K1'], K1[:], k1)
        if q2 is not None:
            fw.dma('sp', [], ['Q2'], Q2[0:64, :], q2)
            fw.dma('act', [], ['K2'], K2[0:64, :], k2)
        for c in range(0, NKT, 16):
            n = min(16, NKT - c)
            fw.dma('sp', [], ['V'], V[:, c:c + n, :], v[c * 128:(c + n) * 128, :].rearrange("(t p) e -> p t e", p=128))

    def store_out(dst, qb):
        fw.dma('sp', [f'out{qb}'], [], dst[:, qb * 512:(qb + 1) * 512], OUT[:, qb * 512:(qb + 1) * 512])

    def softmax_mixer(kind):
        nm = 2 if kind == 'A' else 1
        scale = 64 ** -0.5 if kind == 'A' else 192 ** -0.5
        if kind == 'A':
            Sb = [[ps[0], ps[1]], [ps[2], ps[3]]]
            Sn = [['ps0', 'ps1'], ['ps2', 'ps3']]
            NB, LA = 2, 1
        else:
            Sb = [[ps[0], ps[1], ps[2], ps[3]]]
            Sn = [['ps0', 'ps1', 'ps2', 'ps3']]
            NB, LA = 4, 2
        Ob = [ps[4], ps[5]]; On = ['ps4', 'ps5']
        Db = [ps[6], ps[7]]; Dn = ['ps6', 'ps7']

        deferred = []
        for qb in range(NQB):
            q0 = qb * 512
            nk = 4 * (qb + 1)
            qs = slice(q0, q0 + 512)

            def emit_S(kt):
                ks = slice(kt * 128, kt * 128 + 128)
                b = kt % NB
                for m in range(nm):
                    if kind == 'A':
                        KA, katoks = (K2, ['K2', 'K2z']) if m == 0 else (K3, ['K3', 'K3z'])
                        fw.op('pe', katoks + ['Q1'], [Sn[m][b]], lambda: nc.tensor.matmul(Sb[m][b][:], KA[:, ks], Q1[:, qs], start=True, stop=True))
                    else:
                        fw.op('pe', ['K1', 'Q1'], [Sn[m][b]], lambda: nc.tensor.matmul(Sb[m][b][:], K1[:, ks], Q1[:, qs], start=True, stop=False), inc=False)
                        fw.op('pe', ['K2', 'Q2', 'K2z', 'Q2z'], [Sn[m][b]], lambda: nc.tensor.matmul(Sb[m][b][:], K2[:, ks], Q2[:, qs], start=False, stop=True))

            for kk in range(min(LA, nk)):
                emit_S(kk)
            for kt in range(nk):
                b = kt % NB
                o = kt * 128 - q0
                Pt = []
                for m in range(nm):
                    P, pn = Pp.get()
                    near = ((o >= -128) if kind == 'A' else (o >= 0))
                    if near:
                        oi = o // 128 + 1
                        T, tn = Tp.get()
                        btile = biasA[:, oi, :] if kind == 'A' else maskB[:, oi - 1, :]
                        btok = 'biasA' if kind == 'A' else 'maskB'
                        fw.op('dve', [Sn[m][b], btok], [tn], lambda: nc.vector.scalar_tensor_tensor(T[:], Sb[m][b][:], scale, btile, ALU.mult, ALU.add))
                        fw.op('act', [tn], [pn], lambda: nc.scalar.activation(P[:], T[:], AF.Exp))
                    else:
                        if kind == 'A':
                            fw.op('act', [Sn[m][b], 'b15'], [pn], lambda: nc.scalar.activation(P[:], Sb[m][b][:], AF.Exp, bias=b15[:], scale=scale))
                        else:
                            fw.op('act', [Sn[m][b]], [pn], lambda: nc.scalar.activation(P[:], Sb[m][b][:], AF.Exp, scale=scale))
                    Pt.append((P, pn))
                if kt == 1:
                    while deferred:
                        deferred.pop(0)()
                if kt + LA < nk:
                    emit_S(kt + LA)
                for m in range(nm):
                    P, pn = Pt[m]
                    fw.op('pe', ['V', pn], [On[m]], lambda: nc.tensor.matmul(Ob[m][:], V[:, kt, :], P[:], start=(kt == 0), stop=(kt == nk - 1)), inc=False)
                    fw.op('pe', ['ones', pn], [Dn[m]], lambda: nc.tensor.matmul(Db[m][:], ones[:], P[:], start=(kt == 0), stop=(kt == nk - 1)))
            if kind == 'B':
                R, rn = Tp.get()
                fw.op('dve', [Dn[0]], [rn], lambda: nc.vector.reciprocal(R[:], Db[0][:]))
                fw.op('dve', [On[0], rn], [f'out{qb}'], lambda: nc.vector.tensor_tensor(OUT[:, qs], Ob[0][:], R[:], ALU.mult))
                store_out(d['ob'], qb)
            else:
                ev = []
                for m in range(2):
                    R, rn = Fp.get()
                    fw.op('dve', [Dn[m]], [rn], lambda: nc.vector.reciprocal(R[:], Db[m][:]))
                    C, cn = Fp.get()
                    fw.op('act', [On[m]], [cn], lambda: nc.scalar.copy(C[:], Ob[m][:]))
                    ev.append((R, rn, C, cn))

                def tail(ev=ev, qb=qb, qs=qs):
                    (R1, r1n, C1, c1n), (R2, r2n, C2, c2n) = ev
                    fw.op('dve', [c1n, r1n], [c1n], lambda: nc.vector.tensor_tensor(C1[:], C1[:], R1[:], ALU.mult))
                    fw.op('dve', [c2n, r2n], [c2n], lambda: nc.vector.tensor_tensor(C2[:], C2[:], R2[:], ALU.mult))
                    fw.op('dve', [c1n, c2n, 'sm'], [c1n], lambda: nc.vector.scalar_tensor_tensor(C1[:], C2[:], neglam, C1[:], ALU.mult, ALU.add))
                    SQ, sqn = Pp.get()
                    fw.op('act', [c1n], [sqn], lambda: nc.scalar.activation(SQ[:], C1[:], AF.Square))
                    fw.op('pe', ['ones', sqn], ['ps0'], lambda: nc.tensor.matmul(ps[0][:], ones[:], SQ[:], start=True, stop=True))
                    fw.op('act', ['ps0'], [r2n], lambda: nc.scalar.activation(R2[:], ps[0][:], AF.Ln, bias=EPS, scale=1.0 / 128))
                    fw.op('act', [r2n], [r2n], lambda: nc.scalar.activation(R2[:], R2[:], AF.Exp, scale=-0.5))
                    fw.op('dve', [c1n, r2n, 'sm'], [f'out{qb}'], lambda: nc.vector.scalar_tensor_tensor(OUT[:, qs], C1[:], gcolA, R2[:], ALU.mult, ALU.mult))
                    store_out(d['oa'], qb)
                deferred.append(tail)
        while deferred:
            deferred.pop(0)()

    def sb_mixer():
        Zb = [psz[0], psz[1]]; Zn = ['psz0', 'psz1']
        Ab = [ps[4], ps[5]]; An = ['ps4', 'ps5']
        Tb_, Tn_ = ps[6], 'ps6'
        Op, On_ = ps[7], 'ps7'
        for qb in range(NQB):
            q0 = qb * 512
            qs = slice(q0, q0 + 512)
            kts = list(range(4 * qb + 3, -1, -1))
            nk = len(kts)
            npair = nk // 2
            Ls = {}
            Ws = {}

            def emit_Z(p):
                for h in range(2):
                    kt = kts[2 * p + h]
                    ks = slice(kt * 128, kt * 128 + 128)
                    o = kt * 128 - q0
                    if o >= 0:
                        fw.op('pe', ['K1', 'Q1'], [Zn[p % 2]], lambda: nc.tensor.matmul(Zb[p % 2][:, h * 512:(h + 1) * 512], K1[:, ks], Q1[:, qs], start=True, stop=False), inc=False)
                        fw.op('pe', ['ident', 'maskC'], [Zn[p % 2]], lambda: nc.tensor.matmul(Zb[p % 2][:, h * 512:(h + 1) * 512], ident[:], maskC[:, o // 128, :], start=False, stop=True), inc=(h == 1))
                    else:
                        fw.op('pe', ['K1', 'Q1'], [Zn[p % 2]], lambda: nc.tensor.matmul(Zb[p % 2][:, h * 512:(h + 1) * 512], K1[:, ks], Q1[:, qs], start=True, stop=True), inc=(h == 1))

            def stage1(p):
                b = p % 2
                E, en = E2p.get()
                fw.op('act', [Zn[b]], [en], lambda: nc.scalar.activation(E[:], Zb[b][:], AF.Exp))
                L, ln_ = L2p.get()
                fw.op('act', [en], [ln_], lambda: nc.scalar.activation(L[:], E[:], AF.Ln, bias=1.0))
                Ls[p] = (L, ln_)

            def emit_O(p):
                W, wn = Ws.pop(p)
                for h in range(2):
                    i = 2 * p + h
                    kt = kts[i]
                    fw.op('pe', ['V', wn], [On_], lambda: nc.tensor.matmul(Op[:], V[:, kt, :], W[:, h * 512:(h + 1) * 512], start=(i == 0), stop=(i == nk - 1)), inc=(h == 1))

            Gs = {}

            def emit_W(p):
                G, gn = Gs.pop(p)
                W, wn = W2p.get()
                fw.op('act', [gn], [wn], lambda: nc.scalar.activation(W[:], G[:], AF.Exp))
                Ws[p] = (W, wn)

            fw.op('dve', [], ['carry'], lambda: nc.vector.memset(carry[:], 0.0))
            emit_Z(0)
            if npair > 1:
                emit_Z(1)
            stage1(0)
            for p in range(npair):
                L, ln_ = Ls.pop(p)
                G, gn = E2p.get()
                for h in range(2):
                    i = 2 * p + h
                    kt = kts[i]
                    b = i % 2
                    ks = slice(kt * 128, kt * 128 + 128)
                    hs = slice(h * 512, (h + 1) * 512)
                    fw.op('pe', ['K1', 'Q1'], [An[b]], lambda: nc.tensor.matmul(Ab[b][:], K1[:, ks], Q1[:, qs], start=True, stop=False), inc=False)
                    od = kt * 128 - q0
                    if od >= 0:
                        fw.op('pe', ['ident', 'maskC'], [An[b]], lambda: nc.tensor.matmul(Ab[b][:], ident[:], maskC[:, od // 128, :], start=False, stop=False), inc=False)
                    fw.op('pe', ['mtri', ln_], [An[b]], lambda: nc.tensor.matmul(Ab[b][:], mtri[:], L[:, hs], start=False, stop=True))
                    fw.op('pe', ['ones', ln_], [Tn_], lambda: nc.tensor.matmul(Tb_[:], ones[:], L[:, hs], start=True, stop=True))
                    cur, ctok = (carry, 'carry') if h == 0 else (carry2, 'carry2')
                    nxt, ntok = (carry2, 'carry2') if h == 0 else (carry, 'carry')
                    fw.op('dve', [Tn_, ctok], [ntok], lambda: nc.vector.tensor_tensor(nxt[:], cur[:], Tb_[:], ALU.add))
                    fw.op('dve', [An[b], ctok], [gn], lambda: nc.vector.tensor_tensor(G[:, hs], Ab[b][:], cur[:], ALU.subtract))
                Gs[p] = (G, gn)
                if p >= 2:
                    emit_O(p - 2)
                if p + 1 < npair:
                    stage1(p + 1)
                if p >= 1:
                    emit_W(p - 1)
                if p + 2 < npair:
                    emit_Z(p + 2)
            emit_W(npair - 1)
            if npair >= 2:
                emit_O(npair - 2)
            emit_O(npair - 1)
            fw.op('act', [On_], [f'out{qb}'], lambda: nc.scalar.copy(OUT[:, qs], Op[:]))
            store_out(d['oc'], qb)

    MIX = 'ABC'
    if 'A' in MIX:
        load_qkv(d['qa'], None, d['ka'], None, d['va'])
        fw.dma('act', [], ['K2'], K2[0:64, :], d['ka'][0:64, :])
        fw.dma('act', [], ['K3'], K3[64:128, :], d['ka'][64:128, :])
        softmax_mixer('A')
    if 'B' in MIX:
        load_qkv(d['qbn'], d['qbr'], d['kbn'], d['kbr'], d['vb'])
        softmax_mixer('B')
    if 'C' in MIX:
        load_qkv(d['qc'], None, d['kc'], None, d['vc'])
        sb_mixer()


def t5_bucket_np(rel):
    nb = 16
    max_exact = 8
    n = np.abs(rel)
    large = max_exact + (np.log(np.maximum(n, 1).astype(np.float32) / np.float32(max_exact)) / np.float32(np.log(128 / 8)) * np.float32(nb - max_exact)).astype(np.int32)
    large = np.minimum(large, nb - 1)
    return np.where(rel > 0, nb, 0) + np.where(n < max_exact, n, large)


def attn_consts():
    kk = np.arange(128)[:, None]
    qq = np.arange(512)[None, :]
    offs = [-128, 0, 128, 256, 384]
    bucket = np.stack([t5_bucket_np((o + kk) - qq) for o in offs])
    maskB = np.stack([np.where(((o + kk) // 64) <= (qq // 64), 0.0, NEG) for o in offs[1:]]).astype(np.float32)
    maskC = np.stack([np.where((o + kk) < qq, 0.0, -30000.0) for o in offs[1:]]).astype(ml_dtypes.bfloat16)
    ident = np.eye(128, dtype=np.float32).astype(ml_dtypes.bfloat16)
    jj = np.arange(128)[:, None]
    kc = np.arange(128)[None, :]
    mtri = np.where(jj >= kc, -1.0, 0.0).astype(ml_dtypes.bfloat16)
    ones = np.ones((128, 128), ml_dtypes.bfloat16)
    return dict(bucket=bucket, maskB=maskB, maskC=maskC, mtri=mtri, ones=ones, ident=ident)


TT = 1024
O_AQ, O_AK, O_AV, O_CQ, O_CKV, O_KPE, O_SQ, O_SK, O_SV, O_G = 0, 1024, 2048, 3072, 3584, 3840, 3904, 4928, 5952, 6976


class Slabs:
    def __init__(self, nc, fw, n=3):
        self.nc, self.fw = nc, fw
        self.pool = Pool_(nc, "slab", n, [128, 16, 512], BF16)

    def load(self, w, r0, nrows, c0, ncols):
        t, tok = self.pool.get()
        nch = nrows // 128
        step = 4
        for c in range(0, nch, step):
            n = min(step, nch - c)
            self.fw.dma('pool', [], [tok], t[:, c:c + n, 0:ncols],
                        w[r0 + c * 128:r0 + (c + n) * 128, c0:c0 + ncols].rearrange("(c p) n -> p c n", p=128))
        return t, tok


def emit_norm_h(nc, fw, X, H, acol, bcol, ones, ps, psn, Tp, Pp, RS):
    for tb in range(TT // 512):
        ts = slice(tb * 512, tb * 512 + 512)
        for c in range(16):
            SQ, sqn = Pp.get()
            fw.op('act', ['X'], [sqn], lambda: nc.scalar.activation(SQ[:], X[:, c, ts], AF.Square))
            fw.op('pe', ['ones', sqn], [psn], lambda: nc.tensor.matmul(ps[:], ones[:], SQ[:], start=(c == 0), stop=(c == 15)))
        fw.op('act', [psn], ['RS'], lambda: nc.scalar.activation(RS[:, ts], ps[:], AF.Ln, bias=EPS, scale=1.0 / 2048))
        fw.op('act', ['RS'], ['RS'], lambda: nc.scalar.activation(RS[:, ts], RS[:, ts], AF.Exp, scale=-0.5))
        for c in range(16):
            T, tn = Tp.get()
            fw.op('dve', ['X', 'RS', 'cols'], [tn], lambda: nc.vector.scalar_tensor_tensor(T[:], X[:, c, ts], acol[:, c:c + 1], RS[:, ts], ALU.mult, ALU.mult))
            fw.op('dve', [tn, 'cols'], ['H'], lambda: nc.vector.tensor_scalar(H[:, c, ts], T[:], bcol[:, c:c + 1], None, ALU.add))


def emit_proj(nc, fw, d, ps):
    sb = nc.alloc_sbuf_tensor
    H = sb("H", [128, 16, TT], BF16)
    RS = sb("RS", [128, TT], F32)
    ones = sb("pones", [128, 128], BF16)
    modc = sb("pmodc", [128, 6, 16], F32)
    gmix = sb("pgmix", [128, 16], F32)
    cols = sb("pcols", [128, 2, 16], F32)
    Pp = Pool_(nc, "pP", 4, [128, 512], BF16)
    Tp = Pool_(nc, "pT", 4, [128, 512], F32)
    fw.dma('act', [], ['ones'], ones[:], d['ones'])
    fw.dma('act', [], ['modc'], modc[:], d['modc'])
    fw.dma('act', [], ['gmix'], gmix[:], d['gmix'])
    with nc.sbuf_tensor("X", [128, 16, TT], F32) as X:
        for c in range(0, 16, 4):
            fw.dma('sp', [], ['X'], X[:, c:c + 4, :], d['xT'][c * 128:(c + 4) * 128, :].rearrange("(c p) t -> p c t", p=128))
        fw.op('dve', ['modc', 'gmix'], ['cols'], lambda: nc.vector.scalar_tensor_tensor(cols[:, 0, :], modc[:, 1, :], 1.0, gmix[:], ALU.add, ALU.mult))
        fw.op('dve', ['modc'], ['cols'], lambda: nc.vector.tensor_copy(cols[:, 1, :], modc[:, 0, :]))
        emit_norm_h(nc, fw, X, H, cols[:, 0, :], cols[:, 1, :], ones, ps[4], 'ps4', Tp, Pp, RS)
    for c in range(0, 16, 4):
        fw.dma('sp', ['H'], [], d['hT'][c * 128:(c + 4) * 128, :].rearrange("(c p) t -> p c t", p=128), H[:, c:c + 4, :], semtok='hout')
    fw.barrier()
    emit_proj_body(nc, fw, d, ps, H, ones, Pp, Tp, Slabs(nc, fw))


def emit_proj_body(nc, fw, d, ps, H, ones, Pp, Tp, slabs):
    sb = nc.alloc_sbuf_tensor
    bd64 = sb("pbd64", [128, 128], BF16)
    rotm = sb("protm", [64, 64], BF16)
    dqk = sb("pdqk", [128, 2], F32)
    gq = sb("pgq", [128, 4], F32)
    gkv = sb("pgkv", [128, 2], F32)
    mqk = sb("pmqk", [128, 4], F32)
    cosT = sb("pcos", [64, TT], F32)
    sinT = sb("psin", [64, TT], F32)
    WQ = sb("pwq", [128, 4, 1536], BF16)
    WKV = sb("pwkv", [128, 2, 2048], BF16)
    CQ = sb("pCQ", [128, 4, TT], F32)
    CKV = sb("pCKV", [128, 2, TT], F32)
    KPE = sb("pKPE", [64, TT], F32)
    CQN = sb("pCQN", [128, 4, TT], BF16)
    CKVN = sb("pCKVN", [128, 2, TT], BF16)
    Op = Pool_(nc, "pO", 4, [128, 512], BF16)
    projps = [(ps[i], f'ps{i}') for i in range(4)]
    pi = [0]

    def nextps():
        p = projps[pi[0] % 4]
        pi[0] += 1
        return p

    nps = [(ps[4], 'ps4'), (ps[5], 'ps5')]
    ni = [0]

    def nextnps():
        p = nps[ni[0] % 2]
        ni[0] += 1
        return p

    fw.dma('act', [], ['bd64'], bd64[:], d['bd64'])
    fw.dma('act', [], ['rotm'], rotm[:], d['rotm'])
    fw.dma('act', [], ['dqk'], dqk[:], d['dqk'])
    fw.dma('act', [], ['gq'], gq[:], d['gq'])
    fw.dma('act', [], ['gkv'], gkv[:], d['gkv'])
    fw.dma('act', [], ['mqk'], mqk[:], d['mqk'])
    fw.dma('act', [], ['cos'], cosT[:], d['cosT'])
    fw.dma('act', [], ['sin'], sinT[:], d['sinT'])
    for c in range(4):
        fw.dma('pool', [], ['WQ'], WQ[:, c, :], d['wqup'][c * 128:(c + 1) * 128, :], semtok='WQ')
    for c in range(2):
        fw.dma('pool', [], ['WKV'], WKV[:, c, :], d['wkvup'][c * 128:(c + 1) * 128, :], semtok='WKV')
    w = d['w_in']

    def rstd_of(sq_list, lhsT, ltok, nfeat, P=128):
        pp, ppn = nextnps()
        for i, (sq, sqn) in enumerate(sq_list):
            fw.op('pe', [ltok, sqn], [ppn], lambda: nc.tensor.matmul(pp[0:P, :], lhsT, sq, start=(i == 0), stop=(i == len(sq_list) - 1)))
        R, rn = Tp.get()
        fw.op('act', [ppn], [rn], lambda: nc.scalar.activation(R[0:P, :], pp[0:P, :], AF.Ln, bias=EPS, scale=1.0 / nfeat))
        fw.op('act', [rn], [rn], lambda: nc.scalar.activation(R[0:P, :], R[0:P, :], AF.Exp, scale=-0.5))
        return R, rn

    def fm_chunk(slab, stok, nrc, col, ncols, tb, rhs_src, rtok):
        p, pn = nextps()
        ts = slice(tb * 512, tb * 512 + 512)
        for c in range(nrc):
            fw.op('pe', [stok, rtok], [pn], lambda: nc.tensor.matmul(p[0:ncols, :], slab[:, c, col:col + ncols], rhs_src[:, c, ts], start=(c == 0), stop=(c == nrc - 1)), inc=(c == nrc - 1))
        return p, pn

    def out_dma(dst, O, on):
        fw.dma('sp', [on], [], dst, O)

    def qknorm_out(p, pn, P, lhsT, ltok, nfeat, gcol, gtok, dst, scale_extra=1.0):
        SQ, sqn = Pp.get()
        fw.op('act', [pn], [sqn], lambda: nc.scalar.activation(SQ[0:P, :], p[0:P, :], AF.Square))
        R, rn = rstd_of([(SQ[0:P, :], sqn)], lhsT, ltok, nfeat, P)
        O, on = Op.get()
        fw.op('dve', [pn, rn, gtok], [on], lambda: nc.vector.scalar_tensor_tensor(O[0:P, :], p[0:P, :], gcol, R[0:P, :], ALU.mult, ALU.mult))
        out_dma(dst, O[0:P, :], on)

    def rope_out(src, stok, P, gcol, gtok, tb, dst):
        ts = slice(tb * 512, tb * 512 + 512)
        SQ, sqn = Pp.get()
        fw.op('act', [stok], [sqn], lambda: nc.scalar.activation(SQ[0:64, :], src, AF.Square))
        R, rn = rstd_of([(SQ[0:64, :], sqn)], ones[0:64, 0:64], 'ones', 64, 64)
        QN, qn = Tp.get()
        fw.op('dve', [stok, rn, gtok], [qn], lambda: nc.vector.scalar_tensor_tensor(QN[0:64, :], src, gcol, R[0:64, :], ALU.mult, ALU.mult))
        QB, qbn_ = Pp.get()
        fw.op('act', [qn], [qbn_], lambda: nc.scalar.copy(QB[0:64, :], QN[0:64, :]))
        rp, rpn = (ps[6], 'ps6') if tb == 0 else (ps[7], 'ps7')
        fw.op('pe', ['rotm', qbn_], [rpn], lambda: nc.tensor.matmul(rp[0:64, :], rotm[:], QB[0:64, :], start=True, stop=True))
        T2, t2n = Tp.get()
        fw.op('dve', [rpn, 'sin'], [t2n], lambda: nc.vector.tensor_tensor(T2[0:64, :], rp[0:64, :], sinT[:, ts], ALU.mult))
        fw.op('dve', [qn, 'cos'], [qn], lambda: nc.vector.tensor_tensor(QN[0:64, :], QN[0:64, :], cosT[:, ts], ALU.mult))
        O, on = Op.get()
        fw.op('dve', [qn, t2n], [on], lambda: nc.vector.tensor_tensor(O[0:64, :], QN[0:64, :], T2[0:64, :], ALU.add))
        out_dma(dst, O[0:64, :], on)

    pending = []

    def defer(fn):
        pending.append(fn)
        while len(pending) > 1:
            pending.pop(0)()

    def flush():
        while pending:
            pending.pop(0)()

    def fm_group(c0, ncols_total, handler):
        for s0 in range(0, ncols_total, 512):
            ncs = min(512, ncols_total - s0)
            slab, stok = slabs.load(w, 0, 2048, c0 + s0, ncs)
            for tb in range(2):
                for cc in range(0, ncs, 128):
                    nco = min(128, ncs - cc)
                    p, pn = fm_chunk(slab, stok, 16, cc, nco, tb, H, 'H')
                    defer(lambda ch=(s0 + cc) // 128, tb=tb, p=p, pn=pn, nco=nco: handler(ch, tb, p, pn, nco))
        flush()

    def tsl(tb):
        return slice(tb * 512, tb * 512 + 512)

    fm_group(O_AQ, 1024, lambda ch, tb, p, pn, n: qknorm_out(p, pn, 128, bd64[:], 'bd64', 64, dqk[:, 0:1], 'dqk', d['qa'][ch, :, tsl(tb)]))
    fm_group(O_AK, 1024, lambda ch, tb, p, pn, n: qknorm_out(p, pn, 128, bd64[:], 'bd64', 64, dqk[:, 1:2], 'dqk', d['ka'][ch, :, tsl(tb)]))

    def cq_h(ch, tb, p, pn, n):
        O, on = Op.get()
        fw.op('act', [pn], [on], lambda: nc.scalar.activation(O[:], p[:], AF.Copy, scale=128 ** -0.5))
        out_dma(d['qc'][ch, :, tsl(tb)], O[:], on)

    def ck_h(ch, tb, p, pn, n):
        O, on = Op.get()
        fw.op('act', [pn], [on], lambda: nc.scalar.copy(O[:], p[:]))
        out_dma(d['kc'][ch, :, tsl(tb)], O[:], on)

    fm_group(O_SQ, 1024, cq_h)
    fm_group(O_SK, 1024, ck_h)

    def lat_cq(ch, tb, p, pn, n):
        fw.op('act', [pn], ['CQ'], lambda: nc.scalar.copy(CQ[:, ch, tsl(tb)], p[:]))

    def lat_ckv(ch, tb, p, pn, n):
        if ch < 2:
            fw.op('act', [pn], ['CKV'], lambda: nc.scalar.copy(CKV[:, ch, tsl(tb)], p[:]))
        else:
            fw.op('act', [pn], ['KPE'], lambda: nc.scalar.copy(KPE[:, tsl(tb)], p[0:64, :]))

    fm_group(O_CQ, 512, lat_cq)
    fm_group(O_CKV, 320, lat_ckv)

    for tb in range(2):
        ts = tsl(tb)
        for (RAW, rtok, NRM, ntok, nch, gg, gtok) in [(CQ, 'CQ', CQN, 'CQN', 4, gq, 'gq'), (CKV, 'CKV', CKVN, 'CKVN', 2, gkv, 'gkv')]:
            sql = []
            for c in range(nch):
                SQ, sqn = Pp.get()
                fw.op('act', [rtok], [sqn], lambda: nc.scalar.activation(SQ[:], RAW[:, c, ts], AF.Square))
                sql.append((SQ[:], sqn))
            R, rn = rstd_of(sql, ones[:], 'ones', nch * 128)
            for c in range(nch):
                fw.op('dve', [rtok, rn, gtok], [ntok], lambda: nc.vector.scalar_tensor_tensor(NRM[:, c, ts], RAW[:, c, ts], gg[:, c:c + 1], R[:], ALU.mult, ALU.mult))
        rope_out(KPE[:, ts], 'KPE', 64, mqk[0:64, 3:4], 'mqk', tb, d['kbr'][:, ts])

    for tb in range(2):
        for hh in range(8):
            p, pn = fm_chunk(WQ, 'WQ', 4, hh * 128, 128, tb, CQN, 'CQN')
            defer(lambda p=p, pn=pn, hh=hh, tb=tb: qknorm_out(p, pn, 128, ones[:], 'ones', 128, mqk[:, 0:1], 'mqk', d['qbn'][hh, :, tsl(tb)]))
        for hh in range(8):
            p, pn = fm_chunk(WQ, 'WQ', 4, 1024 + hh * 64, 64, tb, CQN, 'CQN')
            defer(lambda p=p, pn=pn, hh=hh, tb=tb: rope_out(p[0:64, :], pn, 64, mqk[0:64, 2:3], 'mqk', tb, d['qbr'][hh, :, tsl(tb)]))
        for hh in range(8):
            p, pn = fm_chunk(WKV, 'WKV', 2, hh * 128, 128, tb, CKVN, 'CKVN')
            defer(lambda p=p, pn=pn, hh=hh, tb=tb: qknorm_out(p, pn, 128, ones[:], 'ones', 128, mqk[:, 1:2], 'mqk', d['kbn'][hh, :, tsl(tb)]))
        flush()
    for tt in range(TT // 128):
        for nb in range(2):
            p, pn = nextps()
            for c in range(2):
                fw.op('pe', ['CKVN', 'WKV'], [pn], lambda: nc.tensor.matmul(p[:], CKVN[:, c, tt * 128:(tt + 1) * 128], WKV[:, c, 1024 + nb * 512:1024 + (nb + 1) * 512], start=(c == 0), stop=(c == 1)), inc=(c == 1))
            O, on = Op.get()
            fw.op('act', [pn], [on], lambda: nc.scalar.copy(O[:], p[:]))
            out_dma(d['vb'][tt * 128:(tt + 1) * 128, nb * 512:(nb + 1) * 512], O[:], on)

    for (c0, dst) in [(O_AV, d['va']), (O_SV, d['vc'])]:
        for s0 in range(0, 1024, 512):
            slab, stok = slabs.load(w, 0, 2048, c0 + s0, 512)
            for tt in range(TT // 128):
                p, pn = nextps()
                for c in range(16):
                    fw.op('pe', ['H', stok], [pn], lambda: nc.tensor.matmul(p[:], H[:, c, tt * 128:(tt + 1) * 128], slab[:, c, :], start=(c == 0), stop=(c == 15)), inc=(c == 15))
                O, on = Op.get()
                fw.op('act', [pn], [on], lambda: nc.scalar.copy(O[:], p[:]))
                out_dma(dst[tt * 128:(tt + 1) * 128, s0:s0 + 512], O[:], on)
    return ['dramout']


def proj_consts(core):
    half = 32
    inv = (10000.0 ** (-np.arange(half, dtype=np.float32) / half)).astype(np.float32)
    pos = np.arange(core * TT, (core + 1) * TT).astype(np.float32)
    ang = pos[None, :] * np.concatenate([inv, inv])[:, None]
    cosT = np.cos(ang).astype(np.float32)
    sinT = np.sin(ang).astype(np.float32)
    rot = np.zeros((64, 64), np.float32)
    for p in range(32):
        rot[p + 32, p] = -1.0
        rot[p, p + 32] = 1.0
    bd = np.zeros((128, 128), np.float32)
    bd[:64, :64] = 1
    bd[64:, 64:] = 1
    return dict(cosT=cosT, sinT=sinT, rotm=rot.astype(ml_dtypes.bfloat16), bd64=bd.astype(ml_dtypes.bfloat16), ones=np.ones((128, 128), ml_dtypes.bfloat16))


def _common(nc, fw, d, ps, Tp, Pp, H, RS, ones, modc, gvec, gtok, cols, sc_idx, sh_idx, X, mtok='modc'):
    fw.op('dve', [mtok, gtok], ['cols'], lambda: nc.vector.scalar_tensor_tensor(cols[:, 0, :], modc[:, sc_idx, :], 1.0, gvec[:], ALU.add, ALU.mult))
    fw.op('dve', [mtok], ['cols'], lambda: nc.vector.tensor_copy(cols[:, 1, :], modc[:, sh_idx, :]))
    emit_norm_h(nc, fw, X, H, cols[:, 0, :], cols[:, 1, :], ones, ps[7], 'ps7', Tp, Pp, RS)


def tsl(tb):
    return slice(tb * 512, tb * 512 + 512)


def emit_merge(nc, fw, d, ps):
    sb = nc.alloc_sbuf_tensor
    H = sb("H", [128, 16, TT], BF16)
    RS = sb("RS", [128, TT], F32)
    MG = sb("MG", [128, 16, TT], BF16)
    MGF = [[sb(f"MGF{i}_{tb}", [128, 512], F32) for tb in range(2)] for i in range(4)]
    ones = sb("pones", [128, 128], BF16)
    modc = sb("pmodc", [128, 6, 16], F32)
    gmix = sb("pgmix", [128, 16], F32)
    cols = sb("pcols", [128, 2, 16], F32)
    Pp = Pool_(nc, "pP", 4, [128, 512], BF16)
    Tp = Pool_(nc, "pT", 6, [128, 512], F32)
    pslist = [(ps[i], f'ps{i}') for i in range(7)]
    pi = [0]

    def nextps():
        p = pslist[pi[0] % 7]
        pi[0] += 1
        return p

    for c in range(0, 16, 4):
        fw.dma('sp', [], ['H'], H[:, c:c + 4, :], d['hT'][c * 128:(c + 4) * 128, :].rearrange("(c p) t -> p c t", p=128))
    slabs = Slabs(nc, fw, n=3)
    BRp = Pool_(nc, "BR", 2, [128, 8, TT], BF16)
    w_in, wbr = d['w_in'], d['w_branch']
    for s in range(4):
        for i in range(3):
            gs, gtok = slabs.load(w_in, 0, 2048, i * 2048 + s * 512, 512)
            us, utok = slabs.load(wbr[i], 0, 1024, s * 512, 512)
            BR, brn = BRp.get()
            fw.dma('act', [], [brn], BR[:], d['br'][i].rearrange("c p t -> p c t"))
            for tb in range(2):
                ts = tsl(tb)
                for cc in range(4):
                    n = s * 4 + cc
                    mtok = 'MGf%d_%d' % (cc, tb)
                    gp, gpn = nextps()
                    for c in range(16):
                        fw.op('pe', [gtok, 'H'], [gpn], lambda: nc.tensor.matmul(gp[:], gs[:, c, cc * 128:(cc + 1) * 128], H[:, c, ts], start=(c == 0), stop=(c == 15)), inc=(c == 15))
                    up, upn = nextps()
                    for c in range(8):
                        fw.op('pe', [utok, brn], [upn], lambda: nc.tensor.matmul(up[:], us[:, c, cc * 128:(cc + 1) * 128], BR[:, c, ts], start=(c == 0), stop=(c == 7)), inc=(c == 7))
                    SG, sgn = Tp.get()
                    fw.op('act', [gpn], [sgn], lambda: nc.scalar.activation(SG[:], gp[:], AF.Sigmoid))
                    if i == 0:
                        fw.op('dve', [upn, sgn], [mtok], lambda: nc.vector.tensor_tensor(MGF[cc][tb][:], up[:], SG[:], ALU.mult))
                    elif i == 1:
                        fw.op('dve', [upn, sgn], [sgn], lambda: nc.vector.tensor_tensor(SG[:], up[:], SG[:], ALU.mult))
                        fw.op('dve', [sgn, mtok], [mtok], lambda: nc.vector.tensor_tensor(MGF[cc][tb][:], MGF[cc][tb][:], SG[:], ALU.add))
                    else:
                        fw.op('dve', [upn, sgn], [sgn], lambda: nc.vector.tensor_tensor(SG[:], up[:], SG[:], ALU.mult))
                        fw.op('dve', [sgn, mtok], ['MG'], lambda: nc.vector.tensor_tensor(MG[:, n, ts], MGF[cc][tb][:], SG[:], ALU.add))
    for c in range(0, 16, 4):
        fw.dma('sp', ['MG'], [], d['mg'][c * 128:(c + 4) * 128, :].rearrange("(c p) t -> p c t", p=128), MG[:, c:c + 4, :])


def emit_mlp(nc, fw, d, ps):
    sb = nc.alloc_sbuf_tensor
    X = sb("X", [128, 16, TT], F32)
    H = sb("H", [128, 16, TT], BF16)
    RS = sb("RS", [128, TT], F32)
    MG = sb("MG", [128, 16, TT], BF16)
    ones = sb("pones", [128, 128], BF16)
    modc = sb("pmodc", [128, 6, 16], F32)
    gmlp = sb("pgmlp", [128, 16], F32)
    cols = sb("pcols", [128, 2, 16], F32)
    Pp = Pool_(nc, "pP", 4, [128, 512], BF16)
    Tp = Pool_(nc, "pT", 6, [128, 512], F32)
    slabs = Slabs(nc, fw, n=3)
    pslist = [(ps[i], f'ps{i}') for i in range(7)]
    pi = [0]

    def nextps():
        p = pslist[pi[0] % 7]
        pi[0] += 1
        return p

    for c in range(0, 16, 4):
        fw.dma('sp', [], ['X'], X[:, c:c + 4, :], d['xT'][c * 128:(c + 4) * 128, :].rearrange("(c p) t -> p c t", p=128))
        fw.dma('act', [], ['MG'], MG[:, c:c + 4, :], d['mg'][c * 128:(c + 4) * 128, :].rearrange("(c p) t -> p c t", p=128))
    fw.dma('act', [], ['ones'], ones[:], d['ones'])
    fw.dma('act', [], ['modc'], modc[:], d['modc'])
    fw.dma('act', [], ['gmlp'], gmlp[:], d['gmlp'])
    wout, w1, w2 = d['w_out'], d['w_mlp_in'], d['w_mlp_out']

    def resid_matmul(W, r0, gcolidx):
        for s in range(4):
            sl, stk = slabs.load(W, r0, 2048, s * 512, 512)
            for tb in range(2):
                ts = tsl(tb)
                for cc in range(4):
                    n = s * 4 + cc
                    p, pn = nextps()
                    for c in range(16):
                        fw.op('pe', [stk, 'MG'], [pn], lambda: nc.tensor.matmul(p[:], sl[:, c, cc * 128:(cc + 1) * 128], MG[:, c, ts], start=(c == 0), stop=(c == 15)), inc=(c == 15))
                    fw.op('dve', [pn, 'X', 'modc'], ['X'], lambda: nc.vector.scalar_tensor_tensor(X[:, n, ts], p[:], modc[:, gcolidx, n:n + 1], X[:, n, ts], ALU.mult, ALU.add))

    resid_matmul(wout, 0, 2)
    _common(nc, fw, d, ps, Tp, Pp, H, RS, ones, modc, gmlp, 'gmlp', cols, 4, 3, X)
    for g in range(4):
        for s in range(4):
            sl, stk = slabs.load(w1, 0, 2048, g * 2048 + s * 512, 512)
            for tb in range(2):
                ts = tsl(tb)
                for cc in range(4):
                    f = s * 4 + cc
                    p, pn = nextps()
                    for c in range(16):
                        fw.op('pe', [stk, 'H'], [pn], lambda: nc.tensor.matmul(p[:], sl[:, c, cc * 128:(cc + 1) * 128], H[:, c, ts], start=(c == 0), stop=(c == 15)), inc=(c == 15))
                    R, rn = Tp.get()
                    fw.op('act', [pn], [rn], lambda: nc.scalar.activation(R[:], p[:], AF.Relu))
                    fw.op('dve', [rn], ['MG'], lambda: nc.vector.tensor_tensor(MG[:, f, ts], R[:], R[:], ALU.mult))
        resid_matmul(w2, g * 2048, 5)
    for c in range(0, 16, 4):
        fw.dma('sp', ['X'], [], d['xo'][c * 128:(c + 4) * 128, :].rearrange("(c p) t -> p c t", p=128), X[:, c:c + 4, :])


def emit_mod(nc, fw, d, ps):
    sb = nc.alloc_sbuf_tensor
    cT = sb("cT_sb", [128, 16], F32)
    fw.dma('sp', [], ['cT'], cT[:], d['cT'])
    bia = sb("bia_sb", [1, 4 * 1536], F32)
    fw.dma('sp', [], ['bia'], bia[:], d['b'])
    res = sb("res_sb", [1, 4 * 1536], F32)
    wp = Pool_(nc, "wm", 2, [128, 16, 512], F32)
    k = 0
    for l in range(4):
        for s in range(3):
            W, wn = wp.get()
            for c in range(0, 16, 4):
                fw.dma('sp' if (k % 2 == 0) else 'act', [], [wn], W[:, c:c + 4, :], d['w'][l, c * 128:(c + 4) * 128, s * 512:(s + 1) * 512].rearrange("(c p) n -> p c n", p=128))
            p, pn = ps[k % 2], f'ps{k % 2}'
            for c in range(16):
                fw.op('pe', [wn, 'cT'], [pn], lambda: nc.tensor.matmul(p[0:1, :], cT[:, c:c + 1], W[:, c, :], start=(c == 0), stop=(c == 15)), inc=(c == 15))
            o0 = l * 1536 + s * 512
            fw.op('dve', [pn, 'bia'], ['res'], lambda: nc.vector.tensor_tensor(res[:, o0:o0 + 512], p[0:1, :], bia[:, o0:o0 + 512], ALU.add))
            k += 1
    fw.dma('sp', ['res'], [], d['out'], res[:])


def emit_dense(nc, fw, d, ps, with_proj):
    from contextlib import ExitStack
    sb = nc.alloc_sbuf_tensor
    H = sb("H", [128, 16, TT], BF16)
    RS = sb("RS", [128, TT], F32)
    ones = sb("pones", [128, 128], BF16)
    modc = sb("pmodc", [128, 6, 16], F32)
    gmlp = sb("pgmlp", [128, 16], F32)
    cols = sb("pcols", [128, 2, 16], F32)
    Pp = Pool_(nc, "pP", 4, [128, 512], BF16)
    Tp = Pool_(nc, "pT", 6, [128, 512], F32)
    slabs = Slabs(nc, fw, n=3)
    if with_proj:
        modc2 = sb("pmodc2", [128, 6, 16], F32)
        gmix2 = sb("pgmix2", [128, 16], F32)
        fw.dma('act', [], ['modc2'], modc2[:], d['modc2'])
        fw.dma('act', [], ['gmix2'], gmix2[:], d['gmix2'])
    pslist = [(ps[i], f'ps{i}') for i in range(7)]
    pi = [0]

    def nextps():
        p = pslist[pi[0] % 7]
        pi[0] += 1
        return p

    fw.dma('act', [], ['ones'], ones[:], d['ones'])
    fw.dma('act', [], ['modc'], modc[:], d['modc'])
    fw.dma('act', [], ['gmlp'], gmlp[:], d['gmlp'])
    for c in range(0, 16, 4):
        fw.dma('sp', [], ['H'], H[:, c:c + 4, :], d['hT'][c * 128:(c + 4) * 128, :].rearrange("(c p) t -> p c t", p=128))
    w_gate, wbr = d['w_gate'], d['w_branch']
    wout, w1, w2 = d['w_out'], d['w_mlp_in'], d['w_mlp_out']
    with nc.sbuf_tensor("MG", [128, 16, TT], BF16) as MG:
        with ExitStack() as st:
            MGF = [[st.enter_context(nc.sbuf_tensor(f"MGF{i}_{tb}", [128, 512], F32)) for tb in range(2)] for i in range(4)]
            BRt = [st.enter_context(nc.sbuf_tensor(f"BR{i}", [128, 8, TT], BF16)) for i in range(3)]
            for i in range(3):
                fw.dma('act', [], [f"BR{i}"], BRt[i][:], d['br'][i].rearrange("c p t -> p c t"))
            for s in range(4):
                for i in range(3):
                    gs, gtok = slabs.load(w_gate, 0, 2048, i * 2048 + s * 512, 512)
                    us, utok = slabs.load(wbr[i], 0, 1024, s * 512, 512)
                    BR, brn = BRt[i], f"BR{i}"
                    for tb in range(2):
                        ts = tsl(tb)
                        for cc in range(4):
                            n = s * 4 + cc
                            mtok = 'MGf%d_%d' % (cc, tb)
                            gp, gpn = nextps()
                            for c in range(16):
                                fw.op('pe', [gtok, 'H'], [gpn], lambda: nc.tensor.matmul(gp[:], gs[:, c, cc * 128:(cc + 1) * 128], H[:, c, ts], start=(c == 0), stop=(c == 15)), inc=(c == 15))
                            up, upn = nextps()
                            for c in range(8):
                                fw.op('pe', [utok, brn], [upn], lambda: nc.tensor.matmul(up[:], us[:, c, cc * 128:(cc + 1) * 128], BR[:, c, ts], start=(c == 0), stop=(c == 7)), inc=(c == 7))
                            SG, sgn = Tp.get()
                            fw.op('act', [gpn], [sgn], lambda: nc.scalar.activation(SG[:], gp[:], AF.Sigmoid))
                            if i == 0:
                                fw.op('dve', [upn, sgn], [mtok], lambda: nc.vector.tensor_tensor(MGF[cc][tb][:], up[:], SG[:], ALU.mult))
                            elif i == 1:
                                fw.op('dve', [upn, sgn], [sgn], lambda: nc.vector.tensor_tensor(SG[:], up[:], SG[:], ALU.mult))
                                fw.op('dve', [sgn, mtok], [mtok], lambda: nc.vector.tensor_tensor(MGF[cc][tb][:], MGF[cc][tb][:], SG[:], ALU.add))
                            else:
                                fw.op('dve', [upn, sgn], [sgn], lambda: nc.vector.tensor_tensor(SG[:], up[:], SG[:], ALU.mult))
                                fw.op('dve', [sgn, mtok], ['MG'], lambda: nc.vector.tensor_tensor(MG[:, n, ts], MGF[cc][tb][:], SG[:], ALU.add))
        fw.barrier()
        with nc.sbuf_tensor("X", [128, 16, TT], F32) as X:
            for c in range(0, 16, 4):
                fw.dma('sp', [], ['X'], X[:, c:c + 4, :], d['xT'][c * 128:(c + 4) * 128, :].rearrange("(c p) t -> p c t", p=128))

            def resid_matmul(W, r0, gcolidx):
                for s in range(4):
                    sl, stk = slabs.load(W, r0, 2048, s * 512, 512)
                    for tb in range(2):
                        ts = tsl(tb)
                        for cc in range(4):
                            n = s * 4 + cc
                            p, pn = nextps()
                            for c in range(16):
                                fw.op('pe', [stk, 'MG'], [pn], lambda: nc.tensor.matmul(p[:], sl[:, c, cc * 128:(cc + 1) * 128], MG[:, c, ts], start=(c == 0), stop=(c == 15)), inc=(c == 15))
                            fw.op('dve', [pn, 'X', 'modc'], ['X'], lambda: nc.vector.scalar_tensor_tensor(X[:, n, ts], p[:], modc[:, gcolidx, n:n + 1], X[:, n, ts], ALU.mult, ALU.add))

            resid_matmul(wout, 0, 2)
            _common(nc, fw, d, ps, Tp, Pp, H, RS, ones, modc, gmlp, 'gmlp', cols, 4, 3, X)
            for g in range(4):
                for s in range(4):
                    sl, stk = slabs.load(w1, 0, 2048, g * 2048 + s * 512, 512)
                    for tb in range(2):
                        ts = tsl(tb)
                        for cc in range(4):
                            f = s * 4 + cc
                            p, pn = nextps()
                            for c in range(16):
                                fw.op('pe', [stk, 'H'], [pn], lambda: nc.tensor.matmul(p[:], sl[:, c, cc * 128:(cc + 1) * 128], H[:, c, ts], start=(c == 0), stop=(c == 15)), inc=(c == 15))
                            R, rn = Tp.get()
                            fw.op('act', [pn], [rn], lambda: nc.scalar.activation(R[:], p[:], AF.Relu))
                            fw.op('dve', [rn], ['MG'], lambda: nc.vector.tensor_tensor(MG[:, f, ts], R[:], R[:], ALU.mult))
                resid_matmul(w2, g * 2048, 5)
            for c in range(0, 16, 4):
                fw.dma('sp', ['X'], [], d['xo'][c * 128:(c + 4) * 128, :].rearrange("(c p) t -> p c t", p=128), X[:, c:c + 4, :], semtok='xout')
            if with_proj:
                _common(nc, fw, d, ps, Tp, Pp, H, RS, ones, modc2, gmix2, 'gmix2', cols, 1, 0, X, mtok='modc2')
                for c in range(0, 16, 4):
                    fw.dma('sp', ['H'], [], d['hTo'][c * 128:(c + 4) * 128, :].rearrange("(c p) t -> p c t", p=128), H[:, c:c + 4, :], semtok='hout')
        fw.barrier()
    if with_proj:
        emit_proj_body(nc, fw, d, ps, H, ones, Pp, Tp, slabs)


bf = ml_dtypes.bfloat16
NCORES = 8
S = 8192
DEBUG = False
_progs = {}


def _mk(name, builder):
    if name not in _progs:
        _progs[name] = builder()
    return _progs[name]


def _din(nc, d, name, shape, dt):
    d[name] = nc.dram_tensor(name, list(shape), dt, kind="ExternalInput").ap()


def _dout(nc, d, name, shape, dt):
    d[name] = nc.dram_tensor(name, list(shape), dt, kind="ExternalOutput").ap()


def _psum(nc):
    return [nc.alloc_psum_tensor(f"ps{i}", [128, 512], F32) for i in range(8)]


def build_mod():
    nc = bass.Bass("TRN2", target_bir_lowering=False)
    fw = FW(nc)
    d = {}
    _din(nc, d, 'cT', (128, 16), F32)
    _din(nc, d, 'b', (1, 6144), F32)
    _din(nc, d, 'w', (4, 2048, 1536), F32)
    _dout(nc, d, 'out', (1, 6144), F32)
    emit_mod(nc, fw, d, _psum(nc))
    fw.finish_all()
    return nc


def build_proj():
    nc = bass.Bass("TRN2", target_bir_lowering=False)
    fw = FW(nc)
    d = {}
    _din(nc, d, 'xT', (2048, TT), F32)
    _din(nc, d, 'w_in', (2048, 6976), F32)
    for n, s, t in [('ones', (128, 128), BF16), ('bd64', (128, 128), BF16), ('rotm', (64, 64), BF16), ('modc', (128, 6, 16), F32),
                    ('gmix', (128, 16), F32), ('dqk', (128, 2), F32), ('gq', (128, 4), F32), ('gkv', (128, 2), F32), ('mqk', (128, 4), F32),
                    ('cosT', (64, TT), F32), ('sinT', (64, TT), F32), ('wqup', (512, 1536), F32), ('wkvup', (256, 2048), F32)]:
        _din(nc, d, n, s, t)
    for n in ['qa', 'ka', 'qbn', 'kbn', 'qc', 'kc']:
        _dout(nc, d, n, (8, 128, TT), BF16)
    _dout(nc, d, 'qbr', (8, 64, TT), BF16)
    _dout(nc, d, 'kbr', (64, TT), BF16)
    for n in ['va', 'vb', 'vc']:
        _dout(nc, d, n, (TT, 1024), BF16)
    _dout(nc, d, 'hT', (2048, TT), BF16)
    emit_proj(nc, fw, d, _psum(nc))
    fw.finish_all()
    return nc


def build_attn():
    nc = bass.Bass("TRN2", target_bir_lowering=False)
    fw = FW(nc)
    d = {}
    for n in ['qa', 'ka', 'qbn', 'kbn', 'qc', 'kc']:
        _din(nc, d, n, (128, S), BF16)
    for n in ['qbr', 'kbr']:
        _din(nc, d, n, (64, S), BF16)
    for n in ['va', 'vb', 'vc']:
        _din(nc, d, n, (S, 128), BF16)
    for n, s, t in [('ones', (128, 128), BF16), ('mtri', (128, 128), BF16), ('biasA', (5, 128, 512), F32), ('maskB', (4, 128, 512), F32),
                    ('maskC', (4, 128, 512), BF16), ('ident', (128, 128), BF16), ('lamp', (128, 256), F32), ('cst', (128, 4), F32), ('b15', (128, 1), F32), ('gsub', (128, 1), F32)]:
        _din(nc, d, n, s, t)
    for n in ['oa', 'ob', 'oc']:
        _dout(nc, d, n, (128, S), BF16)
    psz = [nc.alloc_psum_tensor(f"psz{i}", [128, 1024], F32) for i in range(2)]
    ps = [psz[0][:, 0:512], psz[0][:, 512:1024], psz[1][:, 0:512], psz[1][:, 512:1024]] + [nc.alloc_psum_tensor(f"ps{i}", [128, 512], F32) for i in range(4, 8)]
    emit_attn(nc, fw, S, d, ps, psz)
    fw.finish_all()
    return nc


def build_merge():
    nc = bass.Bass("TRN2", target_bir_lowering=False)
    fw = FW(nc)
    d = {}
    _din(nc, d, 'hT', (2048, TT), BF16)
    _din(nc, d, 'w_in', (2048, 6144), F32)
    _din(nc, d, 'w_branch', (3, 1024, 2048), F32)
    _din(nc, d, 'br', (3, 8, 128, TT), BF16)
    _dout(nc, d, 'mg', (2048, TT), BF16)
    emit_merge(nc, fw, d, _psum(nc))
    fw.finish_all()
    return nc


def build_mlp():
    nc = bass.Bass("TRN2", target_bir_lowering=False)
    fw = FW(nc)
    d = {}
    _din(nc, d, 'xT', (2048, TT), F32)
    _din(nc, d, 'mg', (2048, TT), BF16)
    _din(nc, d, 'w_out', (2048, 2048), F32)
    _din(nc, d, 'w_mlp_in', (2048, 8192), F32)
    _din(nc, d, 'w_mlp_out', (8192, 2048), F32)
    for n, s, t in [('ones', (128, 128), BF16), ('modc', (128, 6, 16), F32), ('gmlp', (128, 16), F32)]:
        _din(nc, d, n, s, t)
    _dout(nc, d, 'xo', (2048, TT), F32)
    emit_mlp(nc, fw, d, _psum(nc))
    fw.finish_all()
    return nc


def build_dense(with_proj):
    nc = bass.Bass("TRN2", target_bir_lowering=False)
    fw = FW(nc)
    d = {}
    _din(nc, d, 'hT', (2048, TT), BF16)
    _din(nc, d, 'xT', (2048, TT), F32)
    _din(nc, d, 'w_gate', (2048, 6144), F32)
    _din(nc, d, 'w_branch', (3, 1024, 2048), F32)
    _din(nc, d, 'br', (3, 8, 128, TT), BF16)
    _din(nc, d, 'w_out', (2048, 2048), F32)
    _din(nc, d, 'w_mlp_in', (2048, 8192), F32)
    _din(nc, d, 'w_mlp_out', (8192, 2048), F32)
    for n, s, t in [('ones', (128, 128), BF16), ('modc', (128, 6, 16), F32), ('gmlp', (128, 16), F32)]:
        _din(nc, d, n, s, t)
    _dout(nc, d, 'xo', (2048, TT), F32)
    if with_proj:
        _din(nc, d, 'w_in', (2048, 6976), F32)
        for n, s, t in [('modc2', (128, 6, 16), F32), ('gmix2', (128, 16), F32), ('bd64', (128, 128), BF16), ('rotm', (64, 64), BF16),
                        ('dqk', (128, 2), F32), ('gq', (128, 4), F32), ('gkv', (128, 2), F32), ('mqk', (128, 4), F32),
                        ('cosT', (64, TT), F32), ('sinT', (64, TT), F32), ('wqup', (512, 1536), F32), ('wkvup', (256, 2048), F32)]:
            _din(nc, d, n, s, t)
        for n in ['qa', 'ka', 'qbn', 'kbn', 'qc', 'kc']:
            _dout(nc, d, n, (8, 128, TT), BF16)
        _dout(nc, d, 'qbr', (8, 64, TT), BF16)
        _dout(nc, d, 'kbr', (64, TT), BF16)
        for n in ['va', 'vb', 'vc']:
            _dout(nc, d, n, (TT, 1024), BF16)
        _dout(nc, d, 'hTo', (2048, TT), BF16)
    emit_dense(nc, fw, d, _psum(nc), with_proj)
    fw.finish_all()
    return nc


def colvec(v, n):
    return np.ascontiguousarray(np.asarray(v, np.float32).reshape(n, 128).T)


def run(nc, in_maps):
    t = time.time()
    res = run_bass_kernel_spmd(nc, in_maps, core_ids=list(range(NCORES)))
    if DEBUG:
        print("  launch %.1fs" % (time.time() - t), flush=True)
    return res.results


def kernel(x, c, w_ada, b_ada, norm_mix_g, norm_mlp_g, w_in, diff_qk_g, diff_lambda, diff_subln_g, t5_bias,
           mla_q_norm_g, mla_kv_norm_g, w_q_up, w_kv_up, mla_qk_g, w_branch, w_out, w_mlp_in, w_mlp_out):
    f32 = np.float32
    x = np.asarray(x, f32)
    AC = attn_consts()
    PC = [proj_consts(i) for i in range(NCORES)]
    ones = AC['ones']
    nc_mod = _mk('mod', build_mod)
    cT = colvec(np.asarray(c, f32)[0], 16)
    w_ada = np.asarray(w_ada, f32)
    b_ada = np.asarray(b_ada, f32)
    ins = []
    for i in range(NCORES):
        ins.append(dict(cT=cT, b=np.ascontiguousarray(b_ada[:, 1536 * i:1536 * (i + 1)]).reshape(1, 6144),
                        w=np.ascontiguousarray(w_ada[:, :, 1536 * i:1536 * (i + 1)])))
    r = run(nc_mod, ins)
    mod = np.concatenate([r[i]['out'].reshape(4, 1536) for i in range(NCORES)], axis=1)
    if DEBUG:
        ref = np.asarray(c, f32) @ w_ada[0] + b_ada[0]
        print("mod err", np.abs(mod[0] - ref[0]).max(), np.abs(ref).max())

    xT = [np.ascontiguousarray(x[0, i * TT:(i + 1) * TT].T) for i in range(NCORES)]
    nc_proj = _mk('proj', build_proj)
    nc_attn = _mk('attn', build_attn)
    nc_dense = _mk('dense', lambda: build_dense(True))
    nc_last = _mk('dense_last', lambda: build_dense(False))
    t5 = np.asarray(t5_bias, f32)

    def proj_params(l):
        wl = np.asarray(w_in[l], f32)
        dq = np.asarray(diff_qk_g[l], f32)
        mg_ = np.asarray(mla_qk_g[l], f32)
        mqk = np.zeros((128, 4), f32)
        mqk[:, 0] = mg_[0, :128]
        mqk[:, 1] = mg_[1, :128]
        mqk[:64, 2] = mg_[0, 128:]
        mqk[:64, 3] = mg_[1, 128:]
        wq = np.asarray(w_q_up[l], f32).reshape(512, 8, 192)
        wkv = np.asarray(w_kv_up[l], f32).reshape(256, 8, 256)
        return dict(w_in=np.ascontiguousarray(wl[:, :6976]),
                    dqk=np.stack([np.tile(dq[0], 2), np.tile(dq[1], 2)], axis=1),
                    gq=colvec(mla_q_norm_g[l], 4), gkv=colvec(mla_kv_norm_g[l], 2), mqk=mqk,
                    wqup=np.ascontiguousarray(np.concatenate([wq[:, :, :128].reshape(512, 1024), wq[:, :, 128:].reshape(512, 512)], axis=1)),
                    wkvup=np.ascontiguousarray(np.concatenate([wkv[:, :, :128].reshape(256, 1024), wkv[:, :, 128:].reshape(256, 1024)], axis=1)))

    def modcols(l):
        return np.ascontiguousarray(mod[l].reshape(6, 16, 128).transpose(2, 0, 1))

    pp = proj_params(0)
    ins = []
    for i in range(NCORES):
        ins.append(dict(xT=xT[i], ones=ones, bd64=PC[i]['bd64'], rotm=PC[i]['rotm'], modc=modcols(0), gmix=colvec(norm_mix_g[0], 16),
                        cosT=PC[i]['cosT'], sinT=PC[i]['sinT'], **pp))
    rp = run(nc_proj, ins)
    for l in range(4):
        lam_init = 0.8 - 0.6 * np.exp(-0.3 * l)
        cst = np.broadcast_to(np.array([lam_init, 1 - lam_init, 0, 0], f32), (128, 4)).copy()
        lamp = np.broadcast_to(np.asarray(diff_lambda[l], f32).reshape(1, 256), (128, 256)).copy()
        gsub = np.asarray(diff_subln_g[l], f32).reshape(128, 1).copy()
        kbr = np.concatenate([rp[i]['kbr'] for i in range(NCORES)], axis=1)
        ins = []
        for h in range(NCORES):
            dd = dict(ones=ones, mtri=AC['mtri'], biasA=np.ascontiguousarray(t5[AC['bucket'], h]), maskB=AC['maskB'], maskC=AC['maskC'], ident=AC['ident'],
                      lamp=lamp, cst=cst, b15=np.full((128, 1), t5[15, h], f32), gsub=gsub, kbr=kbr)
            for n in ['qa', 'ka', 'qbn', 'kbn', 'qc', 'kc', 'qbr']:
                dd[n] = np.concatenate([rp[i][n][h] for i in range(NCORES)], axis=1)
            for n in ['va', 'vb', 'vc']:
                dd[n] = np.concatenate([rp[i][n][:, 128 * h:128 * (h + 1)] for i in range(NCORES)], axis=0)
            ins.append(dd)
        ra = run(nc_attn, ins)
        wl = np.asarray(w_in[l], f32)
        w_gate = np.ascontiguousarray(wl[:, 6976:])
        last = (l == 3)
        pp = {} if last else proj_params(l + 1)
        ins = []
        for i in range(NCORES):
            br = np.stack([np.stack([ra[h][n][:, i * TT:(i + 1) * TT] for h in range(NCORES)]) for n in ['oa', 'ob', 'oc']])
            dd = dict(hT=rp[i]['hT'], xT=xT[i], w_gate=w_gate, w_branch=np.asarray(w_branch[l], f32), br=br,
                      w_out=np.asarray(w_out[l], f32), w_mlp_in=np.asarray(w_mlp_in[l], f32), w_mlp_out=np.asarray(w_mlp_out[l], f32),
                      ones=ones, modc=modcols(l), gmlp=colvec(norm_mlp_g[l], 16))
            if not last:
                dd.update(dict(modc2=modcols(l + 1), gmix2=colvec(norm_mix_g[l + 1], 16), bd64=PC[i]['bd64'], rotm=PC[i]['rotm'],
                               cosT=PC[i]['cosT'], sinT=PC[i]['sinT'], **pp))
            ins.append(dd)
        rd = run(nc_last if last else nc_dense, ins)
        xT = [rd[i]['xo'] for i in range(NCORES)]
        if not last:
            rp = [dict(rd[i], hT=rd[i]['hTo']) for i in range(NCORES)]
    out = np.concatenate([xT[i].T for i in range(NCORES)], axis=0)[None]
    return np.ascontiguousarray(out.astype(f32))


def _rms(x, g):
    return x / np.sqrt((x * x).mean(-1, keepdims=True) + 1e-6) * g


def _err(name, got, ref):
    got = np.asarray(got).astype(np.float64)
    ref = np.asarray(ref).astype(np.float64)
    print("  DBG %-6s rel %.2e  max %.3e / %.3e" % (name, np.sqrt(((got - ref) ** 2).mean() / ((ref ** 2).mean() + 1e-30)), np.abs(got - ref).max(), np.abs(ref).max()), flush=True)


def debug_proj(x, mod, norm_mix_g, wl, dq, rp, mla_q_norm_g, mla_kv_norm_g, w_q_up, w_kv_up, mla_qk_g):
    T0 = 128
    xs = x[0, :T0].astype(np.float64)
    sh1, sc1 = mod[0, :2048], mod[0, 2048:4096]
    h = _rms(xs, np.asarray(norm_mix_g[0], np.float64)) * (1 + sc1) + sh1
    pr = h @ wl.astype(np.float64)
    r = rp[0]
    f = lambda a: np.asarray(a).astype(np.float32)
    qa = _rms(pr[:, 0:1024].reshape(T0, 8, 2, 64), dq[0]).reshape(T0, 8, 128)
    _err('qa', f(r['qa'])[:, :, :T0].transpose(2, 0, 1), qa)
    ka = _rms(pr[:, 1024:2048].reshape(T0, 8, 2, 64), dq[1]).reshape(T0, 8, 128)
    _err('ka', f(r['ka'])[:, :, :T0].transpose(2, 0, 1), ka)
    _err('va', f(r['va'])[:T0], pr[:, 2048:3072])
    _err('qc', f(r['qc'])[:, :, :T0].transpose(2, 0, 1), pr[:, 3904:4928].reshape(T0, 8, 128) * 128 ** -0.5)
    _err('kc', f(r['kc'])[:, :, :T0].transpose(2, 0, 1), pr[:, 4928:5952].reshape(T0, 8, 128))
    _err('vc', f(r['vc'])[:T0], pr[:, 5952:6976])
    cq = _rms(pr[:, 3072:3584], np.asarray(mla_q_norm_g[0], np.float64))
    ckv = _rms(pr[:, 3584:3840], np.asarray(mla_kv_norm_g[0], np.float64))
    kpe = pr[:, 3840:3904]
    q = (cq @ np.asarray(w_q_up[0], np.float64)).reshape(T0, 8, 192)
    kv = (ckv @ np.asarray(w_kv_up[0], np.float64)).reshape(T0, 8, 256)
    g = np.asarray(mla_qk_g[0], np.float64)
    pos = np.arange(T0)
    inv = 10000.0 ** (-np.arange(32) / 32)
    ang = pos[:, None] * inv[None]
    cos, sin = np.cos(ang)[:, None], np.sin(ang)[:, None]

    def rope(v):
        x1, x2 = v[..., :32], v[..., 32:]
        return np.concatenate([x1 * cos - x2 * sin, x1 * sin + x2 * cos], -1)
    _err('qbn', f(r['qbn'])[:, :, :T0].transpose(2, 0, 1), _rms(q[..., :128], g[0, :128]))
    _err('qbr', f(r['qbr'])[:, :, :T0].transpose(2, 0, 1), rope(_rms(q[..., 128:], g[0, 128:])))
    _err('kbn', f(r['kbn'])[:, :, :T0].transpose(2, 0, 1), _rms(kv[..., :128], g[1, :128]))
    _err('kbr', f(r['kbr'])[:, :T0].T, rope(_rms(kpe.reshape(T0, 1, 64), g[1, 128:]))[:, 0])
    _err('vb', f(r['vb'])[:T0].reshape(T0, 8, 128), kv[..., 128:])


def debug_post(x, mod, norm_mix_g, norm_mlp_g, wl, w_branch, w_out, w_mlp_in, w_mlp_out, ra, rm, rx):
    T0 = 128
    f64 = np.float64
    xs = x[0, :T0].astype(f64)
    sh1, sc1, g1, sh2, sc2, g2 = [mod[0, i * 2048:(i + 1) * 2048].astype(f64) for i in range(6)]
    h = _rms(xs, np.asarray(norm_mix_g[0], f64)) * (1 + sc1) + sh1
    gates = h @ wl[:, 6976:].astype(f64)
    br = np.stack([np.concatenate([np.asarray(ra[hh][n]).astype(np.float32)[:, :T0].T for hh in range(8)], axis=1) for n in ['oa', 'ob', 'oc']]).astype(f64)
    merged = 0
    for i in range(3):
        up = br[i] @ np.asarray(w_branch[0][i], f64)
        merged = merged + 1 / (1 + np.exp(-gates[:, i * 2048:(i + 1) * 2048])) * up
    _err('mg', np.asarray(rm[0]['mg']).astype(np.float32)[:, :T0].T, merged)
    x1 = xs + g1 * (merged @ np.asarray(w_out[0], f64))
    h2 = _rms(x1, np.asarray(norm_mlp_g[0], f64)) * (1 + sc2) + sh2
    hid = np.maximum(h2 @ np.asarray(w_mlp_in[0], f64), 0) ** 2
    x2 = x1 + g2 * (hid @ np.asarray(w_mlp_out[0], f64))
    _err('xo', rx[0]['xo'][:, :T0].T, x2)
    _err('dx', rx[0]['xo'][:, :T0].T - xs, x2 - xs)
```

```python
import os, time
import numpy as np
import ml_dtypes
import concourse.bass as bass
import concourse.mybir as mybir
from concourse.bass_utils import run_bass_kernel_spmd

F32 = mybir.dt.float32
BF16 = mybir.dt.bfloat16
AF = mybir.ActivationFunctionType
ALU = mybir.AluOpType


class FW:
    def __init__(self, nc):
        self.nc = nc
        self.eng = {'pe': nc.tensor, 'act': nc.scalar, 'dve': nc.vector, 'pool': nc.gpsimd, 'sp': nc.sync}
        self.sem = {}
        self.cnt = {}
        for k in ['pe', 'act', 'dve', 'pool']:
            self.sem[k] = nc.alloc_semaphore(name='s_' + k)
            self.cnt[k] = 0
        self.seen = {k: {} for k in self.eng}
        self.lastw = {}
        self.reads = {}
        self.dsem = {}

    def _wait(self, e, ev):
        if ev is None:
            return
        k, v = ev
        if e == 'pe' and k == 'pe':
            return
        if self.seen[e].get(k, 0) >= v:
            return
        self.eng[e].wait_ge(self.sem[k], v)
        self.seen[e][k] = v

    def deps(self, e, R, W):
        for t in R:
            self._wait(e, self.lastw.get(t))
        for t in W:
            self._wait(e, self.lastw.get(t))
            for ev in self.reads.get(t, []):
                self._wait(e, ev)

    def commit(self, ev, R, W):
        for t in R:
            self.reads.setdefault(t, []).append(ev)
        for t in W:
            self.lastw[t] = ev
            self.reads[t] = []

    def op(self, e, R, W, fn, inc=True):
        self.deps(e, R, W)
        ins = fn()
        if inc:
            self.cnt[e] += 1
            ins.then_inc(self.sem[e], 1)
            ev = (e, self.cnt[e])
        else:
            ev = (e, self.cnt[e] + 1)
        self.commit(ev, R, W)
        return ins

    def dma(self, q, R, W, out, in_, semtok=None, **kw):
        st = semtok if semtok is not None else (W[0] if W else R[0])
        if st not in self.dsem:
            key = 'd_' + str(st)
            self.sem[key] = self.nc.alloc_semaphore(name=key)
            self.cnt[key] = 0
            self.dsem[st] = key
        key = self.dsem[st]
        self.deps(q, R, W)
        ins = self.eng[q].dma_start(out=out, in_=in_, **kw)
        self.cnt[key] += 16
        ins.then_inc(self.sem[key], 16)
        ev = (key, self.cnt[key])
        self.commit(ev, R, W)
        return ins

    def finish(self, toks, e='sp'):
        for t in toks:
            self._wait(e, self.lastw.get(t))

    def barrier(self):
        for e in self.eng:
            for k in self.sem:
                if self.cnt[k] > 0 and not (e == 'pe' and k == 'pe') and self.seen[e].get(k, 0) < self.cnt[k]:
                    self.eng[e].wait_ge(self.sem[k], self.cnt[k])
                    self.seen[e][k] = self.cnt[k]

    def finish_all(self, e='sp'):
        for k in self.sem:
            if k.startswith('d_') and self.cnt[k] > 0 and self.seen[e].get(k, 0) < self.cnt[k]:
                self.eng[e].wait_ge(self.sem[k], self.cnt[k])
                self.seen[e][k] = self.cnt[k]

    def finish_dma(self, semtok, e='sp'):
        if semtok not in self.dsem:
            return
        key = self.dsem[semtok]
        self.eng[e].wait_ge(self.sem[key], self.cnt[key])


AX = mybir.AxisListType
NEG = -1.0e30
EPS = 1e-6


class Pool_:
    def __init__(self, nc, name, n, shape, dt):
        self.tiles = [nc.alloc_sbuf_tensor(f"{name}{i}", shape, dt) for i in range(n)]
        self.names = [f"{name}{i}" for i in range(n)]
        self.i = 0

    def get(self):
        t, n = self.tiles[self.i], self.names[self.i]
        self.i = (self.i + 1) % len(self.tiles)
        return t, n


def emit_attn(nc, fw, S, d, ps, psz_in=None):
    NQB = S // 512
    NKT = S // 128
    sb = nc.alloc_sbuf_tensor
    Q1 = sb("aQ1", [128, S], BF16)
    Q2 = sb("aQ2", [128, S], BF16)
    K1 = sb("aK1", [128, S], BF16)
    K2 = sb("aK2", [128, S], BF16)
    K3 = sb("aK3", [128, S], BF16)
    V = sb("aV", [128, NKT, 128], BF16)
    OUT = sb("aOUT", [128, S], BF16)
    ones = sb("aones", [128, 128], BF16)
    mtri = sb("amtri", [128, 128], BF16)
    mones = sb("amones", [128, 128], BF16)
    biasA = sb("abiasA", [128, 5, 512], F32)
    maskB = sb("amaskB", [128, 4, 512], F32)
    maskC = sb("amaskC", [128, 4, 512], BF16)
    ident = sb("aident", [128, 128], BF16)
    lamp = sb("alamp", [128, 256], F32)
    cst = sb("acst", [128, 4], F32)
    b15 = sb("ab15", [128, 1], F32)
    gsub = sb("agsub", [128, 1], F32)
    sm = sb("asm", [128, 16], F32)
    carry = sb("acarry", [128, 512], F32)
    carry2 = sb("acarry2", [128, 512], F32)
    Pp = Pool_(nc, "aP", 6, [128, 512], BF16)
    Tp = Pool_(nc, "aT", 6, [128, 512], F32)
    Fp = Pool_(nc, "aF", 8, [128, 512], F32)
    E2p = Pool_(nc, "aE2", 5, [128, 1024], F32)
    L2p = Pool_(nc, "aL2", 3, [128, 1024], BF16)
    W2p = Pool_(nc, "aW2", 3, [128, 1024], BF16)
    psz = psz_in

    fw.op('pool', [], ['Q2z'], lambda: nc.gpsimd.memset(Q2[64:128, :], 0.0))
    fw.op('pool', [], ['K2z'], lambda: nc.gpsimd.memset(K2[64:128, :], 0.0))
    fw.op('pool', [], ['K3z'], lambda: nc.gpsimd.memset(K3[0:64, :], 0.0))
    fw.dma('sp', [], ['ones'], ones[:], d['ones'])
    fw.dma('sp', [], ['mtri'], mtri[:], d['mtri'])
    fw.dma('sp', [], ['biasA'], biasA[:], d['biasA'].rearrange("o p q -> p o q"))
    fw.dma('sp', [], ['maskB'], maskB[:], d['maskB'].rearrange("o p q -> p o q"))
    fw.dma('sp', [], ['maskC'], maskC[:], d['maskC'].rearrange("o p q -> p o q"))
    fw.dma('sp', [], ['ident'], ident[:], d['ident'])
    fw.dma('sp', [], ['lamp'], lamp[:], d['lamp'])
    fw.dma('sp', [], ['cst'], cst[:], d['cst'])
    fw.dma('sp', [], ['b15'], b15[:], d['b15'])
    fw.dma('sp', [], ['gsub'], gsub[:], d['gsub'])
    for o in range(4):
        fw.op('dve', ['biasA', 'maskB'], ['biasA'], lambda: nc.vector.tensor_tensor(biasA[:, o + 1, :], biasA[:, o + 1, :], maskB[:, o, :], ALU.add))
    T0, t0n = Tp.get()
    fw.op('dve', ['lamp'], [t0n], lambda: nc.vector.tensor_tensor(T0[:, 0:64], lamp[:, 0:64], lamp[:, 64:128], ALU.mult))
    fw.op('dve', [t0n], ['sm'], lambda: nc.vector.reduce_sum(sm[:, 0:1], T0[:, 0:64], AX.X))
    fw.op('dve', ['lamp'], [t0n], lambda: nc.vector.tensor_tensor(T0[:, 0:64], lamp[:, 128:192], lamp[:, 192:256], ALU.mult))
    fw.op('dve', [t0n], ['sm'], lambda: nc.vector.reduce_sum(sm[:, 1:2], T0[:, 0:64], AX.X))
    fw.op('act', ['sm'], ['sm'], lambda: nc.scalar.activation(sm[:, 2:4], sm[:, 0:2], AF.Exp))
    fw.op('dve', ['sm'], ['sm'], lambda: nc.vector.tensor_tensor(sm[:, 4:5], sm[:, 3:4], sm[:, 2:3], ALU.subtract))
    fw.op('dve', ['sm', 'cst'], ['sm'], lambda: nc.vector.tensor_tensor(sm[:, 4:5], sm[:, 4:5], cst[:, 0:1], ALU.subtract))
    fw.op('dve', ['gsub', 'cst'], ['sm'], lambda: nc.vector.tensor_tensor(sm[:, 5:6], gsub[:], cst[:, 1:2], ALU.mult))
    neglam = sm[:, 4:5]
    gcolA = sm[:, 5:6]

    def load_qkv(q1, q2, k1, k2, v):
        fw.dma('sp', [], ['Q1'], Q1[:], q1)
        fw.dma('act', [], ['K1'], K1[:], k1)
        if q2 is not None:
            fw.dma('sp', [], ['Q2'], Q2[0:64, :], q2)
            fw.dma('act', [], ['K2'], K2[0:64, :], k2)
        for c in range(0, NKT, 16):
            n = min(16, NKT - c)
            fw.dma('sp', [], ['V'], V[:, c:c + n, :], v[c * 128:(c + n) * 128, :].rearrange("(t p) e -> p t e", p=128))

    def store_out(dst, qb):
        fw.dma('sp', [f'out{qb}'], [], dst[:, qb * 512:(qb + 1) * 512], OUT[:, qb * 512:(qb + 1) * 512])

    def softmax_mixer(kind):
        nm = 2 if kind == 'A' else 1
        scale = 64 ** -0.5 if kind == 'A' else 192 ** -0.5
        if kind == 'A':
            Sb = [[ps[0], ps[1]], [ps[2], ps[3]]]
            Sn = [['ps0', 'ps1'], ['ps2', 'ps3']]
            NB, LA = 2, 1
        else:
            Sb = [[ps[0], ps[1], ps[2], ps[3]]]
            Sn = [['ps0', 'ps1', 'ps2', 'ps3']]
            NB, LA = 4, 2
        Ob = [ps[4], ps[5]]; On = ['ps4', 'ps5']
        Db = [ps[6], ps[7]]; Dn = ['ps6', 'ps7']

        deferred = []
        for qb in range(NQB):
            q0 = qb * 512
            nk = 4 * (qb + 1)
            qs = slice(q0, q0 + 512)

            def emit_S(kt):
                ks = slice(kt * 128, kt * 128 + 128)
                b = kt % NB
                for m in range(nm):
                    if kind == 'A':
                        KA, katoks = (K2, ['K2', 'K2z']) if m == 0 else (K3, ['K3', 'K3z'])
                        fw.op('pe', katoks + ['Q1'], [Sn[m][b]], lambda: nc.tensor.matmul(Sb[m][b][:], KA[:, ks], Q1[:, qs], start=True, stop=True))
                    else:
                        fw.op('pe', ['K1', 'Q1'], [Sn[m][b]], lambda: nc.tensor.matmul(Sb[m][b][:], K1[:, ks], Q1[:, qs], start=True, stop=False), inc=False)
                        fw.op('pe', ['K2', 'Q2', 'K2z', 'Q2z'], [Sn[m][b]], lambda: nc.tensor.matmul(Sb[m][b][:], K2[:, ks], Q2[:, qs], start=False, stop=True))

            for kk in range(min(LA, nk)):
                emit_S(kk)
            for kt in range(nk):
                b = kt % NB
                o = kt * 128 - q0
                Pt = []
                for m in range(nm):
                    P, pn = Pp.get()
                    near = ((o >= -128) if kind == 'A' else (o >= 0))
                    if near:
                        oi = o // 128 + 1
                        T, tn = Tp.get()
                        btile = biasA[:, oi, :] if kind == 'A' else maskB[:, oi - 1, :]
                        btok = 'biasA' if kind == 'A' else 'maskB'
                        fw.op('dve', [Sn[m][b], btok], [tn], lambda: nc.vector.scalar_tensor_tensor(T[:], Sb[m][b][:], scale, btile, ALU.mult, ALU.add))
                        fw.op('act', [tn], [pn], lambda: nc.scalar.activation(P[:], T[:], AF.Exp))
                    else:
                        if kind == 'A':
                            fw.op('act', [Sn[m][b], 'b15'], [pn], lambda: nc.scalar.activation(P[:], Sb[m][b][:], AF.Exp, bias=b15[:], scale=scale))
                        else:
                            fw.op('act', [Sn[m][b]], [pn], lambda: nc.scalar.activation(P[:], Sb[m][b][:], AF.Exp, scale=scale))
                    Pt.append((P, pn))
                if kt == 1:
                    while deferred:
                        deferred.pop(0)()
                if kt + LA < nk:
                    emit_S(kt + LA)
                for m in range(nm):
                    P, pn = Pt[m]
                    fw.op('pe', ['V', pn], [On[m]], lambda: nc.tensor.matmul(Ob[m][:], V[:, kt, :], P[:], start=(kt == 0), stop=(kt == nk - 1)), inc=False)
                    fw.op('pe', ['ones', pn], [Dn[m]], lambda: nc.tensor.matmul(Db[m][:], ones[:], P[:], start=(kt == 0), stop=(kt == nk - 1)))
            if kind == 'B':
                R, rn = Tp.get()
                fw.op('dve', [Dn[0]], [rn], lambda: nc.vector.reciprocal(R[:], Db[0][:]))
                fw.op('dve', [On[0], rn], [f'out{qb}'], lambda: nc.vector.tensor_tensor(OUT[:, qs], Ob[0][:], R[:], ALU.mult))
                store_out(d['ob'], qb)
            else:
                ev = []
                for m in range(2):
                    R, rn = Fp.get()
                    fw.op('dve', [Dn[m]], [rn], lambda: nc.vector.reciprocal(R[:], Db[m][:]))
                    C, cn = Fp.get()
                    fw.op('act', [On[m]], [cn], lambda: nc.scalar.copy(C[:], Ob[m][:]))
                    ev.append((R, rn, C, cn))

                def tail(ev=ev, qb=qb, qs=qs):
                    (R1, r1n, C1, c1n), (R2, r2n, C2, c2n) = ev
                    fw.op('dve', [c1n, r1n], [c1n], lambda: nc.vector.tensor_tensor(C1[:], C1[:], R1[:], ALU.mult))
                    fw.op('dve', [c2n, r2n], [c2n], lambda: nc.vector.tensor_tensor(C2[:], C2[:], R2[:], ALU.mult))
                    fw.op('dve', [c1n, c2n, 'sm'], [c1n], lambda: nc.vector.scalar_tensor_tensor(C1[:], C2[:], neglam, C1[:], ALU.mult, ALU.add))
                    SQ, sqn = Pp.get()
                    fw.op('act', [c1n], [sqn], lambda: nc.scalar.activation(SQ[:], C1[:], AF.Square))
                    fw.op('pe', ['ones', sqn], ['ps0'], lambda: nc.tensor.matmul(ps[0][:], ones[:], SQ[:], start=True, stop=True))
                    fw.op('act', ['ps0'], [r2n], lambda: nc.scalar.activation(R2[:], ps[0][:], AF.Ln, bias=EPS, scale=1.0 / 128))
                    fw.op('act', [r2n], [r2n], lambda: nc.scalar.activation(R2[:], R2[:], AF.Exp, scale=-0.5))
                    fw.op('dve', [c1n, r2n, 'sm'], [f'out{qb}'], lambda: nc.vector.scalar_tensor_tensor(OUT[:, qs], C1[:], gcolA, R2[:], ALU.mult, ALU.mult))
                    store_out(d['oa'], qb)
                deferred.append(tail)
        while deferred:
            deferred.pop(0)()

    def sb_mixer():
        Zb = [psz[0], psz[1]]; Zn = ['psz0', 'psz1']
        Ab = [ps[4], ps[5]]; An = ['ps4', 'ps5']
        Tb_, Tn_ = ps[6], 'ps6'
        Op, On_ = ps[7], 'ps7'
        for qb in range(NQB):
            q0 = qb * 512
            qs = slice(q0, q0 + 512)
            kts = list(range(4 * qb + 3, -1, -1))
            nk = len(kts)
            npair = nk // 2
            Ls = {}
            Ws = {}

            def emit_Z(p):
                for h in range(2):
                    kt = kts[2 * p + h]
                    ks = slice(kt * 128, kt * 128 + 128)
                    o = kt * 128 - q0
                    if o >= 0:
                        fw.op('pe', ['K1', 'Q1'], [Zn[p % 2]], lambda: nc.tensor.matmul(Zb[p % 2][:, h * 512:(h + 1) * 512], K1[:, ks], Q1[:, qs], start=True, stop=False), inc=False)
                        fw.op('pe', ['ident', 'maskC'], [Zn[p % 2]], lambda: nc.tensor.matmul(Zb[p % 2][:, h * 512:(h + 1) * 512], ident[:], maskC[:, o // 128, :], start=False, stop=True), inc=(h == 1))
                    else:
                        fw.op('pe', ['K1', 'Q1'], [Zn[p % 2]], lambda: nc.tensor.matmul(Zb[p % 2][:, h * 512:(h + 1) * 512], K1[:, ks], Q1[:, qs], start=True, stop=True), inc=(h == 1))

            def stage1(p):
                b = p % 2
                E, en = E2p.get()
                fw.op('act', [Zn[b]], [en], lambda: nc.scalar.activation(E[:], Zb[b][:], AF.Exp))
                L, ln_ = L2p.get()
                fw.op('act', [en], [ln_], lambda: nc.scalar.activation(L[:], E[:], AF.Ln, bias=1.0))
                Ls[p] = (L, ln_)

            def emit_O(p):
                W, wn = Ws.pop(p)
                for h in range(2):
                    i = 2 * p + h
                    kt = kts[i]
                    fw.op('pe', ['V', wn], [On_], lambda: nc.tensor.matmul(Op[:], V[:, kt, :], W[:, h * 512:(h + 1) * 512], start=(i == 0), stop=(i == nk - 1)), inc=(h == 1))

            Gs = {}

            def emit_W(p):
                G, gn = Gs.pop(p)
                W, wn = W2p.get()
                fw.op('act', [gn], [wn], lambda: nc.scalar.activation(W[:], G[:], AF.Exp))
                Ws[p] = (W, wn)

            fw.op('dve', [], ['carry'], lambda: nc.vector.memset(carry[:], 0.0))
            emit_Z(0)
            if npair > 1:
                emit_Z(1)
            stage1(0)
            for p in range(npair):
                L, ln_ = Ls.pop(p)
                G, gn = E2p.get()
                for h in range(2):
                    i = 2 * p + h
                    kt = kts[i]
                    b = i % 2
                    ks = slice(kt * 128, kt * 128 + 128)
                    hs = slice(h * 512, (h + 1) * 512)
                    fw.op('pe', ['K1', 'Q1'], [An[b]], lambda: nc.tensor.matmul(Ab[b][:], K1[:, ks], Q1[:, qs], start=True, stop=False), inc=False)
                    od = kt * 128 - q0
                    if od >= 0:
                        fw.op('pe', ['ident', 'maskC'], [An[b]], lambda: nc.tensor.matmul(Ab[b][:], ident[:], maskC[:, od // 128, :], start=False, stop=False), inc=False)
                    fw.op('pe', ['mtri', ln_], [An[b]], lambda: nc.tensor.matmul(Ab[b][:], mtri[:], L[:, hs], start=False, stop=True))
                    fw.op('pe', ['ones', ln_], [Tn_], lambda: nc.tensor.matmul(Tb_[:], ones[:], L[:, hs], start=True, stop=True))
                    cur, ctok = (carry, 'carry') if h == 0 else (carry2, 'carry2')
                    nxt, ntok = (carry2, 'carry2') if h == 0 else (carry, 'carry')
                    fw.op('dve', [Tn_, ctok], [ntok], lambda: nc.vector.tensor_tensor(nxt[:], cur[:], Tb_[:], ALU.add))
                    fw.op('dve', [An[b], ctok], [gn], lambda: nc.vector.tensor_tensor(G[:, hs], Ab[b][:], cur[:], ALU.subtract))
                Gs[p] = (G, gn)
                if p >= 2:
                    emit_O(p - 2)
                if p + 1 < npair:
                    stage1(p + 1)
                if p >= 1:
                    emit_W(p - 1)
                if p + 2 < npair:
                    emit_Z(p + 2)
            emit_W(npair - 1)
            if npair >= 2:
                emit_O(npair - 2)
            emit_O(npair - 1)
            fw.op('act', [On_], [f'out{qb}'], lambda: nc.scalar.copy(OUT[:, qs], Op[:]))
            store_out(d['oc'], qb)

    MIX = 'ABC'
    if 'A' in MIX:
        load_qkv(d['qa'], None, d['ka'], None, d['va'])
        fw.dma('act', [], ['K2'], K2[0:64, :], d['ka'][0:64, :])
        fw.dma('act', [], ['K3'], K3[64:128, :], d['ka'][64:128, :])
        softmax_mixer('A')
    if 'B' in MIX:
        load_qkv(d['qbn'], d['qbr'], d['kbn'], d['kbr'], d['vb'])
        softmax_mixer('B')
    if 'C' in MIX:
        load_qkv(d['qc'], None, d['kc'], None, d['vc'])
        sb_mixer()


def t5_bucket_np(rel):
    nb = 16
    max_exact = 8
    n = np.abs(rel)
    large = max_exact + (np.log(np.maximum(n, 1).astype(np.float32) / np.float32(max_exact)) / np.float32(np.log(128 / 8)) * np.float32(nb - max_exact)).astype(np.int32)
    large = np.minimum(large, nb - 1)
    return np.where(rel > 0, nb, 0) + np.where(n < max_exact, n, large)


def attn_consts():
    kk = np.arange(128)[:, None]
    qq = np.arange(512)[None, :]
    offs = [-128, 0, 128, 256, 384]
    bucket = np.stack([t5_bucket_np((o + kk) - qq) for o in offs])
    maskB = np.stack([np.where(((o + kk) // 64) <= (qq // 64), 0.0, NEG) for o in offs[1:]]).astype(np.float32)
    maskC = np.stack([np.where((o + kk) < qq, 0.0, -30000.0) for o in offs[1:]]).astype(ml_dtypes.bfloat16)
    ident = np.eye(128, dtype=np.float32).astype(ml_dtypes.bfloat16)
    jj = np.arange(128)[:, None]
    kc = np.arange(128)[None, :]
    mtri = np.where(jj >= kc, -1.0, 0.0).astype(ml_dtypes.bfloat16)
    ones = np.ones((128, 128), ml_dtypes.bfloat16)
    return dict(bucket=bucket, maskB=maskB, maskC=maskC, mtri=mtri, ones=ones, ident=ident)


TT = 1024
O_AQ, O_AK, O_AV, O_CQ, O_CKV, O_KPE, O_SQ, O_SK, O_SV, O_G = 0, 1024, 2048, 3072, 3584, 3840, 3904, 4928, 5952, 6976


class Slabs:
    def __init__(self, nc, fw, n=3):
        self.nc, self.fw = nc, fw
        self.pool = Pool_(nc, "slab", n, [128, 16, 512], BF16)

    def load(self, w, r0, nrows, c0, ncols):
        t, tok = self.pool.get()
        nch = nrows // 128
        step = 4
        for c in range(0, nch, step):
            n = min(step, nch - c)
            self.fw.dma('pool', [], [tok], t[:, c:c + n, 0:ncols],
                        w[r0 + c * 128:r0 + (c + n) * 128, c0:c0 + ncols].rearrange("(c p) n -> p c n", p=128))
        return t, tok


def emit_norm_h(nc, fw, X, H, acol, bcol, ones, ps, psn, Tp, Pp, RS):
    for tb in range(TT // 512):
        ts = slice(tb * 512, tb * 512 + 512)
        for c in range(16):
            SQ, sqn = Pp.get()
            fw.op('act', ['X'], [sqn], lambda: nc.scalar.activation(SQ[:], X[:, c, ts], AF.Square))
            fw.op('pe', ['ones', sqn], [psn], lambda: nc.tensor.matmul(ps[:], ones[:], SQ[:], start=(c == 0), stop=(c == 15)))
        fw.op('act', [psn], ['RS'], lambda: nc.scalar.activation(RS[:, ts], ps[:], AF.Ln, bias=EPS, scale=1.0 / 2048))
        fw.op('act', ['RS'], ['RS'], lambda: nc.scalar.activation(RS[:, ts], RS[:, ts], AF.Exp, scale=-0.5))
        for c in range(16):
            T, tn = Tp.get()
            fw.op('dve', ['X', 'RS', 'cols'], [tn], lambda: nc.vector.scalar_tensor_tensor(T[:], X[:, c, ts], acol[:, c:c + 1], RS[:, ts], ALU.mult, ALU.mult))
            fw.op('dve', [tn, 'cols'], ['H'], lambda: nc.vector.tensor_scalar(H[:, c, ts], T[:], bcol[:, c:c + 1], None, ALU.add))


def emit_proj(nc, fw, d, ps):
    sb = nc.alloc_sbuf_tensor
    H = sb("H", [128, 16, TT], BF16)
    RS = sb("RS", [128, TT], F32)
    ones = sb("pones", [128, 128], BF16)
    modc = sb("pmodc", [128, 6, 16], F32)
    gmix = sb("pgmix", [128, 16], F32)
    cols = sb("pcols", [128, 2, 16], F32)
    Pp = Pool_(nc, "pP", 4, [128, 512], BF16)
    Tp = Pool_(nc, "pT", 4, [128, 512], F32)
    fw.dma('act', [], ['ones'], ones[:], d['ones'])
    fw.dma('act', [], ['modc'], modc[:], d['modc'])
    fw.dma('act', [], ['gmix'], gmix[:], d['gmix'])
    with nc.sbuf_tensor("X", [128, 16, TT], F32) as X:
        for c in range(0, 16, 4):
            fw.dma('sp', [], ['X'], X[:, c:c + 4, :], d['xT'][c * 128:(c + 4) * 128, :].rearrange("(c p) t -> p c t", p=128))
        fw.op('dve', ['modc', 'gmix'], ['cols'], lambda: nc.vector.scalar_tensor_tensor(cols[:, 0, :], modc[:, 1, :], 1.0, gmix[:], ALU.add, ALU.mult))
        fw.op('dve', ['modc'], ['cols'], lambda: nc.vector.tensor_copy(cols[:, 1, :], modc[:, 0, :]))
        emit_norm_h(nc, fw, X, H, cols[:, 0, :], cols[:, 1, :], ones, ps[4], 'ps4', Tp, Pp, RS)
    for c in range(0, 16, 4):
        fw.dma('sp', ['H'], [], d['hT'][c * 128:(c + 4) * 128, :].rearrange("(c p) t -> p c t", p=128), H[:, c:c + 4, :], semtok='hout')
    fw.barrier()
    emit_proj_body(nc, fw, d, ps, H, ones, Pp, Tp, Slabs(nc, fw))


def emit_proj_body(nc, fw, d, ps, H, ones, Pp, Tp, slabs):
    sb = nc.alloc_sbuf_tensor
    bd64 = sb("pbd64", [128, 128], BF16)
    rotm = sb("protm", [64, 64], BF16)
    dqk = sb("pdqk", [128, 2], F32)
    gq = sb("pgq", [128, 4], F32)
    gkv = sb("pgkv", [128, 2], F32)
    mqk = sb("pmqk", [128, 4], F32)
    cosT = sb("pcos", [64, TT], F32)
    sinT = sb("psin", [64, TT], F32)
    WQ = sb("pwq", [128, 4, 1536], BF16)
    WKV = sb("pwkv", [128, 2, 2048], BF16)
    CQ = sb("pCQ", [128, 4, TT], F32)
    CKV = sb("pCKV", [128, 2, TT], F32)
    KPE = sb("pKPE", [64, TT], F32)
    CQN = sb("pCQN", [128, 4, TT], BF16)
    CKVN = sb("pCKVN", [128, 2, TT], BF16)
    Op = Pool_(nc, "pO", 4, [128, 512], BF16)
    projps = [(ps[i], f'ps{i}') for i in range(4)]
    pi = [0]

    def nextps():
        p = projps[pi[0] % 4]
        pi[0] += 1
        return p

    nps = [(ps[4], 'ps4'), (ps[5], 'ps5')]
    ni = [0]

    def nextnps():
        p = nps[ni[0] % 2]
        ni[0] += 1
        return p

    fw.dma('act', [], ['bd64'], bd64[:], d['bd64'])
    fw.dma('act', [], ['rotm'], rotm[:], d['rotm'])
    fw.dma('act', [], ['dqk'], dqk[:], d['dqk'])
    fw.dma('act', [], ['gq'], gq[:], d['gq'])
    fw.dma('act', [], ['gkv'], gkv[:], d['gkv'])
    fw.dma('act', [], ['mqk'], mqk[:], d['mqk'])
    fw.dma('act', [], ['cos'], cosT[:], d['cosT'])
    fw.dma('act', [], ['sin'], sinT[:], d['sinT'])
    for c in range(4):
        fw.dma('pool', [], ['WQ'], WQ[:, c, :], d['wqup'][c * 128:(c + 1) * 128, :], semtok='WQ')
    for c in range(2):
        fw.dma('pool', [], ['WKV'], WKV[:, c, :], d['wkvup'][c * 128:(c + 1) * 128, :], semtok='WKV')
    w = d['w_in']

    def rstd_of(sq_list, lhsT, ltok, nfeat, P=128):
        pp, ppn = nextnps()
        for i, (sq, sqn) in enumerate(sq_list):
            fw.op('pe', [ltok, sqn], [ppn], lambda: nc.tensor.matmul(pp[0:P, :], lhsT, sq, start=(i == 0), stop=(i == len(sq_list) - 1)))
        R, rn = Tp.get()
        fw.op('act', [ppn], [rn], lambda: nc.scalar.activation(R[0:P, :], pp[0:P, :], AF.Ln, bias=EPS, scale=1.0 / nfeat))
        fw.op('act', [rn], [rn], lambda: nc.scalar.activation(R[0:P, :], R[0:P, :], AF.Exp, scale=-0.5))
        return R, rn

    def fm_chunk(slab, stok, nrc, col, ncols, tb, rhs_src, rtok):
        p, pn = nextps()
        ts = slice(tb * 512, tb * 512 + 512)
        for c in range(nrc):
            fw.op('pe', [stok, rtok], [pn], lambda: nc.tensor.matmul(p[0:ncols, :], slab[:, c, col:col + ncols], rhs_src[:, c, ts], start=(c == 0), stop=(c == nrc - 1)), inc=(c == nrc - 1))
        return p, pn

    def out_dma(dst, O, on):
        fw.dma('sp', [on], [], dst, O)

    def qknorm_out(p, pn, P, lhsT, ltok, nfeat, gcol, gtok, dst, scale_extra=1.0):
        SQ, sqn = Pp.get()
        fw.op('act', [pn], [sqn], lambda: nc.scalar.activation(SQ[0:P, :], p[0:P, :], AF.Square))
        R, rn = rstd_of([(SQ[0:P, :], sqn)], lhsT, ltok, nfeat, P)
        O, on = Op.get()
        fw.op('dve', [pn, rn, gtok], [on], lambda: nc.vector.scalar_tensor_tensor(O[0:P, :], p[0:P, :], gcol, R[0:P, :], ALU.mult, ALU.mult))
        out_dma(dst, O[0:P, :], on)

    def rope_out(src, stok, P, gcol, gtok, tb, dst):
        ts = slice(tb * 512, tb * 512 + 512)
        SQ, sqn = Pp.get()
        fw.op('act', [stok], [sqn], lambda: nc.scalar.activation(SQ[0:64, :], src, AF.Square))
        R, rn = rstd_of([(SQ[0:64, :], sqn)], ones[0:64, 0:64], 'ones', 64, 64)
        QN, qn = Tp.get()
        fw.op('dve', [stok, rn, gtok], [qn], lambda: nc.vector.scalar_tensor_tensor(QN[0:64, :], src, gcol, R[0:64, :], ALU.mult, ALU.mult))
        QB, qbn_ = Pp.get()
        fw.op('act', [qn], [qbn_], lambda: nc.scalar.copy(QB[0:64, :], QN[0:64, :]))
        rp, rpn = (ps[6], 'ps6') if tb == 0 else (ps[7], 'ps7')
        fw.op('pe', ['rotm', qbn_], [rpn], lambda: nc.tensor.matmul(rp[0:64, :], rotm[:], QB[0:64, :], start=True, stop=True))
        T2, t2n = Tp.get()
        fw.op('dve', [rpn, 'sin'], [t2n], lambda: nc.vector.tensor_tensor(T2[0:64, :], rp[0:64, :], sinT[:, ts], ALU.mult))
        fw.op('dve', [qn, 'cos'], [qn], lambda: nc.vector.tensor_tensor(QN[0:64, :], QN[0:64, :], cosT[:, ts], ALU.mult))
        O, on = Op.get()
        fw.op('dve', [qn, t2n], [on], lambda: nc.vector.tensor_tensor(O[0:64, :], QN[0:64, :], T2[0:64, :], ALU.add))
        out_dma(dst, O[0:64, :], on)

    pending = []

    def defer(fn):
        pending.append(fn)
        while len(pending) > 1:
            pending.pop(0)()

    def flush():
        while pending:
            pending.pop(0)()

    def fm_group(c0, ncols_total, handler):
        for s0 in range(0, ncols_total, 512):
            ncs = min(512, ncols_total - s0)
            slab, stok = slabs.load(w, 0, 2048, c0 + s0, ncs)
            for tb in range(2):
                for cc in range(0, ncs, 128):
                    nco = min(128, ncs - cc)
                    p, pn = fm_chunk(slab, stok, 16, cc, nco, tb, H, 'H')
                    defer(lambda ch=(s0 + cc) // 128, tb=tb, p=p, pn=pn, nco=nco: handler(ch, tb, p, pn, nco))
        flush()

    def tsl(tb):
        return slice(tb * 512, tb * 512 + 512)

    fm_group(O_AQ, 1024, lambda ch, tb, p, pn, n: qknorm_out(p, pn, 128, bd64[:], 'bd64', 64, dqk[:, 0:1], 'dqk', d['qa'][ch, :, tsl(tb)]))
    fm_group(O_AK, 1024, lambda ch, tb, p, pn, n: qknorm_out(p, pn, 128, bd64[:], 'bd64', 64, dqk[:, 1:2], 'dqk', d['ka'][ch, :, tsl(tb)]))

    def cq_h(ch, tb, p, pn, n):
        O, on = Op.get()
        fw.op('act', [pn], [on], lambda: nc.scalar.activation(O[:], p[:], AF.Copy, scale=128 ** -0.5))
        out_dma(d['qc'][ch, :, tsl(tb)], O[:], on)

    def ck_h(ch, tb, p, pn, n):
        O, on = Op.get()
        fw.op('act', [pn], [on], lambda: nc.scalar.copy(O[:], p[:]))
        out_dma(d['kc'][ch, :, tsl(tb)], O[:], on)

    fm_group(O_SQ, 1024, cq_h)
    fm_group(O_SK, 1024, ck_h)

    def lat_cq(ch, tb, p, pn, n):
        fw.op('act', [pn], ['CQ'], lambda: nc.scalar.copy(CQ[:, ch, tsl(tb)], p[:]))

    def lat_ckv(ch, tb, p, pn, n):
        if ch < 2:
            fw.op('act', [pn], ['CKV'], lambda: nc.scalar.copy(CKV[:, ch, tsl(tb)], p[:]))
        else:
            fw.op('act', [pn], ['KPE'], lambda: nc.scalar.copy(KPE[:, tsl(tb)], p[0:64, :]))

    fm_group(O_CQ, 512, lat_cq)
    fm_group(O_CKV, 320, lat_ckv)

    for tb in range(2):
        ts = tsl(tb)
        for (RAW, rtok, NRM, ntok, nch, gg, gtok) in [(CQ, 'CQ', CQN, 'CQN', 4, gq, 'gq'), (CKV, 'CKV', CKVN, 'CKVN', 2, gkv, 'gkv')]:
            sql = []
            for c in range(nch):
                SQ, sqn = Pp.get()
                fw.op('act', [rtok], [sqn], lambda: nc.scalar.activation(SQ[:], RAW[:, c, ts], AF.Square))
                sql.append((SQ[:], sqn))
            R, rn = rstd_of(sql, ones[:], 'ones', nch * 128)
            for c in range(nch):
                fw.op('dve', [rtok, rn, gtok], [ntok], lambda: nc.vector.scalar_tensor_tensor(NRM[:, c, ts], RAW[:, c, ts], gg[:, c:c + 1], R[:], ALU.mult, ALU.mult))
        rope_out(KPE[:, ts], 'KPE', 64, mqk[0:64, 3:4], 'mqk', tb, d['kbr'][:, ts])

    for tb in range(2):
        for hh in range(8):
            p, pn = fm_chunk(WQ, 'WQ', 4, hh * 128, 128, tb, CQN, 'CQN')
            defer(lambda p=p, pn=pn, hh=hh, tb=tb: qknorm_out(p, pn, 128, ones[:], 'ones', 128, mqk[:, 0:1], 'mqk', d['qbn'][hh, :, tsl(tb)]))
        for hh in range(8):
            p, pn = fm_chunk(WQ, 'WQ', 4, 1024 + hh * 64, 64, tb, CQN, 'CQN')
            defer(lambda p=p, pn=pn, hh=hh, tb=tb: rope_out(p[0:64, :], pn, 64, mqk[0:64, 2:3], 'mqk', tb, d['qbr'][hh, :, tsl(tb)]))
        for hh in range(8):
            p, pn = fm_chunk(WKV, 'WKV', 2, hh * 128, 128, tb, CKVN, 'CKVN')
            defer(lambda p=p, pn=pn, hh=hh, tb=tb: qknorm_out(p, pn, 128, ones[:], 'ones', 128, mqk[:, 1:2], 'mqk', d['kbn'][hh, :, tsl(tb)]))
        flush()
    for tt in range(TT // 128):
        for nb in range(2):
            p, pn = nextps()
            for c in range(2):
                fw.op('pe', ['CKVN', 'WKV'], [pn], lambda: nc.tensor.matmul(p[:], CKVN[:, c, tt * 128:(tt + 1) * 128], WKV[:, c, 1024 + nb * 512:1024 + (nb + 1) * 512], start=(c == 0), stop=(c == 1)), inc=(c == 1))
            O, on = Op.get()
            fw.op('act', [pn], [on], lambda: nc.scalar.copy(O[:], p[:]))
            out_dma(d['vb'][tt * 128:(tt + 1) * 128, nb * 512:(nb + 1) * 512], O[:], on)

    for (c0, dst) in [(O_AV, d['va']), (O_SV, d['vc'])]:
        for s0 in range(0, 1024, 512):
            slab, stok = slabs.load(w, 0, 2048, c0 + s0, 512)
            for tt in range(TT // 128):
                p, pn = nextps()
                for c in range(16):
                    fw.op('pe', ['H', stok], [pn], lambda: nc.tensor.matmul(p[:], H[:, c, tt * 128:(tt + 1) * 128], slab[:, c, :], start=(c == 0), stop=(c == 15)), inc=(c == 15))
                O, on = Op.get()
                fw.op('act', [pn], [on], lambda: nc.scalar.copy(O[:], p[:]))
                out_dma(dst[tt * 128:(tt + 1) * 128, s0:s0 + 512], O[:], on)
    return ['dramout']


def proj_consts(core):
    half = 32
    inv = (10000.0 ** (-np.arange(half, dtype=np.float32) / half)).astype(np.float32)
    pos = np.arange(core * TT, (core + 1) * TT).astype(np.float32)
    ang = pos[None, :] * np.concatenate([inv, inv])[:, None]
    cosT = np.cos(ang).astype(np.float32)
    sinT = np.sin(ang).astype(np.float32)
    rot = np.zeros((64, 64), np.float32)
    for p in range(32):
        rot[p + 32, p] = -1.0
        rot[p, p + 32] = 1.0
    bd = np.zeros((128, 128), np.float32)
    bd[:64, :64] = 1
    bd[64:, 64:] = 1
    return dict(cosT=cosT, sinT=sinT, rotm=rot.astype(ml_dtypes.bfloat16), bd64=bd.astype(ml_dtypes.bfloat16), ones=np.ones((128, 128), ml_dtypes.bfloat16))


def _common(nc, fw, d, ps, Tp, Pp, H, RS, ones, modc, gvec, gtok, cols, sc_idx, sh_idx, X, mtok='modc'):
    fw.op('dve', [mtok, gtok], ['cols'], lambda: nc.vector.scalar_tensor_tensor(cols[:, 0, :], modc[:, sc_idx, :], 1.0, gvec[:], ALU.add, ALU.mult))
    fw.op('dve', [mtok], ['cols'], lambda: nc.vector.tensor_copy(cols[:, 1, :], modc[:, sh_idx, :]))
    emit_norm_h(nc, fw, X, H, cols[:, 0, :], cols[:, 1, :], ones, ps[7], 'ps7', Tp, Pp, RS)


def tsl(tb):
    return slice(tb * 512, tb * 512 + 512)


def emit_merge(nc, fw, d, ps):
    sb = nc.alloc_sbuf_tensor
    H = sb("H", [128, 16, TT], BF16)
    RS = sb("RS", [128, TT], F32)
    MG = sb("MG", [128, 16, TT], BF16)
    MGF = [[sb(f"MGF{i}_{tb}", [128, 512], F32) for tb in range(2)] for i in range(4)]
    ones = sb("pones", [128, 128], BF16)
    modc = sb("pmodc", [128, 6, 16], F32)
    gmix = sb("pgmix", [128, 16], F32)
    cols = sb("pcols", [128, 2, 16], F32)
    Pp = Pool_(nc, "pP", 4, [128, 512], BF16)
    Tp = Pool_(nc, "pT", 6, [128, 512], F32)
    pslist = [(ps[i], f'ps{i}') for i in range(7)]
    pi = [0]

    def nextps():
        p = pslist[pi[0] % 7]
        pi[0] += 1
        return p

    for c in range(0, 16, 4):
        fw.dma('sp', [], ['H'], H[:, c:c + 4, :], d['hT'][c * 128:(c + 4) * 128, :].rearrange("(c p) t -> p c t", p=128))
    slabs = Slabs(nc, fw, n=3)
    BRp = Pool_(nc, "BR", 2, [128, 8, TT], BF16)
    w_in, wbr = d['w_in'], d['w_branch']
    for s in range(4):
        for i in range(3):
            gs, gtok = slabs.load(w_in, 0, 2048, i * 2048 + s * 512, 512)
            us, utok = slabs.load(wbr[i], 0, 1024, s * 512, 512)
            BR, brn = BRp.get()
            fw.dma('act', [], [brn], BR[:], d['br'][i].rearrange("c p t -> p c t"))
            for tb in range(2):
                ts = tsl(tb)
                for cc in range(4):
                    n = s * 4 + cc
                    mtok = 'MGf%d_%d' % (cc, tb)
                    gp, gpn = nextps()
                    for c in range(16):
                        fw.op('pe', [gtok, 'H'], [gpn], lambda: nc.tensor.matmul(gp[:], gs[:, c, cc * 128:(cc + 1) * 128], H[:, c, ts], start=(c == 0), stop=(c == 15)), inc=(c == 15))
                    up, upn = nextps()
                    for c in range(8):
                        fw.op('pe', [utok, brn], [upn], lambda: nc.tensor.matmul(up[:], us[:, c, cc * 128:(cc + 1) * 128], BR[:, c, ts], start=(c == 0), stop=(c == 7)), inc=(c == 7))
                    SG, sgn = Tp.get()
                    fw.op('act', [gpn], [sgn], lambda: nc.scalar.activation(SG[:], gp[:], AF.Sigmoid))
                    if i == 0:
                        fw.op('dve', [upn, sgn], [mtok], lambda: nc.vector.tensor_tensor(MGF[cc][tb][:], up[:], SG[:], ALU.mult))
                    elif i == 1:
                        fw.op('dve', [upn, sgn], [sgn], lambda: nc.vector.tensor_tensor(SG[:], up[:], SG[:], ALU.mult))
                        fw.op('dve', [sgn, mtok], [mtok], lambda: nc.vector.tensor_tensor(MGF[cc][tb][:], MGF[cc][tb][:], SG[:], ALU.add))
                    else:
                        fw.op('dve', [upn, sgn], [sgn], lambda: nc.vector.tensor_tensor(SG[:], up[:], SG[:], ALU.mult))
                        fw.op('dve', [sgn, mtok], ['MG'], lambda: nc.vector.tensor_tensor(MG[:, n, ts], MGF[cc][tb][:], SG[:], ALU.add))
    for c in range(0, 16, 4):
        fw.dma('sp', ['MG'], [], d['mg'][c * 128:(c + 4) * 128, :].rearrange("(c p) t -> p c t", p=128), MG[:, c:c + 4, :])


def emit_mlp(nc, fw, d, ps):
    sb = nc.alloc_sbuf_tensor
    X = sb("X", [128, 16, TT], F32)
    H = sb("H", [128, 16, TT], BF16)
    RS = sb("RS", [128, TT], F32)
    MG = sb("MG", [128, 16, TT], BF16)
    ones = sb("pones", [128, 128], BF16)
    modc = sb("pmodc", [128, 6, 16], F32)
    gmlp = sb("pgmlp", [128, 16], F32)
    cols = sb("pcols", [128, 2, 16], F32)
    Pp = Pool_(nc, "pP", 4, [128, 512], BF16)
    Tp = Pool_(nc, "pT", 6, [128, 512], F32)
    slabs = Slabs(nc, fw, n=3)
    pslist = [(ps[i], f'ps{i}') for i in range(7)]
    pi = [0]

    def nextps():
        p = pslist[pi[0] % 7]
        pi[0] += 1
        return p

    for c in range(0, 16, 4):
        fw.dma('sp', [], ['X'], X[:, c:c + 4, :], d['xT'][c * 128:(c + 4) * 128, :].rearrange("(c p) t -> p c t", p=128))
        fw.dma('act', [], ['MG'], MG[:, c:c + 4, :], d['mg'][c * 128:(c + 4) * 128, :].rearrange("(c p) t -> p c t", p=128))
    fw.dma('act', [], ['ones'], ones[:], d['ones'])
    fw.dma('act', [], ['modc'], modc[:], d['modc'])
    fw.dma('act', [], ['gmlp'], gmlp[:], d['gmlp'])
    wout, w1, w2 = d['w_out'], d['w_mlp_in'], d['w_mlp_out']

    def resid_matmul(W, r0, gcolidx):
        for s in range(4):
            sl, stk = slabs.load(W, r0, 2048, s * 512, 512)
            for tb in range(2):
                ts = tsl(tb)
                for cc in range(4):
                    n = s * 4 + cc
                    p, pn = nextps()
                    for c in range(16):
                        fw.op('pe', [stk, 'MG'], [pn], lambda: nc.tensor.matmul(p[:], sl[:, c, cc * 128:(cc + 1) * 128], MG[:, c, ts], start=(c == 0), stop=(c == 15)), inc=(c == 15))
                    fw.op('dve', [pn, 'X', 'modc'], ['X'], lambda: nc.vector.scalar_tensor_tensor(X[:, n, ts], p[:], modc[:, gcolidx, n:n + 1], X[:, n, ts], ALU.mult, ALU.add))

    resid_matmul(wout, 0, 2)
    _common(nc, fw, d, ps, Tp, Pp, H, RS, ones, modc, gmlp, 'gmlp', cols, 4, 3, X)
    for g in range(4):
        for s in range(4):
            sl, stk = slabs.load(w1, 0, 2048, g * 2048 + s * 512, 512)
            for tb in range(2):
                ts = tsl(tb)
                for cc in range(4):
                    f = s * 4 + cc
                    p, pn = nextps()
                    for c in range(16):
                        fw.op('pe', [stk, 'H'], [pn], lambda: nc.tensor.matmul(p[:], sl[:, c, cc * 128:(cc + 1) * 128], H[:, c, ts], start=(c == 0), stop=(c == 15)), inc=(c == 15))
                    R, rn = Tp.get()
                    fw.op('act', [pn], [rn], lambda: nc.scalar.activation(R[:], p[:], AF.Relu))
                    fw.op('dve', [rn], ['MG'], lambda: nc.vector.tensor_tensor(MG[:, f, ts], R[:], R[:], ALU.mult))
        resid_matmul(w2, g * 2048, 5)
    for c in range(0, 16, 4):
        fw.dma('sp', ['X'], [], d['xo'][c * 128:(c + 4) * 128, :].rearrange("(c p) t -> p c t", p=128), X[:, c:c + 4, :])


def emit_mod(nc, fw, d, ps):
    sb = nc.alloc_sbuf_tensor
    cT = sb("cT_sb", [128, 16], F32)
    fw.dma('sp', [], ['cT'], cT[:], d['cT'])
    bia = sb("bia_sb", [1, 4 * 1536], F32)
    fw.dma('sp', [], ['bia'], bia[:], d['b'])
    res = sb("res_sb", [1, 4 * 1536], F32)
    wp = Pool_(nc, "wm", 2, [128, 16, 512], F32)
    k = 0
    for l in range(4):
        for s in range(3):
            W, wn = wp.get()
            for c in range(0, 16, 4):
                fw.dma('sp' if (k % 2 == 0) else 'act', [], [wn], W[:, c:c + 4, :], d['w'][l, c * 128:(c + 4) * 128, s * 512:(s + 1) * 512].rearrange("(c p) n -> p c n", p=128))
            p, pn = ps[k % 2], f'ps{k % 2}'
            for c in range(16):
                fw.op('pe', [wn, 'cT'], [pn], lambda: nc.tensor.matmul(p[0:1, :], cT[:, c:c + 1], W[:, c, :], start=(c == 0), stop=(c == 15)), inc=(c == 15))
            o0 = l * 1536 + s * 512
            fw.op('dve', [pn, 'bia'], ['res'], lambda: nc.vector.tensor_tensor(res[:, o0:o0 + 512], p[0:1, :], bia[:, o0:o0 + 512], ALU.add))
            k += 1
    fw.dma('sp', ['res'], [], d['out'], res[:])


def emit_dense(nc, fw, d, ps, with_proj):
    from contextlib import ExitStack
    sb = nc.alloc_sbuf_tensor
    H = sb("H", [128, 16, TT], BF16)
    RS = sb("RS", [128, TT], F32)
    ones = sb("pones", [128, 128], BF16)
    modc = sb("pmodc", [128, 6, 16], F32)
    gmlp = sb("pgmlp", [128, 16], F32)
    cols = sb("pcols", [128, 2, 16], F32)
    Pp = Pool_(nc, "pP", 4, [128, 512], BF16)
    Tp = Pool_(nc, "pT", 6, [128, 512], F32)
    slabs = Slabs(nc, fw, n=3)
    if with_proj:
        modc2 = sb("pmodc2", [128, 6, 16], F32)
        gmix2 = sb("pgmix2", [128, 16], F32)
        fw.dma('act', [], ['modc2'], modc2[:], d['modc2'])
        fw.dma('act', [], ['gmix2'], gmix2[:], d['gmix2'])
    pslist = [(ps[i], f'ps{i}') for i in range(7)]
    pi = [0]

    def nextps():
        p = pslist[pi[0] % 7]
        pi[0] += 1
        return p

    fw.dma('act', [], ['ones'], ones[:], d['ones'])
    fw.dma('act', [], ['modc'], modc[:], d['modc'])
    fw.dma('act', [], ['gmlp'], gmlp[:], d['gmlp'])
    for c in range(0, 16, 4):
        fw.dma('sp', [], ['H'], H[:, c:c + 4, :], d['hT'][c * 128:(c + 4) * 128, :].rearrange("(c p) t -> p c t", p=128))
    w_gate, wbr = d['w_gate'], d['w_branch']
    wout, w1, w2 = d['w_out'], d['w_mlp_in'], d['w_mlp_out']
    with nc.sbuf_tensor("MG", [128, 16, TT], BF16) as MG:
        with ExitStack() as st:
            MGF = [[st.enter_context(nc.sbuf_tensor(f"MGF{i}_{tb}", [128, 512], F32)) for tb in range(2)] for i in range(4)]
            BRt = [st.enter_context(nc.sbuf_tensor(f"BR{i}", [128, 8, TT], BF16)) for i in range(3)]
            for i in range(3):
                fw.dma('act', [], [f"BR{i}"], BRt[i][:], d['br'][i].rearrange("c p t -> p c t"))
            def load_to(t, tok, w, r0, nrows, c0, ncols):
                nch = nrows // 128
                for c in range(0, nch, 4):
                    n = min(4, nch - c)
                    fw.dma('pool', [], [tok], t[:, c:c + n, 0:ncols], w[r0 + c * 128:r0 + (c + n) * 128, c0:c0 + ncols].rearrange("(c p) n -> p c n", p=128))

            sl_t = slabs.pool.tiles
            it = 0
            for s in range(4):
                for i in range(3):
                    gs, gtok = sl_t[it % 2], 'slab%d' % (it % 2)
                    load_to(gs, gtok, w_gate, 0, 2048, i * 2048 + s * 512, 512)
                    us, utok = sl_t[2][:, 8 * (it % 2):8 * (it % 2) + 8, :], 'slabU%d' % (it % 2)
                    load_to(us, utok, wbr[i], 0, 1024, s * 512, 512)
                    it += 1
                    BR, brn = BRt[i], f"BR{i}"
                    for tb in range(2):
                        ts = tsl(tb)
                        for cc in range(4):
                            n = s * 4 + cc
                            mtok = 'MGf%d_%d' % (cc, tb)
                            gp, gpn = nextps()
                            for c in range(16):
                                fw.op('pe', [gtok, 'H'], [gpn], lambda: nc.tensor.matmul(gp[:], gs[:, c, cc * 128:(cc + 1) * 128], H[:, c, ts], start=(c == 0), stop=(c == 15)), inc=(c == 15))
                            up, upn = nextps()
                            for c in range(8):
                                fw.op('pe', [utok, brn], [upn], lambda: nc.tensor.matmul(up[:], us[:, c, cc * 128:(cc + 1) * 128], BR[:, c, ts], start=(c == 0), stop=(c == 7)), inc=(c == 7))
                            SG, sgn = Tp.get()
                            fw.op('act', [gpn], [sgn], lambda: nc.scalar.activation(SG[:], gp[:], AF.Sigmoid))
                            if i == 0:
                                fw.op('dve', [upn, sgn], [mtok], lambda: nc.vector.tensor_tensor(MGF[cc][tb][:], up[:], SG[:], ALU.mult))
                            elif i == 1:
                                fw.op('dve', [upn, sgn], [sgn], lambda: nc.vector.tensor_tensor(SG[:], up[:], SG[:], ALU.mult))
                                fw.op('dve', [sgn, mtok], [mtok], lambda: nc.vector.tensor_tensor(MGF[cc][tb][:], MGF[cc][tb][:], SG[:], ALU.add))
                            else:
                                fw.op('dve', [upn, sgn], [sgn], lambda: nc.vector.tensor_tensor(SG[:], up[:], SG[:], ALU.mult))
                                fw.op('dve', [sgn, mtok], ['MG'], lambda: nc.vector.tensor_tensor(MG[:, n, ts], MGF[cc][tb][:], SG[:], ALU.add))
        fw.barrier()
        with nc.sbuf_tensor("X", [128, 16, TT], F32) as X:
            for c in range(0, 16, 4):
                fw.dma('sp', [], ['X'], X[:, c:c + 4, :], d['xT'][c * 128:(c + 4) * 128, :].rearrange("(c p) t -> p c t", p=128))

            def resid_matmul(W, r0, gcolidx):
                for s in range(4):
                    sl, stk = slabs.load(W, r0, 2048, s * 512, 512)
                    for tb in range(2):
                        ts = tsl(tb)
                        for cc in range(4):
                            n = s * 4 + cc
                            p, pn = nextps()
                            for c in range(16):
                                fw.op('pe', [stk, 'MG'], [pn], lambda: nc.tensor.matmul(p[:], sl[:, c, cc * 128:(cc + 1) * 128], MG[:, c, ts], start=(c == 0), stop=(c == 15)), inc=(c == 15))
                            fw.op('dve', [pn, 'X', 'modc'], ['X'], lambda: nc.vector.scalar_tensor_tensor(X[:, n, ts], p[:], modc[:, gcolidx, n:n + 1], X[:, n, ts], ALU.mult, ALU.add))

            resid_matmul(wout, 0, 2)
            _common(nc, fw, d, ps, Tp, Pp, H, RS, ones, modc, gmlp, 'gmlp', cols, 4, 3, X)
            for g in range(4):
                for s in range(4):
                    sl, stk = slabs.load(w1, 0, 2048, g * 2048 + s * 512, 512)
                    for tb in range(2):
                        ts = tsl(tb)
                        for cc in range(4):
                            f = s * 4 + cc
                            p, pn = nextps()
                            for c in range(16):
                                fw.op('pe', [stk, 'H'], [pn], lambda: nc.tensor.matmul(p[:], sl[:, c, cc * 128:(cc + 1) * 128], H[:, c, ts], start=(c == 0), stop=(c == 15)), inc=(c == 15))
                            R, rn = Tp.get()
                            fw.op('act', [pn], [rn], lambda: nc.scalar.activation(R[:], p[:], AF.Relu))
                            fw.op('dve', [rn], ['MG'], lambda: nc.vector.tensor_tensor(MG[:, f, ts], R[:], R[:], ALU.mult))
                resid_matmul(w2, g * 2048, 5)
            for c in range(0, 16, 4):
                fw.dma('sp', ['X'], [], d['xo'][c * 128:(c + 4) * 128, :].rearrange("(c p) t -> p c t", p=128), X[:, c:c + 4, :], semtok='xout')
            if with_proj:
                _common(nc, fw, d, ps, Tp, Pp, H, RS, ones, modc2, gmix2, 'gmix2', cols, 1, 0, X, mtok='modc2')
                for c in range(0, 16, 4):
                    fw.dma('sp', ['H'], [], d['hTo'][c * 128:(c + 4) * 128, :].rearrange("(c p) t -> p c t", p=128), H[:, c:c + 4, :], semtok='hout')
        fw.barrier()
    if with_proj:
        emit_proj_body(nc, fw, d, ps, H, ones, Pp, Tp, slabs)


bf = ml_dtypes.bfloat16
NCORES = 8
S = 8192
DEBUG = False
_progs = {}


def _mk(name, builder):
    if name not in _progs:
        _progs[name] = builder()
    return _progs[name]


def _din(nc, d, name, shape, dt):
    d[name] = nc.dram_tensor(name, list(shape), dt, kind="ExternalInput").ap()


def _dout(nc, d, name, shape, dt):
    d[name] = nc.dram_tensor(name, list(shape), dt, kind="ExternalOutput").ap()


def _psum(nc):
    return [nc.alloc_psum_tensor(f"ps{i}", [128, 512], F32) for i in range(8)]


def build_mod():
    nc = bass.Bass("TRN2", target_bir_lowering=False)
    fw = FW(nc)
    d = {}
    _din(nc, d, 'cT', (128, 16), F32)
    _din(nc, d, 'b', (1, 6144), F32)
    _din(nc, d, 'w', (4, 2048, 1536), F32)
    _dout(nc, d, 'out', (1, 6144), F32)
    emit_mod(nc, fw, d, _psum(nc))
    fw.finish_all()
    return nc


def build_proj():
    nc = bass.Bass("TRN2", target_bir_lowering=False)
    fw = FW(nc)
    d = {}
    _din(nc, d, 'xT', (2048, TT), F32)
    _din(nc, d, 'w_in', (2048, 6976), F32)
    for n, s, t in [('ones', (128, 128), BF16), ('bd64', (128, 128), BF16), ('rotm', (64, 64), BF16), ('modc', (128, 6, 16), F32),
                    ('gmix', (128, 16), F32), ('dqk', (128, 2), F32), ('gq', (128, 4), F32), ('gkv', (128, 2), F32), ('mqk', (128, 4), F32),
                    ('cosT', (64, TT), F32), ('sinT', (64, TT), F32), ('wqup', (512, 1536), F32), ('wkvup', (256, 2048), F32)]:
        _din(nc, d, n, s, t)
    for n in ['qa', 'ka', 'qbn', 'kbn', 'qc', 'kc']:
        _dout(nc, d, n, (8, 128, TT), BF16)
    _dout(nc, d, 'qbr', (8, 64, TT), BF16)
    _dout(nc, d, 'kbr', (64, TT), BF16)
    for n in ['va', 'vb', 'vc']:
        _dout(nc, d, n, (TT, 1024), BF16)
    _dout(nc, d, 'hT', (2048, TT), BF16)
    emit_proj(nc, fw, d, _psum(nc))
    fw.finish_all()
    return nc


def build_attn():
    nc = bass.Bass("TRN2", target_bir_lowering=False)
    fw = FW(nc)
    d = {}
    for n in ['qa', 'ka', 'qbn', 'kbn', 'qc', 'kc']:
        _din(nc, d, n, (128, S), BF16)
    for n in ['qbr', 'kbr']:
        _din(nc, d, n, (64, S), BF16)
    for n in ['va', 'vb', 'vc']:
        _din(nc, d, n, (S, 128), BF16)
    for n, s, t in [('ones', (128, 128), BF16), ('mtri', (128, 128), BF16), ('biasA', (5, 128, 512), F32), ('maskB', (4, 128, 512), F32),
                    ('maskC', (4, 128, 512), BF16), ('ident', (128, 128), BF16), ('lamp', (128, 256), F32), ('cst', (128, 4), F32), ('b15', (128, 1), F32), ('gsub', (128, 1), F32)]:
        _din(nc, d, n, s, t)
    for n in ['oa', 'ob', 'oc']:
        _dout(nc, d, n, (128, S), BF16)
    psz = [nc.alloc_psum_tensor(f"psz{i}", [128, 1024], F32) for i in range(2)]
    ps = [psz[0][:, 0:512], psz[0][:, 512:1024], psz[1][:, 0:512], psz[1][:, 512:1024]] + [nc.alloc_psum_tensor(f"ps{i}", [128, 512], F32) for i in range(4, 8)]
    emit_attn(nc, fw, S, d, ps, psz)
    fw.finish_all()
    return nc


def build_merge():
    nc = bass.Bass("TRN2", target_bir_lowering=False)
    fw = FW(nc)
    d = {}
    _din(nc, d, 'hT', (2048, TT), BF16)
    _din(nc, d, 'w_in', (2048, 6144), F32)
    _din(nc, d, 'w_branch', (3, 1024, 2048), F32)
    _din(nc, d, 'br', (3, 8, 128, TT), BF16)
    _dout(nc, d, 'mg', (2048, TT), BF16)
    emit_merge(nc, fw, d, _psum(nc))
    fw.finish_all()
    return nc


def build_mlp():
    nc = bass.Bass("TRN2", target_bir_lowering=False)
    fw = FW(nc)
    d = {}
    _din(nc, d, 'xT', (2048, TT), F32)
    _din(nc, d, 'mg', (2048, TT), BF16)
    _din(nc, d, 'w_out', (2048, 2048), F32)
    _din(nc, d, 'w_mlp_in', (2048, 8192), F32)
    _din(nc, d, 'w_mlp_out', (8192, 2048), F32)
    for n, s, t in [('ones', (128, 128), BF16), ('modc', (128, 6, 16), F32), ('gmlp', (128, 16), F32)]:
        _din(nc, d, n, s, t)
    _dout(nc, d, 'xo', (2048, TT), F32)
    emit_mlp(nc, fw, d, _psum(nc))
    fw.finish_all()
    return nc


def build_dense(with_proj):
    nc = bass.Bass("TRN2", target_bir_lowering=False)
    fw = FW(nc)
    d = {}
    _din(nc, d, 'hT', (2048, TT), BF16)
    _din(nc, d, 'xT', (2048, TT), F32)
    _din(nc, d, 'w_gate', (2048, 6144), F32)
    _din(nc, d, 'w_branch', (3, 1024, 2048), F32)
    _din(nc, d, 'br', (3, 8, 128, TT), BF16)
    _din(nc, d, 'w_out', (2048, 2048), F32)
    _din(nc, d, 'w_mlp_in', (2048, 8192), F32)
    _din(nc, d, 'w_mlp_out', (8192, 2048), F32)
    for n, s, t in [('ones', (128, 128), BF16), ('modc', (128, 6, 16), F32), ('gmlp', (128, 16), F32)]:
        _din(nc, d, n, s, t)
    _dout(nc, d, 'xo', (2048, TT), F32)
    if with_proj:
        _din(nc, d, 'w_in', (2048, 6976), F32)
        for n, s, t in [('modc2', (128, 6, 16), F32), ('gmix2', (128, 16), F32), ('bd64', (128, 128), BF16), ('rotm', (64, 64), BF16),
                        ('dqk', (128, 2), F32), ('gq', (128, 4), F32), ('gkv', (128, 2), F32), ('mqk', (128, 4), F32),
                        ('cosT', (64, TT), F32), ('sinT', (64, TT), F32), ('wqup', (512, 1536), F32), ('wkvup', (256, 2048), F32)]:
            _din(nc, d, n, s, t)
        for n in ['qa', 'ka', 'qbn', 'kbn', 'qc', 'kc']:
            _dout(nc, d, n, (8, 128, TT), BF16)
        _dout(nc, d, 'qbr', (8, 64, TT), BF16)
        _dout(nc, d, 'kbr', (64, TT), BF16)
        for n in ['va', 'vb', 'vc']:
            _dout(nc, d, n, (TT, 1024), BF16)
        _dout(nc, d, 'hTo', (2048, TT), BF16)
    emit_dense(nc, fw, d, _psum(nc), with_proj)
    fw.finish_all()
    return nc


def colvec(v, n):
    return np.ascontiguousarray(np.asarray(v, np.float32).reshape(n, 128).T)


def run(nc, in_maps):
    t = time.time()
    res = run_bass_kernel_spmd(nc, in_maps, core_ids=list(range(NCORES)))
    if DEBUG:
        print("  launch %.1fs" % (time.time() - t), flush=True)
    return res.results


def kernel(x, c, w_ada, b_ada, norm_mix_g, norm_mlp_g, w_in, diff_qk_g, diff_lambda, diff_subln_g, t5_bias,
           mla_q_norm_g, mla_kv_norm_g, w_q_up, w_kv_up, mla_qk_g, w_branch, w_out, w_mlp_in, w_mlp_out):
    f32 = np.float32
    x = np.asarray(x, f32)
    AC = attn_consts()
    PC = [proj_consts(i) for i in range(NCORES)]
    ones = AC['ones']
    nc_mod = _mk('mod', build_mod)
    cT = colvec(np.asarray(c, f32)[0], 16)
    w_ada = np.asarray(w_ada, f32)
    b_ada = np.asarray(b_ada, f32)
    ins = []
    for i in range(NCORES):
        ins.append(dict(cT=cT, b=np.ascontiguousarray(b_ada[:, 1536 * i:1536 * (i + 1)]).reshape(1, 6144),
                        w=np.ascontiguousarray(w_ada[:, :, 1536 * i:1536 * (i + 1)])))
    r = run(nc_mod, ins)
    mod = np.concatenate([r[i]['out'].reshape(4, 1536) for i in range(NCORES)], axis=1)
    if DEBUG:
        ref = np.asarray(c, f32) @ w_ada[0] + b_ada[0]
        print("mod err", np.abs(mod[0] - ref[0]).max(), np.abs(ref).max())

    xT = [np.ascontiguousarray(x[0, i * TT:(i + 1) * TT].T) for i in range(NCORES)]
    nc_proj = _mk('proj', build_proj)
    nc_attn = _mk('attn', build_attn)
    nc_dense = _mk('dense', lambda: build_dense(True))
    nc_last = _mk('dense_last', lambda: build_dense(False))
    t5 = np.asarray(t5_bias, f32)

    def proj_params(l):
        wl = np.asarray(w_in[l], f32)
        dq = np.asarray(diff_qk_g[l], f32)
        mg_ = np.asarray(mla_qk_g[l], f32)
        mqk = np.zeros((128, 4), f32)
        mqk[:, 0] = mg_[0, :128]
        mqk[:, 1] = mg_[1, :128]
        mqk[:64, 2] = mg_[0, 128:]
        mqk[:64, 3] = mg_[1, 128:]
        wq = np.asarray(w_q_up[l], f32).reshape(512, 8, 192)
        wkv = np.asarray(w_kv_up[l], f32).reshape(256, 8, 256)
        return dict(w_in=np.ascontiguousarray(wl[:, :6976]),
                    dqk=np.stack([np.tile(dq[0], 2), np.tile(dq[1], 2)], axis=1),
                    gq=colvec(mla_q_norm_g[l], 4), gkv=colvec(mla_kv_norm_g[l], 2), mqk=mqk,
                    wqup=np.ascontiguousarray(np.concatenate([wq[:, :, :128].reshape(512, 1024), wq[:, :, 128:].reshape(512, 512)], axis=1)),
                    wkvup=np.ascontiguousarray(np.concatenate([wkv[:, :, :128].reshape(256, 1024), wkv[:, :, 128:].reshape(256, 1024)], axis=1)))

    def modcols(l):
        return np.ascontiguousarray(mod[l].reshape(6, 16, 128).transpose(2, 0, 1))

    pp = proj_params(0)
    ins = []
    for i in range(NCORES):
        ins.append(dict(xT=xT[i], ones=ones, bd64=PC[i]['bd64'], rotm=PC[i]['rotm'], modc=modcols(0), gmix=colvec(norm_mix_g[0], 16),
                        cosT=PC[i]['cosT'], sinT=PC[i]['sinT'], **pp))
    rp = run(nc_proj, ins)
    for l in range(4):
        lam_init = 0.8 - 0.6 * np.exp(-0.3 * l)
        cst = np.broadcast_to(np.array([lam_init, 1 - lam_init, 0, 0], f32), (128, 4)).copy()
        lamp = np.broadcast_to(np.asarray(diff_lambda[l], f32).reshape(1, 256), (128, 256)).copy()
        gsub = np.asarray(diff_subln_g[l], f32).reshape(128, 1).copy()
        kbr = np.concatenate([rp[i]['kbr'] for i in range(NCORES)], axis=1)
        ins = []
        for h in range(NCORES):
            dd = dict(ones=ones, mtri=AC['mtri'], biasA=np.ascontiguousarray(t5[AC['bucket'], h]), maskB=AC['maskB'], maskC=AC['maskC'], ident=AC['ident'],
                      lamp=lamp, cst=cst, b15=np.full((128, 1), t5[15, h], f32), gsub=gsub, kbr=kbr)
            for n in ['qa', 'ka', 'qbn', 'kbn', 'qc', 'kc', 'qbr']:
                dd[n] = np.concatenate([rp[i][n][h] for i in range(NCORES)], axis=1)
            for n in ['va', 'vb', 'vc']:
                dd[n] = np.concatenate([rp[i][n][:, 128 * h:128 * (h + 1)] for i in range(NCORES)], axis=0)
            ins.append(dd)
        ra = run(nc_attn, ins)
        wl = np.asarray(w_in[l], f32)
        w_gate = np.ascontiguousarray(wl[:, 6976:])
        last = (l == 3)
        pp = {} if last else proj_params(l + 1)
        ins = []
        for i in range(NCORES):
            br = np.stack([np.stack([ra[h][n][:, i * TT:(i + 1) * TT] for h in range(NCORES)]) for n in ['oa', 'ob', 'oc']])
            dd = dict(hT=rp[i]['hT'], xT=xT[i], w_gate=w_gate, w_branch=np.asarray(w_branch[l], f32), br=br,
                      w_out=np.asarray(w_out[l], f32), w_mlp_in=np.asarray(w_mlp_in[l], f32), w_mlp_out=np.asarray(w_mlp_out[l], f32),
                      ones=ones, modc=modcols(l), gmlp=colvec(norm_mlp_g[l], 16))
            if not last:
                dd.update(dict(modc2=modcols(l + 1), gmix2=colvec(norm_mix_g[l + 1], 16), bd64=PC[i]['bd64'], rotm=PC[i]['rotm'],
                               cosT=PC[i]['cosT'], sinT=PC[i]['sinT'], **pp))
            ins.append(dd)
        rd = run(nc_last if last else nc_dense, ins)
        xT = [rd[i]['xo'] for i in range(NCORES)]
        if not last:
            rp = [dict(rd[i], hT=rd[i]['hTo']) for i in range(NCORES)]
    out = np.concatenate([xT[i].T for i in range(NCORES)], axis=0)[None]
    return np.ascontiguousarray(out.astype(f32))


def _rms(x, g):
    return x / np.sqrt((x * x).mean(-1, keepdims=True) + 1e-6) * g


def _err(name, got, ref):
    got = np.asarray(got).astype(np.float64)
    ref = np.asarray(ref).astype(np.float64)
    print("  DBG %-6s rel %.2e  max %.3e / %.3e" % (name, np.sqrt(((got - ref) ** 2).mean() / ((ref ** 2).mean() + 1e-30)), np.abs(got - ref).max(), np.abs(ref).max()), flush=True)


def debug_proj(x, mod, norm_mix_g, wl, dq, rp, mla_q_norm_g, mla_kv_norm_g, w_q_up, w_kv_up, mla_qk_g):
    T0 = 128
    xs = x[0, :T0].astype(np.float64)
    sh1, sc1 = mod[0, :2048], mod[0, 2048:4096]
    h = _rms(xs, np.asarray(norm_mix_g[0], np.float64)) * (1 + sc1) + sh1
    pr = h @ wl.astype(np.float64)
    r = rp[0]
    f = lambda a: np.asarray(a).astype(np.float32)
    qa = _rms(pr[:, 0:1024].reshape(T0, 8, 2, 64), dq[0]).reshape(T0, 8, 128)
    _err('qa', f(r['qa'])[:, :, :T0].transpose(2, 0, 1), qa)
    ka = _rms(pr[:, 1024:2048].reshape(T0, 8, 2, 64), dq[1]).reshape(T0, 8, 128)
    _err('ka', f(r['ka'])[:, :, :T0].transpose(2, 0, 1), ka)
    _err('va', f(r['va'])[:T0], pr[:, 2048:3072])
    _err('qc', f(r['qc'])[:, :, :T0].transpose(2, 0, 1), pr[:, 3904:4928].reshape(T0, 8, 128) * 128 ** -0.5)
    _err('kc', f(r['kc'])[:, :, :T0].transpose(2, 0, 1), pr[:, 4928:5952].reshape(T0, 8, 128))
    _err('vc', f(r['vc'])[:T0], pr[:, 5952:6976])
    cq = _rms(pr[:, 3072:3584], np.asarray(mla_q_norm_g[0], np.float64))
    ckv = _rms(pr[:, 3584:3840], np.asarray(mla_kv_norm_g[0], np.float64))
    kpe = pr[:, 3840:3904]
    q = (cq @ np.asarray(w_q_up[0], np.float64)).reshape(T0, 8, 192)
    kv = (ckv @ np.asarray(w_kv_up[0], np.float64)).reshape(T0, 8, 256)
    g = np.asarray(mla_qk_g[0], np.float64)
    pos = np.arange(T0)
    inv = 10000.0 ** (-np.arange(32) / 32)
    ang = pos[:, None] * inv[None]
    cos, sin = np.cos(ang)[:, None], np.sin(ang)[:, None]

    def rope(v):
        x1, x2 = v[..., :32], v[..., 32:]
        return np.concatenate([x1 * cos - x2 * sin, x1 * sin + x2 * cos], -1)
    _err('qbn', f(r['qbn'])[:, :, :T0].transpose(2, 0, 1), _rms(q[..., :128], g[0, :128]))
    _err('qbr', f(r['qbr'])[:, :, :T0].transpose(2, 0, 1), rope(_rms(q[..., 128:], g[0, 128:])))
    _err('kbn', f(r['kbn'])[:, :, :T0].transpose(2, 0, 1), _rms(kv[..., :128], g[1, :128]))
    _err('kbr', f(r['kbr'])[:, :T0].T, rope(_rms(kpe.reshape(T0, 1, 64), g[1, 128:]))[:, 0])
    _err('vb', f(r['vb'])[:T0].reshape(T0, 8, 128), kv[..., 128:])


def debug_post(x, mod, norm_mix_g, norm_mlp_g, wl, w_branch, w_out, w_mlp_in, w_mlp_out, ra, rm, rx):
    T0 = 128
    f64 = np.float64
    xs = x[0, :T0].astype(f64)
    sh1, sc1, g1, sh2, sc2, g2 = [mod[0, i * 2048:(i + 1) * 2048].astype(f64) for i in range(6)]
    h = _rms(xs, np.asarray(norm_mix_g[0], f64)) * (1 + sc1) + sh1
    gates = h @ wl[:, 6976:].astype(f64)
    br = np.stack([np.concatenate([np.asarray(ra[hh][n]).astype(np.float32)[:, :T0].T for hh in range(8)], axis=1) for n in ['oa', 'ob', 'oc']]).astype(f64)
    merged = 0
    for i in range(3):
        up = br[i] @ np.asarray(w_branch[0][i], f64)
        merged = merged + 1 / (1 + np.exp(-gates[:, i * 2048:(i + 1) * 2048])) * up
    _err('mg', np.asarray(rm[0]['mg']).astype(np.float32)[:, :T0].T, merged)
    x1 = xs + g1 * (merged @ np.asarray(w_out[0], f64))
    h2 = _rms(x1, np.asarray(norm_mlp_g[0], f64)) * (1 + sc2) + sh2
    hid = np.maximum(h2 @ np.asarray(w_mlp_in[0], f64), 0) ** 2
    x2 = x1 + g2 * (hid @ np.asarray(w_mlp_out[0], f64))
    _err('xo', rx[0]['xo'][:, :T0].T, x2)
    _err('dx', rx[0]['xo'][:, :T0].T - xs, x2 - xs)
```
